# Optimizing a Trainium2 kernel written in Bass

```python
import math
import jax, jax.numpy as jnp
from jax import lax
import numpy as np

D_MODEL = 1024
BATCH = 16
SEQ = 256
DEPTH = 4
DEC_BATCH = 8
DEC_SEQ = 2048
PAST_LEN = 512

GRID_W = 64
EPS = 1e-6
CONV_W = 4
CHUNK = 128
Q_BLOCK = 128
N_BRANCH = 3
BRANCH_W = D_MODEL
SSD_HEADS = 16
SSD_HEAD_DIM = BRANCH_W // SSD_HEADS
SSD_INNER = SSD_HEADS * SSD_HEAD_DIM
SSD_GROUPS = 4
SSD_STATE = 64
ATT_HEADS = 16
ATT_HEAD_DIM = BRANCH_W // ATT_HEADS
ATT_KV_HEADS = 4
ROPE_THETA = 10000.0
ML_HEADS = 8
ML_HEAD_DIM = BRANCH_W // ML_HEADS
ML_INNER = ML_HEADS * ML_HEAD_DIM
FFN_HIDDEN = ((8 * D_MODEL // 3 + 255) // 256) * 256
IN_SIZES = (
    SSD_INNER, SSD_INNER, SSD_GROUPS * SSD_STATE, SSD_GROUPS * SSD_STATE, 2 * SSD_HEADS,
    ATT_HEADS * ATT_HEAD_DIM, ATT_KV_HEADS * ATT_HEAD_DIM, ATT_KV_HEADS * ATT_HEAD_DIM,
    ML_INNER, ML_INNER, ML_INNER, ML_INNER, 4 * ML_HEADS,
    N_BRANCH * D_MODEL,
)
IN_DIM = sum(IN_SIZES)

kernel_name = 'hybrid_prefix_diffusion_step'


def rms_norm(x, gain=None):
    xf = x.astype(jnp.float32)
    y = xf * lax.rsqrt(jnp.mean(xf * xf, axis=-1, keepdims=True) + EPS)
    if gain is not None:
        y = y * gain.astype(jnp.float32)
    return y.astype(x.dtype)


def flip(t):
    return jnp.flip(t, axis=1)


def dwconv(x, w, b):
    left = (CONV_W - 1) // 2
    right = CONV_W - 1 - left
    out = lax.conv_general_dilated(x, w[:, None, :], window_strides=(1,), padding=[(left, right)],
                                   dimension_numbers=('NWC', 'WIO', 'NWC'),
                                   feature_group_count=x.shape[-1])
    return out + b


def rope_2d(x):
    L, d = x.shape[1], x.shape[-1]
    rows_n = L // GRID_W
    rows = jnp.repeat(jnp.arange(rows_n), GRID_W)
    cols = jnp.tile(jnp.arange(GRID_W), rows_n)
    half = d // 2
    nf = half // 2
    freqs = ROPE_THETA ** (-jnp.arange(nf, dtype=jnp.float32) / nf)

    def rot(xa, pos):
        ang = pos.astype(jnp.float32)[:, None] * freqs
        cos = jnp.cos(ang)[None, :, None, :]
        sin = jnp.sin(ang)[None, :, None, :]
        x1, x2 = xa[..., :nf], xa[..., nf:]
        return jnp.concatenate([x1 * cos - x2 * sin, x1 * sin + x2 * cos], axis=-1)

    xf = x.astype(jnp.float32)
    return jnp.concatenate([rot(xf[..., :half], rows), rot(xf[..., half:], cols)], axis=-1).astype(x.dtype)


def attention(q, k, v):
    b, Lq = q.shape[0], q.shape[1]
    nb = Lq // Q_BLOCK
    grp = ATT_HEADS // ATT_KV_HEADS
    qb = jnp.moveaxis(q.reshape(b, nb, Q_BLOCK, ATT_KV_HEADS, grp, ATT_HEAD_DIM), 1, 0)
    scale = ATT_HEAD_DIM ** -0.5

    def block(qblk):
        s = jnp.einsum('bqkgd,bskd->bkgqs', qblk, k).astype(jnp.float32) * scale
        pr = jax.nn.softmax(s, axis=-1).astype(v.dtype)
        return jnp.einsum('bkgqs,bskd->bqkgd', pr, v)

    o = lax.map(block, qb)
    return jnp.moveaxis(o, 0, 1).reshape(b, Lq, ATT_HEADS * ATT_HEAD_DIM)


def ssd_scan(x, dt, a, bm, cm, h0):
    b, L = x.shape[0], x.shape[1]
    nc = L // CHUNK
    mask = jnp.tril(jnp.ones((CHUNK, CHUNK), bool))[None, :, :, None]

    def chunks(t):
        return jnp.swapaxes(t.reshape((b, nc, CHUNK) + t.shape[2:]), 0, 1)

    def body(h, xs):
        xc, dc, bc, cc = xs
        acum = jnp.cumsum(dc * a, axis=1)
        seg = jnp.where(mask, acum[:, :, None, :] - acum[:, None, :, :], -jnp.inf)
        scores = jnp.einsum('bihn,bjhn->bijh', cc, bc) * jnp.exp(seg) * dc[:, None, :, :]
        y = (jnp.einsum('bijh,bjhp->bihp', scores, xc)
             + jnp.einsum('bihn,bhpn->bihp', cc, h) * jnp.exp(acum)[..., None])
        w_end = jnp.exp(acum[:, -1:, :] - acum) * dc
        h_new = (jnp.exp(acum[:, -1, :])[:, :, None, None] * h
                 + jnp.einsum('bjh,bjhn,bjhp->bhpn', w_end, bc, xc))
        return h_new, y

    h_fin, ys = lax.scan(body, h0, (chunks(x), chunks(dt), chunks(bm), chunks(cm)))
    return jnp.swapaxes(ys, 0, 1).reshape(x.shape), h_fin


def mlstm_scan(q, k, v, log_i, log_f, c0, n0, m0):
    b, L = q.shape[0], q.shape[1]
    nc = L // CHUNK
    mask = jnp.tril(jnp.ones((CHUNK, CHUNK), bool))[None, :, :, None]

    def chunks(t):
        return jnp.swapaxes(t.reshape((b, nc, CHUNK) + t.shape[2:]), 0, 1)

    def body(carry, xs):
        cp, npv, mp = carry
        qc, kc, vc, ic, fc = xs
        bcum = jnp.cumsum(fc, axis=1)
        dmat = jnp.where(mask, bcum[:, :, None, :] - bcum[:, None, :, :] + ic[:, None, :, :], -jnp.inf)
        m_t = jnp.maximum(bcum + mp[:, None, :], jnp.max(dmat, axis=2))
        s = jnp.einsum('bthd,bjhd->btjh', qc, kc) * jnp.exp(dmat - m_t[:, :, None, :])
        inter = jnp.exp(bcum + mp[:, None, :] - m_t)
        num = (jnp.einsum('btjh,bjhe->bthe', s, vc)
               + inter[..., None] * jnp.einsum('bthd,bhde->bthe', qc, cp))
        den = jnp.sum(s, axis=2) + inter * jnp.einsum('bthd,bhd->bth', qc, npv)
        h = num / jnp.maximum(jnp.abs(den), jnp.exp(-m_t))[..., None]
        b_end = bcum[:, -1, :]
        wj = b_end[:, None, :] - bcum + ic
        m_new = jnp.maximum(b_end + mp, jnp.max(wj, axis=1))
        ew = jnp.exp(wj - m_new[:, None, :])
        sc = jnp.exp(b_end + mp - m_new)
        c_new = sc[..., None, None] * cp + jnp.einsum('bjh,bjhd,bjhe->bhde', ew, kc, vc)
        n_new = sc[..., None] * npv + jnp.einsum('bjh,bjhd->bhd', ew, kc)
        return (c_new, n_new, m_new), h

    fin, hs = lax.scan(body, (c0, n0, m0), (chunks(q), chunks(k), chunks(v), chunks(log_i), chunks(log_f)))
    return jnp.swapaxes(hs, 0, 1).reshape(v.shape), fin


def mix(xm, p, ctx):
    b, L, _ = xm.shape
    dtype = xm.dtype
    f32 = jnp.float32
    u = xm @ p['w_in']
    split_at = [int(i) for i in np.cumsum(IN_SIZES)[:-1]]
    (s_x, s_z, s_b, s_c, s_dt, a_q, a_k, a_v,
     m_q, m_k, m_v, m_o, m_g, g) = jnp.split(u, split_at, axis=-1)

    xbc = jax.nn.silu(dwconv(jnp.concatenate([s_x, s_b, s_c], axis=-1), p['ssd_conv_w'], p['ssd_conv_b']))
    sx, sb, sc = jnp.split(xbc, [SSD_INNER, SSD_INNER + SSD_GROUPS * SSD_STATE], axis=-1)
    rep = SSD_HEADS // SSD_GROUPS
    sx = sx.reshape(b, L, SSD_HEADS, SSD_HEAD_DIM).astype(f32)
    sb = jnp.repeat(sb.reshape(b, L, SSD_GROUPS, SSD_STATE), rep, axis=2).astype(f32)
    sc = jnp.repeat(sc.reshape(b, L, SSD_GROUPS, SSD_STATE), rep, axis=2).astype(f32)
    dt = jax.nn.softplus(s_dt.reshape(b, L, 2, SSD_HEADS).astype(f32) + p['ssd_dt_bias'].astype(f32))
    a = -jnp.exp(p['ssd_a_log'].astype(f32))
    if ctx is None:
        h0 = jnp.zeros((b, 2, SSD_HEADS, SSD_HEAD_DIM, SSD_STATE), f32)
    else:
        h0 = ctx[2].astype(f32)
    y_f, hs_f = ssd_scan(sx, dt[:, :, 0], a[0], sb, sc, h0[:, 0])
    y_b, hs_b = ssd_scan(flip(sx), flip(dt[:, :, 1]), a[1], flip(sb), flip(sc), h0[:, 1])
    y_ssd = y_f + flip(y_b) + p['ssd_d'].astype(f32)[:, None] * sx
    y_ssd = y_ssd.reshape(b, L, SSD_INNER).astype(dtype) * jax.nn.silu(s_z)
    y_ssd = rms_norm(y_ssd, p['ssd_norm'])

    q = rms_norm(a_q.reshape(b, L, ATT_HEADS, ATT_HEAD_DIM), p['att_q_norm'])
    k = rms_norm(a_k.reshape(b, L, ATT_KV_HEADS, ATT_HEAD_DIM), p['att_k_norm'])
    v = a_v.reshape(b, L, ATT_KV_HEADS, ATT_HEAD_DIM)
    if ctx is None:
        y_att = attention(q, k, v)
    else:
        y_att = attention(rope_2d(q), jnp.concatenate([ctx[0].astype(dtype), rope_2d(k)], axis=1),
                          jnp.concatenate([ctx[1].astype(dtype), v], axis=1))

    qk = jax.nn.silu(dwconv(jnp.concatenate([m_q, m_k], axis=-1), p['ml_conv_w'], p['ml_conv_b']))
    mq, mk = jnp.split(qk, 2, axis=-1)
    shp = (b, L, ML_HEADS, ML_HEAD_DIM)
    mq = mq.reshape(shp).astype(f32)
    mk = mk.reshape(shp).astype(f32) * (ML_HEAD_DIM ** -0.5)
    mv = m_v.reshape(shp).astype(f32)
    pre = m_g.reshape(b, L, 2, 2, ML_HEADS).astype(f32) + p['ml_gate_bias'].astype(f32)
    log_i = pre[:, :, :, 0]
    log_f = jax.nn.log_sigmoid(pre[:, :, :, 1])
    if ctx is None:
        c0 = jnp.zeros((b, 2, ML_HEADS, ML_HEAD_DIM, ML_HEAD_DIM), f32)
        n0 = jnp.zeros((b, 2, ML_HEADS, ML_HEAD_DIM), f32)
        m0 = jnp.zeros((b, 2, ML_HEADS), f32)
    else:
        c0, n0, m0 = [t.astype(f32) for t in ctx[3:]]
    h_f, st_f = mlstm_scan(mq, mk, mv, log_i[:, :, 0], log_f[:, :, 0], c0[:, 0], n0[:, 0], m0[:, 0])
    h_b, st_b = mlstm_scan(flip(mq), flip(mk), flip(mv), flip(log_i[:, :, 1]), flip(log_f[:, :, 1]),
                           c0[:, 1], n0[:, 1], m0[:, 1])
    h_ml = rms_norm(h_f + flip(h_b)).reshape(b, L, ML_INNER) * p['ml_norm'].astype(f32)
    y_ml = h_ml.astype(dtype) * jax.nn.sigmoid(m_o)

    gate = jax.nn.sigmoid(g.reshape(b, L, N_BRANCH, D_MODEL))
    branches = jnp.stack([y_ssd, y_att, y_ml], axis=2)
    proj = jnp.einsum('blnc,ncd->blnd', branches, p['w_branch'])
    out = jnp.sum(gate * proj, axis=2) @ p['w_out']
    if ctx is None:
        new_ctx = (k, v, jnp.stack([hs_f, hs_b], axis=1),
                   jnp.stack([st_f[0], st_b[0]], axis=1),
                   jnp.stack([st_f[1], st_b[1]], axis=1),
                   jnp.stack([st_f[2], st_b[2]], axis=1))
    else:
        new_ctx = None
    return out, new_ctx


def layer(x, mod, p, ctx):
    shift1, scale1, gate1, shift2, scale2, gate2 = jnp.split(mod.astype(x.dtype), 6, axis=-1)
    h = rms_norm(x) * (1 + scale1) + shift1
    out, new_ctx = mix(h, p, ctx)
    x = x + gate1 * out
    h = rms_norm(x) * (1 + scale2) + shift2
    ab = h @ p['w_ffn_in']
    a_, b_ = jnp.split(ab, 2, axis=-1)
    x = x + gate2 * ((jax.nn.silu(a_) * b_) @ p['w_ffn_out'])
    return x, new_ctx


def setup_inputs(seed: int = 0) -> dict:
    key = jax.random.key(seed)
    ks = jax.random.split(key, 32)
    f32 = jnp.float32

    def nrm(i, shape, scale=1.0):
        return jax.random.normal(ks[i], shape, f32) * scale

    def gain(i, shape):
        return 1.0 + 0.02 * jax.random.normal(ks[i], shape, f32)

    dt0 = jnp.exp(jax.random.uniform(ks[14], (DEPTH, 2, SSD_HEADS), f32, math.log(1e-3), math.log(1e-1)))
    ssd_dt_bias = dt0 + jnp.log(-jnp.expm1(-dt0))
    ssd_a_log = jnp.log(jax.random.uniform(ks[15], (DEPTH, 2, SSD_HEADS), f32, 1.0, 16.0))
    f_bias = 3.0 + 3.0 * jax.random.uniform(ks[23], (DEPTH, 2, ML_HEADS), f32)
    ml_gate_bias = jnp.stack([nrm(22, (DEPTH, 2, ML_HEADS), 0.1), f_bias], axis=2)
    return {
        'x_prompt': nrm(0, (BATCH, SEQ, D_MODEL)),
        'x_sample': nrm(1, (DEC_BATCH, DEC_SEQ, D_MODEL)),
        'cache_k': nrm(2, (DEC_BATCH, DEPTH, PAST_LEN, ATT_KV_HEADS, ATT_HEAD_DIM)),
        'cache_v': nrm(3, (DEC_BATCH, DEPTH, PAST_LEN, ATT_KV_HEADS, ATT_HEAD_DIM)),
        'state_ssd': nrm(4, (DEC_BATCH, DEPTH, 2, SSD_HEADS, SSD_HEAD_DIM, SSD_STATE), 0.5),
        'state_ml_c': nrm(5, (DEC_BATCH, DEPTH, 2, ML_HEADS, ML_HEAD_DIM, ML_HEAD_DIM), 0.1),
        'state_ml_n': nrm(6, (DEC_BATCH, DEPTH, 2, ML_HEADS, ML_HEAD_DIM), 0.1),
        'state_ml_m': nrm(7, (DEC_BATCH, DEPTH, 2, ML_HEADS)),
        'c': nrm(8, (DEC_BATCH, D_MODEL)),
        'c_ctx': nrm(9, (D_MODEL,)),
        'w_ada': nrm(10, (DEPTH, D_MODEL, 6 * D_MODEL), 0.5 * D_MODEL ** -0.5),
        'b_ada': nrm(11, (DEPTH, 6 * D_MODEL), 0.02),
        'w_in': nrm(12, (DEPTH, D_MODEL, IN_DIM), D_MODEL ** -0.5),
        'ssd_conv_w': nrm(13, (DEPTH, CONV_W, SSD_INNER + 2 * SSD_GROUPS * SSD_STATE), CONV_W ** -0.5),
        'ssd_conv_b': nrm(16, (DEPTH, SSD_INNER + 2 * SSD_GROUPS * SSD_STATE), 0.02),
        'ssd_a_log': ssd_a_log,
        'ssd_dt_bias': ssd_dt_bias,
        'ssd_d': gain(17, (DEPTH, SSD_HEADS)),
        'ssd_norm': gain(18, (DEPTH, SSD_INNER)),
        'att_q_norm': gain(19, (DEPTH, ATT_HEAD_DIM)),
        'att_k_norm': gain(20, (DEPTH, ATT_HEAD_DIM)),
        'ml_conv_w': nrm(21, (DEPTH, CONV_W, 2 * ML_INNER), CONV_W ** -0.5),
        'ml_conv_b': nrm(24, (DEPTH, 2 * ML_INNER), 0.02),
        'ml_gate_bias': ml_gate_bias,
        'ml_norm': gain(25, (DEPTH, ML_INNER)),
        'w_branch': nrm(26, (DEPTH, N_BRANCH, BRANCH_W, D_MODEL), BRANCH_W ** -0.5),
        'w_out': nrm(27, (DEPTH, D_MODEL, D_MODEL), D_MODEL ** -0.5),
        'w_ffn_in': nrm(28, (DEPTH, D_MODEL, 2 * FFN_HIDDEN), D_MODEL ** -0.5),
        'w_ffn_out': nrm(29, (DEPTH, FFN_HIDDEN, D_MODEL), FFN_HIDDEN ** -0.5),
        'final_norm': gain(30, (D_MODEL,)),
    }


def reference(x_prompt, x_sample, cache_k, cache_v, state_ssd, state_ml_c, state_ml_n, state_ml_m,
              c, c_ctx, w_ada, b_ada, w_in, ssd_conv_w, ssd_conv_b, ssd_a_log, ssd_dt_bias, ssd_d,
              ssd_norm, att_q_norm, att_k_norm, ml_conv_w, ml_conv_b, ml_gate_bias, ml_norm,
              w_branch, w_out, w_ffn_in, w_ffn_out, final_norm):
    y_p, y_s = x_prompt, x_sample
    ks, vs, ssd_st, ml_c, ml_n, ml_m = [], [], [], [], [], []
    for l in range(DEPTH):
        p = {'w_in': w_in[l], 'ssd_conv_w': ssd_conv_w[l], 'ssd_conv_b': ssd_conv_b[l],
             'ssd_a_log': ssd_a_log[l], 'ssd_dt_bias': ssd_dt_bias[l], 'ssd_d': ssd_d[l],
             'ssd_norm': ssd_norm[l], 'att_q_norm': att_q_norm[l], 'att_k_norm': att_k_norm[l],
             'ml_conv_w': ml_conv_w[l], 'ml_conv_b': ml_conv_b[l], 'ml_gate_bias': ml_gate_bias[l],
             'ml_norm': ml_norm[l], 'w_branch': w_branch[l], 'w_out': w_out[l],
             'w_ffn_in': w_ffn_in[l], 'w_ffn_out': w_ffn_out[l]}
        mod_ctx = (jax.nn.silu(c_ctx) @ w_ada[l] + b_ada[l])[None, None, :]
        mod_lat = (jax.nn.silu(c) @ w_ada[l] + b_ada[l])[:, None, :]
        y_p, nc = layer(y_p, mod_ctx, p, None)
        ks.append(nc[0]); vs.append(nc[1]); ssd_st.append(nc[2])
        ml_c.append(nc[3]); ml_n.append(nc[4]); ml_m.append(nc[5])
        y_s, _ = layer(y_s, mod_lat, p, (cache_k[:, l], cache_v[:, l], state_ssd[:, l],
                                         state_ml_c[:, l], state_ml_n[:, l], state_ml_m[:, l]))
    dt = x_prompt.dtype
    y_prompt = rms_norm(y_p, final_norm)
    y_sample = rms_norm(y_s, final_norm)
    new_cache_k = jnp.stack(ks, axis=1).astype(dt)
    new_cache_v = jnp.stack(vs, axis=1).astype(dt)
    new_state_ssd = jnp.stack(ssd_st, axis=1).astype(dt)
    new_state_ml_c = jnp.stack(ml_c, axis=1).astype(dt)
    new_state_ml_n = jnp.stack(ml_n, axis=1).astype(dt)
    new_state_ml_m = jnp.stack(ml_m, axis=1).astype(dt)
    return (y_prompt, y_sample, new_cache_k, new_cache_v, new_state_ssd, new_state_ml_c, new_state_ml_n, new_state_ml_m)
```

```python
import contextlib
import os
import numpy as np
import concourse.bass as bass
import concourse.mybir as mybir
from concourse.bass_utils import run_bass_kernel_spmd

F32 = mybir.dt.float32
BF16 = mybir.dt.bfloat16
AF = mybir.ActivationFunctionType
ALU = mybir.AluOpType
AX = mybir.AxisListType

D = 1024
KC = 8
EPS = 1e-6
IN_DIM = 11328
FFN_H = 2816
O_SX, O_SZ, O_SB, O_SC, O_SDT = 0, 1024, 2048, 2304, 2560
O_AQ, O_AK, O_AV = 2592, 3616, 3872
O_MQ, O_MK, O_MV, O_MO, O_MG, O_G = 4128, 5152, 6176, 7200, 8224, 8256
P_DTB, P_ALOG, P_SD, P_SSDN, P_MLN, P_MGB, P_TOT = 0, 32, 64, 96, 1120, 2144, 2176
Q_BADA, Q_SCW, Q_SCB, Q_MCW, Q_MCB, Q_QG, Q_KG, Q_TOT = 0, 48, 96, 108, 172, 188, 189, 192
C_ID, C_ONE, C_U, C_L, C_BLK, C_SWP, C_NMF, C_NMB, C_TOT = 0, 128, 256, 384, 512, 640, 768, 896, 1024
NEG = -30000.0
MLK_SCALE = 128.0 ** -0.5


class Buf:
    __slots__ = ("name", "w", "r", "dsem", "dcnt", "last_dma", "excl")
    registry = []

    def __init__(self, name, excl=False):
        self.name = name
        self.excl = excl
        self.w = None
        self.r = []
        self.dsem = None
        self.dcnt = 0
        self.last_dma = None
        Buf.registry.append(self)


class Op:
    __slots__ = ("eng", "fn", "deps", "needs", "event", "isdma", "slot")


class Prog:
    def __init__(self, nc, es):
        self.nc = nc
        self.es = es
        self.ops = []
        self.eng = {"pe": nc.tensor, "act": nc.scalar, "dve": nc.vector, "pool": nc.gpsimd, "sp": nc.sync}
        self.esem = {k: es.enter_context(nc.semaphore("sem_" + k)) for k in self.eng}
        self.nsem = len(self.eng)
        self.dma_bufs = []
        self.free_sems = {"sp": [], "pool": []}
        self.phase_bufs = []
        self.phase_pools = []
        self.seq = {k: 0 for k in self.eng}
        self.waited = {k: {} for k in self.eng}
        self.n_emitted = 0

    def _deps(self, op, reads, writes):
        deps = {}
        wset = set(id(b) for b in writes)
        for b in reads:
            if b.w is not None:
                deps[id(b.w)] = (b.w, True)
            if b.excl:
                for r in b.r:
                    if r.eng != op.eng and id(r) not in deps:
                        deps[id(r)] = (r, False)
        for b in writes:
            if b.w is not None and id(b.w) not in deps:
                deps[id(b.w)] = (b.w, False)
            for r in b.r:
                if id(r) not in deps:
                    deps[id(r)] = (r, False)
        out = []
        for d, raw in deps.values():
            if d is op:
                continue
            if (not d.isdma) and (not op.isdma) and d.eng == op.eng and op.eng == "pe":
                continue
            d.needs = True
            out.append(d)
        for b in writes:
            b.w = op
            b.r = []
        for b in reads:
            if id(b) not in wset:
                b.r.append(op)
        return out

    def add(self, eng, fn, reads=(), writes=()):
        op = Op()
        op.eng, op.fn, op.needs, op.event, op.isdma, op.slot = eng, fn, False, None, False, None
        op.deps = self._deps(op, list(reads), list(writes))
        self.ops.append(op)
        return op

    def dma(self, q, out, in_, reads, writes, slot):
        nc = self.nc
        op = Op()
        op.eng, op.needs, op.isdma, op.slot = q, True, True, slot
        if slot.dsem is None:
            slot.dsem, slot.dcnt, slot.last_dma = {}, {}, {}
        if q not in slot.dsem:
            if self.free_sems[q]:
                slot.dsem[q], slot.dcnt[q] = self.free_sems[q].pop()
            else:
                slot.dsem[q] = self.es.enter_context(nc.semaphore("dsem_%s_%d" % (q, self.nsem)))
                slot.dcnt[q] = 0
                self.nsem += 1
                assert self.nsem <= 96, "too many semaphores"
            self.dma_bufs.append((slot, q))
        slot.dcnt[q] += 16
        op.event = (slot.dsem[q], slot.dcnt[q])
        e = self.eng[q]
        op.fn = lambda: e.dma_start(out=out, in_=in_)
        op.deps = self._deps(op, list(reads), list(writes))
        ld = slot.last_dma.get(q)
        if ld is not None and ld not in op.deps:
            op.deps.append(ld)
        slot.last_dma[q] = op
        self.ops.append(op)
        return op

    def flush(self):
        pend = set(id(o) for o in self.ops)
        for b in Buf.registry:
            if b.w is not None and id(b.w) in pend:
                b.w.needs = True
            for r in b.r:
                if id(r) in pend:
                    r.needs = True
        last = {}
        for o in self.ops:
            if not o.isdma:
                last[o.eng] = o
        for o in last.values():
            o.needs = True
        seq, waited = self.seq, self.waited
        for op in self.ops:
            e = self.eng[op.eng]
            wd = waited[op.eng]
            for d in op.deps:
                sem, val = d.event
                if wd.get(id(sem), 0) < val:
                    e.wait_ge(sem, val)
                    wd[id(sem)] = val
            ins = op.fn()
            if op.isdma:
                ins.then_inc(op.event[0], 16)
            elif op.needs:
                seq[op.eng] += 1
                ins.then_inc(self.esem[op.eng], 1)
                op.event = (self.esem[op.eng], seq[op.eng])
            op.fn = None
        self.n_emitted += len(self.ops)
        self.ops = []
        evs = [(self.esem[k], seq[k], k) for k in self.eng if seq[k] > 0]
        evs += [(b.dsem[q], b.dcnt[q], None) for b, q in self.dma_bufs]
        for k, e in self.eng.items():
            wd = waited[k]
            for sem, val, owner in evs:
                if owner == k and k != "pool":
                    continue
                if wd.get(id(sem), 0) < val:
                    e.wait_ge(sem, val)
                    wd[id(sem)] = val
        ph = set(id(b) for b in self.phase_bufs)
        for b in self.phase_bufs:
            if b.dsem is not None:
                for q in b.dsem:
                    self.free_sems[q].append((b.dsem[q], b.dcnt[q]))
                b.dsem = None
        self.dma_bufs = [(b, q) for b, q in self.dma_bufs if id(b) not in ph]
        Buf.registry = [b for b in Buf.registry if id(b) not in ph]
        self.phase_bufs = []
        for p_ in self.phase_pools:
            p_.dead = True
        self.phase_pools = []


class TPool:
    def __init__(self, P, nc, es, name, shape, dtype, n, phase=True):
        self.t = []
        for i in range(n):
            t = es.enter_context(nc.sbuf_tensor("%s_%d" % (name, i), list(shape), dtype))
            b = Buf("%s_%d" % (name, i))
            if phase:
                P.phase_bufs.append(b)
            self.t.append((t, b))
        self.i = 0
        self.dead = False
        self.name = name
        if phase:
            P.phase_pools.append(self)

    def get(self):
        assert not self.dead, "stale pool " + self.name
        r = self.t[self.i % len(self.t)]
        self.i += 1
        return r


def build(SB, NL, debug=False, stop_after=None):
    NBLK = SB + 2
    NCH = 2 * NBLK
    T = 256 * NBLK
    SL = 256 * SB
    NCTX = 512
    SEQS = [(0, SB, True, 1), (SB, 1, False, 0), (SB + 1, 1, False, 0)]
    hbase = [0, SL + 3, SL + 3 + 259]
    NCOL = SL + 3 + 259 * 2

    def blk_seq(b):
        return 0 if b < SB else (1 if b == SB else 2)

    def hcol(b):
        s = blk_seq(b)
        return hbase[s] + 256 * (b - SEQS[s][0])

    nc = bass.Bass("TRN2", target_bir_lowering=False)
    es = contextlib.ExitStack()
    P = Prog(nc, es)

    def din(name, shape, dt=F32):
        return nc.dram_tensor(name, list(shape), dt, kind="ExternalInput").ap()

    def dout(name, shape, dt=F32):
        return nc.dram_tensor(name, list(shape), dt, kind="ExternalOutput").ap()

    def dscr(name, shape, dt):
        return nc.dram_tensor(name, list(shape), dt, kind=("ExternalOutput" if debug else "Internal")).ap()

    x0 = din("x0", [128, KC, T])
    cvec = din("cvec", [128, KC, 2])
    w_ada = din("w_ada", [NL, D, 6 * D])
    w_in = din("w_in", [NL, D, IN_DIM])
    w_br = din("w_branch", [NL, 3, D, D])
    w_out = din("w_out", [NL, D, D])
    w_f1 = din("w_ffn_in", [NL, D, 2 * FFN_H])
    w_f2 = din("w_ffn_out", [NL, FFN_H, D])
    prep = din("prep", [NL, 128, P_TOT])
    pfm = din("pfm", [128, NL, Q_TOT])
    fng = din("fng", [128, KC])
    cst = din("cst", [128, C_TOT])
    cstf = din("cstf", [128, 128])
    ropec = din("ropec", [128, SL])
    ropes = din("ropes", [128, SL])
    cache_k = din("cache_k", [NL, NCTX, 256])
    cache_v = din("cache_v", [NL, NCTX, 256])
    ssd0 = din("ssd0", [NL, 2, 128, 512])
    mlc0 = din("mlc0", [NL, 2, 128, 8 * 129])
    mlm0 = din("mlm0", [NL, 2, 8, 1])

    yT = dout("yT", [128, KC, T])
    nk_o = dout("nk_o", [2, NL, 128, 2, 256])
    nv_o = dout("nv_o", [2, NL, 256, 256])
    nssd_o = dout("nssd_o", [2, NL, 2, 128, 512])
    nmlc_o = dout("nmlc_o", [2, NL, 2, 128, 8 * 129])
    nmlm_o = dout("nmlm_o", [2, NL, 2, 8, 1])

    xres = dscr("xres", [NBLK, 128, KC, 256], F32)
    sx_tok = dscr("sx_tok", [NCH, 128, 1024], BF16)
    sbcT = dscr("sbcT", [NBLK, 128, 4, 256], BF16)
    sb_tok = dscr("sb_tok", [NCH, 128, 256], BF16)
    sz_tok = dscr("sz_tok", [NCH, 128, 1024], BF16)
    dtg_tok = dscr("dtg_tok", [NCH, 128, 64], F32)
    qT_d = dscr("qT_d", [NBLK, 128, 8, 256], BF16)
    kT_d = dscr("kT_d", [NBLK, 128, 4, 256], BF16)
    v_tok = dscr("v_tok", [NCH, 128, 256], BF16)
    mqT = dscr("mqT", [NBLK, 128, 8, 256], BF16)
    mkT = dscr("mkT", [NBLK, 128, 8, 256], BF16)
    mk_tok = dscr("mk_tok", [NCH, 128, 1024], BF16)
    mv_tok = dscr("mv_tok", [NCH, 128, 1024], BF16)
    mo_tok = dscr("mo_tok", [NCH, 128, 1024], BF16)
    gT = dscr("gT", [3, NBLK, 128, 8, 256], BF16)
    ybr = dscr("ybr", [3, NBLK, 128, 8, 256], BF16)
    yf_d = dscr("yf_d", [NCH, 128, 1024], F32)
    hf_d = dscr("hf_d", [NCH, 128, 1024], F32)
    mgd_d = dscr("mgd_d", [NBLK, 128, 8, 256], BF16)

    dbufs = {}

    def DB(*key):
        if key not in dbufs:
            dbufs[key] = Buf(str(key))
        return dbufs[key]

    cur = {"es": es, "n": 0}

    def sb(name, shape, dt=F32):
        ph = cur["es"] is not es
        cur["n"] += 1
        b = Buf(name)
        if ph:
            P.phase_bufs.append(b)
        t = cur["es"].enter_context(nc.sbuf_tensor("%s_%d" % (name, cur["n"]), list(shape), dt))
        return t, b

    def mkpool(name, shape, dt, n):
        cur["n"] += 1
        return TPool(P, nc, cur["es"], "%s%d" % (name, cur["n"]), shape, dt, n, phase=(cur["es"] is not es))


    hT, _ = sb("hT", [128, KC, NCOL], BF16)
    hB = [Buf("hT%d" % b) for b in range(NBLK)]
    hHalo = Buf("hHalo")
    cb, cbB = sb("cb", [128, C_TOT], BF16)
    cf, cfB = sb("cf", [128, 128], F32)
    onesf, onesfB = sb("onesf", [128, 128], F32)
    modT, modB = sb("modT", [128, NL, 2, 48], F32)
    pfm_s, pfmB = sb("pfm_s", [128, NL, Q_TOT], F32)
    fng_s, fngB = sb("fng_s", [128, KC], F32)
    ropec_s = ropecB = ropes_s = ropesB = None
    prep_t, prepB = sb("prep_t", [128, P_TOT], F32)

    ps_all = es.enter_context(nc.psum_tensor("ps_all", [128, 4096], F32))
    psB = [Buf("psb%d" % i, excl=True) for i in range(8)]
    ps_state = {"A": 0, "B": 0}

    def psum_at(i, n=1):
        return ps_all[:, i * 512:(i + n) * 512], psB[i:i + n]

    def psA():
        i = ps_state["A"] % 6
        ps_state["A"] += 1
        return psum_at(i)

    def psBr():
        i = 6 + ps_state["B"] % 2
        ps_state["B"] += 1
        return psum_at(i)

    ident = cb[:, C_ID:C_ID + 128]
    ones = cb[:, C_ONE:C_ONE + 128]
    Utri = cb[:, C_U:C_U + 128]
    Ltri = cb[:, C_L:C_L + 128]
    blk2 = cb[:, C_BLK:C_BLK + 128]
    pswap = cb[:, C_SWP:C_SWP + 128]
    nmask = [cb[:, C_NMF:C_NMF + 128], cb[:, C_NMB:C_NMB + 128]]
    tri = [Utri, Ltri]

    V, A, G, PE = nc.vector, nc.scalar, nc.gpsimd, nc.tensor

    def mm(out, lhsT, rhs, reads, writes, start=True, stop=True):
        P.add("pe", lambda: PE.matmul(out, lhsT=lhsT, rhs=rhs, start=start, stop=stop), reads, writes)

    def tp(out, in_, idn, reads, writes):
        P.add("pe", lambda: PE.transpose(out, in_, idn), reads, writes)

    def act(out, in_, func, reads, writes, bias=None, scale=None, accum=None):
        kw = {}
        if bias is not None:
            kw["bias"] = bias
        if scale is not None:
            kw["scale"] = scale
        if accum is not None:
            kw["accum_out"] = accum
        P.add("act", lambda: A.activation(out=out, in_=in_, func=func, **kw), reads, writes)

    def tt(eng, out, in0, in1, op, reads, writes):
        e = V if eng == "dve" else G
        P.add(eng, lambda: e.tensor_tensor(out=out, in0=in0, in1=in1, op=op), reads, writes)

    def ts(eng, out, in0, s1, op0, reads, writes, s2=None, op1=None):
        e = V if eng == "dve" else G
        if op1 is None:
            P.add(eng, lambda: e.tensor_scalar(out=out, in0=in0, scalar1=s1, scalar2=None, op0=op0), reads, writes)
        else:
            P.add(eng, lambda: e.tensor_scalar(out=out, in0=in0, scalar1=s1, scalar2=s2, op0=op0, op1=op1),
                  reads, writes)

    def stt(out, in0, scalar, in1, op0, op1, reads, writes):
        P.add("dve", lambda: V.scalar_tensor_tensor(out=out, in0=in0, scalar=scalar, in1=in1, op0=op0, op1=op1),
              reads, writes)

    def cp(eng, out, in_, reads, writes):
        if eng == "act":
            P.add("act", lambda: A.copy(out=out, in_=in_), reads, writes)
        else:
            e = V if eng == "dve" else G
            P.add(eng, lambda: e.tensor_copy(out=out, in_=in_), reads, writes)

    def memset(eng, ap, val, writes):
        e = V if eng == "dve" else G
        P.add(eng, lambda: e.memset(ap, val), (), writes)

    def rsqrt(v_ap, v_buf, shape2, mean_scale):
        p, n = shape2
        ts("dve", v_ap, v_ap, mean_scale, ALU.mult, [v_buf], [v_buf], s2=EPS, op1=ALU.add)
        act(v_ap, v_ap, AF.Ln, [v_buf], [v_buf])
        act(v_ap, v_ap, AF.Exp, [v_buf], [v_buf], scale=-0.5)

    def flat(ap3):
        return ap3.rearrange("p a b -> p (a b)")

    slab_p = None

    def alloc_slab(n=3):
        nonlocal slab_p
        slab_p = mkpool("slab", [128, KC, 1024], BF16, n)

    def load_slab(src2d, ncols, dst_off=0, slab=None):
        if slab is None:
            slab = slab_p.get()
        t, bfr = slab
        P.dma("pool", t[:, :, dst_off:dst_off + ncols], src2d.rearrange("(k p) n -> p k n", p=128), [], [bfr], bfr)
        return slab

    def startup():
        cst_st, cst_stB = sb("cst_st", [128, C_TOT], F32)
        P.dma("sp", cst_st[:], cst[:, :], [], [cst_stB], cst_stB)
        cp("dve", cb[:], cst_st[:], [cst_stB], [cbB])
        P.dma("sp", cf[:], cstf[:, :], [], [cfB], cfB)
        P.dma("sp", pfm_s[:], pfm[:, :, :], [], [pfmB], pfmB)
        P.dma("sp", fng_s[:], fng[:, :], [], [fngB], fngB)
        memset("dve", onesf[:], 1.0, [onesfB])
        memset("pool", flat(hT[:]), 0.0, hB + [hHalo])
        alloc_slab()
        cv, cvB = sb("cv", [128, KC, 2], F32)
        cvb, cvbB = sb("cvb", [128, KC, 2], BF16)
        P.dma("sp", cv[:], cvec[:, :, :], [], [cvB], cvB)
        act(flat(cvb[:]), flat(cv[:]), AF.Silu, [cvB], [cvbB])
        for l in range(NL):
            pm, pmB = psA()
            for s6 in range(6):
                wt, wB = load_slab(w_ada[l, :, s6 * 1024:(s6 + 1) * 1024], 1024)
                for f in range(8):
                    fc = s6 * 8 + f
                    for k in range(KC):
                        mm(pm[:, fc * 2:fc * 2 + 2], wt[:, k, f * 128:(f + 1) * 128], cvb[:, k, :],
                           [wB, cvbB], pmB, start=(k == 0), stop=(k == KC - 1))
            for m in range(2):
                tt("dve", modT[:, l, m, :], pm.rearrange("p (f m) -> p m f", m=2)[:, m, 0:48],
                   pfm_s[:, l, Q_BADA:Q_BADA + 48], ALU.add, pmB + [pfmB], [modB])
            for o in (8, 32):
                ts("dve", modT[:, l, :, o:o + 8], modT[:, l, :, o:o + 8], 1.0, ALU.add, [modB], [modB])
        alloc_norm()
        for b in range(NBLK):
            xt, xtB = xt_p.get()
            P.dma("sp", xt[:], x0[:, :, b * 256:(b + 1) * 256], [], [xtB], xtB)
            P.dma("sp", xres[b], xt[:], [xtB], [DB("x", b)], xtB)
            norm_block(xt, xtB, b, 0, 0)

    xt_p = sq_p = xn_p = rs_p = None

    def alloc_norm():
        nonlocal xt_p, sq_p, xn_p, rs_p
        xt_p = mkpool("xt", [128, KC, 256], F32, 2)
        sq_p = mkpool("sq", [128, KC, 256], BF16, 2)
        xn_p = mkpool("xn", [128, KC, 256], F32, 2)
        rs_p = mkpool("rs", [128, 256], F32, 2)


    def norm_block(xt, xtB, b, l, which, final=False, out_tile=None):
        m = SEQS[blk_seq(b)][3]
        sq, sqB = sq_p.get()
        act(flat(sq[:]), flat(xt[:]), AF.Square, [xtB], [sqB])
        pn, pnB = psBr()
        for k in range(KC):
            mm(pn[:, 0:256], ones, sq[:, k, :], [cbB, sqB], pnB, start=(k == 0), stop=(k == KC - 1))
        rs, rsB = rs_p.get()
        cp("dve", rs[:], pn[:, 0:256], pnB, [rsB])
        rsqrt(rs[:], rsB, (128, 256), 1.0 / D)
        xn, xnB = xn_p.get()
        tt("dve", xn[:], xt[:], rs[:].unsqueeze(1).broadcast_to([128, KC, 256]), ALU.mult, [xtB, rsB], [xnB])
        if final:
            ot, otB = out_tile
            for k in range(KC):
                ts("pool", ot[:, k, :], xn[:, k, :], fng_s[:, k:k + 1], ALU.mult, [xnB, fngB], [otB])
            return
        c0 = hcol(b) + 1
        so, sh = (8, 0) if which == 0 else (32, 24)
        for k in range(KC):
            ts("pool", hT[:, k, c0:c0 + 256], xn[:, k, :], modT[:, l, m, so + k:so + k + 1], ALU.mult,
               [xnB, modB], [hB[b]], s2=modT[:, l, m, sh + k:sh + k + 1], op1=ALU.add)

    def hwin_bufs(b):
        s = blk_seq(b)
        f, n = SEQS[s][0], SEQS[s][1]
        r = [hB[b], hHalo]
        if b > f:
            r.append(hB[b - 1])
        if b < f + n - 1:
            r.append(hB[b + 1])
        return r

    st_p = acc_p = cvo_p = sm_p = cv8_p = qr_p = sq4_p = dg_p = u_p = None

    def alloc_d1():
        nonlocal st_p, acc_p, cvo_p, sm_p, cv8_p, ropec_s, ropecB, ropes_s, ropesB, qr_p, sq4_p, dg_p, u_p
        dg_p = mkpool("dg", [128, 4, 128], BF16, 16)
        u_p = mkpool("ub", [128, 260], BF16, 4)
        dg_live.clear()
        qr_p = mkpool("qr", [128, 4, 256], F32, 6)
        sq4_p = mkpool("sq4", [128, 4, 256], BF16, 4)
        ropec_s, ropecB = sb("ropec_s", [128, SL], F32)
        ropes_s, ropesB = sb("ropes_s", [128, SL], F32)
        P.dma("sp", ropec_s[:], ropec[:, :], [], [ropecB], ropecB)
        P.dma("sp", ropes_s[:], ropes[:, :], [], [ropesB], ropesB)
        st_p = mkpool("st", [128, 2048], BF16, 6)
        acc_p = None
        cvo_p = mkpool("cvo", [128, 256], BF16, 4)
        sm_p = mkpool("sm", [128, 256], F32, 4)
        cv8_p = None


    def proj_fm(wt, wB, wcol, b, n, halo):
        pp, ppB = psA()
        c0 = hcol(b) + (0 if halo else 1)
        rb = hwin_bufs(b) if halo else [hB[b]]
        for k in range(KC):
            mm(pp[:, 0:n], wt[:, k, wcol:wcol + 128], hT[:, k, c0:c0 + n], [wB] + rb, ppB,
               start=(k == 0), stop=(k == KC - 1))
        return pp, ppB

    def proj_tm(wt, wB, wcol, ncol, c):
        b, cc = c // 2, c % 2
        pp, ppB = psA()
        c0 = hcol(b) + 1 + 128 * cc
        for k in range(KC):
            mm(pp[:, 0:ncol], hT[:, k, c0:c0 + 128], wt[:, k, wcol:wcol + ncol], [wB, hB[b]], ppB,
               start=(k == 0), stop=(k == KC - 1))
        return pp, ppB

    dg_live = {}

    def conv_block(wt, wB, b, nfc, l, wofs, bofs, fcbase, stv, stB):
        def conv_b(ub, ubB, fci, dst):
            key = (l, wofs, fci)
            if key not in dg_live:
                dg, dgB = dg_p.get()
                for t in range(4):
                    ts("dve", dg[:, t, :], ident, pfm_s[:, l, wofs + fci * 4 + t:wofs + fci * 4 + t + 1], ALU.mult,
                       [cbB, pfmB], [dgB])
                dg_live[key] = (dg, dgB)
            dg, dgB = dg_live[key]
            pc, pcB = psBr()
            for t in range(4):
                mm(pc[:, 0:256], dg[:, t, :], ub[:, t:t + 256], [dgB, ubB], pcB, start=(t == 0), stop=(t == 3))
            act(dst, pc[:, 0:256], AF.Silu, pcB, [stB], bias=pfm_s[:, l, bofs + fci:bofs + fci + 1])

        pend = None
        for fc in range(nfc):
            pp, ppB = proj_fm(wt, wB, fc * 128, b, 259, True)
            ub, ubB = u_p.get()
            cp("act", ub[:, 0:259], pp[:, 0:259], ppB, [ubB])
            if pend is not None:
                conv_b(*pend)
            pend = (ub, ubB, fcbase + fc, stv[:, fc, :])
        conv_b(*pend)

    def transposes_to(dst, dstB, srcs, reads, scale=None, bank=None):
        n = len(srcs)
        pt, ptB = (psBr() if bank is None else psum_at(bank))
        ptb = pt.bitcast(BF16)
        for i, s_ in enumerate(srcs):
            tp(ptb[:, i * 128:(i + 1) * 128], s_, ident, reads + [cbB], ptB)
        if scale is None:
            cp("act", dst, ptb[:, 0:128 * n], ptB, [dstB])
        else:
            ts("dve", dst, ptb[:, 0:128 * n], scale, ALU.mult, ptB, [dstB])

    D1STOP = int(os.environ.get("K_D1STOP", "99"))

    def d1_layer(l):
        W = w_in[l]

        def ld_simple(off, n):
            return lambda: load_slab(W[:, off:off + n], n)

        def ld_dtg():
            slab = slab_p.get()
            load_slab(W[:, O_SDT:O_SDT + 32], 32, 0, slab)
            return load_slab(W[:, O_MG:O_MG + 32], 32, 32, slab)

        def ld_ak():
            slab = slab_p.get()
            load_slab(W[:, O_AK:O_AK + 256], 256, 0, slab)
            r = None
            for f in range(2):
                load_slab(W[:, O_AK + f * 128 + 64:O_AK + f * 128 + 128], 64, 256 + f * 128, slab)
                r = load_slab(W[:, O_AK + f * 128:O_AK + f * 128 + 64], 64, 256 + f * 128 + 64, slab)
            return r

        loaders = [ld_simple(O_SX, 1024), ld_simple(O_SB, 512), ld_simple(O_SZ, 1024), ld_dtg,
                   ld_simple(O_AQ, 1024), ld_ak, ld_simple(O_AV, 256), ld_simple(O_MQ, 1024),
                   ld_simple(O_MK, 1024), ld_simple(O_MV, 1024), ld_simple(O_MO, 1024),
                   ld_simple(O_G, 1024), ld_simple(O_G + 1024, 1024), ld_simple(O_G + 2048, 1024)]
        loaded = {}

        def get_w(i):
            for k in range(i + 2):
                if k < len(loaders) and k not in loaded:
                    loaded[k] = loaders[k]()
            return loaded[i]

        wt, wB = get_w(0)
        for b in range(NBLK):
            sv, svB = st_p.get()
            svv = sv[:, 0:2048].rearrange("p (f t) -> p f t", f=8)
            conv_block(wt, wB, b, 8, l, Q_SCW, Q_SCB, 0, svv, svB)
            for cc in range(2):
                st, stB = st_p.get()
                transposes_to(st[:, 0:1024], stB, [svv[:, f, cc * 128:(cc + 1) * 128] for f in range(8)], [svB])
                P.dma("sp", sx_tok[2 * b + cc], st[:, 0:1024], [stB], [DB("sx", 2 * b + cc)], stB)
        if D1STOP <= 1:
            return
        wt, wB = get_w(1)
        for b in range(NBLK):
            st, stB = st_p.get()
            stv = st[:, 0:1024].rearrange("p (f t) -> p f t", f=4)
            conv_block(wt, wB, b, 4, l, Q_SCW, Q_SCB, 8, stv, stB)
            P.dma("sp", sbcT[b], stv, [stB], [DB("sbcT", b)], stB)
            for cc in range(2):
                s2, s2B = st_p.get()
                transposes_to(s2[:, 0:256], s2B, [stv[:, f, cc * 128:(cc + 1) * 128] for f in range(2)], [stB])
                P.dma("sp", sb_tok[2 * b + cc], s2[:, 0:256], [s2B], [DB("sbt", 2 * b + cc)], s2B)
        if D1STOP <= 2:
            return
        wt, wB = get_w(2)
        for c in range(NCH):
            st, stB = st_p.get()
            for hh in range(2):
                pp, ppB = proj_tm(wt, wB, hh * 512, 512, c)
                act(st[:, hh * 512:(hh + 1) * 512], pp[:, 0:512], AF.Silu, ppB, [stB])
            P.dma("sp", sz_tok[c], st[:, 0:1024], [stB], [DB("sz", c)], stB)
        if D1STOP <= 3:
            return
        wt, wB = get_w(3)
        for c in range(NCH):
            sm, smB = sm_p.get()
            pp, ppB = proj_tm(wt, wB, 0, 64, c)
            cp("dve", sm[:, 0:64], pp[:, 0:64], ppB, [smB])
            P.dma("sp", dtg_tok[c], sm[:, 0:64], [smB], [DB("dtg", c)], smB)
        if D1STOP <= 4:
            return
        def qk_group(wt, wB, f0, b, gofs, stv, stB, nk_out=False):
            n = 4
            qr, qrB = qr_p.get()
            sq, sqB = sq4_p.get()
            rs, rsB = qr_p.get()
            for i in range(n):
                pp, ppB = proj_fm(wt, wB, (f0 + i) * 128, b, 256, False)
                act(sq[:, i, :], pp[:, 0:256], AF.Square, ppB, [sqB])
                cp("dve", qr[:, i, :], pp[:, 0:256], ppB, [qrB])
            for i in range(n):
                pn, pnB = psBr()
                mm(pn[:, 0:256], blk2, sq[:, i, :], [cbB, sqB], pnB)
                ts("dve", rs[:, i, :], pn[:, 0:256], 1.0 / 64, ALU.mult, pnB, [rsB], s2=EPS, op1=ALU.add)
            act(flat(rs[:]), flat(rs[:]), AF.Ln, [rsB], [rsB])
            act(flat(rs[:]), flat(rs[:]), AF.Exp, [rsB], [rsB], scale=-0.5)
            stt(qr[:], qr[:], pfm_s[:, l, gofs:gofs + 1], rs[:], ALU.mult, ALU.mult, [qrB, pfmB, rsB], [qrB])
            s_ = blk_seq(b)
            if nk_out and s_ != 0:
                P.dma("sp", nk_o[s_ - 1, l], qr[:, 0:2, :], [qrB], [DB("nk", s_, l)], qrB)
            if s_ != 0:
                cp("act", stv[:, f0:f0 + n, :], qr[:], [qrB], [stB])
                return
            qb, qbB = sq4_p.get()
            cp("act", flat(qb[:]), flat(qr[:]), [qrB], [qbB])
            a2, a2B = qr_p.get()
            t0 = 256 * b
            for i in range(n):
                pw, pwB = psBr()
                mm(pw[:, 0:256], pswap, qb[:, i, :], [cbB, qbB], pwB)
                tt("dve", a2[:, i, :], pw[:, 0:256], ropes_s[:, t0:t0 + 256], ALU.mult, pwB + [ropesB], [a2B])
            tt("dve", qr[:], qr[:], ropec_s[:, t0:t0 + 256].unsqueeze(1).broadcast_to([128, n, 256]), ALU.mult,
               [qrB, ropecB], [qrB])
            tt("pool", stv[:, f0:f0 + n, :], qr[:], a2[:], ALU.add, [qrB, a2B], [stB])

        wt, wB = get_w(4)
        for b in range(NBLK):
            st, stB = st_p.get()
            stv = st[:, 0:2048].rearrange("p (f t) -> p f t", f=8)
            for f0 in (0, 4):
                qk_group(wt, wB, f0, b, Q_QG, stv, stB)
            P.dma("sp", qT_d[b], stv, [stB], [DB("qT", b)], stB)
        if D1STOP <= 5:
            return
        wt, wB = get_w(5)
        for b in range(NBLK):
            st, stB = st_p.get()
            stv = st[:, 0:1024].rearrange("p (f t) -> p f t", f=4)
            qk_group(wt, wB, 0, b, Q_KG, stv, stB, nk_out=True)
            P.dma("sp", kT_d[b], stv, [stB], [DB("kT", b)], stB)
        if D1STOP <= 6:
            return
        wt, wB = get_w(6)
        for c in range(NCH):
            st, stB = st_p.get()
            pp, ppB = proj_tm(wt, wB, 0, 256, c)
            cp("act", st[:, 0:256], pp[:, 0:256], ppB, [stB])
            P.dma("sp", v_tok[c], st[:, 0:256], [stB], [DB("v", c)], stB)
            s_ = blk_seq(c // 2)
            if s_ != 0:
                sm, smB = sm_p.get()
                cp("dve", sm[:], pp[:, 0:256], ppB, [smB])
                t0 = (c % 2) * 128
                P.dma("sp", nv_o[s_ - 1, l, t0:t0 + 128, :], sm[:], [smB], [DB("nv", s_, l, c)], smB)
        if D1STOP <= 7:
            return
        for which, off, dstT in ((0, O_MQ, mqT), (1, O_MK, mkT)):
            wt, wB = get_w(7 + which)
            for b in range(NBLK):
                st, stB = st_p.get()
                stv = st[:, 0:2048].rearrange("p (f t) -> p f t", f=8)
                conv_block(wt, wB, b, 8, l, Q_MCW, Q_MCB, which * 8, stv, stB)
                P.dma("sp", dstT[b], stv, [stB], [DB("mqT" if which == 0 else "mkT", b)], stB)
                if which == 1:
                    for cc in range(2):
                        s2, s2B = st_p.get()
                        transposes_to(s2[:, 0:1024], s2B, [stv[:, f, cc * 128:(cc + 1) * 128] for f in range(8)],
                                      [stB], scale=MLK_SCALE)
                        P.dma("sp", mk_tok[2 * b + cc], s2[:, 0:1024], [s2B], [DB("mkt", 2 * b + cc)], s2B)
        if D1STOP <= 8:
            return
        for wi_, (off, dstT, key, fn) in enumerate(((O_MV, mv_tok, "mv", None), (O_MO, mo_tok, "mo", AF.Sigmoid))):
            wt, wB = get_w(9 + wi_)
            for c in range(NCH):
                st, stB = st_p.get()
                for hh in range(2):
                    pp, ppB = proj_tm(wt, wB, hh * 512, 512, c)
                    if fn is None:
                        cp("act", st[:, hh * 512:(hh + 1) * 512], pp[:, 0:512], ppB, [stB])
                    else:
                        act(st[:, hh * 512:(hh + 1) * 512], pp[:, 0:512], fn, ppB, [stB])
                P.dma("sp", dstT[c], st[:, 0:1024], [stB], [DB(key, c)], stB)
        if D1STOP <= 9:
            return
        for n in range(3):
            wt, wB = get_w(11 + n)
            for b in range(NBLK):
                st, stB = st_p.get()
                stv = st[:, 0:2048].rearrange("p (f t) -> p f t", f=8)
                for fc in range(8):
                    pp, ppB = proj_fm(wt, wB, fc * 128, b, 256, False)
                    act(stv[:, fc, :], pp[:, 0:256], AF.Sigmoid, ppB, [stB])
                P.dma("sp", gT[n, b], stv, [stB], [DB("gT", n, b)], stB)

    nsb = sb

    dtg_s = dtgB = g_dt = g_dtB = g_da = g_daB = g_dab = g_dabB = g_a = g_aB = g_acum = g_acumB = g_tot = g_totB = g_ea = g_eaB = g_w = g_wB = g_edec = g_edecB = g_tmp = g_tmpB = None

    def alloc_scan():
        nonlocal dtg_s, dtgB, g_dt, g_dtB, g_da, g_daB, g_dab, g_dabB, g_a, g_aB, g_acum, g_acumB, g_tot, g_totB, g_ea, g_eaB, g_w, g_wB, g_edec, g_edecB, g_tmp, g_tmpB
        dtg_s, dtgB = nsb("dtg_s", [128, NCH, 64])
        g_dt, g_dtB = nsb("g_dt", [128, NCH, 32])
        g_da, g_daB = nsb("g_da", [128, NCH, 32])
        g_dab, g_dabB = nsb("g_dab", [128, NCH, 32], BF16)
        g_a, g_aB = nsb("g_a", [128, 32])
        g_acum, g_acumB = nsb("g_acum", [128, 2, NCH, 16])
        g_tot, g_totB = nsb("g_tot", [128, 2, NCH, 16])
        g_ea, g_eaB = nsb("g_ea", [128, 2, NCH, 16])
        g_w, g_wB = nsb("g_w", [128, 2, NCH, 16])
        g_edec, g_edecB = nsb("g_edec", [128, 2, NCH, 8])
        g_tmp, g_tmpB = nsb("g_tmp", [128, 2, NCH, 16])


    def run_sweeps(chunk_loads, chunkA, chunkB, seq_begin, seq_end, PF=2):
        steps = []
        for s in range(3):
            n = SEQS[s][1]
            orders = [chunk_order(s, 0), chunk_order(s, 1)]
            for i in range(2 * n):
                for d in range(2):
                    steps.append((s, d, orders[d][i], i >= n))
        loaded, fronts = {}, {}
        for k in range(min(PF, len(steps))):
            loaded[k] = chunk_loads(*steps[k])
        fronts[0] = chunkA(*steps[0], loaded[0])
        for k, st in enumerate(steps):
            if k + PF < len(steps):
                loaded[k + PF] = chunk_loads(*steps[k + PF])
            if k + 1 < len(steps):
                fronts[k + 1] = chunkA(*steps[k + 1], loaded[k + 1])
            if k == 0 or steps[k - 1][0] != st[0]:
                seq_begin(st[0])
            chunkB(*st, loaded.pop(k), fronts.pop(k))
            if k == len(steps) - 1 or steps[k + 1][0] != st[0]:
                seq_end(st[0])

    def chunk_order(s, d):
        f, n = SEQS[s][0], SEQS[s][1]
        cs = list(range(2 * f, 2 * (f + n)))
        return cs if d == 0 else cs[::-1]

    xk_p = xk2_p = bt_p = bct_p = big_p = arg_p = yo_p = tok_p = ytT_p = s1_p = None

    def alloc_mix(ssd=True):
        nonlocal xk_p, xk2_p, bt_p, bct_p, big_p, arg_p, yo_p, tok_p, ytT_p, s1_p, cvo_p
        cvo_p = mkpool("cvo", [128, 256], BF16, 6)
        xk_p = mkpool("xk", [128, 1024], BF16, 5 if ssd else 8)
        xk2_p = mkpool("xk2", [128, 1024], BF16, 6)
        if ssd:
            bt_p = mkpool("bt", [128, 256], BF16, 5)
            bct_p = mkpool("bct", [128, 4, 128], BF16, 5)
            big_p = mkpool("big", [128, 2048], BF16, 6)
            arg_p = mkpool("arg", [128, 2048], F32, 2)
        yo_p = mkpool("yo", [128, 1024], F32, 6 if ssd else 5)
        tok_p = mkpool("tok", [128, 1024], BF16, 6)
        ytT_p = mkpool("ytT", [128, 8, 128], BF16, 2)
        s1_p = mkpool("s1", [128, 8], F32, 10)


    def out_transposed(yn, ynB, br, c, bank=7):
        b, cc = c // 2, c % 2
        yt, ytB = ytT_p.get()
        transposes_to(flat(yt[:]), ytB, [yn[:, f * 128:(f + 1) * 128] for f in range(8)], [ynB], bank=bank)
        P.dma("sp", ybr[br, b, :, :, cc * 128:(cc + 1) * 128], yt[:], [ytB], [DB("ybr", br, b)], ytB)

    Hs = Hb = None

    def alloc_ssd():
        nonlocal Hs, Hb
        Hs = [nsb("Hs%d" % d, [128, 2, 256]) for d in range(2)]
        Hb = [nsb("Hb%d" % d, [128, 2, 256], BF16) for d in range(2)]


    def ssd_layer(l, pr, prB):
        P.dma("sp", dtg_s[:], dtg_tok.rearrange("c p n -> p c n"), [DB("dtg", c) for c in range(NCH)],
              [dtgB], dtgB)
        tt("dve", g_dt[:], dtg_s[:, :, 0:32], pr[:, P_DTB:P_DTB + 32].unsqueeze(1).broadcast_to([128, NCH, 32]),
           ALU.add, [dtgB, prB], [g_dtB])
        act(flat(g_dt[:]), flat(g_dt[:]), AF.Exp, [g_dtB], [g_dtB])
        act(flat(g_dt[:]), flat(g_dt[:]), AF.Ln, [g_dtB], [g_dtB], bias=1.0)
        act(g_a[:], pr[:, P_ALOG:P_ALOG + 32], AF.Exp, [prB], [g_aB])
        ts("dve", g_a[:], g_a[:], -1.0, ALU.mult, [g_aB], [g_aB])
        tt("dve", g_da[:], g_dt[:], g_a[:].unsqueeze(1).broadcast_to([128, NCH, 32]), ALU.mult,
           [g_dtB, g_aB], [g_daB])
        cp("dve", g_dab[:], g_da[:], [g_daB], [g_dabB])
        for d in range(2):
            pa, paB = psA()
            mm(pa[:, 0:NCH * 16].rearrange("p (c h) -> p c h", h=16), tri[d], g_dab[:, :, d * 16:(d + 1) * 16],
               [cbB, g_dabB], paB)
            cp("dve", g_acum[:, d], pa[:, 0:NCH * 16].rearrange("p (c h) -> p c h", h=16), paB, [g_acumB])
            pb, pbB = psA()
            mm(pb[:, 0:NCH * 16].rearrange("p (c h) -> p c h", h=16), ones, g_dab[:, :, d * 16:(d + 1) * 16],
               [cbB, g_dabB], pbB)
            cp("dve", g_tot[:, d], pb[:, 0:NCH * 16].rearrange("p (c h) -> p c h", h=16), pbB, [g_totB])
        fl4 = lambda t: t[:].rearrange("p d c h -> p (d c h)")
        act(fl4(g_ea), fl4(g_acum), AF.Exp, [g_acumB], [g_eaB])
        tt("dve", g_tmp[:], g_tot[:], g_acum[:], ALU.subtract, [g_totB, g_acumB], [g_tmpB])
        act(fl4(g_tmp), fl4(g_tmp), AF.Exp, [g_tmpB], [g_tmpB])
        for d in range(2):
            tt("dve", g_w[:, d], g_tmp[:, d], g_dt[:, :, d * 16:(d + 1) * 16], ALU.mult, [g_tmpB, g_dtB], [g_wB])
        for d in range(2):
            tv = g_tot[:, d].rearrange("p c (g f r) -> p c g f r", g=2, f=2)
            for hf in range(2):
                ps_ = slice(hf * 64, (hf + 1) * 64)
                act(g_edec[ps_, d].rearrange("p c (g r) -> p c g r", g=2), tv[ps_, :, :, hf, :], AF.Exp,
                    [g_totB], [g_edecB])

        def chunk_loads(s, d, c, last_sweep):
            b, cc = c // 2, c % 2
            x, xB = xk_p.get()
            P.dma("sp", x[:], sx_tok[c], [DB("sx", c)], [xB], xB)
            bt, btB = bt_p.get()
            P.dma("sp", bt[:], sb_tok[c], [DB("sbt", c)], [btB], btB)
            bct, bctB = bct_p.get()
            P.dma("sp", bct[:], sbcT[b, :, :, cc * 128:(cc + 1) * 128], [DB("sbcT", b)], [bctB], bctB)
            z = zB = None
            if last_sweep:
                z, zB = tok_p.get()
                P.dma("sp", z[:], sz_tok[c], [DB("sz", c)], [zB], zB)
            return x, xB, bt, btB, bct, bctB, z, zB

        def chunk(s, d, c, last_sweep, tl):
            b, cc = c // 2, c % 2
            hs = slice(d * 16, (d + 1) * 16)
            H, HB_ = Hs[d]
            Hbf, HbB = Hb[d]
            x, xB, bt, btB, bct, bctB, z, zB = tl
            dau, dauB = big_p.get()
            dau3 = dau[:].rearrange("p (h i) -> p h i", h=16)
            tt("pool", dau3, tri[d].unsqueeze(1).broadcast_to([128, 16, 128]),
               g_dab[:, c, hs].unsqueeze(2).broadcast_to([128, 16, 128]), ALU.mult, [cbB, g_dabB], [dauB])
            sg, sgB = psum_at(0, 4)
            for q in range(4):
                mm(sg[:, q * 512:(q + 1) * 512], ones, dau[:, q * 512:(q + 1) * 512], [cbB, dauB], sgB,
                   start=True, stop=False)
                mm(sg[:, q * 512:(q + 1) * 512].rearrange("p (h i) -> p h i", h=4), ident,
                   nmask[d].unsqueeze(1).broadcast_to([128, 4, 128]), [cbB], sgB, start=False, stop=True)
            ar, arB = arg_p.get()
            tt("dve", ar[:].rearrange("p (h i) -> p h i", h=16), sg.rearrange("p (h i) -> p h i", h=16),
               g_acum[:, d, c, :].unsqueeze(2).broadcast_to([128, 16, 128]), ALU.subtract, sgB + [g_acumB], [arB])
            Lm, LmB = big_p.get()
            act(Lm[:], ar[:], AF.Exp, [arB], [LmB])
            pcs = [psum_at(4), psum_at(5)]
            for g in range(4):
                ps_ = slice((g % 2) * 64, (g % 2) * 64 + 64)
                pc, pcB = pcs[g % 2]
                mm(pc[:, (g // 2) * 128:(g // 2 + 1) * 128], bct[ps_, g // 2, :], bct[ps_, 2 + g // 2, :], [bctB], pcB)
            cbts = []
            for par in range(2):
                ct, ctB = cvo_p.get()
                cp("act", ct[:], pcs[par][0][:, 0:256], pcs[par][1], [ctB])
                cbts.append((ct, ctB))
            sc, scB = big_p.get()
            scv = sc[:].rearrange("p (gg two r i) -> p gg two r i", gg=2, two=2, r=4)
            Lmv = Lm[:].rearrange("p (gg two r i) -> p gg two r i", gg=2, two=2, r=4)
            for par, (ct, ctB) in enumerate(cbts):
                tt("pool", scv[:, :, par], Lmv[:, :, par],
                   ct[:].rearrange("p (g i) -> p g i", g=2).unsqueeze(2).broadcast_to([128, 2, 4, 128]),
                   ALU.mult, [LmB, ctB], [scB])
            return sc, scB

        def chunkB(s, d, c, last_sweep, tl, sa):
            b, cc = c // 2, c % 2
            hs = slice(d * 16, (d + 1) * 16)
            H, HB_ = Hs[d]
            Hbf, HbB = Hb[d]
            x, xB, bt, btB, bct, bctB, z, zB = tl
            sc, scB = sa
            xd, xdB = xk2_p.get()
            tt("dve", xd[:].rearrange("p (h q) -> p h q", h=16), x[:].rearrange("p (h q) -> p h q", h=16),
               g_dt[:, c, hs].unsqueeze(2).broadcast_to([128, 16, 64]), ALU.mult, [xB, g_dtB], [xdB])
            wx, wxB = xk2_p.get()
            tt("dve", wx[:].rearrange("p (h q) -> p h q", h=16), x[:].rearrange("p (h q) -> p h q", h=16),
               g_w[:, d, c, :].unsqueeze(2).broadcast_to([128, 16, 64]), ALU.mult, [xB, g_wB], [wxB])
            yi, yiB = psum_at(6, 2)
            for h in range(16):
                mm(yi[:, h * 64:(h + 1) * 64], sc[:, h * 128:(h + 1) * 128], xd[:, h * 64:(h + 1) * 64],
                   [scB, xdB], [yiB[h // 8]])
            ys, ysB = psum_at(4, 2)
            for g in range(4):
                ps_ = slice((g % 2) * 64, (g % 2) * 64 + 64)
                co = (g % 2) * 512 + (g // 2) * 256
                mm(ys[:, co:co + 256], bct[ps_, 2 + g // 2, :], Hbf[ps_, g // 2, :], [bctB, HbB], [ysB[g % 2]])
            yo, yoB = yo_p.get()
            yov = yo[:].rearrange("p (gg two r q) -> p gg two r q", gg=2, two=2, r=4)
            eav = g_ea[:, d, c, :].rearrange("p (gg two r) -> p gg two r", gg=2, two=2)
            for par in range(2):
                tt("dve", yov[:, :, par], ys[:, par * 512:(par + 1) * 512].rearrange("p (gg r q) -> p gg r q", gg=2, r=4),
                   eav[:, :, par].unsqueeze(3).broadcast_to([128, 2, 4, 64]), ALU.mult, [ysB[par], g_eaB], [yoB])
            tt("dve", yo[:], yo[:], yi, ALU.add, [yoB] + yiB, [yoB])
            dh, dhB = psum_at(4, 2)
            for gg in range(2):
                mm(dh[:, gg * 512:(gg + 1) * 512], bt[:, gg * 128:(gg + 1) * 128], wx[:, gg * 512:(gg + 1) * 512],
                   [btB, wxB], [dhB[gg]])
            tt("dve", H[:].rearrange("p g (r q) -> p g r q", r=4), H[:].rearrange("p g (r q) -> p g r q", r=4),
               g_edec[:, d, c, :].rearrange("p (g r) -> p g r", g=2).unsqueeze(3).broadcast_to([128, 2, 4, 64]),
               ALU.mult, [HB_, g_edecB], [HB_])
            dhv = dh.rearrange("p (g x) -> p g x", g=2)
            for hf in range(2):
                ps_ = slice(hf * 64, (hf + 1) * 64)
                tt("dve", H[ps_], H[ps_], dhv[ps_, :, hf * 256:(hf + 1) * 256], ALU.add, [HB_] + dhB, [HB_])
            cp("act", flat(Hbf[:]), flat(H[:]), [HB_], [HbB])
            if not last_sweep:
                P.dma("sp", yf_d[c], yo[:], [yoB], [DB("yf", c)], yoB)
                return
            yf, yfB = yo_p.get()
            P.dma("sp", yf[:], yf_d[c], [DB("yf", c)], [yfB], yfB)
            tt("dve", yo[:], yo[:], yf[:], ALU.add, [yoB, yfB], [yoB])
            xd2, xd2B = yo_p.get()
            tt("pool", xd2[:].rearrange("p (h q) -> p h q", h=16), x[:].rearrange("p (h q) -> p h q", h=16),
               pr[:, P_SD:P_SD + 16].unsqueeze(2).broadcast_to([128, 16, 64]), ALU.mult, [xB, prB], [xd2B])
            tt("dve", yo[:], yo[:], xd2[:], ALU.add, [yoB, xd2B], [yoB])
            tt("dve", yo[:], yo[:], z[:], ALU.mult, [yoB, zB], [yoB])
            s1, s1B = s1_p.get()
            act(yf[:], yo[:], AF.Square, [yoB], [yfB, s1B], accum=s1[:, 0:1])
            rsqrt(s1[:, 0:1], s1B, (128, 1), 1.0 / D)
            yn, ynB = tok_p.get()
            stt(yn[:], yo[:], s1[:, 0:1], pr[:, P_SSDN:P_SSDN + 1024], ALU.mult, ALU.mult, [yoB, s1B, prB], [ynB])
            out_transposed(yn, ynB, 0, c, bank=6)

        def seq_begin(s):
            has_ctx = SEQS[s][2]
            for d in range(2):
                H, HB_ = Hs[d]
                if has_ctx:
                    P.dma("sp", flat(H[:]), ssd0[l, d], [], [HB_], HB_)
                else:
                    memset("dve", flat(H[:]), 0.0, [HB_])
                cp("pool", Hb[d][0][:], H[:], [HB_], [Hb[d][1]])

        def seq_end(s):
            if not SEQS[s][2]:
                for d in range(2):
                    H, HB_ = Hs[d]
                    P.dma("sp", nssd_o[s - 1, l, d], flat(H[:]), [HB_], [DB("nssd", s, l, d)], HB_)

        run_sweeps(chunk_loads, chunk, chunkB, seq_begin, seq_end)

    CS = CSb = m_li = m_liB = m_lf = m_lfB = m_lfb = m_lfbB = m_b = m_bB = m_g = m_gB = m_wt = m_wtB = m_fl = m_flB = m_RB = m_RBB = m_SCB = m_SCBB = m_GM = m_GMB = m_BT = m_BTB = m_R = m_RB2 = m_MD = m_MDB = m_mp = m_mpB = m_RD = m_RDB = vp_p = mT_p = None

    def alloc_ml():
        nonlocal CS, CSb, m_li, m_liB, m_lf, m_lfB, m_lfb, m_lfbB, m_b, m_bB, m_g, m_gB, m_wt, m_wtB, m_fl, m_flB, m_RB, m_RBB, m_SCB, m_SCBB, m_GM, m_GMB, m_BT, m_BTB, m_R, m_RB2, m_MD, m_MDB, m_mp, m_mpB, m_RD, m_RDB, vp_p, mT_p, dtg_s, dtgB, g_da, g_daB
        dtg_s, dtgB = nsb("dtg_s", [128, NCH, 64])
        g_da, g_daB = nsb("g_da", [128, NCH, 32])
        CS = [nsb("CS%d" % d, [128, 8, 129]) for d in range(2)]
        CSb = [nsb("CSb%d" % d, [128, 8, 129], BF16) for d in range(2)]
        m_li, m_liB = nsb("m_li", [128, 2, NCH, 8])
        m_lf, m_lfB = nsb("m_lf", [128, 2, NCH, 8])
        m_lfb, m_lfbB = nsb("m_lfb", [128, 2, NCH, 8], BF16)
        m_b, m_bB = nsb("m_b", [128, 2, NCH, 8])
        m_g, m_gB = nsb("m_g", [128, 2, NCH, 8])
        m_wt, m_wtB = nsb("m_wt", [128, 2, NCH, 8])
        m_fl, m_flB = nsb("m_fl", [128, 2, NCH, 8])
        m_RB, m_RBB = nsb("m_RB", [128, 2, NCH, 8])
        m_SCB, m_SCBB = nsb("m_SCB", [128, 2, NCH, 8])
        m_GM, m_GMB = nsb("m_GM", [8, 2, NCH])
        m_BT, m_BTB = nsb("m_BT", [8, 2, NCH])
        m_R, m_RB2 = nsb("m_R", [8, 2, NCH])
        m_MD, m_MDB = nsb("m_MD", [8, 2, NCH])
        m_mp, m_mpB = nsb("m_mp", [8, 2, 3, NCH + 1])
        m_RD, m_RDB = nsb("m_RD", [8, 2, 2, NCH, 8])
        vp_p = mkpool("vp", [128, 8, 129], BF16, 4)
        mT_p = mkpool("mT", [128, 8, 128], BF16, 8)


    def ml_layer(l, pr, prB):
        pre = g_da
        preB = g_daB
        P.dma("sp", dtg_s[:], dtg_tok.rearrange("c p n -> p c n"), [DB("dtg", c) for c in range(NCH)],
              [dtgB], dtgB)
        tt("dve", pre[:], dtg_s[:, :, 32:64], pr[:, P_MGB:P_MGB + 32].unsqueeze(1).broadcast_to([128, NCH, 32]),
           ALU.add, [dtgB, prB], [preB])
        for d in range(2):
            cp("dve", m_li[:, d], pre[:, :, d * 16:d * 16 + 8], [preB], [m_liB])
            act(m_lf[:, d], pre[:, :, d * 16 + 8:d * 16 + 16], AF.Exp, [preB], [m_lfB], scale=-1.0)
        fl4 = lambda t: t[:].rearrange("p d c h -> p (d c h)")
        act(fl4(m_lf), fl4(m_lf), AF.Ln, [m_lfB], [m_lfB], bias=1.0)
        ts("dve", fl4(m_lf), fl4(m_lf), -1.0, ALU.mult, [m_lfB], [m_lfB])
        cp("dve", fl4(m_lfb), fl4(m_lf), [m_lfB], [m_lfbB])
        for d in range(2):
            pa, paB = psA()
            pav = pa[:, 0:NCH * 8].rearrange("p (c h) -> p c h", h=8)
            mm(pav, tri[d], m_lfb[:, d], [cbB, m_lfbB], paB)
            cp("dve", m_b[:, d], pav, paB, [m_bB])
        tt("dve", m_g[:], m_li[:], m_b[:], ALU.subtract, [m_liB, m_bB], [m_gB])
        for d in range(2):
            for c0 in range(0, NCH, 4):
                n = min(4, NCH - c0)
                pt, ptB = psA()
                for i in range(n):
                    tp(pt[0:8, i * 128:(i + 1) * 128], m_g[:, d, c0 + i, :], cf[:], [m_gB, cfB], ptB)
                P.add("dve", (lambda o=m_GM[:, d, c0:c0 + n], i_=pt[0:8, 0:n * 128].rearrange("p (c j) -> p c j", j=128):
                              V.tensor_reduce(out=o, in_=i_, axis=AX.X, op=ALU.max)), ptB, [m_GMB])
            pb, pbB = psBr()
            for c in range(NCH):
                mm(pb[0:8, c:c + 1], m_lfb[:, d, c, :], ones[:, 0:1], [m_lfbB, cbB], pbB)
            cp("dve", m_BT[:, d], pb[0:8, 0:NCH], pbB, [m_BTB])
        for d in range(2):
            for s in range(3):
                f, n, has_ctx, _ = SEQS[s]
                mp = m_mp[:, d, s]
                if has_ctx:
                    P.dma("sp", mp[:, 0:1], mlm0[l, d], [], [m_mpB], m_mpB)
                else:
                    memset("dve", mp[:, 0:1], 0.0, [m_mpB])
                for i, c in enumerate(chunk_order(s, d)):
                    tt("dve", m_R[:, d, c:c + 1], mp[:, i:i + 1], m_GM[:, d, c:c + 1], ALU.max,
                       [m_mpB, m_GMB], [m_RB2])
                    tt("dve", m_MD[:, d, c:c + 1], mp[:, i:i + 1], m_R[:, d, c:c + 1], ALU.subtract,
                       [m_mpB, m_RB2], [m_MDB])
                    tt("dve", mp[:, i + 1:i + 2], m_R[:, d, c:c + 1], m_BT[:, d, c:c + 1], ALU.add,
                       [m_RB2, m_BTB], [m_mpB])
                if not has_ctx:
                    P.dma("sp", nmlm_o[s - 1, l, d], mp[:, 2 * n:2 * n + 1], [m_mpB], [DB("nmlm", s, l, d)], m_mpB)
        for d in range(2):
            for wi, (src, srcB) in enumerate(((m_R, m_RB2), (m_MD, m_MDB))):
                tt("dve", m_RD[:, d, wi], src[:, d, :].unsqueeze(2).broadcast_to([8, NCH, 8]),
                   cf[0:8, 0:8].unsqueeze(1).broadcast_to([8, NCH, 8]), ALU.mult, [srcB, cfB], [m_RDB])
            pr_, prB_ = psBr()
            mm(pr_[:, 0:2 * NCH * 8], onesf[0:8, :], m_RD[:, d].rearrange("p w c h -> p (w c h)"),
               [onesfB, m_RDB], prB_)
            cp("dve", m_RB[:, d], pr_[:, 0:NCH * 8].rearrange("p (c h) -> p c h", h=8), prB_, [m_RBB])
            act(m_SCB[:, d], pr_[:, NCH * 8:2 * NCH * 8].rearrange("p (c h) -> p c h", h=8), AF.Exp, prB_, [m_SCBB])
        tt("dve", m_wt[:], m_g[:], m_RB[:], ALU.subtract, [m_gB, m_RBB], [m_wtB])
        act(fl4(m_wt), fl4(m_wt), AF.Exp, [m_wtB], [m_wtB])
        tt("dve", m_fl[:], m_b[:], m_RB[:], ALU.add, [m_bB, m_RBB], [m_flB])
        ts("dve", fl4(m_fl), fl4(m_fl), -1.0, ALU.mult, [m_flB], [m_flB], s2=80.0, op1=ALU.min)
        act(fl4(m_fl), fl4(m_fl), AF.Exp, [m_flB], [m_flB])

        def chunk_loads(s, d, c, last_sweep):
            b, cc = c // 2, c % 2
            q, qB = mT_p.get()
            P.dma("sp", q[:], mqT[b, :, :, cc * 128:(cc + 1) * 128], [DB("mqT", b)], [qB], qB)
            k, kB = mT_p.get()
            P.dma("sp", k[:], mkT[b, :, :, cc * 128:(cc + 1) * 128], [DB("mkT", b)], [kB], kB)
            kt, ktB = xk_p.get()
            P.dma("sp", kt[:], mk_tok[c], [DB("mkt", c)], [ktB], ktB)
            v, vB = xk_p.get()
            P.dma("sp", v[:], mv_tok[c], [DB("mv", c)], [vB], vB)
            mo = moB = None
            if last_sweep:
                mo, moB = tok_p.get()
                P.dma("sp", mo[:], mo_tok[c], [DB("mo", c)], [moB], moB)
            return q, qB, k, kB, kt, ktB, v, vB, mo, moB

        def chunk(s, d, c, last_sweep, tl):
            b, cc = c // 2, c % 2
            C, CB_ = CS[d]
            Cb, CbB = CSb[d]
            q, qB, k, kB, kt, ktB, v, vB, mo, moB = tl
            sc, scB = psum_at(0, 2)
            for h in range(8):
                mm(sc[:, h * 128:(h + 1) * 128], k[:, h, :], q[:, h, :], [kB, qB], [scB[h // 4]])
            sm_, smB_ = xk2_p.get()
            stt(sm_[:].rearrange("p (h t) -> p h t", h=8), sc.rearrange("p (h t) -> p h t", h=8), MLK_SCALE,
                tri[d].unsqueeze(1).broadcast_to([128, 8, 128]), ALU.mult, ALU.mult, scB + [cbB], [smB_])
            vp, vpB = vp_p.get()
            tt("pool", vp[:, :, 0:128], v[:].rearrange("p (h e) -> p h e", h=8),
               m_wt[:, d, c, :].unsqueeze(2).broadcast_to([128, 8, 128]), ALU.mult, [vB, m_wtB], [vpB])
            cp("pool", vp[:, :, 128:129], m_wt[:, d, c, :].unsqueeze(2), [m_wtB], [vpB])
            return sm_, smB_, vp, vpB

        def chunkB(s, d, c, last_sweep, tl, sa):
            b, cc = c // 2, c % 2
            C, CB_ = CS[d]
            Cb, CbB = CSb[d]
            q, qB, k, kB, kt, ktB, v, vB, mo, moB = tl
            sm_, smB_, vp, vpB = sa
            tt("dve", C[:], C[:], m_SCB[:, d, c, :].unsqueeze(2).broadcast_to([128, 8, 129]), ALU.mult,
               [CB_, m_SCBB], [CB_])
            cp("act", Cb[:].rearrange("p h e -> p (h e)"), C[:].rearrange("p h e -> p (h e)"), [CB_], [CbB])
            nm, nmB = psum_at(2, 2)
            dn, dnB = psum_at(4)
            for h in range(8):
                mm(nm[:, h * 128:(h + 1) * 128], sm_[:, h * 128:(h + 1) * 128], vp[:, h, 0:128], [smB_, vpB],
                   [nmB[h // 4]], start=True, stop=False)
                mm(nm[:, h * 128:(h + 1) * 128], q[:, h, :], Cb[:, h, 0:128], [qB, CbB], [nmB[h // 4]],
                   start=False, stop=True)
            for h in range(8):
                mm(dn[:, h:h + 1], sm_[:, h * 128:(h + 1) * 128], vp[:, h, 128:129], [smB_, vpB], dnB,
                   start=True, stop=False)
                mm(dn[:, h:h + 1], q[:, h, :], Cb[:, h, 128:129], [qB, CbB], dnB, start=False, stop=True)
            dc, dcB = psum_at(5, 2)
            for h in range(8):
                mm(dc[:, h * 128:(h + 1) * 128], kt[:, h * 128:(h + 1) * 128], vp[:, h, 0:128], [ktB, vpB],
                   [dcB[h // 4]])
            for h in range(8):
                mm(dn[:, 8 + h:9 + h], kt[:, h * 128:(h + 1) * 128], vp[:, h, 128:129], [ktB, vpB], dnB)
            dd, ddB = s1_p.get()
            cp("dve", dd[:], dn[:, 0:8], dnB, [ddB])
            stt(dd[:], dd[:], -1.0, dd[:], ALU.mult, ALU.max, [ddB], [ddB])
            tt("dve", dd[:], dd[:], m_fl[:, d, c, :], ALU.max, [ddB, m_flB], [ddB])
            P.add("dve", lambda o=dd[:]: V.reciprocal(out=o, in_=o), [ddB], [ddB])
            hd, hdB = yo_p.get()
            tt("dve", hd[:].rearrange("p (h e) -> p h e", h=8), nm.rearrange("p (h e) -> p h e", h=8),
               dd[:].unsqueeze(2).broadcast_to([128, 8, 128]), ALU.mult, nmB + [ddB], [hdB])
            tt("dve", C[:, :, 0:128], C[:, :, 0:128], dc.rearrange("p (h e) -> p h e", h=8), ALU.add,
               [CB_] + dcB, [CB_])
            tt("dve", C[:, :, 128:129], C[:, :, 128:129], dn[:, 8:16].unsqueeze(2), ALU.add, [CB_] + dnB, [CB_])
            if not last_sweep:
                P.dma("sp", hf_d[c], hd[:], [hdB], [DB("hf", c)], hdB)
                return
            hf, hfB = yo_p.get()
            P.dma("sp", hf[:], hf_d[c], [DB("hf", c)], [hfB], hfB)
            tt("dve", hd[:], hd[:], hf[:], ALU.add, [hdB, hfB], [hdB])
            act(hf[:], hd[:], AF.Square, [hdB], [hfB])
            s1, s1B = s1_p.get()
            P.add("dve", lambda o=s1[:], i_=hf[:].rearrange("p (h e) -> p h e", h=8):
                  V.tensor_reduce(out=o, in_=i_, axis=AX.X, op=ALU.add), [hfB], [s1B])
            rsqrt(s1[:], s1B, (128, 8), 1.0 / 128)
            tt("dve", hd[:].rearrange("p (h e) -> p h e", h=8), hd[:].rearrange("p (h e) -> p h e", h=8),
               s1[:].unsqueeze(2).broadcast_to([128, 8, 128]), ALU.mult, [hdB, s1B], [hdB])
            tt("dve", hd[:], hd[:], pr[:, P_MLN:P_MLN + 1024], ALU.mult, [hdB, prB], [hdB])
            yn, ynB = tok_p.get()
            tt("dve", yn[:], hd[:], mo[:], ALU.mult, [hdB, moB], [ynB])
            out_transposed(yn, ynB, 2, c)

        def seq_begin(s):
            for d in range(2):
                C, CB_ = CS[d]
                if SEQS[s][2]:
                    P.dma("sp", C[:].rearrange("p h e -> p (h e)"), mlc0[l, d], [], [CB_], CB_)
                else:
                    memset("dve", C[:].rearrange("p h e -> p (h e)"), 0.0, [CB_])

        def seq_end(s):
            if not SEQS[s][2]:
                for d in range(2):
                    C, CB_ = CS[d]
                    P.dma("sp", nmlc_o[s - 1, l, d], C[:].rearrange("p h e -> p (h e)"), [CB_],
                          [DB("nmlc", s, l, d)], CB_)

        run_sweeps(chunk_loads, chunk, chunkB, seq_begin, seq_end)

    NKS = NCTX + SL
    KT = KTB = VA = VAB = VB_ = VBB = ckd = ckdB = q_p = e_p = yat_p = dsb_p = ev_p = None

    def alloc_att():
        nonlocal KT, KTB, VA, VAB, VB_, VBB, ckd, ckdB, q_p, e_p, yat_p, dsb_p, ev_p
        KT, KTB = nsb("KT", [128, 8, NKS], BF16)
        memset("dve", KT[:].rearrange("p a b -> p (a b)"), 0.0, [KTB])
        VA, VAB = nsb("VA", [128, (NKS // 128), 4, 128], BF16)
        VB_, VBB = nsb("VBt", [128, (NKS // 128), 4, 128], BF16)
        ckd, ckdB = nsb("ckd", [128, 4, 4, 128], BF16)
        memset("pool", VA[:].rearrange("p a b c -> p (a b c)"), 0.0, [VAB])
        memset("pool", VB_[:].rearrange("p a b c -> p (a b c)"), 0.0, [VBB])
        for kc_ in range(NKS // 128):
            memset("pool", VA[:, kc_, :, 64:128], 1.0, [VAB])
            memset("pool", VB_[:, kc_, :, 0:64], 1.0, [VBB])
        q_p = mkpool("qatt", [128, 8, 512], BF16, 2)
        e_p = mkpool("eatt", [128, 2, 512], BF16, 3)
        yat_p = mkpool("yat", [128, 8, 512], BF16, 2)
        dsb_p = mkpool("dsb", [128, 512], F32, 3)
        ev_p = mkpool("ev", [128, 2, 512], F32, 3)


    def att_layer(l):
        for s in range(3):
            f, n, has_ctx, _ = SEQS[s]
            L = 256 * n
            nctx = NCTX if has_ctx else 0
            nk = nctx + L
            nkc = nk // 128
            if has_ctx:
                src = cache_k[l].rearrange("(kc p) (f c) -> p kc f c", p=128, f=2)
                P.dma("pool", ckd[:, :, 0:2, :], src, [], [ckdB], ckdB)
                for f2 in range(2):
                    P.dma("pool", ckd[:, :, 2 + f2, 0:64], src[:, :, f2, 64:128], [], [ckdB], ckdB)
                    P.dma("pool", ckd[:, :, 2 + f2, 64:128], src[:, :, f2, 0:64], [], [ckdB], ckdB)
                KTv = KT[:].rearrange("p (f sw hh) t -> p sw f hh t", f=2, sw=2, hh=2)
                for kc in range(4):
                    pt, ptB = psBr()
                    ptb = pt.bitcast(BF16)
                    for fv in range(4):
                        tp(ptb[:, fv * 128:(fv + 1) * 128], ckd[:, kc, fv, :], ident, [ckdB, cbB], ptB)
                    ptv = ptb[:, 0:512].rearrange("p (sw f t) -> p sw f t", sw=2, f=2)
                    ks = slice(kc * 128, (kc + 1) * 128)
                    cp("act", KTv[0:64, :, :, 0, ks], ptv[0:64], ptB, [KTB])
                    for sw in range(2):
                        cp("act", KTv[64:128, 1 - sw, :, 1, ks], ptv[64:128, sw], ptB, [KTB])
                srcv = cache_v[l].rearrange("(kc p) (g c) -> p kc g c", p=128, g=4)
                for kc in range(4):
                    P.dma("pool", VA[:, kc, :, 0:64], srcv[:, kc], [], [VAB], VAB)
                    P.dma("pool", VB_[:, kc, :, 64:128], srcv[:, kc], [], [VBB], VBB)
            for g in range(4):
                for hh in range(2):
                    fc = (g // 2) if (g % 2) == hh else 2 + g // 2
                    ps_ = slice(hh * 64, hh * 64 + 64)
                    P.dma("sp", KT[ps_, g * 2 + hh, nctx:nctx + L].rearrange("p (b t) -> p b t", b=n),
                          kT_d[f:f + n, ps_, fc, :].rearrange("b p t -> p b t"),
                          [DB("kT", f + bi) for bi in range(n)], [KTB], KTB)
            c0 = 2 * f
            kc0 = nctx // 128
            for i in range(2 * n):
                srcv = v_tok[c0 + i].rearrange("p (g e) -> p g e", g=4)
                P.dma("sp", VA[:, kc0 + i, :, 0:64], srcv, [DB("v", c0 + i)], [VAB], VAB)
                P.dma("sp", VB_[:, kc0 + i, :, 64:128], srcv, [DB("v", c0 + i)], [VBB], VBB)
            NQ = min(512, L)
            nqb = NQ // 256
            for qb in range(L // NQ):
                qt, qtB = q_p.get()
                for i in range(nqb):
                    b = f + qb * nqb + i
                    P.dma("sp", qt[:, :, i * 256:(i + 1) * 256], qT_d[b], [DB("qT", b)], [qtB], qtB)
                ya, yaB = yat_p.get()
                tasks = [(j, hh, k0) for j in range(8) for hh in range(2) for k0 in range(0, nkc, 2)]

                def hinfo(j, hh):
                    h = 2 * j + hh
                    g = h // 4
                    return g, slice(0, 128), g * 2 + hh

                def emit_scores(ti):
                    j, hh, k0 = tasks[ti]
                    g, ps_, fv = hinfo(j, hh)
                    scp, scpB = psum_at(2 + 2 * (ti % 2), 2)
                    for kk in range(2):
                        kc = k0 + kk
                        mm(scp[:, kk * 512:kk * 512 + NQ], KT[ps_, fv, kc * 128:(kc + 1) * 128],
                           qt[ps_, j, 0:NQ], [KTB, qtB], [scpB[kk]])
                    return scp, scpB

                ohs = {}

                def emit_epv(ti, scp, scpB):
                    j, hh, k0 = tasks[ti]
                    g, ps_, fv = hinfo(j, hh)
                    Vt, VtB = (VA, VAB) if hh == 0 else (VB_, VBB)
                    if k0 == 0:
                        ohs[(j, hh)] = psum_at((0 if j % 2 == 0 else 6) + hh)
                    oh, ohB = ohs[(j, hh)]
                    e, eB = e_p.get()
                    act(e[:, :, 0:NQ], scp.rearrange("p (k q) -> p k q", k=2)[:, :, 0:NQ], AF.Exp,
                        scpB, [eB], scale=0.125)
                    for kk in range(2):
                        kc = k0 + kk
                        mm(oh[:, 0:NQ], Vt[:, kc, g, :], e[:, kk, 0:NQ], [VtB, eB], ohB,
                           start=(kc == 0), stop=(kc == nkc - 1))
                    if hh == 1 and k0 + 2 >= nkc:
                        (oa, oaB), (ob, obB) = ohs[(j, 0)], ohs[(j, 1)]
                        ev, evB = ev_p.get()
                        d2, d2B = dsb_p.get()
                        cp("dve", ev[:, 0, 0:NQ], oa[:, 0:NQ], oaB, [evB])
                        cp("dve", ev[:, 1, 0:NQ], ob[:, 0:NQ], obB, [evB])
                        P.dma("sp", d2[64:128, 0:NQ], ev[0:64, 1, 0:NQ], [evB], [d2B], d2B)
                        P.dma("sp", d2[0:64, 0:NQ], ev[64:128, 0, 0:NQ], [evB], [d2B], d2B)
                        P.add("dve", lambda o=d2[:, 0:NQ]: V.reciprocal(out=o, in_=o), [d2B], [d2B])
                        tt("dve", ya[0:64, j, 0:NQ], ev[0:64, 0, 0:NQ], d2[0:64, 0:NQ], ALU.mult, [evB, d2B], [yaB])
                        tt("dve", ya[64:128, j, 0:NQ], ev[64:128, 1, 0:NQ], d2[64:128, 0:NQ], ALU.mult,
                           [evB, d2B], [yaB])

                pend = emit_scores(0)
                for ti in range(len(tasks)):
                    nxt = emit_scores(ti + 1) if ti + 1 < len(tasks) else None
                    emit_epv(ti, *pend)
                    pend = nxt
                for i in range(nqb):
                    b = f + qb * nqb + i
                    P.dma("sp", ybr[1, b], ya[:, :, i * 256:(i + 1) * 256], [yaB], [DB("ybr", 1, b)], yaB)


    wbr_s = wo_s = yb_p = gt_p = mg_p = t3_p = None

    def alloc_mrg():
        nonlocal wbr_s, wo_s, yb_p, gt_p, mg_p, t3_p
        wbr_s = [nsb("wbr%d" % n, [128, KC, 1024], BF16) for n in range(3)]
        yb_p = mkpool("ybl", [128, 8, 256], BF16, 6)
        gt_p = mkpool("gtl", [128, 8, 256], BF16, 6)
        mg_p = mkpool("mgd", [128, 8, 256], BF16, 2)
        t3_p = mkpool("t3", [128, 256], F32, 9)

    def merge_a(l):
        for n in range(3):
            load_slab(w_br[l, n], 1024, 0, wbr_s[n])

        def loads(b):
            ys_, gs_ = [], []
            for n in range(3):
                y, yB = yb_p.get()
                P.dma("sp", y[:], ybr[n, b], [DB("ybr", n, b)], [yB], yB)
                g, gB = gt_p.get()
                P.dma("sp", g[:], gT[n, b], [DB("gT", n, b)], [gB], gB)
                ys_.append((y, yB))
                gs_.append((g, gB))
            return ys_, gs_

        nxt = loads(0)
        for b in range(NBLK):
            ys_, gs_ = nxt
            if b + 1 < NBLK:
                nxt = loads(b + 1)
            mg, mgB = mg_p.get()
            for oc in range(8):
                ts_ = []
                for n in range(3):
                    pp, ppB = psA()
                    for k in range(KC):
                        mm(pp[:, 0:256], wbr_s[n][0][:, k, oc * 128:(oc + 1) * 128], ys_[n][0][:, k, :],
                           [wbr_s[n][1], ys_[n][1]], ppB, start=(k == 0), stop=(k == KC - 1))
                    t, tB = t3_p.get()
                    tt("dve", t[:], pp[:, 0:256], gs_[n][0][:, oc, :], ALU.mult, ppB + [gs_[n][1]], [tB])
                    ts_.append((t, tB))
                tt("pool", ts_[0][0][:], ts_[0][0][:], ts_[1][0][:], ALU.add, [ts_[0][1], ts_[1][1]], [ts_[0][1]])
                tt("pool", mg[:, oc, :], ts_[0][0][:], ts_[2][0][:], ALU.add, [ts_[0][1], ts_[2][1]], [mgB])
            P.dma("sp", mgd_d[b], mg[:], [mgB], [DB("mgd", b)], mgB)

    def merge_b(l):
        wo_t, wo_B = nsb("wo_s", [128, KC, 1024], BF16)
        mgl_p = mkpool("mgl", [128, 8, 256], BF16, 2)
        load_slab(w_out[l], 1024, 0, (wo_t, wo_B))
        for b in range(NBLK):
            m = SEQS[blk_seq(b)][3]
            mg, mgB = mgl_p.get()
            P.dma("sp", mg[:], mgd_d[b], [DB("mgd", b)], [mgB], mgB)
            xt, xtB = xt_p.get()
            P.dma("sp", xt[:], xres[b], [DB("x", b)], [xtB], xtB)
            for oc in range(8):
                pp, ppB = psA()
                for k in range(KC):
                    mm(pp[:, 0:256], wo_t[:, k, oc * 128:(oc + 1) * 128], mg[:, k, :], [wo_B, mgB], ppB,
                       start=(k == 0), stop=(k == KC - 1))
                stt(xt[:, oc, :], pp[:, 0:256], modT[:, l, m, 16 + oc:17 + oc], xt[:, oc, :], ALU.mult, ALU.add,
                    ppB + [modB, xtB], [xtB])
            P.dma("sp", xres[b], xt[:], [xtB], [DB("x", b)], xtB)
            norm_block(xt, xtB, b, l, 1)

    FG = [6, 6, 5, 5]
    wf1_p = wf2_p = ac_p = sa_p = fo_p = None

    def alloc_ffn():
        nonlocal wf1_p, wf2_p, ac_p, sa_p, fo_p
        wf1_p = mkpool("wf1", [128, KC, 2, 768], BF16, 2)
        wf2_p = mkpool("wf2", [128, 6, 1024], BF16, 2)
        ac_p = mkpool("ffa", [128, 256], BF16, 8)
        sa_p = mkpool("ffs", [128, 256], F32, 3)
        fo_p = mkpool("fo", [128, KC, 256], F32, 1)


    def ffn_layer(l, last):
        hc0 = 0
        for gi, ng in enumerate(FG):
            w1, w1B = wf1_p.get()
            w2, w2B = wf2_p.get()
            for ab in range(2):
                P.dma("pool", w1[:, :, ab, 0:ng * 128],
                      w_f1[l, :, ab * FFN_H + hc0 * 128:ab * FFN_H + (hc0 + ng) * 128].rearrange(
                          "(k p) n -> p k n", p=128), [], [w1B], w1B)
            P.dma("pool", w2[:, 0:ng, :], w_f2[l, hc0 * 128:(hc0 + ng) * 128, :].rearrange("(c p) n -> p c n", p=128),
                  [], [w2B], w2B)
            for b in range(NBLK):
                m = SEQS[blk_seq(b)][3]
                c0 = hcol(b) + 1
                acts = []
                for hc in range(ng):
                    pa, paB = psA()
                    for k in range(KC):
                        mm(pa[:, 0:256], w1[:, k, 0, hc * 128:(hc + 1) * 128], hT[:, k, c0:c0 + 256], [w1B, hB[b]],
                           paB, start=(k == 0), stop=(k == KC - 1))
                    pb, pbB = psA()
                    for k in range(KC):
                        mm(pb[:, 0:256], w1[:, k, 1, hc * 128:(hc + 1) * 128], hT[:, k, c0:c0 + 256], [w1B, hB[b]],
                           pbB, start=(k == 0), stop=(k == KC - 1))
                    sa, saB = sa_p.get()
                    act(sa[:], pa[:, 0:256], AF.Silu, paB, [saB])
                    a, aB = ac_p.get()
                    tt("dve", a[:], pb[:, 0:256], sa[:], ALU.mult, pbB + [saB], [aB])
                    acts.append((a, aB))
                xt, xtB = xt_p.get()
                P.dma("sp", xt[:], xres[b], [DB("x", b)], [xtB], xtB)
                for oc in range(8):
                    pp, ppB = psA()
                    for hc in range(ng):
                        mm(pp[:, 0:256], w2[:, hc, oc * 128:(oc + 1) * 128], acts[hc][0][:], [w2B, acts[hc][1]], ppB,
                           start=(hc == 0), stop=(hc == ng - 1))
                    stt(xt[:, oc, :], pp[:, 0:256], modT[:, l, m, 40 + oc:41 + oc], xt[:, oc, :], ALU.mult, ALU.add,
                        ppB + [modB, xtB], [xtB])
                if gi < len(FG) - 1 or not last:
                    P.dma("sp", xres[b], xt[:], [xtB], [DB("x", b)], xtB)
                if gi == len(FG) - 1:
                    if last:
                        fo = fo_p.get()
                        norm_block(xt, xtB, b, l, 0, final=True, out_tile=(fo[0], fo[1]))
                        P.dma("sp", yT[:, :, b * 256:(b + 1) * 256], fo[0][:], [fo[1]], [DB("yT", b)], fo[1])
                    else:
                        norm_block(xt, xtB, b, l + 1, 0)
            hc0 += ng

    def phase(fn, *a):
        with contextlib.ExitStack() as pes:
            cur["es"] = pes
            fn(*a)
            P.flush()
        cur["es"] = es

    def ph_d1(l):
        alloc_slab()
        alloc_d1()
        d1_layer(l)

    def ph_ssd(l):
        alloc_scan()
        alloc_mix()
        alloc_ssd()
        ssd_layer(l, prep_t, prepB)

    def ph_att(l):
        alloc_att()
        att_layer(l)

    def ph_ml(l):
        alloc_mix(False)
        alloc_ml()
        ml_layer(l, prep_t, prepB)

    def ph_mrg_a(l):
        alloc_mrg()
        merge_a(l)

    def ph_mrg_b(l):
        alloc_norm()
        merge_b(l)

    def ph_ffn(l):
        alloc_norm()
        alloc_ffn()
        ffn_layer(l, l == NL - 1)

    nph = [0]

    def go(fn, *a):
        nph[0] += 1
        if stop_after is not None and nph[0] > stop_after:
            return
        phase(fn, *a)

    go(startup)
    for l in range(NL):
        P.dma("sp", prep_t[:], prep[l], [], [prepB], prepB)
        go(ph_d1, l)
        go(ph_ssd, l)
        go(ph_att, l)
        go(ph_ml, l)
        go(ph_mrg_a, l)
        go(ph_mrg_b, l)
        go(ph_ffn, l)
    P.flush()
    print("kernel build: ops=%d sems=%d" % (P.n_emitted, P.nsem))
    es.close()
    return nc


def _consts(SL):
    c = np.zeros((128, C_TOT), np.float32)
    k = np.arange(128)[:, None]
    i = np.arange(128)[None, :]
    c[:, C_ID:C_ID + 128] = (k == i)
    c[:, C_ONE:C_ONE + 128] = 1.0
    c[:, C_U:C_U + 128] = (k <= i)
    c[:, C_L:C_L + 128] = (k >= i)
    c[:, C_BLK:C_BLK + 128] = ((k // 64) == (i // 64))
    part = np.arange(128)
    d = part % 64
    partner = np.where((d % 32) < 16, part + 16, part - 16)
    c[:, C_SWP:C_SWP + 128] = (k == partner[None, :])
    c[:, C_NMF:C_NMF + 128] = np.where(i < k, NEG, 0.0)
    c[:, C_NMB:C_NMB + 128] = np.where(i > k, NEG, 0.0)
    cf = np.eye(128, dtype=np.float32)
    t = np.arange(SL)
    rows, cols = t // 64, t % 64
    f = d % 16
    freqs = (10000.0 ** (-(f.astype(np.float32)) / 16.0)).astype(np.float32)
    pos = np.where((d < 32)[:, None], rows[None, :], cols[None, :]).astype(np.float32)
    ang = pos * freqs[:, None]
    cosT = np.cos(ang).astype(np.float32)
    sgn = np.where((d % 32) < 16, -1.0, 1.0).astype(np.float32)
    sinT = (np.sin(ang) * sgn[:, None]).astype(np.float32)
    return c, cf, cosT, sinT


def _fm(v):
    v = np.asarray(v)
    n = v.shape[-1] // 128
    r = v.reshape(v.shape[:-1] + (n, 128))
    return np.ascontiguousarray(np.moveaxis(r, -1, 0))


def _prepare(inp, SB, NL):
    NBLK = SB + 2
    SL = 256 * SB
    f32 = np.float32
    g = lambda k: np.asarray(inp[k], dtype=f32)
    c, cf, cosT, sinT = _consts(SL)
    shared = {
        "w_ada": np.ascontiguousarray(g("w_ada")[:NL]), "w_in": np.ascontiguousarray(g("w_in")[:NL]),
        "w_branch": np.ascontiguousarray(g("w_branch")[:NL]), "w_out": np.ascontiguousarray(g("w_out")[:NL]),
        "w_ffn_in": np.ascontiguousarray(g("w_ffn_in")[:NL]), "w_ffn_out": np.ascontiguousarray(g("w_ffn_out")[:NL]),
        "cst": c, "cstf": cf, "ropec": cosT, "ropes": sinT,
    }
    prep = np.zeros((NL, 128, P_TOT), f32)
    pfm = np.zeros((128, NL, Q_TOT), f32)
    for l in range(NL):
        prep[l, :, P_DTB:P_DTB + 32] = g("ssd_dt_bias")[l].reshape(32)[None]
        prep[l, :, P_ALOG:P_ALOG + 32] = g("ssd_a_log")[l].reshape(32)[None]
        prep[l, :, P_SD:P_SD + 16] = g("ssd_d")[l][None]
        prep[l, :, P_SSDN:P_SSDN + 1024] = g("ssd_norm")[l][None]
        prep[l, :, P_MLN:P_MLN + 1024] = g("ml_norm")[l][None]
        prep[l, :, P_MGB:P_MGB + 32] = g("ml_gate_bias")[l].reshape(32)[None]
        pfm[:, l, Q_BADA:Q_BADA + 48] = _fm(g("b_ada")[l])
        pfm[:, l, Q_SCW:Q_SCW + 48] = np.moveaxis(_fm(g("ssd_conv_w")[l]), 1, 2).reshape(128, 48)
        pfm[:, l, Q_SCB:Q_SCB + 12] = _fm(g("ssd_conv_b")[l])
        pfm[:, l, Q_MCW:Q_MCW + 64] = np.moveaxis(_fm(g("ml_conv_w")[l]), 1, 2).reshape(128, 64)
        pfm[:, l, Q_MCB:Q_MCB + 16] = _fm(g("ml_conv_b")[l])
        pfm[:, l, Q_QG] = np.tile(g("att_q_norm")[l], 2)
        pfm[:, l, Q_KG] = np.tile(g("att_k_norm")[l], 2)
    shared["prep"] = prep
    shared["pfm"] = pfm
    shared["fng"] = _fm(g("final_norm"))
    xp, xs = g("x_prompt"), g("x_sample")
    ncore = xs.shape[0]
    maps = []
    for i in range(ncore):
        toks = np.concatenate([xs[i][:SL], xp[2 * i], xp[2 * i + 1]], axis=0)
        x0 = np.ascontiguousarray(toks.reshape(-1, 8, 128).transpose(2, 1, 0))
        cvec = np.stack([_fm(g("c_ctx")), _fm(g("c")[i])], axis=-1)
        st = g("state_ssd")[i][:NL]
        st = st.reshape(NL, 2, 2, 2, 4, 64, 64)
        ssd0 = np.ascontiguousarray(st.transpose(0, 1, 3, 6, 2, 4, 5)).reshape(NL, 2, 128, 512)
        mc = g("state_ml_c")[i][:NL]
        mn = g("state_ml_n")[i][:NL]
        mlc0 = np.concatenate([mc.transpose(0, 1, 3, 2, 4), mn.transpose(0, 1, 3, 2)[..., None]], axis=-1)
        m = dict(shared)
        m.update({
            "x0": x0, "cvec": np.ascontiguousarray(cvec),
            "cache_k": np.ascontiguousarray(g("cache_k")[i][:NL].reshape(NL, 512, 256)),
            "cache_v": np.ascontiguousarray(g("cache_v")[i][:NL].reshape(NL, 512, 256)),
            "ssd0": ssd0, "mlc0": np.ascontiguousarray(mlc0).reshape(NL, 2, 128, 8 * 129),
            "mlm0": np.ascontiguousarray(g("state_ml_m")[i][:NL].reshape(NL, 2, 8, 1)),
        })
        maps.append(m)
    return maps


def _assemble(results, SB, NL):
    SL = 256 * SB
    n = len(results)
    y_s = np.zeros((n, SL, D), np.float32)
    y_p = np.zeros((2 * n, 256, D), np.float32)
    nk = np.zeros((2 * n, NL, 256, 4, 64), np.float32)
    nv = np.zeros((2 * n, NL, 256, 4, 64), np.float32)
    nssd = np.zeros((2 * n, NL, 2, 16, 64, 64), np.float32)
    nc_ = np.zeros((2 * n, NL, 2, 8, 128, 128), np.float32)
    nn_ = np.zeros((2 * n, NL, 2, 8, 128), np.float32)
    nm = np.zeros((2 * n, NL, 2, 8), np.float32)
    for i, r in enumerate(results):
        y = np.asarray(r["yT"]).transpose(2, 1, 0).reshape(-1, D)
        y_s[i] = y[:SL]
        for j in range(2):
            y_p[2 * i + j] = y[SL + 256 * j:SL + 256 * (j + 1)]
            k = np.asarray(r["nk_o"])[j]
            nk[2 * i + j] = k.transpose(0, 3, 2, 1).reshape(NL, 256, 4, 64)
            nv[2 * i + j] = np.asarray(r["nv_o"])[j].reshape(NL, 256, 4, 64)
            s_ = np.asarray(r["nssd_o"])[j].reshape(NL, 2, 2, 64, 2, 4, 64)
            nssd[2 * i + j] = s_.transpose(0, 1, 4, 2, 5, 6, 3).reshape(NL, 2, 16, 64, 64)
            c_ = np.asarray(r["nmlc_o"])[j].reshape(NL, 2, 128, 8, 129)
            nc_[2 * i + j] = c_[..., :128].transpose(0, 1, 3, 2, 4)
            nn_[2 * i + j] = c_[..., 128].transpose(0, 1, 3, 2)
            nm[2 * i + j] = np.asarray(r["nmlm_o"])[j].reshape(NL, 2, 8)
    return (y_p, y_s, nk, nv, nssd, nc_, nn_, nm)


_CACHE = {}


def run(inputs, SB=8, NL=4, debug=False, stop_after=None):
    key = (SB, NL, debug, stop_after)
    if key not in _CACHE:
        _CACHE[key] = build(SB, NL, debug, stop_after)
    nc = _CACHE[key]
    maps = _prepare(inputs, SB, NL)
    res = run_bass_kernel_spmd(nc, maps, core_ids=list(range(len(maps))))
    return _assemble(res.results, SB, NL), res


def kernel(**inputs):
    out, _ = run(inputs, 8, 4, False)
    return out
```

```python
import contextlib
import os
import numpy as np
import concourse.bass as bass
import concourse.mybir as mybir
from concourse.bass_utils import run_bass_kernel_spmd

F32 = mybir.dt.float32
BF16 = mybir.dt.bfloat16
AF = mybir.ActivationFunctionType
ALU = mybir.AluOpType
AX = mybir.AxisListType

D = 1024
KC = 8
EPS = 1e-6
IN_DIM = 11328
FFN_H = 2816
O_SX, O_SZ, O_SB, O_SC, O_SDT = 0, 1024, 2048, 2304, 2560
O_AQ, O_AK, O_AV = 2592, 3616, 3872
O_MQ, O_MK, O_MV, O_MO, O_MG, O_G = 4128, 5152, 6176, 7200, 8224, 8256
P_DTB, P_ALOG, P_SD, P_SSDN, P_MLN, P_MGB, P_TOT = 0, 32, 64, 96, 1120, 2144, 2176
Q_BADA, Q_SCW, Q_SCB, Q_MCW, Q_MCB, Q_QG, Q_KG, Q_TOT = 0, 48, 96, 108, 172, 188, 189, 192
C_ID, C_ONE, C_U, C_L, C_BLK, C_SWP, C_NMF, C_NMB, C_TOT = 0, 128, 256, 384, 512, 640, 768, 896, 1024
NEG = -30000.0
MLK_SCALE = 128.0 ** -0.5


class Buf:
    __slots__ = ("name", "w", "r", "dsem", "dcnt", "last_dma", "excl")
    registry = []

    def __init__(self, name, excl=False):
        self.name = name
        self.excl = excl
        self.w = None
        self.r = []
        self.dsem = None
        self.dcnt = 0
        self.last_dma = None
        Buf.registry.append(self)


class Op:
    __slots__ = ("eng", "fn", "deps", "needs", "event", "isdma", "slot")


class Prog:
    def __init__(self, nc, es):
        self.nc = nc
        self.es = es
        self.ops = []
        self.eng = {"pe": nc.tensor, "act": nc.scalar, "dve": nc.vector, "pool": nc.gpsimd, "sp": nc.sync}
        self.esem = {k: es.enter_context(nc.semaphore("sem_" + k)) for k in self.eng}
        self.nsem = len(self.eng)
        self.dma_bufs = []
        self.free_sems = {"sp": [], "pool": []}
        self.phase_bufs = []
        self.phase_pools = []
        self.seq = {k: 0 for k in self.eng}
        self.waited = {k: {} for k in self.eng}
        self.n_emitted = 0

    def _deps(self, op, reads, writes):
        deps = {}
        wset = set(id(b) for b in writes)
        for b in reads:
            if b.w is not None:
                deps[id(b.w)] = (b.w, True)
            if b.excl:
                for r in b.r:
                    if r.eng != op.eng and id(r) not in deps:
                        deps[id(r)] = (r, False)
        for b in writes:
            if b.w is not None and id(b.w) not in deps:
                deps[id(b.w)] = (b.w, False)
            for r in b.r:
                if id(r) not in deps:
                    deps[id(r)] = (r, False)
        out = []
        for d, raw in deps.values():
            if d is op:
                continue
            if (not d.isdma) and (not op.isdma) and d.eng == op.eng and op.eng == "pe":
                continue
            d.needs = True
            out.append(d)
        for b in writes:
            b.w = op
            b.r = []
        for b in reads:
            if id(b) not in wset:
                b.r.append(op)
        return out

    def add(self, eng, fn, reads=(), writes=()):
        op = Op()
        op.eng, op.fn, op.needs, op.event, op.isdma, op.slot = eng, fn, False, None, False, None
        op.deps = self._deps(op, list(reads), list(writes))
        self.ops.append(op)
        return op

    def dma(self, q, out, in_, reads, writes, slot):
        nc = self.nc
        op = Op()
        op.eng, op.needs, op.isdma, op.slot = q, True, True, slot
        if slot.dsem is None:
            slot.dsem, slot.dcnt, slot.last_dma = {}, {}, {}
        if q not in slot.dsem:
            if self.free_sems[q]:
                slot.dsem[q], slot.dcnt[q] = self.free_sems[q].pop()
            else:
                slot.dsem[q] = self.es.enter_context(nc.semaphore("dsem_%s_%d" % (q, self.nsem)))
                slot.dcnt[q] = 0
                self.nsem += 1
                assert self.nsem <= 96, "too many semaphores"
            self.dma_bufs.append((slot, q))
        slot.dcnt[q] += 16
        op.event = (slot.dsem[q], slot.dcnt[q])
        e = self.eng[q]
        op.fn = lambda: e.dma_start(out=out, in_=in_)
        op.deps = self._deps(op, list(reads), list(writes))
        ld = slot.last_dma.get(q)
        if ld is not None and ld not in op.deps:
            op.deps.append(ld)
        slot.last_dma[q] = op
        self.ops.append(op)
        return op

    def flush(self):
        pend = set(id(o) for o in self.ops)
        for b in Buf.registry:
            if b.w is not None and id(b.w) in pend:
                b.w.needs = True
            for r in b.r:
                if id(r) in pend:
                    r.needs = True
        last = {}
        for o in self.ops:
            if not o.isdma:
                last[o.eng] = o
        for o in last.values():
            o.needs = True
        seq, waited = self.seq, self.waited
        for op in self.ops:
            e = self.eng[op.eng]
            wd = waited[op.eng]
            for d in op.deps:
                sem, val = d.event
                if wd.get(id(sem), 0) < val:
                    e.wait_ge(sem, val)
                    wd[id(sem)] = val
            ins = op.fn()
            if op.isdma:
                ins.then_inc(op.event[0], 16)
            elif op.needs:
                seq[op.eng] += 1
                ins.then_inc(self.esem[op.eng], 1)
                op.event = (self.esem[op.eng], seq[op.eng])
            op.fn = None
        self.n_emitted += len(self.ops)
        self.ops = []
        evs = [(self.esem[k], seq[k], k) for k in self.eng if seq[k] > 0]
        evs += [(b.dsem[q], b.dcnt[q], None) for b, q in self.dma_bufs]
        for k, e in self.eng.items():
            wd = waited[k]
            for sem, val, owner in evs:
                if owner == k and k != "pool":
                    continue
                if wd.get(id(sem), 0) < val:
                    e.wait_ge(sem, val)
                    wd[id(sem)] = val
        ph = set(id(b) for b in self.phase_bufs)
        for b in self.phase_bufs:
            if b.dsem is not None:
                for q in b.dsem:
                    self.free_sems[q].append((b.dsem[q], b.dcnt[q]))
                b.dsem = None
        self.dma_bufs = [(b, q) for b, q in self.dma_bufs if id(b) not in ph]
        Buf.registry = [b for b in Buf.registry if id(b) not in ph]
        self.phase_bufs = []
        for p_ in self.phase_pools:
            p_.dead = True
        self.phase_pools = []


class TPool:
    def __init__(self, P, nc, es, name, shape, dtype, n, phase=True):
        self.t = []
        for i in range(n):
            t = es.enter_context(nc.sbuf_tensor("%s_%d" % (name, i), list(shape), dtype))
            b = Buf("%s_%d" % (name, i))
            if phase:
                P.phase_bufs.append(b)
            self.t.append((t, b))
        self.i = 0
        self.dead = False
        self.name = name
        if phase:
            P.phase_pools.append(self)

    def get(self):
        assert not self.dead, "stale pool " + self.name
        r = self.t[self.i % len(self.t)]
        self.i += 1
        return r


def build(SB, NL, debug=False, stop_after=None):
    NBLK = SB + 2
    NCH = 2 * NBLK
    T = 256 * NBLK
    SL = 256 * SB
    NCTX = 512
    SEQS = [(0, SB, True, 1), (SB, 1, False, 0), (SB + 1, 1, False, 0)]
    hbase = [0, SL + 3, SL + 3 + 259]
    NCOL = SL + 3 + 259 * 2

    def blk_seq(b):
        return 0 if b < SB else (1 if b == SB else 2)

    def hcol(b):
        s = blk_seq(b)
        return hbase[s] + 256 * (b - SEQS[s][0])

    nc = bass.Bass("TRN2", target_bir_lowering=False)
    es = contextlib.ExitStack()
    P = Prog(nc, es)

    def din(name, shape, dt=F32):
        return nc.dram_tensor(name, list(shape), dt, kind="ExternalInput").ap()

    def dout(name, shape, dt=F32):
        return nc.dram_tensor(name, list(shape), dt, kind="ExternalOutput").ap()

    def dscr(name, shape, dt):
        return nc.dram_tensor(name, list(shape), dt, kind=("ExternalOutput" if debug else "Internal")).ap()

    x0 = din("x0", [128, KC, T])
    cvec = din("cvec", [128, KC, 2])
    w_ada = din("w_ada", [NL, D, 6 * D])
    w_in = din("w_in", [NL, D, IN_DIM])
    w_br = din("w_branch", [NL, 3, D, D])
    w_out = din("w_out", [NL, D, D])
    w_f1 = din("w_ffn_in", [NL, D, 2 * FFN_H])
    w_f2 = din("w_ffn_out", [NL, FFN_H, D])
    prep = din("prep", [NL, 128, P_TOT])
    pfm = din("pfm", [128, NL, Q_TOT])
    fng = din("fng", [128, KC])
    cst = din("cst", [128, C_TOT])
    cstf = din("cstf", [128, 128])
    ropec = din("ropec", [128, SL])
    ropes = din("ropes", [128, SL])
    cache_k = din("cache_k", [NL, NCTX, 256])
    cache_v = din("cache_v", [NL, NCTX, 256])
    ssd0 = din("ssd0", [NL, 2, 128, 512])
    mlc0 = din("mlc0", [NL, 2, 128, 8 * 129])
    mlm0 = din("mlm0", [NL, 2, 8, 1])

    yT = dout("yT", [128, KC, T])
    nk_o = dout("nk_o", [2, NL, 128, 2, 256])
    nv_o = dout("nv_o", [2, NL, 256, 256])
    nssd_o = dout("nssd_o", [2, NL, 2, 128, 512])
    nmlc_o = dout("nmlc_o", [2, NL, 2, 128, 8 * 129])
    nmlm_o = dout("nmlm_o", [2, NL, 2, 8, 1])

    xres = dscr("xres", [NBLK, 128, KC, 256], F32)
    sx_tok = dscr("sx_tok", [NCH, 128, 1024], BF16)
    sbcT = dscr("sbcT", [NBLK, 128, 4, 256], BF16)
    sb_tok = dscr("sb_tok", [NCH, 128, 256], BF16)
    sz_tok = dscr("sz_tok", [NCH, 128, 1024], BF16)
    dtg_tok = dscr("dtg_tok", [NCH, 128, 64], F32)
    qT_d = dscr("qT_d", [NBLK, 128, 8, 256], BF16)
    kT_d = dscr("kT_d", [NBLK, 128, 4, 256], BF16)
    v_tok = dscr("v_tok", [NCH, 128, 256], BF16)
    mqT = dscr("mqT", [NBLK, 128, 8, 256], BF16)
    mkT = dscr("mkT", [NBLK, 128, 8, 256], BF16)
    mk_tok = dscr("mk_tok", [NCH, 128, 1024], BF16)
    mv_tok = dscr("mv_tok", [NCH, 128, 1024], BF16)
    mo_tok = dscr("mo_tok", [NCH, 128, 1024], BF16)
    gT = dscr("gT", [3, NBLK, 128, 8, 256], BF16)
    ybr = dscr("ybr", [3, NBLK, 128, 8, 256], BF16)
    yf_d = dscr("yf_d", [NCH, 128, 1024], F32)
    hf_d = dscr("hf_d", [NCH, 128, 1024], F32)
    mgd_d = dscr("mgd_d", [NBLK, 128, 8, 256], BF16)

    dbufs = {}

    def DB(*key):
        if key not in dbufs:
            dbufs[key] = Buf(str(key))
        return dbufs[key]

    cur = {"es": es, "n": 0}

    def sb(name, shape, dt=F32):
        ph = cur["es"] is not es
        cur["n"] += 1
        b = Buf(name)
        if ph:
            P.phase_bufs.append(b)
        t = cur["es"].enter_context(nc.sbuf_tensor("%s_%d" % (name, cur["n"]), list(shape), dt))
        return t, b

    def mkpool(name, shape, dt, n):
        cur["n"] += 1
        return TPool(P, nc, cur["es"], "%s%d" % (name, cur["n"]), shape, dt, n, phase=(cur["es"] is not es))


    hT, _ = sb("hT", [128, KC, NCOL], BF16)
    hB = [Buf("hT%d" % b) for b in range(NBLK)]
    hHalo = Buf("hHalo")
    cb, cbB = sb("cb", [128, C_TOT], BF16)
    cf, cfB = sb("cf", [128, 128], F32)
    onesf, onesfB = sb("onesf", [128, 128], F32)
    modT, modB = sb("modT", [128, NL, 2, 48], F32)
    pfm_s, pfmB = sb("pfm_s", [128, NL, Q_TOT], F32)
    fng_s, fngB = sb("fng_s", [128, KC], F32)
    ropec_s = ropecB = ropes_s = ropesB = None
    prep_t, prepB = sb("prep_t", [128, P_TOT], F32)

    ps_all = es.enter_context(nc.psum_tensor("ps_all", [128, 4096], F32))
    psB = [Buf("psb%d" % i, excl=True) for i in range(8)]
    ps_state = {"A": 0, "B": 0}

    def psum_at(i, n=1):
        return ps_all[:, i * 512:(i + n) * 512], psB[i:i + n]

    def psA():
        i = ps_state["A"] % 6
        ps_state["A"] += 1
        return psum_at(i)

    def psBr():
        i = 6 + ps_state["B"] % 2
        ps_state["B"] += 1
        return psum_at(i)

    ident = cb[:, C_ID:C_ID + 128]
    ones = cb[:, C_ONE:C_ONE + 128]
    Utri = cb[:, C_U:C_U + 128]
    Ltri = cb[:, C_L:C_L + 128]
    blk2 = cb[:, C_BLK:C_BLK + 128]
    pswap = cb[:, C_SWP:C_SWP + 128]
    nmask = [cb[:, C_NMF:C_NMF + 128], cb[:, C_NMB:C_NMB + 128]]
    tri = [Utri, Ltri]

    V, A, G, PE = nc.vector, nc.scalar, nc.gpsimd, nc.tensor

    def mm(out, lhsT, rhs, reads, writes, start=True, stop=True):
        P.add("pe", lambda: PE.matmul(out, lhsT=lhsT, rhs=rhs, start=start, stop=stop), reads, writes)

    def tp(out, in_, idn, reads, writes):
        P.add("pe", lambda: PE.transpose(out, in_, idn), reads, writes)

    def act(out, in_, func, reads, writes, bias=None, scale=None, accum=None):
        kw = {}
        if bias is not None:
            kw["bias"] = bias
        if scale is not None:
            kw["scale"] = scale
        if accum is not None:
            kw["accum_out"] = accum
        P.add("act", lambda: A.activation(out=out, in_=in_, func=func, **kw), reads, writes)

    def tt(eng, out, in0, in1, op, reads, writes):
        e = V if eng == "dve" else G
        P.add(eng, lambda: e.tensor_tensor(out=out, in0=in0, in1=in1, op=op), reads, writes)

    def ts(eng, out, in0, s1, op0, reads, writes, s2=None, op1=None):
        e = V if eng == "dve" else G
        if op1 is None:
            P.add(eng, lambda: e.tensor_scalar(out=out, in0=in0, scalar1=s1, scalar2=None, op0=op0), reads, writes)
        else:
            P.add(eng, lambda: e.tensor_scalar(out=out, in0=in0, scalar1=s1, scalar2=s2, op0=op0, op1=op1),
                  reads, writes)

    def stt(out, in0, scalar, in1, op0, op1, reads, writes):
        P.add("dve", lambda: V.scalar_tensor_tensor(out=out, in0=in0, scalar=scalar, in1=in1, op0=op0, op1=op1),
              reads, writes)

    def cp(eng, out, in_, reads, writes):
        if eng == "act":
            P.add("act", lambda: A.copy(out=out, in_=in_), reads, writes)
        else:
            e = V if eng == "dve" else G
            P.add(eng, lambda: e.tensor_copy(out=out, in_=in_), reads, writes)

    def memset(eng, ap, val, writes):
        e = V if eng == "dve" else G
        P.add(eng, lambda: e.memset(ap, val), (), writes)

    def rsqrt(v_ap, v_buf, shape2, mean_scale):
        p, n = shape2
        ts("dve", v_ap, v_ap, mean_scale, ALU.mult, [v_buf], [v_buf], s2=EPS, op1=ALU.add)
        act(v_ap, v_ap, AF.Ln, [v_buf], [v_buf])
        act(v_ap, v_ap, AF.Exp, [v_buf], [v_buf], scale=-0.5)

    def flat(ap3):
        return ap3.rearrange("p a b -> p (a b)")

    slab_p = None

    def alloc_slab(n=3):
        nonlocal slab_p
        slab_p = mkpool("slab", [128, KC, 1024], BF16, n)

    def load_slab(src2d, ncols, dst_off=0, slab=None):
        if slab is None:
            slab = slab_p.get()
        t, bfr = slab
        P.dma("pool", t[:, :, dst_off:dst_off + ncols], src2d.rearrange("(k p) n -> p k n", p=128), [], [bfr], bfr)
        return slab

    def startup():
        cst_st, cst_stB = sb("cst_st", [128, C_TOT], F32)
        P.dma("sp", cst_st[:], cst[:, :], [], [cst_stB], cst_stB)
        cp("dve", cb[:], cst_st[:], [cst_stB], [cbB])
        P.dma("sp", cf[:], cstf[:, :], [], [cfB], cfB)
        P.dma("sp", pfm_s[:], pfm[:, :, :], [], [pfmB], pfmB)
        P.dma("sp", fng_s[:], fng[:, :], [], [fngB], fngB)
        memset("dve", onesf[:], 1.0, [onesfB])
        memset("pool", flat(hT[:]), 0.0, hB + [hHalo])
        alloc_slab()
        cv, cvB = sb("cv", [128, KC, 2], F32)
        cvb, cvbB = sb("cvb", [128, KC, 2], BF16)
        P.dma("sp", cv[:], cvec[:, :, :], [], [cvB], cvB)
        act(flat(cvb[:]), flat(cv[:]), AF.Silu, [cvB], [cvbB])
        for l in range(NL):
            pm, pmB = psA()
            for s6 in range(6):
                wt, wB = load_slab(w_ada[l, :, s6 * 1024:(s6 + 1) * 1024], 1024)
                for f in range(8):
                    fc = s6 * 8 + f
                    for k in range(KC):
                        mm(pm[:, fc * 2:fc * 2 + 2], wt[:, k, f * 128:(f + 1) * 128], cvb[:, k, :],
                           [wB, cvbB], pmB, start=(k == 0), stop=(k == KC - 1))
            for m in range(2):
                tt("dve", modT[:, l, m, :], pm.rearrange("p (f m) -> p m f", m=2)[:, m, 0:48],
                   pfm_s[:, l, Q_BADA:Q_BADA + 48], ALU.add, pmB + [pfmB], [modB])
            for o in (8, 32):
                ts("dve", modT[:, l, :, o:o + 8], modT[:, l, :, o:o + 8], 1.0, ALU.add, [modB], [modB])
        alloc_norm()
        for b in range(NBLK):
            xt, xtB = xt_p.get()
            P.dma("sp", xt[:], x0[:, :, b * 256:(b + 1) * 256], [], [xtB], xtB)
            P.dma("sp", xres[b], xt[:], [xtB], [DB("x", b)], xtB)
            norm_block(xt, xtB, b, 0, 0)

    xt_p = sq_p = xn_p = rs_p = None

    def alloc_norm():
        nonlocal xt_p, sq_p, xn_p, rs_p
        xt_p = mkpool("xt", [128, KC, 256], F32, 2)
        sq_p = mkpool("sq", [128, KC, 256], BF16, 2)
        xn_p = mkpool("xn", [128, KC, 256], F32, 2)
        rs_p = mkpool("rs", [128, 256], F32, 2)


    def norm_block(xt, xtB, b, l, which, final=False, out_tile=None):
        m = SEQS[blk_seq(b)][3]
        sq, sqB = sq_p.get()
        act(flat(sq[:]), flat(xt[:]), AF.Square, [xtB], [sqB])
        pn, pnB = psBr()
        for k in range(KC):
            mm(pn[:, 0:256], ones, sq[:, k, :], [cbB, sqB], pnB, start=(k == 0), stop=(k == KC - 1))
        rs, rsB = rs_p.get()
        cp("dve", rs[:], pn[:, 0:256], pnB, [rsB])
        rsqrt(rs[:], rsB, (128, 256), 1.0 / D)
        xn, xnB = xn_p.get()
        tt("dve", xn[:], xt[:], rs[:].unsqueeze(1).broadcast_to([128, KC, 256]), ALU.mult, [xtB, rsB], [xnB])
        if final:
            ot, otB = out_tile
            for k in range(KC):
                ts("pool", ot[:, k, :], xn[:, k, :], fng_s[:, k:k + 1], ALU.mult, [xnB, fngB], [otB])
            return
        c0 = hcol(b) + 1
        so, sh = (8, 0) if which == 0 else (32, 24)
        for k in range(KC):
            ts("pool", hT[:, k, c0:c0 + 256], xn[:, k, :], modT[:, l, m, so + k:so + k + 1], ALU.mult,
               [xnB, modB], [hB[b]], s2=modT[:, l, m, sh + k:sh + k + 1], op1=ALU.add)

    def hwin_bufs(b):
        s = blk_seq(b)
        f, n = SEQS[s][0], SEQS[s][1]
        r = [hB[b], hHalo]
        if b > f:
            r.append(hB[b - 1])
        if b < f + n - 1:
            r.append(hB[b + 1])
        return r

    st_p = acc_p = cvo_p = sm_p = cv8_p = qr_p = sq4_p = dg_p = u_p = None

    def alloc_d1():
        nonlocal st_p, acc_p, cvo_p, sm_p, cv8_p, ropec_s, ropecB, ropes_s, ropesB, qr_p, sq4_p, dg_p, u_p
        dg_p = mkpool("dg", [128, 4, 128], BF16, 16)
        u_p = mkpool("ub", [128, 260], BF16, 4)
        dg_live.clear()
        qr_p = mkpool("qr", [128, 4, 256], F32, 6)
        sq4_p = mkpool("sq4", [128, 4, 256], BF16, 4)
        ropec_s, ropecB = sb("ropec_s", [128, SL], F32)
        ropes_s, ropesB = sb("ropes_s", [128, SL], F32)
        P.dma("sp", ropec_s[:], ropec[:, :], [], [ropecB], ropecB)
        P.dma("sp", ropes_s[:], ropes[:, :], [], [ropesB], ropesB)
        st_p = mkpool("st", [128, 2048], BF16, 6)
        acc_p = None
        cvo_p = mkpool("cvo", [128, 256], BF16, 4)
        sm_p = mkpool("sm", [128, 256], F32, 4)
        cv8_p = None


    def proj_fm(wt, wB, wcol, b, n, halo):
        pp, ppB = psA()
        c0 = hcol(b) + (0 if halo else 1)
        rb = hwin_bufs(b) if halo else [hB[b]]
        for k in range(KC):
            mm(pp[:, 0:n], wt[:, k, wcol:wcol + 128], hT[:, k, c0:c0 + n], [wB] + rb, ppB,
               start=(k == 0), stop=(k == KC - 1))
        return pp, ppB

    def proj_tm(wt, wB, wcol, ncol, c):
        b, cc = c // 2, c % 2
        pp, ppB = psA()
        c0 = hcol(b) + 1 + 128 * cc
        for k in range(KC):
            mm(pp[:, 0:ncol], hT[:, k, c0:c0 + 128], wt[:, k, wcol:wcol + ncol], [wB, hB[b]], ppB,
               start=(k == 0), stop=(k == KC - 1))
        return pp, ppB

    dg_live = {}

    def conv_block(wt, wB, b, nfc, l, wofs, bofs, fcbase, stv, stB):
        def conv_b(ub, ubB, fci, dst):
            key = (l, wofs, fci)
            if key not in dg_live:
                dg, dgB = dg_p.get()
                for t in range(4):
                    ts("dve", dg[:, t, :], ident, pfm_s[:, l, wofs + fci * 4 + t:wofs + fci * 4 + t + 1], ALU.mult,
                       [cbB, pfmB], [dgB])
                dg_live[key] = (dg, dgB)
            dg, dgB = dg_live[key]
            pc, pcB = psBr()
            for t in range(4):
                mm(pc[:, 0:256], dg[:, t, :], ub[:, t:t + 256], [dgB, ubB], pcB, start=(t == 0), stop=(t == 3))
            act(dst, pc[:, 0:256], AF.Silu, pcB, [stB], bias=pfm_s[:, l, bofs + fci:bofs + fci + 1])

        pend = None
        for fc in range(nfc):
            pp, ppB = proj_fm(wt, wB, fc * 128, b, 259, True)
            ub, ubB = u_p.get()
            cp("act", ub[:, 0:259], pp[:, 0:259], ppB, [ubB])
            if pend is not None:
                conv_b(*pend)
            pend = (ub, ubB, fcbase + fc, stv[:, fc, :])
        conv_b(*pend)

    def transposes_to(dst, dstB, srcs, reads, scale=None, bank=None):
        n = len(srcs)
        pt, ptB = (psBr() if bank is None else psum_at(bank))
        ptb = pt.bitcast(BF16)
        for i, s_ in enumerate(srcs):
            tp(ptb[:, i * 128:(i + 1) * 128], s_, ident, reads + [cbB], ptB)
        if scale is None:
            cp("act", dst, ptb[:, 0:128 * n], ptB, [dstB])
        else:
            ts("dve", dst, ptb[:, 0:128 * n], scale, ALU.mult, ptB, [dstB])

    D1STOP = int(os.environ.get("K_D1STOP", "99"))

    def d1_layer(l):
        W = w_in[l]

        def ld_simple(off, n):
            return lambda: load_slab(W[:, off:off + n], n)

        def ld_dtg():
            slab = slab_p.get()
            load_slab(W[:, O_SDT:O_SDT + 32], 32, 0, slab)
            return load_slab(W[:, O_MG:O_MG + 32], 32, 32, slab)

        def ld_ak():
            slab = slab_p.get()
            load_slab(W[:, O_AK:O_AK + 256], 256, 0, slab)
            r = None
            for f in range(2):
                load_slab(W[:, O_AK + f * 128 + 64:O_AK + f * 128 + 128], 64, 256 + f * 128, slab)
                r = load_slab(W[:, O_AK + f * 128:O_AK + f * 128 + 64], 64, 256 + f * 128 + 64, slab)
            return r

        loaders = [ld_simple(O_SX, 1024), ld_simple(O_SB, 512), ld_simple(O_SZ, 1024), ld_dtg,
                   ld_simple(O_AQ, 1024), ld_ak, ld_simple(O_AV, 256), ld_simple(O_MQ, 1024),
                   ld_simple(O_MK, 1024), ld_simple(O_MV, 1024), ld_simple(O_MO, 1024),
                   ld_simple(O_G, 1024), ld_simple(O_G + 1024, 1024), ld_simple(O_G + 2048, 1024)]
        loaded = {}

        def get_w(i):
            for k in range(i + 2):
                if k < len(loaders) and k not in loaded:
                    loaded[k] = loaders[k]()
            return loaded[i]

        wt, wB = get_w(0)
        for b in range(NBLK):
            sv, svB = st_p.get()
            svv = sv[:, 0:2048].rearrange("p (f t) -> p f t", f=8)
            conv_block(wt, wB, b, 8, l, Q_SCW, Q_SCB, 0, svv, svB)
            for cc in range(2):
                st, stB = st_p.get()
                transposes_to(st[:, 0:1024], stB, [svv[:, f, cc * 128:(cc + 1) * 128] for f in range(8)], [svB])
                P.dma("sp", sx_tok[2 * b + cc], st[:, 0:1024], [stB], [DB("sx", 2 * b + cc)], stB)
        if D1STOP <= 1:
            return
        wt, wB = get_w(1)
        for b in range(NBLK):
            st, stB = st_p.get()
            stv = st[:, 0:1024].rearrange("p (f t) -> p f t", f=4)
            conv_block(wt, wB, b, 4, l, Q_SCW, Q_SCB, 8, stv, stB)
            P.dma("sp", sbcT[b], stv, [stB], [DB("sbcT", b)], stB)
            for cc in range(2):
                s2, s2B = st_p.get()
                transposes_to(s2[:, 0:256], s2B, [stv[:, f, cc * 128:(cc + 1) * 128] for f in range(2)], [stB])
                P.dma("sp", sb_tok[2 * b + cc], s2[:, 0:256], [s2B], [DB("sbt", 2 * b + cc)], s2B)
        if D1STOP <= 2:
            return
        wt, wB = get_w(2)
        for c in range(NCH):
            st, stB = st_p.get()
            for hh in range(2):
                pp, ppB = proj_tm(wt, wB, hh * 512, 512, c)
                act(st[:, hh * 512:(hh + 1) * 512], pp[:, 0:512], AF.Silu, ppB, [stB])
            P.dma("sp", sz_tok[c], st[:, 0:1024], [stB], [DB("sz", c)], stB)
        if D1STOP <= 3:
            return
        wt, wB = get_w(3)
        for c in range(NCH):
            sm, smB = sm_p.get()
            pp, ppB = proj_tm(wt, wB, 0, 64, c)
            cp("dve", sm[:, 0:64], pp[:, 0:64], ppB, [smB])
            P.dma("sp", dtg_tok[c], sm[:, 0:64], [smB], [DB("dtg", c)], smB)
        if D1STOP <= 4:
            return
        def qk_group(wt, wB, f0, b, gofs, stv, stB, nk_out=False):
            n = 4
            qr, qrB = qr_p.get()
            sq, sqB = sq4_p.get()
            rs, rsB = qr_p.get()
            for i in range(n):
                pp, ppB = proj_fm(wt, wB, (f0 + i) * 128, b, 256, False)
                act(sq[:, i, :], pp[:, 0:256], AF.Square, ppB, [sqB])
                cp("dve", qr[:, i, :], pp[:, 0:256], ppB, [qrB])
            for i in range(n):
                pn, pnB = psBr()
                mm(pn[:, 0:256], blk2, sq[:, i, :], [cbB, sqB], pnB)
                ts("dve", rs[:, i, :], pn[:, 0:256], 1.0 / 64, ALU.mult, pnB, [rsB], s2=EPS, op1=ALU.add)
            act(flat(rs[:]), flat(rs[:]), AF.Ln, [rsB], [rsB])
            act(flat(rs[:]), flat(rs[:]), AF.Exp, [rsB], [rsB], scale=-0.5)
            stt(qr[:], qr[:], pfm_s[:, l, gofs:gofs + 1], rs[:], ALU.mult, ALU.mult, [qrB, pfmB, rsB], [qrB])
            s_ = blk_seq(b)
            if nk_out and s_ != 0:
                P.dma("sp", nk_o[s_ - 1, l], qr[:, 0:2, :], [qrB], [DB("nk", s_, l)], qrB)
            if s_ != 0:
                cp("act", stv[:, f0:f0 + n, :], qr[:], [qrB], [stB])
                return
            qb, qbB = sq4_p.get()
            cp("act", flat(qb[:]), flat(qr[:]), [qrB], [qbB])
            a2, a2B = qr_p.get()
            t0 = 256 * b
            for i in range(n):
                pw, pwB = psBr()
                mm(pw[:, 0:256], pswap, qb[:, i, :], [cbB, qbB], pwB)
                tt("dve", a2[:, i, :], pw[:, 0:256], ropes_s[:, t0:t0 + 256], ALU.mult, pwB + [ropesB], [a2B])
            tt("dve", qr[:], qr[:], ropec_s[:, t0:t0 + 256].unsqueeze(1).broadcast_to([128, n, 256]), ALU.mult,
               [qrB, ropecB], [qrB])
            tt("pool", stv[:, f0:f0 + n, :], qr[:], a2[:], ALU.add, [qrB, a2B], [stB])

        wt, wB = get_w(4)
        for b in range(NBLK):
            st, stB = st_p.get()
            stv = st[:, 0:2048].rearrange("p (f t) -> p f t", f=8)
            for f0 in (0, 4):
                qk_group(wt, wB, f0, b, Q_QG, stv, stB)
            P.dma("sp", qT_d[b], stv, [stB], [DB("qT", b)], stB)
        if D1STOP <= 5:
            return
        wt, wB = get_w(5)
        for b in range(NBLK):
            st, stB = st_p.get()
            stv = st[:, 0:1024].rearrange("p (f t) -> p f t", f=4)
            qk_group(wt, wB, 0, b, Q_KG, stv, stB, nk_out=True)
            P.dma("sp", kT_d[b], stv, [stB], [DB("kT", b)], stB)
        if D1STOP <= 6:
            return
        wt, wB = get_w(6)
        for c in range(NCH):
            st, stB = st_p.get()
            pp, ppB = proj_tm(wt, wB, 0, 256, c)
            cp("act", st[:, 0:256], pp[:, 0:256], ppB, [stB])
            P.dma("sp", v_tok[c], st[:, 0:256], [stB], [DB("v", c)], stB)
            s_ = blk_seq(c // 2)
            if s_ != 0:
                sm, smB = sm_p.get()
                cp("dve", sm[:], pp[:, 0:256], ppB, [smB])
                t0 = (c % 2) * 128
                P.dma("sp", nv_o[s_ - 1, l, t0:t0 + 128, :], sm[:], [smB], [DB("nv", s_, l, c)], smB)
        if D1STOP <= 7:
            return
        for which, off, dstT in ((0, O_MQ, mqT), (1, O_MK, mkT)):
            wt, wB = get_w(7 + which)
            for b in range(NBLK):
                st, stB = st_p.get()
                stv = st[:, 0:2048].rearrange("p (f t) -> p f t", f=8)
                conv_block(wt, wB, b, 8, l, Q_MCW, Q_MCB, which * 8, stv, stB)
                P.dma("sp", dstT[b], stv, [stB], [DB("mqT" if which == 0 else "mkT", b)], stB)
                if which == 1:
                    for cc in range(2):
                        s2, s2B = st_p.get()
                        transposes_to(s2[:, 0:1024], s2B, [stv[:, f, cc * 128:(cc + 1) * 128] for f in range(8)],
                                      [stB], scale=MLK_SCALE)
                        P.dma("sp", mk_tok[2 * b + cc], s2[:, 0:1024], [s2B], [DB("mkt", 2 * b + cc)], s2B)
        if D1STOP <= 8:
            return
        for wi_, (off, dstT, key, fn) in enumerate(((O_MV, mv_tok, "mv", None), (O_MO, mo_tok, "mo", AF.Sigmoid))):
            wt, wB = get_w(9 + wi_)
            for c in range(NCH):
                st, stB = st_p.get()
                for hh in range(2):
                    pp, ppB = proj_tm(wt, wB, hh * 512, 512, c)
                    if fn is None:
                        cp("act", st[:, hh * 512:(hh + 1) * 512], pp[:, 0:512], ppB, [stB])
                    else:
                        act(st[:, hh * 512:(hh + 1) * 512], pp[:, 0:512], fn, ppB, [stB])
                P.dma("sp", dstT[c], st[:, 0:1024], [stB], [DB(key, c)], stB)
        if D1STOP <= 9:
            return
        for n in range(3):
            wt, wB = get_w(11 + n)
            for b in range(NBLK):
                st, stB = st_p.get()
                stv = st[:, 0:2048].rearrange("p (f t) -> p f t", f=8)
                for fc in range(8):
                    pp, ppB = proj_fm(wt, wB, fc * 128, b, 256, False)
                    act(stv[:, fc, :], pp[:, 0:256], AF.Sigmoid, ppB, [stB])
                P.dma("sp", gT[n, b], stv, [stB], [DB("gT", n, b)], stB)

    nsb = sb

    dtg_s = dtgB = g_dt = g_dtB = g_da = g_daB = g_dab = g_dabB = g_a = g_aB = g_acum = g_acumB = g_tot = g_totB = g_ea = g_eaB = g_w = g_wB = g_edec = g_edecB = g_tmp = g_tmpB = None

    def alloc_scan():
        nonlocal dtg_s, dtgB, g_dt, g_dtB, g_da, g_daB, g_dab, g_dabB, g_a, g_aB, g_acum, g_acumB, g_tot, g_totB, g_ea, g_eaB, g_w, g_wB, g_edec, g_edecB, g_tmp, g_tmpB
        dtg_s, dtgB = nsb("dtg_s", [128, NCH, 64])
        g_dt, g_dtB = nsb("g_dt", [128, NCH, 32])
        g_da, g_daB = nsb("g_da", [128, NCH, 32])
        g_dab, g_dabB = nsb("g_dab", [128, NCH, 32], BF16)
        g_a, g_aB = nsb("g_a", [128, 32])
        g_acum, g_acumB = nsb("g_acum", [128, 2, NCH, 16])
        g_tot, g_totB = nsb("g_tot", [128, 2, NCH, 16])
        g_ea, g_eaB = nsb("g_ea", [128, 2, NCH, 16])
        g_w, g_wB = nsb("g_w", [128, 2, NCH, 16])
        g_edec, g_edecB = nsb("g_edec", [128, 2, NCH, 8])
        g_tmp, g_tmpB = nsb("g_tmp", [128, 2, NCH, 16])


    def run_sweeps(chunk_loads, chunkA, chunkB, seq_begin, seq_end, PF=2):
        steps = []
        for s in range(3):
            n = SEQS[s][1]
            orders = [chunk_order(s, 0), chunk_order(s, 1)]
            for i in range(2 * n):
                for d in range(2):
                    steps.append((s, d, orders[d][i], i >= n))
        loaded, fronts = {}, {}
        for k in range(min(PF, len(steps))):
            loaded[k] = chunk_loads(*steps[k])
        fronts[0] = chunkA(*steps[0], loaded[0])
        for k, st in enumerate(steps):
            if k + PF < len(steps):
                loaded[k + PF] = chunk_loads(*steps[k + PF])
            if k + 1 < len(steps):
                fronts[k + 1] = chunkA(*steps[k + 1], loaded[k + 1])
            if k == 0 or steps[k - 1][0] != st[0]:
                seq_begin(st[0])
            chunkB(*st, loaded.pop(k), fronts.pop(k))
            if k == len(steps) - 1 or steps[k + 1][0] != st[0]:
                seq_end(st[0])

    def chunk_order(s, d):
        f, n = SEQS[s][0], SEQS[s][1]
        cs = list(range(2 * f, 2 * (f + n)))
        return cs if d == 0 else cs[::-1]

    xk_p = xk2_p = bt_p = bct_p = big_p = arg_p = yo_p = tok_p = ytT_p = s1_p = None

    def alloc_mix(ssd=True):
        nonlocal xk_p, xk2_p, bt_p, bct_p, big_p, arg_p, yo_p, tok_p, ytT_p, s1_p, cvo_p
        cvo_p = mkpool("cvo", [128, 256], BF16, 6)
        xk_p = mkpool("xk", [128, 1024], BF16, 5 if ssd else 8)
        xk2_p = mkpool("xk2", [128, 1024], BF16, 6)
        if ssd:
            bt_p = mkpool("bt", [128, 256], BF16, 5)
            bct_p = mkpool("bct", [128, 4, 128], BF16, 5)
            big_p = mkpool("big", [128, 2048], BF16, 6)
            arg_p = mkpool("arg", [128, 2048], F32, 2)
        yo_p = mkpool("yo", [128, 1024], F32, 6 if ssd else 5)
        tok_p = mkpool("tok", [128, 1024], BF16, 6)
        ytT_p = mkpool("ytT", [128, 8, 128], BF16, 2)
        s1_p = mkpool("s1", [128, 8], F32, 10)


    def out_transposed(yn, ynB, br, c, bank=7):
        b, cc = c // 2, c % 2
        yt, ytB = ytT_p.get()
        transposes_to(flat(yt[:]), ytB, [yn[:, f * 128:(f + 1) * 128] for f in range(8)], [ynB], bank=bank)
        P.dma("sp", ybr[br, b, :, :, cc * 128:(cc + 1) * 128], yt[:], [ytB], [DB("ybr", br, b)], ytB)

    Hs = Hb = None

    def alloc_ssd():
        nonlocal Hs, Hb
        Hs = [nsb("Hs%d" % d, [128, 2, 256]) for d in range(2)]
        Hb = [nsb("Hb%d" % d, [128, 2, 256], BF16) for d in range(2)]


    def ssd_layer(l, pr, prB):
        P.dma("sp", dtg_s[:], dtg_tok.rearrange("c p n -> p c n"), [DB("dtg", c) for c in range(NCH)],
              [dtgB], dtgB)
        tt("dve", g_dt[:], dtg_s[:, :, 0:32], pr[:, P_DTB:P_DTB + 32].unsqueeze(1).broadcast_to([128, NCH, 32]),
           ALU.add, [dtgB, prB], [g_dtB])
        act(flat(g_dt[:]), flat(g_dt[:]), AF.Exp, [g_dtB], [g_dtB])
        act(flat(g_dt[:]), flat(g_dt[:]), AF.Ln, [g_dtB], [g_dtB], bias=1.0)
        act(g_a[:], pr[:, P_ALOG:P_ALOG + 32], AF.Exp, [prB], [g_aB])
        ts("dve", g_a[:], g_a[:], -1.0, ALU.mult, [g_aB], [g_aB])
        tt("dve", g_da[:], g_dt[:], g_a[:].unsqueeze(1).broadcast_to([128, NCH, 32]), ALU.mult,
           [g_dtB, g_aB], [g_daB])
        cp("dve", g_dab[:], g_da[:], [g_daB], [g_dabB])
        for d in range(2):
            pa, paB = psA()
            mm(pa[:, 0:NCH * 16].rearrange("p (c h) -> p c h", h=16), tri[d], g_dab[:, :, d * 16:(d + 1) * 16],
               [cbB, g_dabB], paB)
            cp("dve", g_acum[:, d], pa[:, 0:NCH * 16].rearrange("p (c h) -> p c h", h=16), paB, [g_acumB])
            pb, pbB = psA()
            mm(pb[:, 0:NCH * 16].rearrange("p (c h) -> p c h", h=16), ones, g_dab[:, :, d * 16:(d + 1) * 16],
               [cbB, g_dabB], pbB)
            cp("dve", g_tot[:, d], pb[:, 0:NCH * 16].rearrange("p (c h) -> p c h", h=16), pbB, [g_totB])
        fl4 = lambda t: t[:].rearrange("p d c h -> p (d c h)")
        act(fl4(g_ea), fl4(g_acum), AF.Exp, [g_acumB], [g_eaB])
        tt("dve", g_tmp[:], g_tot[:], g_acum[:], ALU.subtract, [g_totB, g_acumB], [g_tmpB])
        act(fl4(g_tmp), fl4(g_tmp), AF.Exp, [g_tmpB], [g_tmpB])
        for d in range(2):
            tt("dve", g_w[:, d], g_tmp[:, d], g_dt[:, :, d * 16:(d + 1) * 16], ALU.mult, [g_tmpB, g_dtB], [g_wB])
        for d in range(2):
            tv = g_tot[:, d].rearrange("p c (g f r) -> p c g f r", g=2, f=2)
            for hf in range(2):
                ps_ = slice(hf * 64, (hf + 1) * 64)
                act(g_edec[ps_, d].rearrange("p c (g r) -> p c g r", g=2), tv[ps_, :, :, hf, :], AF.Exp,
                    [g_totB], [g_edecB])

        def chunk_loads(s, d, c, last_sweep):
            b, cc = c // 2, c % 2
            x, xB = xk_p.get()
            P.dma("sp", x[:], sx_tok[c], [DB("sx", c)], [xB], xB)
            bt, btB = bt_p.get()
            P.dma("sp", bt[:], sb_tok[c], [DB("sbt", c)], [btB], btB)
            bct, bctB = bct_p.get()
            P.dma("sp", bct[:], sbcT[b, :, :, cc * 128:(cc + 1) * 128], [DB("sbcT", b)], [bctB], bctB)
            z = zB = None
            if last_sweep:
                z, zB = tok_p.get()
                P.dma("sp", z[:], sz_tok[c], [DB("sz", c)], [zB], zB)
            return x, xB, bt, btB, bct, bctB, z, zB

        def chunk(s, d, c, last_sweep, tl):
            b, cc = c // 2, c % 2
            hs = slice(d * 16, (d + 1) * 16)
            H, HB_ = Hs[d]
            Hbf, HbB = Hb[d]
            x, xB, bt, btB, bct, bctB, z, zB = tl
            dau, dauB = big_p.get()
            dau3 = dau[:].rearrange("p (h i) -> p h i", h=16)
            tt("pool", dau3, tri[d].unsqueeze(1).broadcast_to([128, 16, 128]),
               g_dab[:, c, hs].unsqueeze(2).broadcast_to([128, 16, 128]), ALU.mult, [cbB, g_dabB], [dauB])
            sg, sgB = psum_at(0, 4)
            for q in range(4):
                mm(sg[:, q * 512:(q + 1) * 512], ones, dau[:, q * 512:(q + 1) * 512], [cbB, dauB], sgB,
                   start=True, stop=False)
                mm(sg[:, q * 512:(q + 1) * 512].rearrange("p (h i) -> p h i", h=4), ident,
                   nmask[d].unsqueeze(1).broadcast_to([128, 4, 128]), [cbB], sgB, start=False, stop=True)
            ar, arB = arg_p.get()
            tt("dve", ar[:].rearrange("p (h i) -> p h i", h=16), sg.rearrange("p (h i) -> p h i", h=16),
               g_acum[:, d, c, :].unsqueeze(2).broadcast_to([128, 16, 128]), ALU.subtract, sgB + [g_acumB], [arB])
            Lm, LmB = big_p.get()
            act(Lm[:], ar[:], AF.Exp, [arB], [LmB])
            pcs = [psum_at(4), psum_at(5)]
            for g in range(4):
                ps_ = slice((g % 2) * 64, (g % 2) * 64 + 64)
                pc, pcB = pcs[g % 2]
                mm(pc[:, (g // 2) * 128:(g // 2 + 1) * 128], bct[ps_, g // 2, :], bct[ps_, 2 + g // 2, :], [bctB], pcB)
            cbts = []
            for par in range(2):
                ct, ctB = cvo_p.get()
                cp("act", ct[:], pcs[par][0][:, 0:256], pcs[par][1], [ctB])
                cbts.append((ct, ctB))
            sc, scB = big_p.get()
            scv = sc[:].rearrange("p (gg two r i) -> p gg two r i", gg=2, two=2, r=4)
            Lmv = Lm[:].rearrange("p (gg two r i) -> p gg two r i", gg=2, two=2, r=4)
            for par, (ct, ctB) in enumerate(cbts):
                tt("pool", scv[:, :, par], Lmv[:, :, par],
                   ct[:].rearrange("p (g i) -> p g i", g=2).unsqueeze(2).broadcast_to([128, 2, 4, 128]),
                   ALU.mult, [LmB, ctB], [scB])
            return sc, scB

        def chunkB(s, d, c, last_sweep, tl, sa):
            b, cc = c // 2, c % 2
            hs = slice(d * 16, (d + 1) * 16)
            H, HB_ = Hs[d]
            Hbf, HbB = Hb[d]
            x, xB, bt, btB, bct, bctB, z, zB = tl
            sc, scB = sa
            xd, xdB = xk2_p.get()
            tt("dve", xd[:].rearrange("p (h q) -> p h q", h=16), x[:].rearrange("p (h q) -> p h q", h=16),
               g_dt[:, c, hs].unsqueeze(2).broadcast_to([128, 16, 64]), ALU.mult, [xB, g_dtB], [xdB])
            wx, wxB = xk2_p.get()
            tt("dve", wx[:].rearrange("p (h q) -> p h q", h=16), x[:].rearrange("p (h q) -> p h q", h=16),
               g_w[:, d, c, :].unsqueeze(2).broadcast_to([128, 16, 64]), ALU.mult, [xB, g_wB], [wxB])
            yi, yiB = psum_at(6, 2)
            for h in range(16):
                mm(yi[:, h * 64:(h + 1) * 64], sc[:, h * 128:(h + 1) * 128], xd[:, h * 64:(h + 1) * 64],
                   [scB, xdB], [yiB[h // 8]])
            ys, ysB = psum_at(4, 2)
            for g in range(4):
                ps_ = slice((g % 2) * 64, (g % 2) * 64 + 64)
                co = (g % 2) * 512 + (g // 2) * 256
                mm(ys[:, co:co + 256], bct[ps_, 2 + g // 2, :], Hbf[ps_, g // 2, :], [bctB, HbB], [ysB[g % 2]])
            yo, yoB = yo_p.get()
            yov = yo[:].rearrange("p (gg two r q) -> p gg two r q", gg=2, two=2, r=4)
            eav = g_ea[:, d, c, :].rearrange("p (gg two r) -> p gg two r", gg=2, two=2)
            for par in range(2):
                tt("dve", yov[:, :, par], ys[:, par * 512:(par + 1) * 512].rearrange("p (gg r q) -> p gg r q", gg=2, r=4),
                   eav[:, :, par].unsqueeze(3).broadcast_to([128, 2, 4, 64]), ALU.mult, [ysB[par], g_eaB], [yoB])
            tt("dve", yo[:], yo[:], yi, ALU.add, [yoB] + yiB, [yoB])
            dh, dhB = psum_at(4, 2)
            for gg in range(2):
                mm(dh[:, gg * 512:(gg + 1) * 512], bt[:, gg * 128:(gg + 1) * 128], wx[:, gg * 512:(gg + 1) * 512],
                   [btB, wxB], [dhB[gg]])
            tt("dve", H[:].rearrange("p g (r q) -> p g r q", r=4), H[:].rearrange("p g (r q) -> p g r q", r=4),
               g_edec[:, d, c, :].rearrange("p (g r) -> p g r", g=2).unsqueeze(3).broadcast_to([128, 2, 4, 64]),
               ALU.mult, [HB_, g_edecB], [HB_])
            dhv = dh.rearrange("p (g x) -> p g x", g=2)
            for hf in range(2):
                ps_ = slice(hf * 64, (hf + 1) * 64)
                tt("dve", H[ps_], H[ps_], dhv[ps_, :, hf * 256:(hf + 1) * 256], ALU.add, [HB_] + dhB, [HB_])
            cp("act", flat(Hbf[:]), flat(H[:]), [HB_], [HbB])
            if not last_sweep:
                P.dma("sp", yf_d[c], yo[:], [yoB], [DB("yf", c)], yoB)
                return
            yf, yfB = yo_p.get()
            P.dma("sp", yf[:], yf_d[c], [DB("yf", c)], [yfB], yfB)
            tt("dve", yo[:], yo[:], yf[:], ALU.add, [yoB, yfB], [yoB])
            xd2, xd2B = yo_p.get()
            tt("pool", xd2[:].rearrange("p (h q) -> p h q", h=16), x[:].rearrange("p (h q) -> p h q", h=16),
               pr[:, P_SD:P_SD + 16].unsqueeze(2).broadcast_to([128, 16, 64]), ALU.mult, [xB, prB], [xd2B])
            tt("dve", yo[:], yo[:], xd2[:], ALU.add, [yoB, xd2B], [yoB])
            tt("dve", yo[:], yo[:], z[:], ALU.mult, [yoB, zB], [yoB])
            s1, s1B = s1_p.get()
            act(yf[:], yo[:], AF.Square, [yoB], [yfB, s1B], accum=s1[:, 0:1])
            rsqrt(s1[:, 0:1], s1B, (128, 1), 1.0 / D)
            yn, ynB = tok_p.get()
            stt(yn[:], yo[:], s1[:, 0:1], pr[:, P_SSDN:P_SSDN + 1024], ALU.mult, ALU.mult, [yoB, s1B, prB], [ynB])
            out_transposed(yn, ynB, 0, c, bank=6)

        def seq_begin(s):
            has_ctx = SEQS[s][2]
            for d in range(2):
                H, HB_ = Hs[d]
                if has_ctx:
                    P.dma("sp", flat(H[:]), ssd0[l, d], [], [HB_], HB_)
                else:
                    memset("dve", flat(H[:]), 0.0, [HB_])
                cp("pool", Hb[d][0][:], H[:], [HB_], [Hb[d][1]])

        def seq_end(s):
            if not SEQS[s][2]:
                for d in range(2):
                    H, HB_ = Hs[d]
                    P.dma("sp", nssd_o[s - 1, l, d], flat(H[:]), [HB_], [DB("nssd", s, l, d)], HB_)

        run_sweeps(chunk_loads, chunk, chunkB, seq_begin, seq_end)

    CS = CSb = m_li = m_liB = m_lf = m_lfB = m_lfb = m_lfbB = m_b = m_bB = m_g = m_gB = m_wt = m_wtB = m_fl = m_flB = m_RB = m_RBB = m_SCB = m_SCBB = m_GM = m_GMB = m_BT = m_BTB = m_R = m_RB2 = m_MD = m_MDB = m_mp = m_mpB = m_RD = m_RDB = vp_p = mT_p = None

    def alloc_ml():
        nonlocal CS, CSb, m_li, m_liB, m_lf, m_lfB, m_lfb, m_lfbB, m_b, m_bB, m_g, m_gB, m_wt, m_wtB, m_fl, m_flB, m_RB, m_RBB, m_SCB, m_SCBB, m_GM, m_GMB, m_BT, m_BTB, m_R, m_RB2, m_MD, m_MDB, m_mp, m_mpB, m_RD, m_RDB, vp_p, mT_p, dtg_s, dtgB, g_da, g_daB
        dtg_s, dtgB = nsb("dtg_s", [128, NCH, 64])
        g_da, g_daB = nsb("g_da", [128, NCH, 32])
        CS = [nsb("CS%d" % d, [128, 8, 129]) for d in range(2)]
        CSb = [nsb("CSb%d" % d, [128, 8, 129], BF16) for d in range(2)]
        m_li, m_liB = nsb("m_li", [128, 2, NCH, 8])
        m_lf, m_lfB = nsb("m_lf", [128, 2, NCH, 8])
        m_lfb, m_lfbB = nsb("m_lfb", [128, 2, NCH, 8], BF16)
        m_b, m_bB = nsb("m_b", [128, 2, NCH, 8])
        m_g, m_gB = nsb("m_g", [128, 2, NCH, 8])
        m_wt, m_wtB = nsb("m_wt", [128, 2, NCH, 8])
        m_fl, m_flB = nsb("m_fl", [128, 2, NCH, 8])
        m_RB, m_RBB = nsb("m_RB", [128, 2, NCH, 8])
        m_SCB, m_SCBB = nsb("m_SCB", [128, 2, NCH, 8])
        m_GM, m_GMB = nsb("m_GM", [8, 2, NCH])
        m_BT, m_BTB = nsb("m_BT", [8, 2, NCH])
        m_R, m_RB2 = nsb("m_R", [8, 2, NCH])
        m_MD, m_MDB = nsb("m_MD", [8, 2, NCH])
        m_mp, m_mpB = nsb("m_mp", [8, 2, 3, NCH + 1])
        m_RD, m_RDB = nsb("m_RD", [8, 2, 2, NCH, 8])
        vp_p = mkpool("vp", [128, 8, 129], BF16, 4)
        mT_p = mkpool("mT", [128, 8, 128], BF16, 8)


    def ml_layer(l, pr, prB):
        pre = g_da
        preB = g_daB
        P.dma("sp", dtg_s[:], dtg_tok.rearrange("c p n -> p c n"), [DB("dtg", c) for c in range(NCH)],
              [dtgB], dtgB)
        tt("dve", pre[:], dtg_s[:, :, 32:64], pr[:, P_MGB:P_MGB + 32].unsqueeze(1).broadcast_to([128, NCH, 32]),
           ALU.add, [dtgB, prB], [preB])
        for d in range(2):
            cp("dve", m_li[:, d], pre[:, :, d * 16:d * 16 + 8], [preB], [m_liB])
            act(m_lf[:, d], pre[:, :, d * 16 + 8:d * 16 + 16], AF.Exp, [preB], [m_lfB], scale=-1.0)
        fl4 = lambda t: t[:].rearrange("p d c h -> p (d c h)")
        act(fl4(m_lf), fl4(m_lf), AF.Ln, [m_lfB], [m_lfB], bias=1.0)
        ts("dve", fl4(m_lf), fl4(m_lf), -1.0, ALU.mult, [m_lfB], [m_lfB])
        cp("dve", fl4(m_lfb), fl4(m_lf), [m_lfB], [m_lfbB])
        for d in range(2):
            pa, paB = psA()
            pav = pa[:, 0:NCH * 8].rearrange("p (c h) -> p c h", h=8)
            mm(pav, tri[d], m_lfb[:, d], [cbB, m_lfbB], paB)
            cp("dve", m_b[:, d], pav, paB, [m_bB])
        tt("dve", m_g[:], m_li[:], m_b[:], ALU.subtract, [m_liB, m_bB], [m_gB])
        for d in range(2):
            for c0 in range(0, NCH, 4):
                n = min(4, NCH - c0)
                pt, ptB = psA()
                for i in range(n):
                    tp(pt[0:8, i * 128:(i + 1) * 128], m_g[:, d, c0 + i, :], cf[:], [m_gB, cfB], ptB)
                P.add("dve", (lambda o=m_GM[:, d, c0:c0 + n], i_=pt[0:8, 0:n * 128].rearrange("p (c j) -> p c j", j=128):
                              V.tensor_reduce(out=o, in_=i_, axis=AX.X, op=ALU.max)), ptB, [m_GMB])
            pb, pbB = psBr()
            for c in range(NCH):
                mm(pb[0:8, c:c + 1], m_lfb[:, d, c, :], ones[:, 0:1], [m_lfbB, cbB], pbB)
            cp("dve", m_BT[:, d], pb[0:8, 0:NCH], pbB, [m_BTB])
        for d in range(2):
            for s in range(3):
                f, n, has_ctx, _ = SEQS[s]
                mp = m_mp[:, d, s]
                if has_ctx:
                    P.dma("sp", mp[:, 0:1], mlm0[l, d], [], [m_mpB], m_mpB)
                else:
                    memset("dve", mp[:, 0:1], 0.0, [m_mpB])
                for i, c in enumerate(chunk_order(s, d)):
                    tt("dve", m_R[:, d, c:c + 1], mp[:, i:i + 1], m_GM[:, d, c:c + 1], ALU.max,
                       [m_mpB, m_GMB], [m_RB2])
                    tt("dve", m_MD[:, d, c:c + 1], mp[:, i:i + 1], m_R[:, d, c:c + 1], ALU.subtract,
                       [m_mpB, m_RB2], [m_MDB])
                    tt("dve", mp[:, i + 1:i + 2], m_R[:, d, c:c + 1], m_BT[:, d, c:c + 1], ALU.add,
                       [m_RB2, m_BTB], [m_mpB])
                if not has_ctx:
                    P.dma("sp", nmlm_o[s - 1, l, d], mp[:, 2 * n:2 * n + 1], [m_mpB], [DB("nmlm", s, l, d)], m_mpB)
        for d in range(2):
            for wi, (src, srcB) in enumerate(((m_R, m_RB2), (m_MD, m_MDB))):
                tt("dve", m_RD[:, d, wi], src[:, d, :].unsqueeze(2).broadcast_to([8, NCH, 8]),
                   cf[0:8, 0:8].unsqueeze(1).broadcast_to([8, NCH, 8]), ALU.mult, [srcB, cfB], [m_RDB])
            pr_, prB_ = psBr()
            mm(pr_[:, 0:2 * NCH * 8], onesf[0:8, :], m_RD[:, d].rearrange("p w c h -> p (w c h)"),
               [onesfB, m_RDB], prB_)
            cp("dve", m_RB[:, d], pr_[:, 0:NCH * 8].rearrange("p (c h) -> p c h", h=8), prB_, [m_RBB])
            act(m_SCB[:, d], pr_[:, NCH * 8:2 * NCH * 8].rearrange("p (c h) -> p c h", h=8), AF.Exp, prB_, [m_SCBB])
        tt("dve", m_wt[:], m_g[:], m_RB[:], ALU.subtract, [m_gB, m_RBB], [m_wtB])
        act(fl4(m_wt), fl4(m_wt), AF.Exp, [m_wtB], [m_wtB])
        tt("dve", m_fl[:], m_b[:], m_RB[:], ALU.add, [m_bB, m_RBB], [m_flB])
        ts("dve", fl4(m_fl), fl4(m_fl), -1.0, ALU.mult, [m_flB], [m_flB], s2=80.0, op1=ALU.min)
        act(fl4(m_fl), fl4(m_fl), AF.Exp, [m_flB], [m_flB])

        def chunk_loads(s, d, c, last_sweep):
            b, cc = c // 2, c % 2
            q, qB = mT_p.get()
            P.dma("sp", q[:], mqT[b, :, :, cc * 128:(cc + 1) * 128], [DB("mqT", b)], [qB], qB)
            k, kB = mT_p.get()
            P.dma("sp", k[:], mkT[b, :, :, cc * 128:(cc + 1) * 128], [DB("mkT", b)], [kB], kB)
            kt, ktB = xk_p.get()
            P.dma("sp", kt[:], mk_tok[c], [DB("mkt", c)], [ktB], ktB)
            v, vB = xk_p.get()
            P.dma("sp", v[:], mv_tok[c], [DB("mv", c)], [vB], vB)
            mo = moB = None
            if last_sweep:
                mo, moB = tok_p.get()
                P.dma("sp", mo[:], mo_tok[c], [DB("mo", c)], [moB], moB)
            return q, qB, k, kB, kt, ktB, v, vB, mo, moB

        def chunk(s, d, c, last_sweep, tl):
            b, cc = c // 2, c % 2
            C, CB_ = CS[d]
            Cb, CbB = CSb[d]
            q, qB, k, kB, kt, ktB, v, vB, mo, moB = tl
            sc, scB = psum_at(0, 2)
            for h in range(8):
                mm(sc[:, h * 128:(h + 1) * 128], k[:, h, :], q[:, h, :], [kB, qB], [scB[h // 4]])
            sm_, smB_ = xk2_p.get()
            stt(sm_[:].rearrange("p (h t) -> p h t", h=8), sc.rearrange("p (h t) -> p h t", h=8), MLK_SCALE,
                tri[d].unsqueeze(1).broadcast_to([128, 8, 128]), ALU.mult, ALU.mult, scB + [cbB], [smB_])
            vp, vpB = vp_p.get()
            tt("pool", vp[:, :, 0:128], v[:].rearrange("p (h e) -> p h e", h=8),
               m_wt[:, d, c, :].unsqueeze(2).broadcast_to([128, 8, 128]), ALU.mult, [vB, m_wtB], [vpB])
            cp("pool", vp[:, :, 128:129], m_wt[:, d, c, :].unsqueeze(2), [m_wtB], [vpB])
            return sm_, smB_, vp, vpB

        def chunkB(s, d, c, last_sweep, tl, sa):
            b, cc = c // 2, c % 2
            C, CB_ = CS[d]
            Cb, CbB = CSb[d]
            q, qB, k, kB, kt, ktB, v, vB, mo, moB = tl
            sm_, smB_, vp, vpB = sa
            tt("dve", C[:], C[:], m_SCB[:, d, c, :].unsqueeze(2).broadcast_to([128, 8, 129]), ALU.mult,
               [CB_, m_SCBB], [CB_])
            cp("act", Cb[:].rearrange("p h e -> p (h e)"), C[:].rearrange("p h e -> p (h e)"), [CB_], [CbB])
            nm, nmB = psum_at(2, 2)
            dn, dnB = psum_at(4)
            for h in range(8):
                mm(nm[:, h * 128:(h + 1) * 128], sm_[:, h * 128:(h + 1) * 128], vp[:, h, 0:128], [smB_, vpB],
                   [nmB[h // 4]], start=True, stop=False)
                mm(nm[:, h * 128:(h + 1) * 128], q[:, h, :], Cb[:, h, 0:128], [qB, CbB], [nmB[h // 4]],
                   start=False, stop=True)
            for h in range(8):
                mm(dn[:, h:h + 1], sm_[:, h * 128:(h + 1) * 128], vp[:, h, 128:129], [smB_, vpB], dnB,
                   start=True, stop=False)
                mm(dn[:, h:h + 1], q[:, h, :], Cb[:, h, 128:129], [qB, CbB], dnB, start=False, stop=True)
            dc, dcB = psum_at(5, 2)
            for h in range(8):
                mm(dc[:, h * 128:(h + 1) * 128], kt[:, h * 128:(h + 1) * 128], vp[:, h, 0:128], [ktB, vpB],
                   [dcB[h // 4]])
            for h in range(8):
                mm(dn[:, 8 + h:9 + h], kt[:, h * 128:(h + 1) * 128], vp[:, h, 128:129], [ktB, vpB], dnB)
            dd, ddB = s1_p.get()
            cp("dve", dd[:], dn[:, 0:8], dnB, [ddB])
            stt(dd[:], dd[:], -1.0, dd[:], ALU.mult, ALU.max, [ddB], [ddB])
            tt("dve", dd[:], dd[:], m_fl[:, d, c, :], ALU.max, [ddB, m_flB], [ddB])
            P.add("dve", lambda o=dd[:]: V.reciprocal(out=o, in_=o), [ddB], [ddB])
            hd, hdB = yo_p.get()
            tt("dve", hd[:].rearrange("p (h e) -> p h e", h=8), nm.rearrange("p (h e) -> p h e", h=8),
               dd[:].unsqueeze(2).broadcast_to([128, 8, 128]), ALU.mult, nmB + [ddB], [hdB])
            tt("dve", C[:, :, 0:128], C[:, :, 0:128], dc.rearrange("p (h e) -> p h e", h=8), ALU.add,
               [CB_] + dcB, [CB_])
            tt("dve", C[:, :, 128:129], C[:, :, 128:129], dn[:, 8:16].unsqueeze(2), ALU.add, [CB_] + dnB, [CB_])
            if not last_sweep:
                P.dma("sp", hf_d[c], hd[:], [hdB], [DB("hf", c)], hdB)
                return
            hf, hfB = yo_p.get()
            P.dma("sp", hf[:], hf_d[c], [DB("hf", c)], [hfB], hfB)
            tt("dve", hd[:], hd[:], hf[:], ALU.add, [hdB, hfB], [hdB])
            act(hf[:], hd[:], AF.Square, [hdB], [hfB])
            s1, s1B = s1_p.get()
            P.add("dve", lambda o=s1[:], i_=hf[:].rearrange("p (h e) -> p h e", h=8):
                  V.tensor_reduce(out=o, in_=i_, axis=AX.X, op=ALU.add), [hfB], [s1B])
            rsqrt(s1[:], s1B, (128, 8), 1.0 / 128)
            tt("dve", hd[:].rearrange("p (h e) -> p h e", h=8), hd[:].rearrange("p (h e) -> p h e", h=8),
               s1[:].unsqueeze(2).broadcast_to([128, 8, 128]), ALU.mult, [hdB, s1B], [hdB])
            tt("dve", hd[:], hd[:], pr[:, P_MLN:P_MLN + 1024], ALU.mult, [hdB, prB], [hdB])
            yn, ynB = tok_p.get()
            tt("dve", yn[:], hd[:], mo[:], ALU.mult, [hdB, moB], [ynB])
            out_transposed(yn, ynB, 2, c)

        def seq_begin(s):
            for d in range(2):
                C, CB_ = CS[d]
                if SEQS[s][2]:
                    P.dma("sp", C[:].rearrange("p h e -> p (h e)"), mlc0[l, d], [], [CB_], CB_)
                else:
                    memset("dve", C[:].rearrange("p h e -> p (h e)"), 0.0, [CB_])

        def seq_end(s):
            if not SEQS[s][2]:
                for d in range(2):
                    C, CB_ = CS[d]
                    P.dma("sp", nmlc_o[s - 1, l, d], C[:].rearrange("p h e -> p (h e)"), [CB_],
                          [DB("nmlc", s, l, d)], CB_)

        run_sweeps(chunk_loads, chunk, chunkB, seq_begin, seq_end)

    NKS = NCTX + SL
    KT = KTB = VA = VAB = VB_ = VBB = ckd = ckdB = q_p = e_p = yat_p = dsb_p = ev_p = None

    def alloc_att():
        nonlocal KT, KTB, VA, VAB, VB_, VBB, ckd, ckdB, q_p, e_p, yat_p, dsb_p, ev_p
        KT, KTB = nsb("KT", [128, 8, NKS], BF16)
        memset("dve", KT[:].rearrange("p a b -> p (a b)"), 0.0, [KTB])
        VA, VAB = nsb("VA", [128, (NKS // 128), 4, 128], BF16)
        VB_, VBB = nsb("VBt", [128, (NKS // 128), 4, 128], BF16)
        ckd, ckdB = nsb("ckd", [128, 4, 4, 128], BF16)
        memset("pool", VA[:].rearrange("p a b c -> p (a b c)"), 0.0, [VAB])
        memset("pool", VB_[:].rearrange("p a b c -> p (a b c)"), 0.0, [VBB])
        for kc_ in range(NKS // 128):
            memset("pool", VA[:, kc_, :, 64:128], 1.0, [VAB])
            memset("pool", VB_[:, kc_, :, 0:64], 1.0, [VBB])
        q_p = mkpool("qatt", [128, 8, 512], BF16, 2)
        e_p = mkpool("eatt", [128, 2, 512], BF16, 2)
        yat_p = mkpool("yat", [128, 8, 512], BF16, 2)
        dsb_p = mkpool("dsb", [128, 512], F32, 4)
        ev_p = mkpool("ev", [128, 2, 512], F32, 4)


    def att_layer(l):
        for s in range(3):
            f, n, has_ctx, _ = SEQS[s]
            L = 256 * n
            nctx = NCTX if has_ctx else 0
            nk = nctx + L
            nkc = nk // 128
            if has_ctx:
                src = cache_k[l].rearrange("(kc p) (f c) -> p kc f c", p=128, f=2)
                P.dma("pool", ckd[:, :, 0:2, :], src, [], [ckdB], ckdB)
                for f2 in range(2):
                    P.dma("pool", ckd[:, :, 2 + f2, 0:64], src[:, :, f2, 64:128], [], [ckdB], ckdB)
                    P.dma("pool", ckd[:, :, 2 + f2, 64:128], src[:, :, f2, 0:64], [], [ckdB], ckdB)
                KTv = KT[:].rearrange("p (f sw hh) t -> p sw f hh t", f=2, sw=2, hh=2)
                for kc in range(4):
                    pt, ptB = psBr()
                    ptb = pt.bitcast(BF16)
                    for fv in range(4):
                        tp(ptb[:, fv * 128:(fv + 1) * 128], ckd[:, kc, fv, :], ident, [ckdB, cbB], ptB)
                    ptv = ptb[:, 0:512].rearrange("p (sw f t) -> p sw f t", sw=2, f=2)
                    ks = slice(kc * 128, (kc + 1) * 128)
                    cp("act", KTv[0:64, :, :, 0, ks], ptv[0:64], ptB, [KTB])
                    for sw in range(2):
                        cp("act", KTv[64:128, 1 - sw, :, 1, ks], ptv[64:128, sw], ptB, [KTB])
                srcv = cache_v[l].rearrange("(kc p) (g c) -> p kc g c", p=128, g=4)
                for kc in range(4):
                    P.dma("pool", VA[:, kc, :, 0:64], srcv[:, kc], [], [VAB], VAB)
                    P.dma("pool", VB_[:, kc, :, 64:128], srcv[:, kc], [], [VBB], VBB)
            for g in range(4):
                for hh in range(2):
                    fc = (g // 2) if (g % 2) == hh else 2 + g // 2
                    ps_ = slice(hh * 64, hh * 64 + 64)
                    P.dma("sp", KT[ps_, g * 2 + hh, nctx:nctx + L].rearrange("p (b t) -> p b t", b=n),
                          kT_d[f:f + n, ps_, fc, :].rearrange("b p t -> p b t"),
                          [DB("kT", f + bi) for bi in range(n)], [KTB], KTB)
            c0 = 2 * f
            kc0 = nctx // 128
            for i in range(2 * n):
                srcv = v_tok[c0 + i].rearrange("p (g e) -> p g e", g=4)
                P.dma("sp", VA[:, kc0 + i, :, 0:64], srcv, [DB("v", c0 + i)], [VAB], VAB)
                P.dma("sp", VB_[:, kc0 + i, :, 64:128], srcv, [DB("v", c0 + i)], [VBB], VBB)
            NQ = min(512, L)
            nqb = NQ // 256
            for qb in range(L // NQ):
                qt, qtB = q_p.get()
                for i in range(nqb):
                    b = f + qb * nqb + i
                    P.dma("sp", qt[:, :, i * 256:(i + 1) * 256], qT_d[b], [DB("qT", b)], [qtB], qtB)
                ya, yaB = yat_p.get()
                tasks = [(j, hh, k0) for j in range(8) for hh in range(2) for k0 in range(0, nkc, 2)]

                def hinfo(j, hh):
                    h = 2 * j + hh
                    g = h // 4
                    return g, slice(0, 128), g * 2 + hh

                def emit_scores(ti):
                    j, hh, k0 = tasks[ti]
                    g, ps_, fv = hinfo(j, hh)
                    scp, scpB = psum_at(2 + 2 * (ti % 2), 2)
                    for kk in range(2):
                        kc = k0 + kk
                        mm(scp[:, kk * 512:kk * 512 + NQ], KT[ps_, fv, kc * 128:(kc + 1) * 128],
                           qt[ps_, j, 0:NQ], [KTB, qtB], [scpB[kk]])
                    return scp, scpB

                ohs = {}

                def emit_epv(ti, scp, scpB):
                    j, hh, k0 = tasks[ti]
                    g, ps_, fv = hinfo(j, hh)
                    Vt, VtB = (VA, VAB) if hh == 0 else (VB_, VBB)
                    if k0 == 0:
                        ohs[(j, hh)] = psum_at((0 if j % 2 == 0 else 6) + hh)
                    oh, ohB = ohs[(j, hh)]
                    e, eB = e_p.get()
                    act(e[:, :, 0:NQ], scp.rearrange("p (k q) -> p k q", k=2)[:, :, 0:NQ], AF.Exp,
                        scpB, [eB], scale=0.125)
                    for kk in range(2):
                        kc = k0 + kk
                        mm(oh[:, 0:NQ], Vt[:, kc, g, :], e[:, kk, 0:NQ], [VtB, eB], ohB,
                           start=(kc == 0), stop=(kc == nkc - 1))
                    if hh == 1 and k0 + 2 >= nkc:
                        (oa, oaB), (ob, obB) = ohs[(j, 0)], ohs[(j, 1)]
                        ev, evB = ev_p.get()
                        d2, d2B = dsb_p.get()
                        cp("dve", ev[:, 0, 0:NQ], oa[:, 0:NQ], oaB, [evB])
                        cp("dve", ev[:, 1, 0:NQ], ob[:, 0:NQ], obB, [evB])
                        P.dma("sp", d2[64:128, 0:NQ], ev[0:64, 1, 0:NQ], [evB], [d2B], d2B)
                        P.dma("sp", d2[0:64, 0:NQ], ev[64:128, 0, 0:NQ], [evB], [d2B], d2B)
                        def norm(j=j, ev=ev, evB=evB, d2=d2, d2B=d2B):
                            P.add("dve", lambda o=d2[:, 0:NQ]: V.reciprocal(out=o, in_=o), [d2B], [d2B])
                            tt("dve", ya[0:64, j, 0:NQ], ev[0:64, 0, 0:NQ], d2[0:64, 0:NQ], ALU.mult,
                               [evB, d2B], [yaB])
                            tt("dve", ya[64:128, j, 0:NQ], ev[64:128, 1, 0:NQ], d2[64:128, 0:NQ], ALU.mult,
                               [evB, d2B], [yaB])
                        while len(deferred) >= 2:
                            deferred.pop(0)()
                        deferred.append(norm)

                deferred = []
                pend = emit_scores(0)
                for ti in range(len(tasks)):
                    nxt = emit_scores(ti + 1) if ti + 1 < len(tasks) else None
                    emit_epv(ti, *pend)
                    pend = nxt
                while deferred:
                    deferred.pop(0)()
                for i in range(nqb):
                    b = f + qb * nqb + i
                    P.dma("sp", ybr[1, b], ya[:, :, i * 256:(i + 1) * 256], [yaB], [DB("ybr", 1, b)], yaB)


    wbr_s = wo_s = yb_p = gt_p = mg_p = t3_p = None

    def alloc_mrg():
        nonlocal wbr_s, wo_s, yb_p, gt_p, mg_p, t3_p
        wbr_s = [nsb("wbr%d" % n, [128, KC, 1024], BF16) for n in range(3)]
        yb_p = mkpool("ybl", [128, 8, 256], BF16, 6)
        gt_p = mkpool("gtl", [128, 8, 256], BF16, 6)
        mg_p = mkpool("mgd", [128, 8, 256], BF16, 2)
        t3_p = mkpool("t3", [128, 256], F32, 9)

    def merge_a(l):
        for n in range(3):
            load_slab(w_br[l, n], 1024, 0, wbr_s[n])

        def loads(b):
            ys_, gs_ = [], []
            for n in range(3):
                y, yB = yb_p.get()
                P.dma("sp", y[:], ybr[n, b], [DB("ybr", n, b)], [yB], yB)
                g, gB = gt_p.get()
                P.dma("sp", g[:], gT[n, b], [DB("gT", n, b)], [gB], gB)
                ys_.append((y, yB))
                gs_.append((g, gB))
            return ys_, gs_

        nxt = loads(0)
        for b in range(NBLK):
            ys_, gs_ = nxt
            if b + 1 < NBLK:
                nxt = loads(b + 1)
            mg, mgB = mg_p.get()
            for oc in range(8):
                ts_ = []
                for n in range(3):
                    pp, ppB = psA()
                    for k in range(KC):
                        mm(pp[:, 0:256], wbr_s[n][0][:, k, oc * 128:(oc + 1) * 128], ys_[n][0][:, k, :],
                           [wbr_s[n][1], ys_[n][1]], ppB, start=(k == 0), stop=(k == KC - 1))
                    t, tB = t3_p.get()
                    tt("dve", t[:], pp[:, 0:256], gs_[n][0][:, oc, :], ALU.mult, ppB + [gs_[n][1]], [tB])
                    ts_.append((t, tB))
                tt("pool", ts_[0][0][:], ts_[0][0][:], ts_[1][0][:], ALU.add, [ts_[0][1], ts_[1][1]], [ts_[0][1]])
                tt("pool", mg[:, oc, :], ts_[0][0][:], ts_[2][0][:], ALU.add, [ts_[0][1], ts_[2][1]], [mgB])
            P.dma("sp", mgd_d[b], mg[:], [mgB], [DB("mgd", b)], mgB)

    def merge_b(l):
        wo_t, wo_B = nsb("wo_s", [128, KC, 1024], BF16)
        mgl_p = mkpool("mgl", [128, 8, 256], BF16, 2)
        load_slab(w_out[l], 1024, 0, (wo_t, wo_B))
        for b in range(NBLK):
            m = SEQS[blk_seq(b)][3]
            mg, mgB = mgl_p.get()
            P.dma("sp", mg[:], mgd_d[b], [DB("mgd", b)], [mgB], mgB)
            xt, xtB = xt_p.get()
            P.dma("sp", xt[:], xres[b], [DB("x", b)], [xtB], xtB)
            for oc in range(8):
                pp, ppB = psA()
                for k in range(KC):
                    mm(pp[:, 0:256], wo_t[:, k, oc * 128:(oc + 1) * 128], mg[:, k, :], [wo_B, mgB], ppB,
                       start=(k == 0), stop=(k == KC - 1))
                stt(xt[:, oc, :], pp[:, 0:256], modT[:, l, m, 16 + oc:17 + oc], xt[:, oc, :], ALU.mult, ALU.add,
                    ppB + [modB, xtB], [xtB])
            P.dma("sp", xres[b], xt[:], [xtB], [DB("x", b)], xtB)
            norm_block(xt, xtB, b, l, 1)

    FG = [6, 6, 5, 5]
    wf1_p = wf2_p = ac_p = sa_p = fo_p = None

    def alloc_ffn():
        nonlocal wf1_p, wf2_p, ac_p, sa_p, fo_p
        wf1_p = mkpool("wf1", [128, KC, 2, 768], BF16, 2)
        wf2_p = mkpool("wf2", [128, 6, 1024], BF16, 2)
        ac_p = mkpool("ffa", [128, 256], BF16, 8)
        sa_p = mkpool("ffs", [128, 256], F32, 3)
        fo_p = mkpool("fo", [128, KC, 256], F32, 1)


    def ffn_layer(l, last):
        hc0 = 0
        for gi, ng in enumerate(FG):
            w1, w1B = wf1_p.get()
            w2, w2B = wf2_p.get()
            for ab in range(2):
                P.dma("pool", w1[:, :, ab, 0:ng * 128],
                      w_f1[l, :, ab * FFN_H + hc0 * 128:ab * FFN_H + (hc0 + ng) * 128].rearrange(
                          "(k p) n -> p k n", p=128), [], [w1B], w1B)
            P.dma("pool", w2[:, 0:ng, :], w_f2[l, hc0 * 128:(hc0 + ng) * 128, :].rearrange("(c p) n -> p c n", p=128),
                  [], [w2B], w2B)
            for b in range(NBLK):
                m = SEQS[blk_seq(b)][3]
                c0 = hcol(b) + 1
                acts = []
                for hc in range(ng):
                    pa, paB = psA()
                    for k in range(KC):
                        mm(pa[:, 0:256], w1[:, k, 0, hc * 128:(hc + 1) * 128], hT[:, k, c0:c0 + 256], [w1B, hB[b]],
                           paB, start=(k == 0), stop=(k == KC - 1))
                    pb, pbB = psA()
                    for k in range(KC):
                        mm(pb[:, 0:256], w1[:, k, 1, hc * 128:(hc + 1) * 128], hT[:, k, c0:c0 + 256], [w1B, hB[b]],
                           pbB, start=(k == 0), stop=(k == KC - 1))
                    sa, saB = sa_p.get()
                    act(sa[:], pa[:, 0:256], AF.Silu, paB, [saB])
                    a, aB = ac_p.get()
                    tt("dve", a[:], pb[:, 0:256], sa[:], ALU.mult, pbB + [saB], [aB])
                    acts.append((a, aB))
                xt, xtB = xt_p.get()
                P.dma("sp", xt[:], xres[b], [DB("x", b)], [xtB], xtB)
                for oc in range(8):
                    pp, ppB = psA()
                    for hc in range(ng):
                        mm(pp[:, 0:256], w2[:, hc, oc * 128:(oc + 1) * 128], acts[hc][0][:], [w2B, acts[hc][1]], ppB,
                           start=(hc == 0), stop=(hc == ng - 1))
                    stt(xt[:, oc, :], pp[:, 0:256], modT[:, l, m, 40 + oc:41 + oc], xt[:, oc, :], ALU.mult, ALU.add,
                        ppB + [modB, xtB], [xtB])
                if gi < len(FG) - 1 or not last:
                    P.dma("sp", xres[b], xt[:], [xtB], [DB("x", b)], xtB)
                if gi == len(FG) - 1:
                    if last:
                        fo = fo_p.get()
                        norm_block(xt, xtB, b, l, 0, final=True, out_tile=(fo[0], fo[1]))
                        P.dma("sp", yT[:, :, b * 256:(b + 1) * 256], fo[0][:], [fo[1]], [DB("yT", b)], fo[1])
                    else:
                        norm_block(xt, xtB, b, l + 1, 0)
            hc0 += ng

    def phase(fn, *a):
        with contextlib.ExitStack() as pes:
            cur["es"] = pes
            fn(*a)
            P.flush()
        cur["es"] = es

    def ph_d1(l):
        alloc_slab()
        alloc_d1()
        d1_layer(l)

    def ph_ssd(l):
        alloc_scan()
        alloc_mix()
        alloc_ssd()
        ssd_layer(l, prep_t, prepB)

    def ph_att(l):
        alloc_att()
        att_layer(l)

    def ph_ml(l):
        alloc_mix(False)
        alloc_ml()
        ml_layer(l, prep_t, prepB)

    def ph_mrg_a(l):
        alloc_mrg()
        merge_a(l)

    def ph_mrg_b(l):
        alloc_norm()
        merge_b(l)

    def ph_ffn(l):
        alloc_norm()
        alloc_ffn()
        ffn_layer(l, l == NL - 1)

    nph = [0]

    def go(fn, *a):
        nph[0] += 1
        if stop_after is not None and nph[0] > stop_after:
            return
        phase(fn, *a)

    go(startup)
    for l in range(NL):
        P.dma("sp", prep_t[:], prep[l], [], [prepB], prepB)
        go(ph_d1, l)
        go(ph_ssd, l)
        go(ph_att, l)
        go(ph_ml, l)
        go(ph_mrg_a, l)
        go(ph_mrg_b, l)
        go(ph_ffn, l)
    P.flush()
    print("kernel build: ops=%d sems=%d" % (P.n_emitted, P.nsem))
    es.close()
    return nc


def _consts(SL):
    c = np.zeros((128, C_TOT), np.float32)
    k = np.arange(128)[:, None]
    i = np.arange(128)[None, :]
    c[:, C_ID:C_ID + 128] = (k == i)
    c[:, C_ONE:C_ONE + 128] = 1.0
    c[:, C_U:C_U + 128] = (k <= i)
    c[:, C_L:C_L + 128] = (k >= i)
    c[:, C_BLK:C_BLK + 128] = ((k // 64) == (i // 64))
    part = np.arange(128)
    d = part % 64
    partner = np.where((d % 32) < 16, part + 16, part - 16)
    c[:, C_SWP:C_SWP + 128] = (k == partner[None, :])
    c[:, C_NMF:C_NMF + 128] = np.where(i < k, NEG, 0.0)
    c[:, C_NMB:C_NMB + 128] = np.where(i > k, NEG, 0.0)
    cf = np.eye(128, dtype=np.float32)
    t = np.arange(SL)
    rows, cols = t // 64, t % 64
    f = d % 16
    freqs = (10000.0 ** (-(f.astype(np.float32)) / 16.0)).astype(np.float32)
    pos = np.where((d < 32)[:, None], rows[None, :], cols[None, :]).astype(np.float32)
    ang = pos * freqs[:, None]
    cosT = np.cos(ang).astype(np.float32)
    sgn = np.where((d % 32) < 16, -1.0, 1.0).astype(np.float32)
    sinT = (np.sin(ang) * sgn[:, None]).astype(np.float32)
    return c, cf, cosT, sinT


def _fm(v):
    v = np.asarray(v)
    n = v.shape[-1] // 128
    r = v.reshape(v.shape[:-1] + (n, 128))
    return np.ascontiguousarray(np.moveaxis(r, -1, 0))


def _prepare(inp, SB, NL):
    NBLK = SB + 2
    SL = 256 * SB
    f32 = np.float32
    g = lambda k: np.asarray(inp[k], dtype=f32)
    c, cf, cosT, sinT = _consts(SL)
    shared = {
        "w_ada": np.ascontiguousarray(g("w_ada")[:NL]), "w_in": np.ascontiguousarray(g("w_in")[:NL]),
        "w_branch": np.ascontiguousarray(g("w_branch")[:NL]), "w_out": np.ascontiguousarray(g("w_out")[:NL]),
        "w_ffn_in": np.ascontiguousarray(g("w_ffn_in")[:NL]), "w_ffn_out": np.ascontiguousarray(g("w_ffn_out")[:NL]),
        "cst": c, "cstf": cf, "ropec": cosT, "ropes": sinT,
    }
    prep = np.zeros((NL, 128, P_TOT), f32)
    pfm = np.zeros((128, NL, Q_TOT), f32)
    for l in range(NL):
        prep[l, :, P_DTB:P_DTB + 32] = g("ssd_dt_bias")[l].reshape(32)[None]
        prep[l, :, P_ALOG:P_ALOG + 32] = g("ssd_a_log")[l].reshape(32)[None]
        prep[l, :, P_SD:P_SD + 16] = g("ssd_d")[l][None]
        prep[l, :, P_SSDN:P_SSDN + 1024] = g("ssd_norm")[l][None]
        prep[l, :, P_MLN:P_MLN + 1024] = g("ml_norm")[l][None]
        prep[l, :, P_MGB:P_MGB + 32] = g("ml_gate_bias")[l].reshape(32)[None]
        pfm[:, l, Q_BADA:Q_BADA + 48] = _fm(g("b_ada")[l])
        pfm[:, l, Q_SCW:Q_SCW + 48] = np.moveaxis(_fm(g("ssd_conv_w")[l]), 1, 2).reshape(128, 48)
        pfm[:, l, Q_SCB:Q_SCB + 12] = _fm(g("ssd_conv_b")[l])
        pfm[:, l, Q_MCW:Q_MCW + 64] = np.moveaxis(_fm(g("ml_conv_w")[l]), 1, 2).reshape(128, 64)
        pfm[:, l, Q_MCB:Q_MCB + 16] = _fm(g("ml_conv_b")[l])
        pfm[:, l, Q_QG] = np.tile(g("att_q_norm")[l], 2)
        pfm[:, l, Q_KG] = np.tile(g("att_k_norm")[l], 2)
    shared["prep"] = prep
    shared["pfm"] = pfm
    shared["fng"] = _fm(g("final_norm"))
    xp, xs = g("x_prompt"), g("x_sample")
    ncore = xs.shape[0]
    maps = []
    for i in range(ncore):
        toks = np.concatenate([xs[i][:SL], xp[2 * i], xp[2 * i + 1]], axis=0)
        x0 = np.ascontiguousarray(toks.reshape(-1, 8, 128).transpose(2, 1, 0))
        cvec = np.stack([_fm(g("c_ctx")), _fm(g("c")[i])], axis=-1)
        st = g("state_ssd")[i][:NL]
        st = st.reshape(NL, 2, 2, 2, 4, 64, 64)
        ssd0 = np.ascontiguousarray(st.transpose(0, 1, 3, 6, 2, 4, 5)).reshape(NL, 2, 128, 512)
        mc = g("state_ml_c")[i][:NL]
        mn = g("state_ml_n")[i][:NL]
        mlc0 = np.concatenate([mc.transpose(0, 1, 3, 2, 4), mn.transpose(0, 1, 3, 2)[..., None]], axis=-1)
        m = dict(shared)
        m.update({
            "x0": x0, "cvec": np.ascontiguousarray(cvec),
            "cache_k": np.ascontiguousarray(g("cache_k")[i][:NL].reshape(NL, 512, 256)),
            "cache_v": np.ascontiguousarray(g("cache_v")[i][:NL].reshape(NL, 512, 256)),
            "ssd0": ssd0, "mlc0": np.ascontiguousarray(mlc0).reshape(NL, 2, 128, 8 * 129),
            "mlm0": np.ascontiguousarray(g("state_ml_m")[i][:NL].reshape(NL, 2, 8, 1)),
        })
        maps.append(m)
    return maps


def _assemble(results, SB, NL):
    SL = 256 * SB
    n = len(results)
    y_s = np.zeros((n, SL, D), np.float32)
    y_p = np.zeros((2 * n, 256, D), np.float32)
    nk = np.zeros((2 * n, NL, 256, 4, 64), np.float32)
    nv = np.zeros((2 * n, NL, 256, 4, 64), np.float32)
    nssd = np.zeros((2 * n, NL, 2, 16, 64, 64), np.float32)
    nc_ = np.zeros((2 * n, NL, 2, 8, 128, 128), np.float32)
    nn_ = np.zeros((2 * n, NL, 2, 8, 128), np.float32)
    nm = np.zeros((2 * n, NL, 2, 8), np.float32)
    for i, r in enumerate(results):
        y = np.asarray(r["yT"]).transpose(2, 1, 0).reshape(-1, D)
        y_s[i] = y[:SL]
        for j in range(2):
            y_p[2 * i + j] = y[SL + 256 * j:SL + 256 * (j + 1)]
            k = np.asarray(r["nk_o"])[j]
            nk[2 * i + j] = k.transpose(0, 3, 2, 1).reshape(NL, 256, 4, 64)
            nv[2 * i + j] = np.asarray(r["nv_o"])[j].reshape(NL, 256, 4, 64)
            s_ = np.asarray(r["nssd_o"])[j].reshape(NL, 2, 2, 64, 2, 4, 64)
            nssd[2 * i + j] = s_.transpose(0, 1, 4, 2, 5, 6, 3).reshape(NL, 2, 16, 64, 64)
            c_ = np.asarray(r["nmlc_o"])[j].reshape(NL, 2, 128, 8, 129)
            nc_[2 * i + j] = c_[..., :128].transpose(0, 1, 3, 2, 4)
            nn_[2 * i + j] = c_[..., 128].transpose(0, 1, 3, 2)
            nm[2 * i + j] = np.asarray(r["nmlm_o"])[j].reshape(NL, 2, 8)
    return (y_p, y_s, nk, nv, nssd, nc_, nn_, nm)


_CACHE = {}


def run(inputs, SB=8, NL=4, debug=False, stop_after=None):
    key = (SB, NL, debug, stop_after)
    if key not in _CACHE:
        _CACHE[key] = build(SB, NL, debug, stop_after)
    nc = _CACHE[key]
    maps = _prepare(inputs, SB, NL)
    res = run_bass_kernel_spmd(nc, maps, core_ids=list(range(len(maps))))
    return _assemble(res.results, SB, NL), res


def kernel(**inputs):
    out, _ = run(inputs, 8, 4, False)
    return out
```

```python
import contextlib
import os
import numpy as np
import concourse.bass as bass
import concourse.mybir as mybir
from concourse.bass_utils import run_bass_kernel_spmd

F32 = mybir.dt.float32
BF16 = mybir.dt.bfloat16
AF = mybir.ActivationFunctionType
ALU = mybir.AluOpType
AX = mybir.AxisListType

D = 1024
KC = 8
EPS = 1e-6
IN_DIM = 11328
FFN_H = 2816
O_SX, O_SZ, O_SB, O_SC, O_SDT = 0, 1024, 2048, 2304, 2560
O_AQ, O_AK, O_AV = 2592, 3616, 3872
O_MQ, O_MK, O_MV, O_MO, O_MG, O_G = 4128, 5152, 6176, 7200, 8224, 8256
P_DTB, P_ALOG, P_SD, P_SSDN, P_MLN, P_MGB, P_TOT = 0, 32, 64, 96, 1120, 2144, 2176
Q_BADA, Q_SCW, Q_SCB, Q_MCW, Q_MCB, Q_QG, Q_KG, Q_TOT = 0, 48, 96, 108, 172, 188, 189, 192
C_ID, C_ONE, C_U, C_L, C_BLK, C_SWP, C_NMF, C_NMB, C_TOT = 0, 128, 256, 384, 512, 640, 768, 896, 1024
NEG = -30000.0
MLK_SCALE = 128.0 ** -0.5


class Buf:
    __slots__ = ("name", "w", "r", "dsem", "dcnt", "last_dma", "excl", "dead")
    registry = []

    def __init__(self, name, excl=False):
        self.name = name
        self.excl = excl
        self.dead = False
        self.w = None
        self.r = []
        self.dsem = None
        self.dcnt = 0
        self.last_dma = None
        Buf.registry.append(self)


class Op:
    __slots__ = ("eng", "fn", "deps", "needs", "event", "isdma", "slot")


class Prog:
    def __init__(self, nc, es):
        self.nc = nc
        self.es = es
        self.ops = []
        self.eng = {"pe": nc.tensor, "act": nc.scalar, "dve": nc.vector, "pool": nc.gpsimd, "sp": nc.sync}
        self.esem = {k: es.enter_context(nc.semaphore("sem_" + k)) for k in self.eng}
        self.nsem = len(self.eng)
        self.dma_bufs = []
        self.free_sems = {"sp": [], "pool": []}
        self.phase_bufs = []
        self.phase_pools = []
        self.seq = {k: 0 for k in self.eng}
        self.waited = {k: {} for k in self.eng}
        self.n_emitted = 0

    def swap_buf(self, old, nb):
        self.phase_bufs.append(nb)
        for i, (b, q) in enumerate(self.dma_bufs):
            if b is old:
                self.dma_bufs[i] = (nb, q)

    def _deps(self, op, reads, writes):
        for b in reads:
            assert not b.dead, "live-range violation (read of recycled tile) " + b.name
        for b in writes:
            assert not b.dead, "live-range violation (write of recycled tile) " + b.name
        deps = {}
        wset = set(id(b) for b in writes)
        for b in reads:
            if b.w is not None:
                deps[id(b.w)] = (b.w, True)
            if b.excl:
                for r in b.r:
                    if r.eng != op.eng and id(r) not in deps:
                        deps[id(r)] = (r, False)
        for b in writes:
            if b.w is not None and id(b.w) not in deps:
                deps[id(b.w)] = (b.w, False)
            for r in b.r:
                if id(r) not in deps:
                    deps[id(r)] = (r, False)
        out = []
        for d, raw in deps.values():
            if d is op:
                continue
            if (not d.isdma) and (not op.isdma) and d.eng == op.eng and op.eng == "pe":
                continue
            d.needs = True
            out.append(d)
        for b in writes:
            b.w = op
            b.r = []
        for b in reads:
            if id(b) not in wset:
                b.r.append(op)
        return out

    def add(self, eng, fn, reads=(), writes=()):
        op = Op()
        op.eng, op.fn, op.needs, op.event, op.isdma, op.slot = eng, fn, False, None, False, None
        op.deps = self._deps(op, list(reads), list(writes))
        self.ops.append(op)
        return op

    def dma(self, q, out, in_, reads, writes, slot):
        nc = self.nc
        op = Op()
        op.eng, op.needs, op.isdma, op.slot = q, True, True, slot
        if slot.dsem is None:
            slot.dsem, slot.dcnt, slot.last_dma = {}, {}, {}
        if q not in slot.dsem:
            if self.free_sems[q]:
                slot.dsem[q], slot.dcnt[q] = self.free_sems[q].pop()
            else:
                slot.dsem[q] = self.es.enter_context(nc.semaphore("dsem_%s_%d" % (q, self.nsem)))
                slot.dcnt[q] = 0
                self.nsem += 1
                assert self.nsem <= 96, "too many semaphores"
            self.dma_bufs.append((slot, q))
        slot.dcnt[q] += 16
        op.event = (slot.dsem[q], slot.dcnt[q])
        e = self.eng[q]
        op.fn = lambda: e.dma_start(out=out, in_=in_)
        op.deps = self._deps(op, list(reads), list(writes))
        ld = slot.last_dma.get(q)
        if ld is not None and ld not in op.deps:
            op.deps.append(ld)
        slot.last_dma[q] = op
        self.ops.append(op)
        return op

    def flush(self):
        pend = set(id(o) for o in self.ops)
        for b in Buf.registry:
            if b.w is not None and id(b.w) in pend:
                b.w.needs = True
            for r in b.r:
                if id(r) in pend:
                    r.needs = True
        last = {}
        for o in self.ops:
            if not o.isdma:
                last[o.eng] = o
        for o in last.values():
            o.needs = True
        seq, waited = self.seq, self.waited
        for op in self.ops:
            e = self.eng[op.eng]
            wd = waited[op.eng]
            for d in op.deps:
                sem, val = d.event
                if wd.get(id(sem), 0) < val:
                    e.wait_ge(sem, val)
                    wd[id(sem)] = val
            ins = op.fn()
            if op.isdma:
                ins.then_inc(op.event[0], 16)
            elif op.needs:
                seq[op.eng] += 1
                ins.then_inc(self.esem[op.eng], 1)
                op.event = (self.esem[op.eng], seq[op.eng])
            op.fn = None
        self.n_emitted += len(self.ops)
        self.ops = []
        evs = [(self.esem[k], seq[k], k) for k in self.eng if seq[k] > 0]
        evs += [(b.dsem[q], b.dcnt[q], None) for b, q in self.dma_bufs]
        for k, e in self.eng.items():
            wd = waited[k]
            for sem, val, owner in evs:
                if owner == k and k != "pool":
                    continue
                if wd.get(id(sem), 0) < val:
                    e.wait_ge(sem, val)
                    wd[id(sem)] = val
        ph = set(id(b) for b in self.phase_bufs)
        for b in self.phase_bufs:
            if b.dsem is not None:
                for q in b.dsem:
                    self.free_sems[q].append((b.dsem[q], b.dcnt[q]))
                b.dsem = None
        self.dma_bufs = [(b, q) for b, q in self.dma_bufs if id(b) not in ph]
        Buf.registry = [b for b in Buf.registry if id(b) not in ph]
        self.phase_bufs = []
        for p_ in self.phase_pools:
            p_.dead = True
        self.phase_pools = []


class TPool:
    def __init__(self, P, nc, es, name, shape, dtype, n, phase=True):
        self.t = []
        for i in range(n):
            t = es.enter_context(nc.sbuf_tensor("%s_%d" % (name, i), list(shape), dtype))
            b = Buf("%s_%d" % (name, i))
            if phase:
                P.phase_bufs.append(b)
            self.t.append((t, b))
        self.i = 0
        self.dead = False
        self.name = name
        self.P = P
        self.phase = phase
        if phase:
            P.phase_pools.append(self)

    def get(self):
        assert not self.dead, "stale pool " + self.name
        k = self.i % len(self.t)
        t, old = self.t[k]
        if self.i >= len(self.t):
            nb = Buf(old.name)
            nb.w, nb.r, nb.dsem, nb.dcnt, nb.last_dma = old.w, old.r, old.dsem, old.dcnt, old.last_dma
            old.dead = True
            old.dsem = None
            self.P.swap_buf(old, nb)
            self.t[k] = (t, nb)
        self.i += 1
        return self.t[k]


def build(SB, NL, debug=False, stop_after=None):
    NBLK = SB + 2
    NCH = 2 * NBLK
    T = 256 * NBLK
    SL = 256 * SB
    NCTX = 512
    SEQS = [(0, SB, True, 1), (SB, 1, False, 0), (SB + 1, 1, False, 0)]
    hbase = [0, SL + 3, SL + 3 + 259]
    NCOL = SL + 3 + 259 * 2

    def blk_seq(b):
        return 0 if b < SB else (1 if b == SB else 2)

    def hcol(b):
        s = blk_seq(b)
        return hbase[s] + 256 * (b - SEQS[s][0])

    nc = bass.Bass("TRN2", target_bir_lowering=False)
    es = contextlib.ExitStack()
    P = Prog(nc, es)

    def din(name, shape, dt=F32):
        return nc.dram_tensor(name, list(shape), dt, kind="ExternalInput").ap()

    def dout(name, shape, dt=F32):
        return nc.dram_tensor(name, list(shape), dt, kind="ExternalOutput").ap()

    def dscr(name, shape, dt):
        return nc.dram_tensor(name, list(shape), dt, kind=("ExternalOutput" if debug else "Internal")).ap()

    x0 = din("x0", [128, KC, T])
    cvec = din("cvec", [128, KC, 2])
    w_ada = din("w_ada", [NL, D, 6 * D])
    w_in = din("w_in", [NL, D, IN_DIM])
    w_br = din("w_branch", [NL, 3, D, D])
    w_out = din("w_out", [NL, D, D])
    w_f1 = din("w_ffn_in", [NL, D, 2 * FFN_H])
    w_f2 = din("w_ffn_out", [NL, FFN_H, D])
    prep = din("prep", [NL, 128, P_TOT])
    pfm = din("pfm", [128, NL, Q_TOT])
    fng = din("fng", [128, KC])
    cst = din("cst", [128, C_TOT])
    cstf = din("cstf", [128, 128])
    ropec = din("ropec", [128, SL])
    ropes = din("ropes", [128, SL])
    cache_k = din("cache_k", [NL, NCTX, 256])
    cache_v = din("cache_v", [NL, NCTX, 256])
    ssd0 = din("ssd0", [NL, 2, 128, 512])
    mlc0 = din("mlc0", [NL, 2, 128, 8 * 129])
    mlm0 = din("mlm0", [NL, 2, 8, 1])

    yT = dout("yT", [128, KC, T])
    nk_o = dout("nk_o", [2, NL, 128, 2, 256])
    nv_o = dout("nv_o", [2, NL, 256, 256])
    nssd_o = dout("nssd_o", [2, NL, 2, 128, 512])
    nmlc_o = dout("nmlc_o", [2, NL, 2, 128, 8 * 129])
    nmlm_o = dout("nmlm_o", [2, NL, 2, 8, 1])

    xres = dscr("xres", [NBLK, 128, KC, 256], F32)
    sx_tok = dscr("sx_tok", [NCH, 128, 1024], BF16)
    sbcT = dscr("sbcT", [NBLK, 128, 4, 256], BF16)
    sb_tok = dscr("sb_tok", [NCH, 128, 256], BF16)
    sz_tok = dscr("sz_tok", [NCH, 128, 1024], BF16)
    dtg_tok = dscr("dtg_tok", [NCH, 128, 64], F32)
    qT_d = dscr("qT_d", [NBLK, 128, 8, 256], BF16)
    kT_d = dscr("kT_d", [NBLK, 128, 4, 256], BF16)
    v_tok = dscr("v_tok", [NCH, 128, 256], BF16)
    mqT = dscr("mqT", [NBLK, 128, 8, 256], BF16)
    mkT = dscr("mkT", [NBLK, 128, 8, 256], BF16)
    mk_tok = dscr("mk_tok", [NCH, 128, 1024], BF16)
    mv_tok = dscr("mv_tok", [NCH, 128, 1024], BF16)
    mo_tok = dscr("mo_tok", [NCH, 128, 1024], BF16)
    gT = dscr("gT", [3, NBLK, 128, 8, 256], BF16)
    ybr = dscr("ybr", [3, NBLK, 128, 8, 256], BF16)
    yf_d = dscr("yf_d", [NCH, 128, 1024], F32)
    hf_d = dscr("hf_d", [NCH, 128, 1024], F32)
    mgd_d = dscr("mgd_d", [NBLK, 128, 8, 256], BF16)

    dbufs = {}

    def DB(*key):
        if key not in dbufs:
            dbufs[key] = Buf(str(key))
        return dbufs[key]

    cur = {"es": es, "n": 0}

    def sb(name, shape, dt=F32):
        ph = cur["es"] is not es
        cur["n"] += 1
        b = Buf(name)
        if ph:
            P.phase_bufs.append(b)
        t = cur["es"].enter_context(nc.sbuf_tensor("%s_%d" % (name, cur["n"]), list(shape), dt))
        return t, b

    def mkpool(name, shape, dt, n):
        cur["n"] += 1
        return TPool(P, nc, cur["es"], "%s%d" % (name, cur["n"]), shape, dt, n, phase=(cur["es"] is not es))


    hT, _ = sb("hT", [128, KC, NCOL], BF16)
    hB = [Buf("hT%d" % b) for b in range(NBLK)]
    hHalo = Buf("hHalo")
    cb, cbB = sb("cb", [128, C_TOT], BF16)
    cf, cfB = sb("cf", [128, 128], F32)
    onesf, onesfB = sb("onesf", [128, 128], F32)
    modT, modB = sb("modT", [128, NL, 2, 48], F32)
    pfm_s, pfmB = sb("pfm_s", [128, NL, Q_TOT], F32)
    fng_s, fngB = sb("fng_s", [128, KC], F32)
    ropec_s = ropecB = ropes_s = ropesB = None
    prep_t, prepB = sb("prep_t", [128, P_TOT], F32)

    ps_all = es.enter_context(nc.psum_tensor("ps_all", [128, 4096], F32))
    psB = [Buf("psb%d" % i, excl=True) for i in range(8)]
    ps_state = {"A": 0, "B": 0}

    def psum_at(i, n=1):
        return ps_all[:, i * 512:(i + n) * 512], psB[i:i + n]

    def psA():
        i = ps_state["A"] % 6
        ps_state["A"] += 1
        return psum_at(i)

    def psBr():
        i = 6 + ps_state["B"] % 2
        ps_state["B"] += 1
        return psum_at(i)

    ident = cb[:, C_ID:C_ID + 128]
    ones = cb[:, C_ONE:C_ONE + 128]
    Utri = cb[:, C_U:C_U + 128]
    Ltri = cb[:, C_L:C_L + 128]
    blk2 = cb[:, C_BLK:C_BLK + 128]
    pswap = cb[:, C_SWP:C_SWP + 128]
    nmask = [cb[:, C_NMF:C_NMF + 128], cb[:, C_NMB:C_NMB + 128]]
    tri = [Utri, Ltri]

    V, A, G, PE = nc.vector, nc.scalar, nc.gpsimd, nc.tensor

    def mm(out, lhsT, rhs, reads, writes, start=True, stop=True):
        P.add("pe", lambda: PE.matmul(out, lhsT=lhsT, rhs=rhs, start=start, stop=stop), reads, writes)

    def tp(out, in_, idn, reads, writes):
        P.add("pe", lambda: PE.transpose(out, in_, idn), reads, writes)

    def act(out, in_, func, reads, writes, bias=None, scale=None, accum=None):
        kw = {}
        if bias is not None:
            kw["bias"] = bias
        if scale is not None:
            kw["scale"] = scale
        if accum is not None:
            kw["accum_out"] = accum
        P.add("act", lambda: A.activation(out=out, in_=in_, func=func, **kw), reads, writes)

    def tt(eng, out, in0, in1, op, reads, writes):
        e = V if eng == "dve" else G
        P.add(eng, lambda: e.tensor_tensor(out=out, in0=in0, in1=in1, op=op), reads, writes)

    def ts(eng, out, in0, s1, op0, reads, writes, s2=None, op1=None):
        e = V if eng == "dve" else G
        if op1 is None:
            P.add(eng, lambda: e.tensor_scalar(out=out, in0=in0, scalar1=s1, scalar2=None, op0=op0), reads, writes)
        else:
            P.add(eng, lambda: e.tensor_scalar(out=out, in0=in0, scalar1=s1, scalar2=s2, op0=op0, op1=op1),
                  reads, writes)

    def stt(out, in0, scalar, in1, op0, op1, reads, writes):
        P.add("dve", lambda: V.scalar_tensor_tensor(out=out, in0=in0, scalar=scalar, in1=in1, op0=op0, op1=op1),
              reads, writes)

    def cp(eng, out, in_, reads, writes):
        if eng == "act":
            P.add("act", lambda: A.copy(out=out, in_=in_), reads, writes)
        else:
            e = V if eng == "dve" else G
            P.add(eng, lambda: e.tensor_copy(out=out, in_=in_), reads, writes)

    def memset(eng, ap, val, writes):
        e = V if eng == "dve" else G
        P.add(eng, lambda: e.memset(ap, val), (), writes)

    def rsqrt(v_ap, v_buf, shape2, mean_scale):
        p, n = shape2
        ts("dve", v_ap, v_ap, mean_scale, ALU.mult, [v_buf], [v_buf], s2=EPS, op1=ALU.add)
        act(v_ap, v_ap, AF.Ln, [v_buf], [v_buf])
        act(v_ap, v_ap, AF.Exp, [v_buf], [v_buf], scale=-0.5)

    def flat(ap3):
        return ap3.rearrange("p a b -> p (a b)")

    slab_p = None

    def alloc_slab(n=3):
        nonlocal slab_p
        slab_p = mkpool("slab", [128, KC, 1024], BF16, n)

    def load_slab(src2d, ncols, dst_off=0, slab=None):
        if slab is None:
            slab = slab_p.get()
        t, bfr = slab
        P.dma("pool", t[:, :, dst_off:dst_off + ncols], src2d.rearrange("(k p) n -> p k n", p=128), [], [bfr], bfr)
        return slab

    def startup():
        cst_st, cst_stB = sb("cst_st", [128, C_TOT], F32)
        P.dma("sp", cst_st[:], cst[:, :], [], [cst_stB], cst_stB)
        cp("dve", cb[:], cst_st[:], [cst_stB], [cbB])
        P.dma("sp", cf[:], cstf[:, :], [], [cfB], cfB)
        P.dma("sp", pfm_s[:], pfm[:, :, :], [], [pfmB], pfmB)
        P.dma("sp", fng_s[:], fng[:, :], [], [fngB], fngB)
        memset("dve", onesf[:], 1.0, [onesfB])
        memset("pool", flat(hT[:]), 0.0, hB + [hHalo])
        alloc_slab()
        cv, cvB = sb("cv", [128, KC, 2], F32)
        cvb, cvbB = sb("cvb", [128, KC, 2], BF16)
        P.dma("sp", cv[:], cvec[:, :, :], [], [cvB], cvB)
        act(flat(cvb[:]), flat(cv[:]), AF.Silu, [cvB], [cvbB])
        for l in range(NL):
            pm, pmB = psA()
            for s6 in range(6):
                wt, wB = load_slab(w_ada[l, :, s6 * 1024:(s6 + 1) * 1024], 1024)
                for f in range(8):
                    fc = s6 * 8 + f
                    for k in range(KC):
                        mm(pm[:, fc * 2:fc * 2 + 2], wt[:, k, f * 128:(f + 1) * 128], cvb[:, k, :],
                           [wB, cvbB], pmB, start=(k == 0), stop=(k == KC - 1))
            for m in range(2):
                tt("dve", modT[:, l, m, :], pm.rearrange("p (f m) -> p m f", m=2)[:, m, 0:48],
                   pfm_s[:, l, Q_BADA:Q_BADA + 48], ALU.add, pmB + [pfmB], [modB])
            for o in (8, 32):
                ts("dve", modT[:, l, :, o:o + 8], modT[:, l, :, o:o + 8], 1.0, ALU.add, [modB], [modB])
        alloc_norm()
        for b in range(NBLK):
            xt, xtB = xt_p.get()
            P.dma("sp", xt[:], x0[:, :, b * 256:(b + 1) * 256], [], [xtB], xtB)
            P.dma("sp", xres[b], xt[:], [xtB], [DB("x", b)], xtB)
            norm_block(xt, xtB, b, 0, 0)

    xt_p = sq_p = xn_p = rs_p = None

    def alloc_norm():
        nonlocal xt_p, sq_p, xn_p, rs_p
        xt_p = mkpool("xt", [128, KC, 256], F32, 2)
        sq_p = mkpool("sq", [128, KC, 256], BF16, 2)
        xn_p = mkpool("xn", [128, KC, 256], F32, 2)
        rs_p = mkpool("rs", [128, 256], F32, 2)


    def norm_block(xt, xtB, b, l, which, final=False, out_tile=None):
        m = SEQS[blk_seq(b)][3]
        sq, sqB = sq_p.get()
        act(flat(sq[:]), flat(xt[:]), AF.Square, [xtB], [sqB])
        pn, pnB = psBr()
        for k in range(KC):
            mm(pn[:, 0:256], ones, sq[:, k, :], [cbB, sqB], pnB, start=(k == 0), stop=(k == KC - 1))
        rs, rsB = rs_p.get()
        cp("dve", rs[:], pn[:, 0:256], pnB, [rsB])
        rsqrt(rs[:], rsB, (128, 256), 1.0 / D)
        xn, xnB = xn_p.get()
        tt("dve", xn[:], xt[:], rs[:].unsqueeze(1).broadcast_to([128, KC, 256]), ALU.mult, [xtB, rsB], [xnB])
        if final:
            ot, otB = out_tile
            for k in range(KC):
                ts("pool", ot[:, k, :], xn[:, k, :], fng_s[:, k:k + 1], ALU.mult, [xnB, fngB], [otB])
            return
        c0 = hcol(b) + 1
        so, sh = (8, 0) if which == 0 else (32, 24)
        for k in range(KC):
            ts("pool", hT[:, k, c0:c0 + 256], xn[:, k, :], modT[:, l, m, so + k:so + k + 1], ALU.mult,
               [xnB, modB], [hB[b]], s2=modT[:, l, m, sh + k:sh + k + 1], op1=ALU.add)

    def hwin_bufs(b):
        s = blk_seq(b)
        f, n = SEQS[s][0], SEQS[s][1]
        r = [hB[b], hHalo]
        if b > f:
            r.append(hB[b - 1])
        if b < f + n - 1:
            r.append(hB[b + 1])
        return r

    st_p = acc_p = cvo_p = sm_p = cv8_p = qr_p = sq4_p = dg_p = u_p = None

    def alloc_d1():
        nonlocal st_p, acc_p, cvo_p, sm_p, cv8_p, ropec_s, ropecB, ropes_s, ropesB, qr_p, sq4_p, dg_p, u_p
        dg_p = mkpool("dg", [128, 4, 128], BF16, 16)
        u_p = mkpool("ub", [128, 260], BF16, 4)
        dg_live.clear()
        qr_p = mkpool("qr", [128, 4, 256], F32, 6)
        sq4_p = mkpool("sq4", [128, 4, 256], BF16, 4)
        ropec_s, ropecB = sb("ropec_s", [128, SL], F32)
        ropes_s, ropesB = sb("ropes_s", [128, SL], F32)
        P.dma("sp", ropec_s[:], ropec[:, :], [], [ropecB], ropecB)
        P.dma("sp", ropes_s[:], ropes[:, :], [], [ropesB], ropesB)
        st_p = mkpool("st", [128, 2048], BF16, 6)
        acc_p = None
        cvo_p = mkpool("cvo", [128, 256], BF16, 4)
        sm_p = mkpool("sm", [128, 256], F32, 4)
        cv8_p = None


    def proj_fm(wt, wB, wcol, b, n, halo):
        pp, ppB = psA()
        c0 = hcol(b) + (0 if halo else 1)
        rb = hwin_bufs(b) if halo else [hB[b]]
        for k in range(KC):
            mm(pp[:, 0:n], wt[:, k, wcol:wcol + 128], hT[:, k, c0:c0 + n], [wB] + rb, ppB,
               start=(k == 0), stop=(k == KC - 1))
        return pp, ppB

    def proj_tm(wt, wB, wcol, ncol, c):
        b, cc = c // 2, c % 2
        pp, ppB = psA()
        c0 = hcol(b) + 1 + 128 * cc
        for k in range(KC):
            mm(pp[:, 0:ncol], hT[:, k, c0:c0 + 128], wt[:, k, wcol:wcol + ncol], [wB, hB[b]], ppB,
               start=(k == 0), stop=(k == KC - 1))
        return pp, ppB

    dg_live = {}

    def conv_block(wt, wB, b, nfc, l, wofs, bofs, fcbase, stv, stB):
        def conv_b(ub, ubB, fci, dst):
            key = (l, wofs, fci)
            if key not in dg_live:
                dg, dgB = dg_p.get()
                for t in range(4):
                    ts("dve", dg[:, t, :], ident, pfm_s[:, l, wofs + fci * 4 + t:wofs + fci * 4 + t + 1], ALU.mult,
                       [cbB, pfmB], [dgB])
                dg_live[key] = (dg, dgB)
            dg, dgB = dg_live[key]
            pc, pcB = psBr()
            for t in range(4):
                mm(pc[:, 0:256], dg[:, t, :], ub[:, t:t + 256], [dgB, ubB], pcB, start=(t == 0), stop=(t == 3))
            act(dst, pc[:, 0:256], AF.Silu, pcB, [stB], bias=pfm_s[:, l, bofs + fci:bofs + fci + 1])

        pend = None
        for fc in range(nfc):
            pp, ppB = proj_fm(wt, wB, fc * 128, b, 259, True)
            ub, ubB = u_p.get()
            cp("act", ub[:, 0:259], pp[:, 0:259], ppB, [ubB])
            if pend is not None:
                conv_b(*pend)
            pend = (ub, ubB, fcbase + fc, stv[:, fc, :])
        conv_b(*pend)

    def transposes_to(dst, dstB, srcs, reads, scale=None, bank=None):
        n = len(srcs)
        pt, ptB = (psBr() if bank is None else psum_at(bank))
        ptb = pt.bitcast(BF16)
        for i, s_ in enumerate(srcs):
            tp(ptb[:, i * 128:(i + 1) * 128], s_, ident, reads + [cbB], ptB)
        if scale is None:
            cp("act", dst, ptb[:, 0:128 * n], ptB, [dstB])
        else:
            ts("dve", dst, ptb[:, 0:128 * n], scale, ALU.mult, ptB, [dstB])

    D1STOP = int(os.environ.get("K_D1STOP", "99"))

    def d1_layer(l):
        W = w_in[l]

        def ld_simple(off, n):
            return lambda: load_slab(W[:, off:off + n], n)

        def ld_dtg():
            slab = slab_p.get()
            load_slab(W[:, O_SDT:O_SDT + 32], 32, 0, slab)
            return load_slab(W[:, O_MG:O_MG + 32], 32, 32, slab)

        def ld_ak():
            slab = slab_p.get()
            load_slab(W[:, O_AK:O_AK + 256], 256, 0, slab)
            r = None
            for f in range(2):
                load_slab(W[:, O_AK + f * 128 + 64:O_AK + f * 128 + 128], 64, 256 + f * 128, slab)
                r = load_slab(W[:, O_AK + f * 128:O_AK + f * 128 + 64], 64, 256 + f * 128 + 64, slab)
            return r

        loaders = [ld_simple(O_SX, 1024), ld_simple(O_SB, 512), ld_simple(O_SZ, 1024), ld_dtg,
                   ld_simple(O_AQ, 1024), ld_ak, ld_simple(O_AV, 256), ld_simple(O_MQ, 1024),
                   ld_simple(O_MK, 1024), ld_simple(O_MV, 1024), ld_simple(O_MO, 1024),
                   ld_simple(O_G, 1024), ld_simple(O_G + 1024, 1024), ld_simple(O_G + 2048, 1024)]
        loaded = {}

        def get_w(i):
            for k in range(i + 2):
                if k < len(loaders) and k not in loaded:
                    loaded[k] = loaders[k]()
            return loaded[i]

        wt, wB = get_w(0)
        for b in range(NBLK):
            sv, svB = st_p.get()
            svv = sv[:, 0:2048].rearrange("p (f t) -> p f t", f=8)
            conv_block(wt, wB, b, 8, l, Q_SCW, Q_SCB, 0, svv, svB)
            for cc in range(2):
                st, stB = st_p.get()
                transposes_to(st[:, 0:1024], stB, [svv[:, f, cc * 128:(cc + 1) * 128] for f in range(8)], [svB])
                P.dma("sp", sx_tok[2 * b + cc], st[:, 0:1024], [stB], [DB("sx", 2 * b + cc)], stB)
        if D1STOP <= 1:
            return
        wt, wB = get_w(1)
        for b in range(NBLK):
            st, stB = st_p.get()
            stv = st[:, 0:1024].rearrange("p (f t) -> p f t", f=4)
            conv_block(wt, wB, b, 4, l, Q_SCW, Q_SCB, 8, stv, stB)
            P.dma("sp", sbcT[b], stv, [stB], [DB("sbcT", b)], stB)
            for cc in range(2):
                s2, s2B = st_p.get()
                transposes_to(s2[:, 0:256], s2B, [stv[:, f, cc * 128:(cc + 1) * 128] for f in range(2)], [stB])
                P.dma("sp", sb_tok[2 * b + cc], s2[:, 0:256], [s2B], [DB("sbt", 2 * b + cc)], s2B)
        if D1STOP <= 2:
            return
        wt, wB = get_w(2)
        for c in range(NCH):
            st, stB = st_p.get()
            for hh in range(2):
                pp, ppB = proj_tm(wt, wB, hh * 512, 512, c)
                act(st[:, hh * 512:(hh + 1) * 512], pp[:, 0:512], AF.Silu, ppB, [stB])
            P.dma("sp", sz_tok[c], st[:, 0:1024], [stB], [DB("sz", c)], stB)
        if D1STOP <= 3:
            return
        wt, wB = get_w(3)
        for c in range(NCH):
            sm, smB = sm_p.get()
            pp, ppB = proj_tm(wt, wB, 0, 64, c)
            cp("dve", sm[:, 0:64], pp[:, 0:64], ppB, [smB])
            P.dma("sp", dtg_tok[c], sm[:, 0:64], [smB], [DB("dtg", c)], smB)
        if D1STOP <= 4:
            return
        def qk_group(wt, wB, f0, b, gofs, stv, stB, nk_out=False):
            n = 4
            qr, qrB = qr_p.get()
            sq, sqB = sq4_p.get()
            rs, rsB = qr_p.get()
            for i in range(n):
                pp, ppB = proj_fm(wt, wB, (f0 + i) * 128, b, 256, False)
                act(sq[:, i, :], pp[:, 0:256], AF.Square, ppB, [sqB])
                cp("dve", qr[:, i, :], pp[:, 0:256], ppB, [qrB])
            for i in range(n):
                pn, pnB = psBr()
                mm(pn[:, 0:256], blk2, sq[:, i, :], [cbB, sqB], pnB)
                ts("dve", rs[:, i, :], pn[:, 0:256], 1.0 / 64, ALU.mult, pnB, [rsB], s2=EPS, op1=ALU.add)
            act(flat(rs[:]), flat(rs[:]), AF.Ln, [rsB], [rsB])
            act(flat(rs[:]), flat(rs[:]), AF.Exp, [rsB], [rsB], scale=-0.5)
            stt(qr[:], qr[:], pfm_s[:, l, gofs:gofs + 1], rs[:], ALU.mult, ALU.mult, [qrB, pfmB, rsB], [qrB])
            s_ = blk_seq(b)
            if nk_out and s_ != 0:
                P.dma("sp", nk_o[s_ - 1, l], qr[:, 0:2, :], [qrB], [DB("nk", s_, l)], qrB)
            if s_ != 0:
                cp("act", stv[:, f0:f0 + n, :], qr[:], [qrB], [stB])
                return
            qb, qbB = sq4_p.get()
            cp("act", flat(qb[:]), flat(qr[:]), [qrB], [qbB])
            a2, a2B = qr_p.get()
            t0 = 256 * b
            for i in range(n):
                pw, pwB = psBr()
                mm(pw[:, 0:256], pswap, qb[:, i, :], [cbB, qbB], pwB)
                tt("dve", a2[:, i, :], pw[:, 0:256], ropes_s[:, t0:t0 + 256], ALU.mult, pwB + [ropesB], [a2B])
            tt("dve", qr[:], qr[:], ropec_s[:, t0:t0 + 256].unsqueeze(1).broadcast_to([128, n, 256]), ALU.mult,
               [qrB, ropecB], [qrB])
            tt("pool", stv[:, f0:f0 + n, :], qr[:], a2[:], ALU.add, [qrB, a2B], [stB])

        wt, wB = get_w(4)
        for b in range(NBLK):
            st, stB = st_p.get()
            stv = st[:, 0:2048].rearrange("p (f t) -> p f t", f=8)
            for f0 in (0, 4):
                qk_group(wt, wB, f0, b, Q_QG, stv, stB)
            P.dma("sp", qT_d[b], stv, [stB], [DB("qT", b)], stB)
        if D1STOP <= 5:
            return
        wt, wB = get_w(5)
        for b in range(NBLK):
            st, stB = st_p.get()
            stv = st[:, 0:1024].rearrange("p (f t) -> p f t", f=4)
            qk_group(wt, wB, 0, b, Q_KG, stv, stB, nk_out=True)
            P.dma("sp", kT_d[b], stv, [stB], [DB("kT", b)], stB)
        if D1STOP <= 6:
            return
        wt, wB = get_w(6)
        for c in range(NCH):
            st, stB = st_p.get()
            pp, ppB = proj_tm(wt, wB, 0, 256, c)
            cp("act", st[:, 0:256], pp[:, 0:256], ppB, [stB])
            P.dma("sp", v_tok[c], st[:, 0:256], [stB], [DB("v", c)], stB)
            s_ = blk_seq(c // 2)
            if s_ != 0:
                sm, smB = sm_p.get()
                cp("dve", sm[:], pp[:, 0:256], ppB, [smB])
                t0 = (c % 2) * 128
                P.dma("sp", nv_o[s_ - 1, l, t0:t0 + 128, :], sm[:], [smB], [DB("nv", s_, l, c)], smB)
        if D1STOP <= 7:
            return
        for which, off, dstT in ((0, O_MQ, mqT), (1, O_MK, mkT)):
            wt, wB = get_w(7 + which)
            for b in range(NBLK):
                st, stB = st_p.get()
                stv = st[:, 0:2048].rearrange("p (f t) -> p f t", f=8)
                conv_block(wt, wB, b, 8, l, Q_MCW, Q_MCB, which * 8, stv, stB)
                P.dma("sp", dstT[b], stv, [stB], [DB("mqT" if which == 0 else "mkT", b)], stB)
                if which == 1:
                    for cc in range(2):
                        s2, s2B = st_p.get()
                        transposes_to(s2[:, 0:1024], s2B, [stv[:, f, cc * 128:(cc + 1) * 128] for f in range(8)],
                                      [stB], scale=MLK_SCALE)
                        P.dma("sp", mk_tok[2 * b + cc], s2[:, 0:1024], [s2B], [DB("mkt", 2 * b + cc)], s2B)
        if D1STOP <= 8:
            return
        for wi_, (off, dstT, key, fn) in enumerate(((O_MV, mv_tok, "mv", None), (O_MO, mo_tok, "mo", AF.Sigmoid))):
            wt, wB = get_w(9 + wi_)
            for c in range(NCH):
                st, stB = st_p.get()
                for hh in range(2):
                    pp, ppB = proj_tm(wt, wB, hh * 512, 512, c)
                    if fn is None:
                        cp("act", st[:, hh * 512:(hh + 1) * 512], pp[:, 0:512], ppB, [stB])
                    else:
                        act(st[:, hh * 512:(hh + 1) * 512], pp[:, 0:512], fn, ppB, [stB])
                P.dma("sp", dstT[c], st[:, 0:1024], [stB], [DB(key, c)], stB)
        if D1STOP <= 9:
            return
        for n in range(3):
            wt, wB = get_w(11 + n)
            for b in range(NBLK):
                st, stB = st_p.get()
                stv = st[:, 0:2048].rearrange("p (f t) -> p f t", f=8)
                for fc in range(8):
                    pp, ppB = proj_fm(wt, wB, fc * 128, b, 256, False)
                    act(stv[:, fc, :], pp[:, 0:256], AF.Sigmoid, ppB, [stB])
                P.dma("sp", gT[n, b], stv, [stB], [DB("gT", n, b)], stB)

    nsb = sb

    dtg_s = dtgB = g_dt = g_dtB = g_da = g_daB = g_dab = g_dabB = g_a = g_aB = g_acum = g_acumB = g_tot = g_totB = g_ea = g_eaB = g_w = g_wB = g_edec = g_edecB = g_tmp = g_tmpB = None

    def alloc_scan():
        nonlocal dtg_s, dtgB, g_dt, g_dtB, g_da, g_daB, g_dab, g_dabB, g_a, g_aB, g_acum, g_acumB, g_tot, g_totB, g_ea, g_eaB, g_w, g_wB, g_edec, g_edecB, g_tmp, g_tmpB
        dtg_s, dtgB = nsb("dtg_s", [128, NCH, 64])
        g_dt, g_dtB = nsb("g_dt", [128, NCH, 32])
        g_da, g_daB = nsb("g_da", [128, NCH, 32])
        g_dab, g_dabB = nsb("g_dab", [128, NCH, 32], BF16)
        g_a, g_aB = nsb("g_a", [128, 32])
        g_acum, g_acumB = nsb("g_acum", [128, 2, NCH, 16])
        g_tot, g_totB = nsb("g_tot", [128, 2, NCH, 16])
        g_ea, g_eaB = nsb("g_ea", [128, 2, NCH, 16])
        g_w, g_wB = nsb("g_w", [128, 2, NCH, 16])
        g_edec, g_edecB = nsb("g_edec", [128, 2, NCH, 8])
        g_tmp, g_tmpB = nsb("g_tmp", [128, 2, NCH, 16])


    def run_sweeps(chunk_loads, stages, seq_begin, seq_end):
        steps = []
        for s in range(3):
            n = SEQS[s][1]
            orders = [chunk_order(s, 0), chunk_order(s, 1)]
            for i in range(2 * n):
                for d in range(2):
                    steps.append((s, d, orders[d][i], i >= n))
        ns = len(stages)
        N = len(steps)
        loaded, state = {}, {}
        for k in range(-ns, N):
            li = k + ns
            if 0 <= li < N:
                loaded[li] = chunk_loads(*steps[li])
            for si, f in enumerate(stages):
                idx = k + (ns - 1 - si)
                if not (0 <= idx < N):
                    continue
                last = si == ns - 1
                if last and (idx == 0 or steps[idx - 1][0] != steps[idx][0]):
                    seq_begin(steps[idx][0])
                state[idx] = f(*steps[idx], loaded[idx], state.get(idx))
                if last:
                    if idx == N - 1 or steps[idx + 1][0] != steps[idx][0]:
                        seq_end(steps[idx][0])
                    loaded.pop(idx)
                    state.pop(idx)

    def chunk_order(s, d):
        f, n = SEQS[s][0], SEQS[s][1]
        cs = list(range(2 * f, 2 * (f + n)))
        return cs if d == 0 else cs[::-1]

    xk_p = xk2_p = bt_p = bct_p = big_p = arg_p = yo_p = tok_p = ytT_p = s1_p = z_p = None

    def alloc_mix(ssd=True):
        nonlocal xk_p, xk2_p, bt_p, bct_p, big_p, arg_p, yo_p, tok_p, ytT_p, s1_p, cvo_p, z_p
        cvo_p = mkpool("cvo", [128, 256], BF16, 6)
        xk_p = mkpool("xk", [128, 1024], BF16, 6 if ssd else 8)
        xk2_p = mkpool("xk2", [128, 1024], BF16, 5 if ssd else 6)
        if ssd:
            bt_p = mkpool("bt", [128, 256], BF16, 6)
            bct_p = mkpool("bct", [128, 4, 128], BF16, 6)
            big_p = mkpool("big", [128, 2048], BF16, 5)
            arg_p = mkpool("arg", [128, 2048], F32, 3)
        yo_p = mkpool("yo", [128, 1024], F32, 5)
        tok_p = mkpool("tok", [128, 1024], BF16, 3 if ssd else 6)
        z_p = mkpool("zt", [128, 1024], BF16, 5) if ssd else None
        ytT_p = mkpool("ytT", [128, 8, 128], BF16, 2)
        s1_p = mkpool("s1", [128, 8], F32, 10)


    def out_transposed(yn, ynB, br, c, bank=7):
        b, cc = c // 2, c % 2
        yt, ytB = ytT_p.get()
        transposes_to(flat(yt[:]), ytB, [yn[:, f * 128:(f + 1) * 128] for f in range(8)], [ynB], bank=bank)
        P.dma("sp", ybr[br, b, :, :, cc * 128:(cc + 1) * 128], yt[:], [ytB], [DB("ybr", br, b)], ytB)

    Hs = Hb = None

    def alloc_ssd():
        nonlocal Hs, Hb
        Hs = [nsb("Hs%d" % d, [128, 2, 256]) for d in range(2)]
        Hb = [nsb("Hb%d" % d, [128, 2, 256], BF16) for d in range(2)]


    def ssd_layer(l, pr, prB):
        P.dma("sp", dtg_s[:], dtg_tok.rearrange("c p n -> p c n"), [DB("dtg", c) for c in range(NCH)],
              [dtgB], dtgB)
        tt("dve", g_dt[:], dtg_s[:, :, 0:32], pr[:, P_DTB:P_DTB + 32].unsqueeze(1).broadcast_to([128, NCH, 32]),
           ALU.add, [dtgB, prB], [g_dtB])
        act(flat(g_dt[:]), flat(g_dt[:]), AF.Exp, [g_dtB], [g_dtB])
        act(flat(g_dt[:]), flat(g_dt[:]), AF.Ln, [g_dtB], [g_dtB], bias=1.0)
        act(g_a[:], pr[:, P_ALOG:P_ALOG + 32], AF.Exp, [prB], [g_aB])
        ts("dve", g_a[:], g_a[:], -1.0, ALU.mult, [g_aB], [g_aB])
        tt("dve", g_da[:], g_dt[:], g_a[:].unsqueeze(1).broadcast_to([128, NCH, 32]), ALU.mult,
           [g_dtB, g_aB], [g_daB])
        cp("dve", g_dab[:], g_da[:], [g_daB], [g_dabB])
        for d in range(2):
            pa, paB = psA()
            mm(pa[:, 0:NCH * 16].rearrange("p (c h) -> p c h", h=16), tri[d], g_dab[:, :, d * 16:(d + 1) * 16],
               [cbB, g_dabB], paB)
            cp("dve", g_acum[:, d], pa[:, 0:NCH * 16].rearrange("p (c h) -> p c h", h=16), paB, [g_acumB])
            pb, pbB = psA()
            mm(pb[:, 0:NCH * 16].rearrange("p (c h) -> p c h", h=16), ones, g_dab[:, :, d * 16:(d + 1) * 16],
               [cbB, g_dabB], pbB)
            cp("dve", g_tot[:, d], pb[:, 0:NCH * 16].rearrange("p (c h) -> p c h", h=16), pbB, [g_totB])
        fl4 = lambda t: t[:].rearrange("p d c h -> p (d c h)")
        act(fl4(g_ea), fl4(g_acum), AF.Exp, [g_acumB], [g_eaB])
        tt("dve", g_tmp[:], g_tot[:], g_acum[:], ALU.subtract, [g_totB, g_acumB], [g_tmpB])
        act(fl4(g_tmp), fl4(g_tmp), AF.Exp, [g_tmpB], [g_tmpB])
        for d in range(2):
            tt("dve", g_w[:, d], g_tmp[:, d], g_dt[:, :, d * 16:(d + 1) * 16], ALU.mult, [g_tmpB, g_dtB], [g_wB])
        for d in range(2):
            tv = g_tot[:, d].rearrange("p c (g f r) -> p c g f r", g=2, f=2)
            for hf in range(2):
                ps_ = slice(hf * 64, (hf + 1) * 64)
                act(g_edec[ps_, d].rearrange("p c (g r) -> p c g r", g=2), tv[ps_, :, :, hf, :], AF.Exp,
                    [g_totB], [g_edecB])

        def chunk_loads(s, d, c, last_sweep):
            b, cc = c // 2, c % 2
            x, xB = xk_p.get()
            P.dma("sp", x[:], sx_tok[c], [DB("sx", c)], [xB], xB)
            bt, btB = bt_p.get()
            P.dma("sp", bt[:], sb_tok[c], [DB("sbt", c)], [btB], btB)
            bct, bctB = bct_p.get()
            P.dma("sp", bct[:], sbcT[b, :, :, cc * 128:(cc + 1) * 128], [DB("sbcT", b)], [bctB], bctB)
            z = zB = None
            if last_sweep:
                z, zB = z_p.get()
                P.dma("sp", z[:], sz_tok[c], [DB("sz", c)], [zB], zB)
            return x, xB, bt, btB, bct, bctB, z, zB

        def chunkA0(s, d, c, last_sweep, tl, _st):
            hs = slice(d * 16, (d + 1) * 16)
            dau, dauB = big_p.get()
            dau3 = dau[:].rearrange("p (h i) -> p h i", h=16)
            tt("pool", dau3, tri[d].unsqueeze(1).broadcast_to([128, 16, 128]),
               g_dab[:, c, hs].unsqueeze(2).broadcast_to([128, 16, 128]), ALU.mult, [cbB, g_dabB], [dauB])
            sg, sgB = psum_at(0, 4)
            for q in range(4):
                mm(sg[:, q * 512:(q + 1) * 512], ones, dau[:, q * 512:(q + 1) * 512], [cbB, dauB], sgB,
                   start=True, stop=False)
                mm(sg[:, q * 512:(q + 1) * 512].rearrange("p (h i) -> p h i", h=4), ident,
                   nmask[d].unsqueeze(1).broadcast_to([128, 4, 128]), [cbB], sgB, start=False, stop=True)
            ar, arB = arg_p.get()
            tt("dve", ar[:].rearrange("p (h i) -> p h i", h=16), sg.rearrange("p (h i) -> p h i", h=16),
               g_acum[:, d, c, :].unsqueeze(2).broadcast_to([128, 16, 128]), ALU.subtract, sgB + [g_acumB], [arB])
            Lm, LmB = big_p.get()
            act(Lm[:], ar[:], AF.Exp, [arB], [LmB])
            return Lm, LmB

        def chunkA1(s, d, c, last_sweep, tl, st0):
            x, xB, bt, btB, bct, bctB, z, zB = tl
            Lm, LmB = st0
            pcs = [psum_at(4), psum_at(5)]
            for g in range(4):
                ps_ = slice((g % 2) * 64, (g % 2) * 64 + 64)
                pc, pcB = pcs[g % 2]
                mm(pc[:, (g // 2) * 128:(g // 2 + 1) * 128], bct[ps_, g // 2, :], bct[ps_, 2 + g // 2, :], [bctB], pcB)
            cbts = []
            for par in range(2):
                ct, ctB = cvo_p.get()
                cp("act", ct[:], pcs[par][0][:, 0:256], pcs[par][1], [ctB])
                cbts.append((ct, ctB))
            sc, scB = big_p.get()
            scv = sc[:].rearrange("p (gg two r i) -> p gg two r i", gg=2, two=2, r=4)
            Lmv = Lm[:].rearrange("p (gg two r i) -> p gg two r i", gg=2, two=2, r=4)
            for par, (ct, ctB) in enumerate(cbts):
                tt("pool", scv[:, :, par], Lmv[:, :, par],
                   ct[:].rearrange("p (g i) -> p g i", g=2).unsqueeze(2).broadcast_to([128, 2, 4, 128]),
                   ALU.mult, [LmB, ctB], [scB])
            return sc, scB

        def chunkB(s, d, c, last_sweep, tl, sa):
            b, cc = c // 2, c % 2
            hs = slice(d * 16, (d + 1) * 16)
            H, HB_ = Hs[d]
            Hbf, HbB = Hb[d]
            x, xB, bt, btB, bct, bctB, z, zB = tl
            sc, scB = sa
            xd, xdB = xk2_p.get()
            tt("dve", xd[:].rearrange("p (h q) -> p h q", h=16), x[:].rearrange("p (h q) -> p h q", h=16),
               g_dt[:, c, hs].unsqueeze(2).broadcast_to([128, 16, 64]), ALU.mult, [xB, g_dtB], [xdB])
            wx, wxB = xk2_p.get()
            tt("dve", wx[:].rearrange("p (h q) -> p h q", h=16), x[:].rearrange("p (h q) -> p h q", h=16),
               g_w[:, d, c, :].unsqueeze(2).broadcast_to([128, 16, 64]), ALU.mult, [xB, g_wB], [wxB])
            yi, yiB = psum_at(6, 2)
            for h in range(16):
                mm(yi[:, h * 64:(h + 1) * 64], sc[:, h * 128:(h + 1) * 128], xd[:, h * 64:(h + 1) * 64],
                   [scB, xdB], [yiB[h // 8]])
            ys, ysB = psum_at(4, 2)
            for g in range(4):
                ps_ = slice((g % 2) * 64, (g % 2) * 64 + 64)
                co = (g % 2) * 512 + (g // 2) * 256
                mm(ys[:, co:co + 256], bct[ps_, 2 + g // 2, :], Hbf[ps_, g // 2, :], [bctB, HbB], [ysB[g % 2]])
            yo, yoB = yo_p.get()
            yov = yo[:].rearrange("p (gg two r q) -> p gg two r q", gg=2, two=2, r=4)
            eav = g_ea[:, d, c, :].rearrange("p (gg two r) -> p gg two r", gg=2, two=2)
            for par in range(2):
                tt("dve", yov[:, :, par], ys[:, par * 512:(par + 1) * 512].rearrange("p (gg r q) -> p gg r q", gg=2, r=4),
                   eav[:, :, par].unsqueeze(3).broadcast_to([128, 2, 4, 64]), ALU.mult, [ysB[par], g_eaB], [yoB])
            tt("dve", yo[:], yo[:], yi, ALU.add, [yoB] + yiB, [yoB])
            dh, dhB = psum_at(4, 2)
            for gg in range(2):
                mm(dh[:, gg * 512:(gg + 1) * 512], bt[:, gg * 128:(gg + 1) * 128], wx[:, gg * 512:(gg + 1) * 512],
                   [btB, wxB], [dhB[gg]])
            tt("dve", H[:].rearrange("p g (r q) -> p g r q", r=4), H[:].rearrange("p g (r q) -> p g r q", r=4),
               g_edec[:, d, c, :].rearrange("p (g r) -> p g r", g=2).unsqueeze(3).broadcast_to([128, 2, 4, 64]),
               ALU.mult, [HB_, g_edecB], [HB_])
            dhv = dh.rearrange("p (g x) -> p g x", g=2)
            for hf in range(2):
                ps_ = slice(hf * 64, (hf + 1) * 64)
                tt("dve", H[ps_], H[ps_], dhv[ps_, :, hf * 256:(hf + 1) * 256], ALU.add, [HB_] + dhB, [HB_])
            cp("act", flat(Hbf[:]), flat(H[:]), [HB_], [HbB])
            if not last_sweep:
                P.dma("sp", yf_d[c], yo[:], [yoB], [DB("yf", c)], yoB)
                return
            yf, yfB = yo_p.get()
            P.dma("sp", yf[:], yf_d[c], [DB("yf", c)], [yfB], yfB)
            tt("dve", yo[:], yo[:], yf[:], ALU.add, [yoB, yfB], [yoB])
            xd2, xd2B = yo_p.get()
            tt("pool", xd2[:].rearrange("p (h q) -> p h q", h=16), x[:].rearrange("p (h q) -> p h q", h=16),
               pr[:, P_SD:P_SD + 16].unsqueeze(2).broadcast_to([128, 16, 64]), ALU.mult, [xB, prB], [xd2B])
            tt("dve", yo[:], yo[:], xd2[:], ALU.add, [yoB, xd2B], [yoB])
            tt("dve", yo[:], yo[:], z[:], ALU.mult, [yoB, zB], [yoB])
            s1, s1B = s1_p.get()
            act(yf[:], yo[:], AF.Square, [yoB], [yfB, s1B], accum=s1[:, 0:1])
            rsqrt(s1[:, 0:1], s1B, (128, 1), 1.0 / D)
            yn, ynB = tok_p.get()
            stt(yn[:], yo[:], s1[:, 0:1], pr[:, P_SSDN:P_SSDN + 1024], ALU.mult, ALU.mult, [yoB, s1B, prB], [ynB])
            out_transposed(yn, ynB, 0, c, bank=6)

        def seq_begin(s):
            has_ctx = SEQS[s][2]
            for d in range(2):
                H, HB_ = Hs[d]
                if has_ctx:
                    P.dma("sp", flat(H[:]), ssd0[l, d], [], [HB_], HB_)
                else:
                    memset("dve", flat(H[:]), 0.0, [HB_])
                cp("pool", Hb[d][0][:], H[:], [HB_], [Hb[d][1]])

        def seq_end(s):
            if not SEQS[s][2]:
                for d in range(2):
                    H, HB_ = Hs[d]
                    P.dma("sp", nssd_o[s - 1, l, d], flat(H[:]), [HB_], [DB("nssd", s, l, d)], HB_)

        run_sweeps(chunk_loads, [chunkA0, chunkA1, chunkB], seq_begin, seq_end)

    CS = CSb = m_li = m_liB = m_lf = m_lfB = m_lfb = m_lfbB = m_b = m_bB = m_g = m_gB = m_wt = m_wtB = m_fl = m_flB = m_RB = m_RBB = m_SCB = m_SCBB = m_GM = m_GMB = m_BT = m_BTB = m_R = m_RB2 = m_MD = m_MDB = m_mp = m_mpB = m_RD = m_RDB = vp_p = mT_p = None

    def alloc_ml():
        nonlocal CS, CSb, m_li, m_liB, m_lf, m_lfB, m_lfb, m_lfbB, m_b, m_bB, m_g, m_gB, m_wt, m_wtB, m_fl, m_flB, m_RB, m_RBB, m_SCB, m_SCBB, m_GM, m_GMB, m_BT, m_BTB, m_R, m_RB2, m_MD, m_MDB, m_mp, m_mpB, m_RD, m_RDB, vp_p, mT_p, dtg_s, dtgB, g_da, g_daB
        dtg_s, dtgB = nsb("dtg_s", [128, NCH, 64])
        g_da, g_daB = nsb("g_da", [128, NCH, 32])
        CS = [nsb("CS%d" % d, [128, 8, 129]) for d in range(2)]
        CSb = [nsb("CSb%d" % d, [128, 8, 129], BF16) for d in range(2)]
        m_li, m_liB = nsb("m_li", [128, 2, NCH, 8])
        m_lf, m_lfB = nsb("m_lf", [128, 2, NCH, 8])
        m_lfb, m_lfbB = nsb("m_lfb", [128, 2, NCH, 8], BF16)
        m_b, m_bB = nsb("m_b", [128, 2, NCH, 8])
        m_g, m_gB = nsb("m_g", [128, 2, NCH, 8])
        m_wt, m_wtB = nsb("m_wt", [128, 2, NCH, 8])
        m_fl, m_flB = nsb("m_fl", [128, 2, NCH, 8])
        m_RB, m_RBB = nsb("m_RB", [128, 2, NCH, 8])
        m_SCB, m_SCBB = nsb("m_SCB", [128, 2, NCH, 8])
        m_GM, m_GMB = nsb("m_GM", [8, 2, NCH])
        m_BT, m_BTB = nsb("m_BT", [8, 2, NCH])
        m_R, m_RB2 = nsb("m_R", [8, 2, NCH])
        m_MD, m_MDB = nsb("m_MD", [8, 2, NCH])
        m_mp, m_mpB = nsb("m_mp", [8, 2, 3, NCH + 1])
        m_RD, m_RDB = nsb("m_RD", [8, 2, 2, NCH, 8])
        vp_p = mkpool("vp", [128, 8, 129], BF16, 4)
        mT_p = mkpool("mT", [128, 8, 128], BF16, 8)


    def ml_layer(l, pr, prB):
        pre = g_da
        preB = g_daB
        P.dma("sp", dtg_s[:], dtg_tok.rearrange("c p n -> p c n"), [DB("dtg", c) for c in range(NCH)],
              [dtgB], dtgB)
        tt("dve", pre[:], dtg_s[:, :, 32:64], pr[:, P_MGB:P_MGB + 32].unsqueeze(1).broadcast_to([128, NCH, 32]),
           ALU.add, [dtgB, prB], [preB])
        for d in range(2):
            cp("dve", m_li[:, d], pre[:, :, d * 16:d * 16 + 8], [preB], [m_liB])
            act(m_lf[:, d], pre[:, :, d * 16 + 8:d * 16 + 16], AF.Exp, [preB], [m_lfB], scale=-1.0)
        fl4 = lambda t: t[:].rearrange("p d c h -> p (d c h)")
        act(fl4(m_lf), fl4(m_lf), AF.Ln, [m_lfB], [m_lfB], bias=1.0)
        ts("dve", fl4(m_lf), fl4(m_lf), -1.0, ALU.mult, [m_lfB], [m_lfB])
        cp("dve", fl4(m_lfb), fl4(m_lf), [m_lfB], [m_lfbB])
        for d in range(2):
            pa, paB = psA()
            pav = pa[:, 0:NCH * 8].rearrange("p (c h) -> p c h", h=8)
            mm(pav, tri[d], m_lfb[:, d], [cbB, m_lfbB], paB)
            cp("dve", m_b[:, d], pav, paB, [m_bB])
        tt("dve", m_g[:], m_li[:], m_b[:], ALU.subtract, [m_liB, m_bB], [m_gB])
        for d in range(2):
            for c0 in range(0, NCH, 4):
                n = min(4, NCH - c0)
                pt, ptB = psA()
                for i in range(n):
                    tp(pt[0:8, i * 128:(i + 1) * 128], m_g[:, d, c0 + i, :], cf[:], [m_gB, cfB], ptB)
                P.add("dve", (lambda o=m_GM[:, d, c0:c0 + n], i_=pt[0:8, 0:n * 128].rearrange("p (c j) -> p c j", j=128):
                              V.tensor_reduce(out=o, in_=i_, axis=AX.X, op=ALU.max)), ptB, [m_GMB])
            pb, pbB = psBr()
            for c in range(NCH):
                mm(pb[0:8, c:c + 1], m_lfb[:, d, c, :], ones[:, 0:1], [m_lfbB, cbB], pbB)
            cp("dve", m_BT[:, d], pb[0:8, 0:NCH], pbB, [m_BTB])
        for d in range(2):
            for s in range(3):
                f, n, has_ctx, _ = SEQS[s]
                mp = m_mp[:, d, s]
                if has_ctx:
                    P.dma("sp", mp[:, 0:1], mlm0[l, d], [], [m_mpB], m_mpB)
                else:
                    memset("dve", mp[:, 0:1], 0.0, [m_mpB])
                for i, c in enumerate(chunk_order(s, d)):
                    tt("dve", m_R[:, d, c:c + 1], mp[:, i:i + 1], m_GM[:, d, c:c + 1], ALU.max,
                       [m_mpB, m_GMB], [m_RB2])
                    tt("dve", m_MD[:, d, c:c + 1], mp[:, i:i + 1], m_R[:, d, c:c + 1], ALU.subtract,
                       [m_mpB, m_RB2], [m_MDB])
                    tt("dve", mp[:, i + 1:i + 2], m_R[:, d, c:c + 1], m_BT[:, d, c:c + 1], ALU.add,
                       [m_RB2, m_BTB], [m_mpB])
                if not has_ctx:
                    P.dma("sp", nmlm_o[s - 1, l, d], mp[:, 2 * n:2 * n + 1], [m_mpB], [DB("nmlm", s, l, d)], m_mpB)
        for d in range(2):
            for wi, (src, srcB) in enumerate(((m_R, m_RB2), (m_MD, m_MDB))):
                tt("dve", m_RD[:, d, wi], src[:, d, :].unsqueeze(2).broadcast_to([8, NCH, 8]),
                   cf[0:8, 0:8].unsqueeze(1).broadcast_to([8, NCH, 8]), ALU.mult, [srcB, cfB], [m_RDB])
            pr_, prB_ = psBr()
            mm(pr_[:, 0:2 * NCH * 8], onesf[0:8, :], m_RD[:, d].rearrange("p w c h -> p (w c h)"),
               [onesfB, m_RDB], prB_)
            cp("dve", m_RB[:, d], pr_[:, 0:NCH * 8].rearrange("p (c h) -> p c h", h=8), prB_, [m_RBB])
            act(m_SCB[:, d], pr_[:, NCH * 8:2 * NCH * 8].rearrange("p (c h) -> p c h", h=8), AF.Exp, prB_, [m_SCBB])
        tt("dve", m_wt[:], m_g[:], m_RB[:], ALU.subtract, [m_gB, m_RBB], [m_wtB])
        act(fl4(m_wt), fl4(m_wt), AF.Exp, [m_wtB], [m_wtB])
        tt("dve", m_fl[:], m_b[:], m_RB[:], ALU.add, [m_bB, m_RBB], [m_flB])
        ts("dve", fl4(m_fl), fl4(m_fl), -1.0, ALU.mult, [m_flB], [m_flB], s2=80.0, op1=ALU.min)
        act(fl4(m_fl), fl4(m_fl), AF.Exp, [m_flB], [m_flB])

        def chunk_loads(s, d, c, last_sweep):
            b, cc = c // 2, c % 2
            q, qB = mT_p.get()
            P.dma("sp", q[:], mqT[b, :, :, cc * 128:(cc + 1) * 128], [DB("mqT", b)], [qB], qB)
            k, kB = mT_p.get()
            P.dma("sp", k[:], mkT[b, :, :, cc * 128:(cc + 1) * 128], [DB("mkT", b)], [kB], kB)
            kt, ktB = xk_p.get()
            P.dma("sp", kt[:], mk_tok[c], [DB("mkt", c)], [ktB], ktB)
            v, vB = xk_p.get()
            P.dma("sp", v[:], mv_tok[c], [DB("mv", c)], [vB], vB)
            mo = moB = None
            if last_sweep:
                mo, moB = tok_p.get()
                P.dma("sp", mo[:], mo_tok[c], [DB("mo", c)], [moB], moB)
            return q, qB, k, kB, kt, ktB, v, vB, mo, moB

        def chunk(s, d, c, last_sweep, tl):
            b, cc = c // 2, c % 2
            C, CB_ = CS[d]
            Cb, CbB = CSb[d]
            q, qB, k, kB, kt, ktB, v, vB, mo, moB = tl
            sc, scB = psum_at(0, 2)
            for h in range(8):
                mm(sc[:, h * 128:(h + 1) * 128], k[:, h, :], q[:, h, :], [kB, qB], [scB[h // 4]])
            sm_, smB_ = xk2_p.get()
            stt(sm_[:].rearrange("p (h t) -> p h t", h=8), sc.rearrange("p (h t) -> p h t", h=8), MLK_SCALE,
                tri[d].unsqueeze(1).broadcast_to([128, 8, 128]), ALU.mult, ALU.mult, scB + [cbB], [smB_])
            vp, vpB = vp_p.get()
            tt("pool", vp[:, :, 0:128], v[:].rearrange("p (h e) -> p h e", h=8),
               m_wt[:, d, c, :].unsqueeze(2).broadcast_to([128, 8, 128]), ALU.mult, [vB, m_wtB], [vpB])
            cp("pool", vp[:, :, 128:129], m_wt[:, d, c, :].unsqueeze(2), [m_wtB], [vpB])
            return sm_, smB_, vp, vpB

        def chunkB(s, d, c, last_sweep, tl, sa):
            b, cc = c // 2, c % 2
            C, CB_ = CS[d]
            Cb, CbB = CSb[d]
            q, qB, k, kB, kt, ktB, v, vB, mo, moB = tl
            sm_, smB_, vp, vpB = sa
            tt("dve", C[:], C[:], m_SCB[:, d, c, :].unsqueeze(2).broadcast_to([128, 8, 129]), ALU.mult,
               [CB_, m_SCBB], [CB_])
            cp("act", Cb[:].rearrange("p h e -> p (h e)"), C[:].rearrange("p h e -> p (h e)"), [CB_], [CbB])
            nm, nmB = psum_at(2, 2)
            dn, dnB = psum_at(4)
            for h in range(8):
                mm(nm[:, h * 128:(h + 1) * 128], sm_[:, h * 128:(h + 1) * 128], vp[:, h, 0:128], [smB_, vpB],
                   [nmB[h // 4]], start=True, stop=False)
                mm(nm[:, h * 128:(h + 1) * 128], q[:, h, :], Cb[:, h, 0:128], [qB, CbB], [nmB[h // 4]],
                   start=False, stop=True)
            for h in range(8):
                mm(dn[:, h:h + 1], sm_[:, h * 128:(h + 1) * 128], vp[:, h, 128:129], [smB_, vpB], dnB,
                   start=True, stop=False)
                mm(dn[:, h:h + 1], q[:, h, :], Cb[:, h, 128:129], [qB, CbB], dnB, start=False, stop=True)
            dc, dcB = psum_at(5, 2)
            for h in range(8):
                mm(dc[:, h * 128:(h + 1) * 128], kt[:, h * 128:(h + 1) * 128], vp[:, h, 0:128], [ktB, vpB],
                   [dcB[h // 4]])
            for h in range(8):
                mm(dn[:, 8 + h:9 + h], kt[:, h * 128:(h + 1) * 128], vp[:, h, 128:129], [ktB, vpB], dnB)
            dd, ddB = s1_p.get()
            cp("dve", dd[:], dn[:, 0:8], dnB, [ddB])
            stt(dd[:], dd[:], -1.0, dd[:], ALU.mult, ALU.max, [ddB], [ddB])
            tt("dve", dd[:], dd[:], m_fl[:, d, c, :], ALU.max, [ddB, m_flB], [ddB])
            P.add("dve", lambda o=dd[:]: V.reciprocal(out=o, in_=o), [ddB], [ddB])
            hd, hdB = yo_p.get()
            tt("dve", hd[:].rearrange("p (h e) -> p h e", h=8), nm.rearrange("p (h e) -> p h e", h=8),
               dd[:].unsqueeze(2).broadcast_to([128, 8, 128]), ALU.mult, nmB + [ddB], [hdB])
            tt("dve", C[:, :, 0:128], C[:, :, 0:128], dc.rearrange("p (h e) -> p h e", h=8), ALU.add,
               [CB_] + dcB, [CB_])
            tt("dve", C[:, :, 128:129], C[:, :, 128:129], dn[:, 8:16].unsqueeze(2), ALU.add, [CB_] + dnB, [CB_])
            if not last_sweep:
                P.dma("sp", hf_d[c], hd[:], [hdB], [DB("hf", c)], hdB)
                return
            hf, hfB = yo_p.get()
            P.dma("sp", hf[:], hf_d[c], [DB("hf", c)], [hfB], hfB)
            tt("dve", hd[:], hd[:], hf[:], ALU.add, [hdB, hfB], [hdB])
            act(hf[:], hd[:], AF.Square, [hdB], [hfB])
            s1, s1B = s1_p.get()
            P.add("dve", lambda o=s1[:], i_=hf[:].rearrange("p (h e) -> p h e", h=8):
                  V.tensor_reduce(out=o, in_=i_, axis=AX.X, op=ALU.add), [hfB], [s1B])
            rsqrt(s1[:], s1B, (128, 8), 1.0 / 128)
            tt("dve", hd[:].rearrange("p (h e) -> p h e", h=8), hd[:].rearrange("p (h e) -> p h e", h=8),
               s1[:].unsqueeze(2).broadcast_to([128, 8, 128]), ALU.mult, [hdB, s1B], [hdB])
            tt("dve", hd[:], hd[:], pr[:, P_MLN:P_MLN + 1024], ALU.mult, [hdB, prB], [hdB])
            yn, ynB = tok_p.get()
            tt("dve", yn[:], hd[:], mo[:], ALU.mult, [hdB, moB], [ynB])
            out_transposed(yn, ynB, 2, c)

        def seq_begin(s):
            for d in range(2):
                C, CB_ = CS[d]
                if SEQS[s][2]:
                    P.dma("sp", C[:].rearrange("p h e -> p (h e)"), mlc0[l, d], [], [CB_], CB_)
                else:
                    memset("dve", C[:].rearrange("p h e -> p (h e)"), 0.0, [CB_])

        def seq_end(s):
            if not SEQS[s][2]:
                for d in range(2):
                    C, CB_ = CS[d]
                    P.dma("sp", nmlc_o[s - 1, l, d], C[:].rearrange("p h e -> p (h e)"), [CB_],
                          [DB("nmlc", s, l, d)], CB_)

        run_sweeps(chunk_loads, [lambda s_, d_, c_, f_, tl, _st: chunk(s_, d_, c_, f_, tl), chunkB], seq_begin, seq_end)

    NKS = NCTX + SL
    KT = KTB = VA = VAB = VB_ = VBB = ckd = ckdB = q_p = e_p = yat_p = dsb_p = ev_p = None

    def alloc_att():
        nonlocal KT, KTB, VA, VAB, VB_, VBB, ckd, ckdB, q_p, e_p, yat_p, dsb_p, ev_p
        KT, KTB = nsb("KT", [128, 8, NKS], BF16)
        memset("dve", KT[:].rearrange("p a b -> p (a b)"), 0.0, [KTB])
        VA, VAB = nsb("VA", [128, (NKS // 128), 4, 128], BF16)
        VB_, VBB = nsb("VBt", [128, (NKS // 128), 4, 128], BF16)
        ckd, ckdB = nsb("ckd", [128, 4, 4, 128], BF16)
        memset("pool", VA[:].rearrange("p a b c -> p (a b c)"), 0.0, [VAB])
        memset("pool", VB_[:].rearrange("p a b c -> p (a b c)"), 0.0, [VBB])
        for kc_ in range(NKS // 128):
            memset("pool", VA[:, kc_, :, 64:128], 1.0, [VAB])
            memset("pool", VB_[:, kc_, :, 0:64], 1.0, [VBB])
        q_p = mkpool("qatt", [128, 8, 512], BF16, 2)
        e_p = mkpool("eatt", [128, 2, 512], BF16, 2)
        yat_p = mkpool("yat", [128, 8, 512], BF16, 2)
        dsb_p = mkpool("dsb", [128, 512], F32, 4)
        ev_p = mkpool("ev", [128, 2, 512], F32, 4)


    def att_layer(l):
        for s in range(3):
            f, n, has_ctx, _ = SEQS[s]
            L = 256 * n
            nctx = NCTX if has_ctx else 0
            nk = nctx + L
            nkc = nk // 128
            if has_ctx:
                src = cache_k[l].rearrange("(kc p) (f c) -> p kc f c", p=128, f=2)
                P.dma("pool", ckd[:, :, 0:2, :], src, [], [ckdB], ckdB)
                for f2 in range(2):
                    P.dma("pool", ckd[:, :, 2 + f2, 0:64], src[:, :, f2, 64:128], [], [ckdB], ckdB)
                    P.dma("pool", ckd[:, :, 2 + f2, 64:128], src[:, :, f2, 0:64], [], [ckdB], ckdB)
                KTv = KT[:].rearrange("p (f sw hh) t -> p sw f hh t", f=2, sw=2, hh=2)
                for kc in range(4):
                    pt, ptB = psBr()
                    ptb = pt.bitcast(BF16)
                    for fv in range(4):
                        tp(ptb[:, fv * 128:(fv + 1) * 128], ckd[:, kc, fv, :], ident, [ckdB, cbB], ptB)
                    ptv = ptb[:, 0:512].rearrange("p (sw f t) -> p sw f t", sw=2, f=2)
                    ks = slice(kc * 128, (kc + 1) * 128)
                    cp("act", KTv[0:64, :, :, 0, ks], ptv[0:64], ptB, [KTB])
                    for sw in range(2):
                        cp("act", KTv[64:128, 1 - sw, :, 1, ks], ptv[64:128, sw], ptB, [KTB])
                srcv = cache_v[l].rearrange("(kc p) (g c) -> p kc g c", p=128, g=4)
                for kc in range(4):
                    P.dma("pool", VA[:, kc, :, 0:64], srcv[:, kc], [], [VAB], VAB)
                    P.dma("pool", VB_[:, kc, :, 64:128], srcv[:, kc], [], [VBB], VBB)
            for g in range(4):
                for hh in range(2):
                    fc = (g // 2) if (g % 2) == hh else 2 + g // 2
                    ps_ = slice(hh * 64, hh * 64 + 64)
                    P.dma("sp", KT[ps_, g * 2 + hh, nctx:nctx + L].rearrange("p (b t) -> p b t", b=n),
                          kT_d[f:f + n, ps_, fc, :].rearrange("b p t -> p b t"),
                          [DB("kT", f + bi) for bi in range(n)], [KTB], KTB)
            c0 = 2 * f
            kc0 = nctx // 128
            for i in range(2 * n):
                srcv = v_tok[c0 + i].rearrange("p (g e) -> p g e", g=4)
                P.dma("sp", VA[:, kc0 + i, :, 0:64], srcv, [DB("v", c0 + i)], [VAB], VAB)
                P.dma("sp", VB_[:, kc0 + i, :, 64:128], srcv, [DB("v", c0 + i)], [VBB], VBB)
            NQ = min(512, L)
            nqb = NQ // 256
            for qb in range(L // NQ):
                qt, qtB = q_p.get()
                for i in range(nqb):
                    b = f + qb * nqb + i
                    P.dma("sp", qt[:, :, i * 256:(i + 1) * 256], qT_d[b], [DB("qT", b)], [qtB], qtB)
                ya, yaB = yat_p.get()
                tasks = [(j, hh, k0) for j in range(8) for hh in range(2) for k0 in range(0, nkc, 2)]

                def hinfo(j, hh):
                    h = 2 * j + hh
                    g = h // 4
                    return g, slice(0, 128), g * 2 + hh

                def emit_scores(ti):
                    j, hh, k0 = tasks[ti]
                    g, ps_, fv = hinfo(j, hh)
                    scp, scpB = psum_at(2 + 2 * (ti % 2), 2)
                    for kk in range(2):
                        kc = k0 + kk
                        mm(scp[:, kk * 512:kk * 512 + NQ], KT[ps_, fv, kc * 128:(kc + 1) * 128],
                           qt[ps_, j, 0:NQ], [KTB, qtB], [scpB[kk]])
                    return scp, scpB

                ohs = {}

                def emit_epv(ti, scp, scpB):
                    j, hh, k0 = tasks[ti]
                    g, ps_, fv = hinfo(j, hh)
                    Vt, VtB = (VA, VAB) if hh == 0 else (VB_, VBB)
                    if k0 == 0:
                        ohs[(j, hh)] = psum_at((0 if j % 2 == 0 else 6) + hh)
                    oh, ohB = ohs[(j, hh)]
                    e, eB = e_p.get()
                    act(e[:, :, 0:NQ], scp.rearrange("p (k q) -> p k q", k=2)[:, :, 0:NQ], AF.Exp,
                        scpB, [eB], scale=0.125)
                    for kk in range(2):
                        kc = k0 + kk
                        mm(oh[:, 0:NQ], Vt[:, kc, g, :], e[:, kk, 0:NQ], [VtB, eB], ohB,
                           start=(kc == 0), stop=(kc == nkc - 1))
                    if hh == 1 and k0 + 2 >= nkc:
                        (oa, oaB), (ob, obB) = ohs[(j, 0)], ohs[(j, 1)]
                        ev, evB = ev_p.get()
                        d2, d2B = dsb_p.get()
                        cp("dve", ev[:, 0, 0:NQ], oa[:, 0:NQ], oaB, [evB])
                        cp("dve", ev[:, 1, 0:NQ], ob[:, 0:NQ], obB, [evB])
                        P.dma("sp", d2[64:128, 0:NQ], ev[0:64, 1, 0:NQ], [evB], [d2B], d2B)
                        P.dma("sp", d2[0:64, 0:NQ], ev[64:128, 0, 0:NQ], [evB], [d2B], d2B)
                        def norm(j=j, ev=ev, evB=evB, d2=d2, d2B=d2B):
                            P.add("dve", lambda o=d2[:, 0:NQ]: V.reciprocal(out=o, in_=o), [d2B], [d2B])
                            tt("dve", ya[0:64, j, 0:NQ], ev[0:64, 0, 0:NQ], d2[0:64, 0:NQ], ALU.mult,
                               [evB, d2B], [yaB])
                            tt("dve", ya[64:128, j, 0:NQ], ev[64:128, 1, 0:NQ], d2[64:128, 0:NQ], ALU.mult,
                               [evB, d2B], [yaB])
                        while len(deferred) >= 2:
                            deferred.pop(0)()
                        deferred.append(norm)

                deferred = []
                pend = emit_scores(0)
                for ti in range(len(tasks)):
                    nxt = emit_scores(ti + 1) if ti + 1 < len(tasks) else None
                    emit_epv(ti, *pend)
                    pend = nxt
                while deferred:
                    deferred.pop(0)()
                for i in range(nqb):
                    b = f + qb * nqb + i
                    P.dma("sp", ybr[1, b], ya[:, :, i * 256:(i + 1) * 256], [yaB], [DB("ybr", 1, b)], yaB)


    wbr_s = wo_s = yb_p = gt_p = mg_p = t3_p = None

    def alloc_mrg():
        nonlocal wbr_s, wo_s, yb_p, gt_p, mg_p, t3_p
        wbr_s = [nsb("wbr%d" % n, [128, KC, 1024], BF16) for n in range(3)]
        yb_p = mkpool("ybl", [128, 8, 256], BF16, 6)
        gt_p = mkpool("gtl", [128, 8, 256], BF16, 6)
        mg_p = mkpool("mgd", [128, 8, 256], BF16, 2)
        t3_p = mkpool("t3", [128, 256], F32, 9)

    def merge_a(l):
        for n in range(3):
            load_slab(w_br[l, n], 1024, 0, wbr_s[n])

        def loads(b):
            ys_, gs_ = [], []
            for n in range(3):
                y, yB = yb_p.get()
                P.dma("sp", y[:], ybr[n, b], [DB("ybr", n, b)], [yB], yB)
                g, gB = gt_p.get()
                P.dma("sp", g[:], gT[n, b], [DB("gT", n, b)], [gB], gB)
                ys_.append((y, yB))
                gs_.append((g, gB))
            return ys_, gs_

        nxt = loads(0)
        for b in range(NBLK):
            ys_, gs_ = nxt
            if b + 1 < NBLK:
                nxt = loads(b + 1)
            mg, mgB = mg_p.get()
            for oc in range(8):
                ts_ = []
                for n in range(3):
                    pp, ppB = psA()
                    for k in range(KC):
                        mm(pp[:, 0:256], wbr_s[n][0][:, k, oc * 128:(oc + 1) * 128], ys_[n][0][:, k, :],
                           [wbr_s[n][1], ys_[n][1]], ppB, start=(k == 0), stop=(k == KC - 1))
                    t, tB = t3_p.get()
                    tt("dve", t[:], pp[:, 0:256], gs_[n][0][:, oc, :], ALU.mult, ppB + [gs_[n][1]], [tB])
                    ts_.append((t, tB))
                tt("pool", ts_[0][0][:], ts_[0][0][:], ts_[1][0][:], ALU.add, [ts_[0][1], ts_[1][1]], [ts_[0][1]])
                tt("pool", mg[:, oc, :], ts_[0][0][:], ts_[2][0][:], ALU.add, [ts_[0][1], ts_[2][1]], [mgB])
            P.dma("sp", mgd_d[b], mg[:], [mgB], [DB("mgd", b)], mgB)

    def merge_b(l):
        wo_t, wo_B = nsb("wo_s", [128, KC, 1024], BF16)
        mgl_p = mkpool("mgl", [128, 8, 256], BF16, 2)
        load_slab(w_out[l], 1024, 0, (wo_t, wo_B))
        for b in range(NBLK):
            m = SEQS[blk_seq(b)][3]
            mg, mgB = mgl_p.get()
            P.dma("sp", mg[:], mgd_d[b], [DB("mgd", b)], [mgB], mgB)
            xt, xtB = xt_p.get()
            P.dma("sp", xt[:], xres[b], [DB("x", b)], [xtB], xtB)
            for oc in range(8):
                pp, ppB = psA()
                for k in range(KC):
                    mm(pp[:, 0:256], wo_t[:, k, oc * 128:(oc + 1) * 128], mg[:, k, :], [wo_B, mgB], ppB,
                       start=(k == 0), stop=(k == KC - 1))
                stt(xt[:, oc, :], pp[:, 0:256], modT[:, l, m, 16 + oc:17 + oc], xt[:, oc, :], ALU.mult, ALU.add,
                    ppB + [modB, xtB], [xtB])
            P.dma("sp", xres[b], xt[:], [xtB], [DB("x", b)], xtB)
            norm_block(xt, xtB, b, l, 1)

    FG = [6, 6, 5, 5]
    wf1_p = wf2_p = ac_p = sa_p = fo_p = None

    def alloc_ffn():
        nonlocal wf1_p, wf2_p, ac_p, sa_p, fo_p
        wf1_p = mkpool("wf1", [128, KC, 2, 768], BF16, 2)
        wf2_p = mkpool("wf2", [128, 6, 1024], BF16, 2)
        ac_p = mkpool("ffa", [128, 256], BF16, 8)
        sa_p = mkpool("ffs", [128, 256], F32, 3)
        fo_p = mkpool("fo", [128, KC, 256], F32, 1)


    def ffn_layer(l, last):
        hc0 = 0
        for gi, ng in enumerate(FG):
            w1, w1B = wf1_p.get()
            w2, w2B = wf2_p.get()
            for ab in range(2):
                P.dma("pool", w1[:, :, ab, 0:ng * 128],
                      w_f1[l, :, ab * FFN_H + hc0 * 128:ab * FFN_H + (hc0 + ng) * 128].rearrange(
                          "(k p) n -> p k n", p=128), [], [w1B], w1B)
            P.dma("pool", w2[:, 0:ng, :], w_f2[l, hc0 * 128:(hc0 + ng) * 128, :].rearrange("(c p) n -> p c n", p=128),
                  [], [w2B], w2B)
            for b in range(NBLK):
                m = SEQS[blk_seq(b)][3]
                c0 = hcol(b) + 1
                acts = []
                for hc in range(ng):
                    pa, paB = psA()
                    for k in range(KC):
                        mm(pa[:, 0:256], w1[:, k, 0, hc * 128:(hc + 1) * 128], hT[:, k, c0:c0 + 256], [w1B, hB[b]],
                           paB, start=(k == 0), stop=(k == KC - 1))
                    pb, pbB = psA()
                    for k in range(KC):
                        mm(pb[:, 0:256], w1[:, k, 1, hc * 128:(hc + 1) * 128], hT[:, k, c0:c0 + 256], [w1B, hB[b]],
                           pbB, start=(k == 0), stop=(k == KC - 1))
                    sa, saB = sa_p.get()
                    act(sa[:], pa[:, 0:256], AF.Silu, paB, [saB])
                    a, aB = ac_p.get()
                    tt("dve", a[:], pb[:, 0:256], sa[:], ALU.mult, pbB + [saB], [aB])
                    acts.append((a, aB))
                xt, xtB = xt_p.get()
                P.dma("sp", xt[:], xres[b], [DB("x", b)], [xtB], xtB)
                for oc in range(8):
                    pp, ppB = psA()
                    for hc in range(ng):
                        mm(pp[:, 0:256], w2[:, hc, oc * 128:(oc + 1) * 128], acts[hc][0][:], [w2B, acts[hc][1]], ppB,
                           start=(hc == 0), stop=(hc == ng - 1))
                    stt(xt[:, oc, :], pp[:, 0:256], modT[:, l, m, 40 + oc:41 + oc], xt[:, oc, :], ALU.mult, ALU.add,
                        ppB + [modB, xtB], [xtB])
                if gi < len(FG) - 1 or not last:
                    P.dma("sp", xres[b], xt[:], [xtB], [DB("x", b)], xtB)
                if gi == len(FG) - 1:
                    if last:
                        fo = fo_p.get()
                        norm_block(xt, xtB, b, l, 0, final=True, out_tile=(fo[0], fo[1]))
                        P.dma("sp", yT[:, :, b * 256:(b + 1) * 256], fo[0][:], [fo[1]], [DB("yT", b)], fo[1])
                    else:
                        norm_block(xt, xtB, b, l + 1, 0)
            hc0 += ng

    def phase(fn, *a):
        with contextlib.ExitStack() as pes:
            cur["es"] = pes
            fn(*a)
            P.flush()
        cur["es"] = es

    def ph_d1(l):
        alloc_slab()
        alloc_d1()
        d1_layer(l)

    def ph_ssd(l):
        alloc_scan()
        alloc_mix()
        alloc_ssd()
        ssd_layer(l, prep_t, prepB)

    def ph_att(l):
        alloc_att()
        att_layer(l)

    def ph_ml(l):
        alloc_mix(False)
        alloc_ml()
        ml_layer(l, prep_t, prepB)

    def ph_mrg_a(l):
        alloc_mrg()
        merge_a(l)

    def ph_mrg_b(l):
        alloc_norm()
        merge_b(l)

    def ph_ffn(l):
        alloc_norm()
        alloc_ffn()
        ffn_layer(l, l == NL - 1)

    nph = [0]

    def go(fn, *a):
        nph[0] += 1
        if stop_after is not None and nph[0] > stop_after:
            return
        phase(fn, *a)

    go(startup)
    for l in range(NL):
        P.dma("sp", prep_t[:], prep[l], [], [prepB], prepB)
        go(ph_d1, l)
        go(ph_ssd, l)
        go(ph_att, l)
        go(ph_ml, l)
        go(ph_mrg_a, l)
        go(ph_mrg_b, l)
        go(ph_ffn, l)
    P.flush()
    print("kernel build: ops=%d sems=%d" % (P.n_emitted, P.nsem))
    es.close()
    return nc


def _consts(SL):
    c = np.zeros((128, C_TOT), np.float32)
    k = np.arange(128)[:, None]
    i = np.arange(128)[None, :]
    c[:, C_ID:C_ID + 128] = (k == i)
    c[:, C_ONE:C_ONE + 128] = 1.0
    c[:, C_U:C_U + 128] = (k <= i)
    c[:, C_L:C_L + 128] = (k >= i)
    c[:, C_BLK:C_BLK + 128] = ((k // 64) == (i // 64))
    part = np.arange(128)
    d = part % 64
    partner = np.where((d % 32) < 16, part + 16, part - 16)
    c[:, C_SWP:C_SWP + 128] = (k == partner[None, :])
    c[:, C_NMF:C_NMF + 128] = np.where(i < k, NEG, 0.0)
    c[:, C_NMB:C_NMB + 128] = np.where(i > k, NEG, 0.0)
    cf = np.eye(128, dtype=np.float32)
    t = np.arange(SL)
    rows, cols = t // 64, t % 64
    f = d % 16
    freqs = (10000.0 ** (-(f.astype(np.float32)) / 16.0)).astype(np.float32)
    pos = np.where((d < 32)[:, None], rows[None, :], cols[None, :]).astype(np.float32)
    ang = pos * freqs[:, None]
    cosT = np.cos(ang).astype(np.float32)
    sgn = np.where((d % 32) < 16, -1.0, 1.0).astype(np.float32)
    sinT = (np.sin(ang) * sgn[:, None]).astype(np.float32)
    return c, cf, cosT, sinT


def _fm(v):
    v = np.asarray(v)
    n = v.shape[-1] // 128
    r = v.reshape(v.shape[:-1] + (n, 128))
    return np.ascontiguousarray(np.moveaxis(r, -1, 0))


def _prepare(inp, SB, NL):
    NBLK = SB + 2
    SL = 256 * SB
    f32 = np.float32
    g = lambda k: np.asarray(inp[k], dtype=f32)
    c, cf, cosT, sinT = _consts(SL)
    shared = {
        "w_ada": np.ascontiguousarray(g("w_ada")[:NL]), "w_in": np.ascontiguousarray(g("w_in")[:NL]),
        "w_branch": np.ascontiguousarray(g("w_branch")[:NL]), "w_out": np.ascontiguousarray(g("w_out")[:NL]),
        "w_ffn_in": np.ascontiguousarray(g("w_ffn_in")[:NL]), "w_ffn_out": np.ascontiguousarray(g("w_ffn_out")[:NL]),
        "cst": c, "cstf": cf, "ropec": cosT, "ropes": sinT,
    }
    prep = np.zeros((NL, 128, P_TOT), f32)
    pfm = np.zeros((128, NL, Q_TOT), f32)
    for l in range(NL):
        prep[l, :, P_DTB:P_DTB + 32] = g("ssd_dt_bias")[l].reshape(32)[None]
        prep[l, :, P_ALOG:P_ALOG + 32] = g("ssd_a_log")[l].reshape(32)[None]
        prep[l, :, P_SD:P_SD + 16] = g("ssd_d")[l][None]
        prep[l, :, P_SSDN:P_SSDN + 1024] = g("ssd_norm")[l][None]
        prep[l, :, P_MLN:P_MLN + 1024] = g("ml_norm")[l][None]
        prep[l, :, P_MGB:P_MGB + 32] = g("ml_gate_bias")[l].reshape(32)[None]
        pfm[:, l, Q_BADA:Q_BADA + 48] = _fm(g("b_ada")[l])
        pfm[:, l, Q_SCW:Q_SCW + 48] = np.moveaxis(_fm(g("ssd_conv_w")[l]), 1, 2).reshape(128, 48)
        pfm[:, l, Q_SCB:Q_SCB + 12] = _fm(g("ssd_conv_b")[l])
        pfm[:, l, Q_MCW:Q_MCW + 64] = np.moveaxis(_fm(g("ml_conv_w")[l]), 1, 2).reshape(128, 64)
        pfm[:, l, Q_MCB:Q_MCB + 16] = _fm(g("ml_conv_b")[l])
        pfm[:, l, Q_QG] = np.tile(g("att_q_norm")[l], 2)
        pfm[:, l, Q_KG] = np.tile(g("att_k_norm")[l], 2)
    shared["prep"] = prep
    shared["pfm"] = pfm
    shared["fng"] = _fm(g("final_norm"))
    xp, xs = g("x_prompt"), g("x_sample")
    ncore = xs.shape[0]
    maps = []
    for i in range(ncore):
        toks = np.concatenate([xs[i][:SL], xp[2 * i], xp[2 * i + 1]], axis=0)
        x0 = np.ascontiguousarray(toks.reshape(-1, 8, 128).transpose(2, 1, 0))
        cvec = np.stack([_fm(g("c_ctx")), _fm(g("c")[i])], axis=-1)
        st = g("state_ssd")[i][:NL]
        st = st.reshape(NL, 2, 2, 2, 4, 64, 64)
        ssd0 = np.ascontiguousarray(st.transpose(0, 1, 3, 6, 2, 4, 5)).reshape(NL, 2, 128, 512)
        mc = g("state_ml_c")[i][:NL]
        mn = g("state_ml_n")[i][:NL]
        mlc0 = np.concatenate([mc.transpose(0, 1, 3, 2, 4), mn.transpose(0, 1, 3, 2)[..., None]], axis=-1)
        m = dict(shared)
        m.update({
            "x0": x0, "cvec": np.ascontiguousarray(cvec),
            "cache_k": np.ascontiguousarray(g("cache_k")[i][:NL].reshape(NL, 512, 256)),
            "cache_v": np.ascontiguousarray(g("cache_v")[i][:NL].reshape(NL, 512, 256)),
            "ssd0": ssd0, "mlc0": np.ascontiguousarray(mlc0).reshape(NL, 2, 128, 8 * 129),
            "mlm0": np.ascontiguousarray(g("state_ml_m")[i][:NL].reshape(NL, 2, 8, 1)),
        })
        maps.append(m)
    return maps


def _assemble(results, SB, NL):
    SL = 256 * SB
    n = len(results)
    y_s = np.zeros((n, SL, D), np.float32)
    y_p = np.zeros((2 * n, 256, D), np.float32)
    nk = np.zeros((2 * n, NL, 256, 4, 64), np.float32)
    nv = np.zeros((2 * n, NL, 256, 4, 64), np.float32)
    nssd = np.zeros((2 * n, NL, 2, 16, 64, 64), np.float32)
    nc_ = np.zeros((2 * n, NL, 2, 8, 128, 128), np.float32)
    nn_ = np.zeros((2 * n, NL, 2, 8, 128), np.float32)
    nm = np.zeros((2 * n, NL, 2, 8), np.float32)
    for i, r in enumerate(results):
        y = np.asarray(r["yT"]).transpose(2, 1, 0).reshape(-1, D)
        y_s[i] = y[:SL]
        for j in range(2):
            y_p[2 * i + j] = y[SL + 256 * j:SL + 256 * (j + 1)]
            k = np.asarray(r["nk_o"])[j]
            nk[2 * i + j] = k.transpose(0, 3, 2, 1).reshape(NL, 256, 4, 64)
            nv[2 * i + j] = np.asarray(r["nv_o"])[j].reshape(NL, 256, 4, 64)
            s_ = np.asarray(r["nssd_o"])[j].reshape(NL, 2, 2, 64, 2, 4, 64)
            nssd[2 * i + j] = s_.transpose(0, 1, 4, 2, 5, 6, 3).reshape(NL, 2, 16, 64, 64)
            c_ = np.asarray(r["nmlc_o"])[j].reshape(NL, 2, 128, 8, 129)
            nc_[2 * i + j] = c_[..., :128].transpose(0, 1, 3, 2, 4)
            nn_[2 * i + j] = c_[..., 128].transpose(0, 1, 3, 2)
            nm[2 * i + j] = np.asarray(r["nmlm_o"])[j].reshape(NL, 2, 8)
    return (y_p, y_s, nk, nv, nssd, nc_, nn_, nm)


_CACHE = {}


def run(inputs, SB=8, NL=4, debug=False, stop_after=None):
    key = (SB, NL, debug, stop_after)
    if key not in _CACHE:
        _CACHE[key] = build(SB, NL, debug, stop_after)
    nc = _CACHE[key]
    maps = _prepare(inputs, SB, NL)
    res = run_bass_kernel_spmd(nc, maps, core_ids=list(range(len(maps))))
    return _assemble(res.results, SB, NL), res


def kernel(**inputs):
    out, _ = run(inputs, 8, 4, False)
    return out
```

```python
import contextlib
import os
import numpy as np
import concourse.bass as bass
import concourse.mybir as mybir
from concourse.bass_utils import run_bass_kernel_spmd

F32 = mybir.dt.float32
BF16 = mybir.dt.bfloat16
AF = mybir.ActivationFunctionType
ALU = mybir.AluOpType
AX = mybir.AxisListType

D = 1024
KC = 8
EPS = 1e-6
IN_DIM = 11328
FFN_H = 2816
O_SX, O_SZ, O_SB, O_SC, O_SDT = 0, 1024, 2048, 2304, 2560
O_AQ, O_AK, O_AV = 2592, 3616, 3872
O_MQ, O_MK, O_MV, O_MO, O_MG, O_G = 4128, 5152, 6176, 7200, 8224, 8256
P_DTB, P_ALOG, P_SD, P_SSDN, P_MLN, P_MGB, P_TOT = 0, 32, 64, 96, 1120, 2144, 2176
Q_BADA, Q_SCW, Q_SCB, Q_MCW, Q_MCB, Q_QG, Q_KG, Q_TOT = 0, 48, 96, 108, 172, 188, 189, 192
C_ID, C_ONE, C_U, C_L, C_BLK, C_SWP, C_NMF, C_NMB, C_TOT = 0, 128, 256, 384, 512, 640, 768, 896, 1024
NEG = -30000.0
MLK_SCALE = 128.0 ** -0.5


class Buf:
    __slots__ = ("name", "w", "r", "dsem", "dcnt", "last_dma", "excl", "dead")
    registry = []

    def __init__(self, name, excl=False):
        self.name = name
        self.excl = excl
        self.dead = False
        self.w = None
        self.r = []
        self.dsem = None
        self.dcnt = 0
        self.last_dma = None
        Buf.registry.append(self)


class Op:
    __slots__ = ("eng", "fn", "deps", "needs", "event", "isdma", "slot")


class Prog:
    def __init__(self, nc, es):
        self.nc = nc
        self.es = es
        self.ops = []
        self.eng = {"pe": nc.tensor, "act": nc.scalar, "dve": nc.vector, "pool": nc.gpsimd, "sp": nc.sync}
        self.esem = {k: es.enter_context(nc.semaphore("sem_" + k)) for k in self.eng}
        self.nsem = len(self.eng)
        self.dma_bufs = []
        self.free_sems = {"sp": [], "pool": []}
        self.phase_bufs = []
        self.phase_pools = []
        self.seq = {k: 0 for k in self.eng}
        self.waited = {k: {} for k in self.eng}
        self.n_emitted = 0

    def swap_buf(self, old, nb):
        self.phase_bufs.append(nb)
        for i, (b, q) in enumerate(self.dma_bufs):
            if b is old:
                self.dma_bufs[i] = (nb, q)

    def _deps(self, op, reads, writes):
        for b in reads:
            assert not b.dead, "live-range violation (read of recycled tile) " + b.name
        for b in writes:
            assert not b.dead, "live-range violation (write of recycled tile) " + b.name
        deps = {}
        wset = set(id(b) for b in writes)
        for b in reads:
            if b.w is not None:
                deps[id(b.w)] = (b.w, True)
            if b.excl:
                for r in b.r:
                    if r.eng != op.eng and id(r) not in deps:
                        deps[id(r)] = (r, False)
        for b in writes:
            if b.w is not None and id(b.w) not in deps:
                deps[id(b.w)] = (b.w, False)
            for r in b.r:
                if id(r) not in deps:
                    deps[id(r)] = (r, False)
        out = []
        for d, raw in deps.values():
            if d is op:
                continue
            if (not d.isdma) and (not op.isdma) and d.eng == op.eng and op.eng == "pe":
                continue
            d.needs = True
            out.append(d)
        for b in writes:
            b.w = op
            b.r = []
        for b in reads:
            if id(b) not in wset:
                b.r.append(op)
        return out

    def add(self, eng, fn, reads=(), writes=()):
        op = Op()
        op.eng, op.fn, op.needs, op.event, op.isdma, op.slot = eng, fn, False, None, False, None
        op.deps = self._deps(op, list(reads), list(writes))
        self.ops.append(op)
        return op

    def dma(self, q, out, in_, reads, writes, slot):
        nc = self.nc
        op = Op()
        op.eng, op.needs, op.isdma, op.slot = q, True, True, slot
        if slot.dsem is None:
            slot.dsem, slot.dcnt, slot.last_dma = {}, {}, {}
        if q not in slot.dsem:
            if self.free_sems[q]:
                slot.dsem[q], slot.dcnt[q] = self.free_sems[q].pop()
            else:
                slot.dsem[q] = self.es.enter_context(nc.semaphore("dsem_%s_%d" % (q, self.nsem)))
                slot.dcnt[q] = 0
                self.nsem += 1
                assert self.nsem <= 96, "too many semaphores"
            self.dma_bufs.append((slot, q))
        slot.dcnt[q] += 16
        op.event = (slot.dsem[q], slot.dcnt[q])
        e = self.eng[q]
        op.fn = lambda: e.dma_start(out=out, in_=in_)
        op.deps = self._deps(op, list(reads), list(writes))
        ld = slot.last_dma.get(q)
        if ld is not None and ld not in op.deps:
            op.deps.append(ld)
        slot.last_dma[q] = op
        self.ops.append(op)
        return op

    def flush(self):
        pend = set(id(o) for o in self.ops)
        for b in Buf.registry:
            if b.w is not None and id(b.w) in pend:
                b.w.needs = True
            for r in b.r:
                if id(r) in pend:
                    r.needs = True
        last = {}
        for o in self.ops:
            if not o.isdma:
                last[o.eng] = o
        for o in last.values():
            o.needs = True
        seq, waited = self.seq, self.waited
        for op in self.ops:
            e = self.eng[op.eng]
            wd = waited[op.eng]
            for d in op.deps:
                sem, val = d.event
                if wd.get(id(sem), 0) < val:
                    e.wait_ge(sem, val)
                    wd[id(sem)] = val
            ins = op.fn()
            if op.isdma:
                ins.then_inc(op.event[0], 16)
            elif op.needs:
                seq[op.eng] += 1
                ins.then_inc(self.esem[op.eng], 1)
                op.event = (self.esem[op.eng], seq[op.eng])
            op.fn = None
        self.n_emitted += len(self.ops)
        self.ops = []
        evs = [(self.esem[k], seq[k], k) for k in self.eng if seq[k] > 0]
        evs += [(b.dsem[q], b.dcnt[q], None) for b, q in self.dma_bufs]
        for k, e in self.eng.items():
            wd = waited[k]
            for sem, val, owner in evs:
                if owner == k and k != "pool":
                    continue
                if wd.get(id(sem), 0) < val:
                    e.wait_ge(sem, val)
                    wd[id(sem)] = val
        ph = set(id(b) for b in self.phase_bufs)
        for b in self.phase_bufs:
            if b.dsem is not None:
                for q in b.dsem:
                    self.free_sems[q].append((b.dsem[q], b.dcnt[q]))
                b.dsem = None
        self.dma_bufs = [(b, q) for b, q in self.dma_bufs if id(b) not in ph]
        Buf.registry = [b for b in Buf.registry if id(b) not in ph]
        self.phase_bufs = []
        for p_ in self.phase_pools:
            p_.dead = True
        self.phase_pools = []


class TPool:
    def __init__(self, P, nc, es, name, shape, dtype, n, phase=True):
        self.t = []
        for i in range(n):
            t = es.enter_context(nc.sbuf_tensor("%s_%d" % (name, i), list(shape), dtype))
            b = Buf("%s_%d" % (name, i))
            if phase:
                P.phase_bufs.append(b)
            self.t.append((t, b))
        self.i = 0
        self.dead = False
        self.name = name
        self.P = P
        self.phase = phase
        if phase:
            P.phase_pools.append(self)

    def get(self):
        assert not self.dead, "stale pool " + self.name
        k = self.i % len(self.t)
        t, old = self.t[k]
        if self.i >= len(self.t):
            nb = Buf(old.name)
            nb.w, nb.r, nb.dsem, nb.dcnt, nb.last_dma = old.w, old.r, old.dsem, old.dcnt, old.last_dma
            old.dead = True
            old.dsem = None
            self.P.swap_buf(old, nb)
            self.t[k] = (t, nb)
        self.i += 1
        return self.t[k]


def build(SB, NL, debug=False, stop_after=None):
    NBLK = SB + 2
    NCH = 2 * NBLK
    T = 256 * NBLK
    SL = 256 * SB
    NCTX = 512
    SEQS = [(0, SB, True, 1), (SB, 1, False, 0), (SB + 1, 1, False, 0)]
    hbase = [0, SL + 3, SL + 3 + 259]
    NCOL = SL + 3 + 259 * 2

    def blk_seq(b):
        return 0 if b < SB else (1 if b == SB else 2)

    def hcol(b):
        s = blk_seq(b)
        return hbase[s] + 256 * (b - SEQS[s][0])

    nc = bass.Bass("TRN2", target_bir_lowering=False)
    es = contextlib.ExitStack()
    P = Prog(nc, es)

    def din(name, shape, dt=F32):
        return nc.dram_tensor(name, list(shape), dt, kind="ExternalInput").ap()

    def dout(name, shape, dt=F32):
        return nc.dram_tensor(name, list(shape), dt, kind="ExternalOutput").ap()

    def dscr(name, shape, dt):
        return nc.dram_tensor(name, list(shape), dt, kind=("ExternalOutput" if debug else "Internal")).ap()

    x0 = din("x0", [128, KC, T])
    cvec = din("cvec", [128, KC, 2])
    w_ada = din("w_ada", [NL, D, 6 * D])
    w_in = din("w_in", [NL, D, IN_DIM])
    w_br = din("w_branch", [NL, 3, D, D])
    w_out = din("w_out", [NL, D, D])
    w_f1 = din("w_ffn_in", [NL, D, 2 * FFN_H])
    w_f2 = din("w_ffn_out", [NL, FFN_H, D])
    prep = din("prep", [NL, 128, P_TOT])
    pfm = din("pfm", [128, NL, Q_TOT])
    fng = din("fng", [128, KC])
    cst = din("cst", [128, C_TOT])
    cstf = din("cstf", [128, 128])
    ropec = din("ropec", [128, SL])
    ropes = din("ropes", [128, SL])
    cache_k = din("cache_k", [NL, NCTX, 256])
    cache_v = din("cache_v", [NL, NCTX, 256])
    ssd0 = din("ssd0", [NL, 2, 128, 512])
    mlc0 = din("mlc0", [NL, 2, 128, 8 * 129])
    mlm0 = din("mlm0", [NL, 2, 8, 1])

    yT = dout("yT", [128, KC, T])
    nk_o = dout("nk_o", [2, NL, 128, 2, 256])
    nv_o = dout("nv_o", [2, NL, 256, 256])
    nssd_o = dout("nssd_o", [2, NL, 2, 128, 512])
    nmlc_o = dout("nmlc_o", [2, NL, 2, 128, 8 * 129])
    nmlm_o = dout("nmlm_o", [2, NL, 2, 8, 1])

    xres = dscr("xres", [NBLK, 128, KC, 256], F32)
    sx_tok = dscr("sx_tok", [NCH, 128, 1024], BF16)
    sbcT = dscr("sbcT", [NBLK, 128, 4, 256], BF16)
    sb_tok = dscr("sb_tok", [NCH, 128, 256], BF16)
    sz_tok = dscr("sz_tok", [NCH, 128, 1024], BF16)
    dtg_tok = dscr("dtg_tok", [NCH, 128, 64], F32)
    qT_d = dscr("qT_d", [NBLK, 128, 8, 256], BF16)
    kT_d = dscr("kT_d", [NBLK, 128, 4, 256], BF16)
    v_tok = dscr("v_tok", [NCH, 128, 256], BF16)
    mqT = dscr("mqT", [NBLK, 128, 8, 256], BF16)
    mkT = dscr("mkT", [NBLK, 128, 8, 256], BF16)
    mk_tok = dscr("mk_tok", [NCH, 128, 1024], BF16)
    mv_tok = dscr("mv_tok", [NCH, 128, 1024], BF16)
    mo_tok = dscr("mo_tok", [NCH, 128, 1024], BF16)
    gT = dscr("gT", [3, NBLK, 128, 8, 256], BF16)
    ybr = dscr("ybr", [3, NBLK, 128, 8, 256], BF16)
    yf_d = dscr("yf_d", [NCH, 128, 1024], F32)
    hf_d = dscr("hf_d", [NCH, 128, 1024], F32)
    mgd_d = dscr("mgd_d", [NBLK, 128, 8, 256], BF16)

    dbufs = {}

    def DB(*key):
        if key not in dbufs:
            dbufs[key] = Buf(str(key))
        return dbufs[key]

    cur = {"es": es, "n": 0}

    def sb(name, shape, dt=F32):
        ph = cur["es"] is not es
        cur["n"] += 1
        b = Buf(name)
        if ph:
            P.phase_bufs.append(b)
        t = cur["es"].enter_context(nc.sbuf_tensor("%s_%d" % (name, cur["n"]), list(shape), dt))
        return t, b

    def mkpool(name, shape, dt, n):
        cur["n"] += 1
        return TPool(P, nc, cur["es"], "%s%d" % (name, cur["n"]), shape, dt, n, phase=(cur["es"] is not es))


    hT, _ = sb("hT", [128, KC, NCOL], BF16)
    hB = [Buf("hT%d" % b) for b in range(NBLK)]
    hHalo = Buf("hHalo")
    cb, cbB = sb("cb", [128, C_TOT], BF16)
    cf, cfB = sb("cf", [128, 128], F32)
    onesf, onesfB = sb("onesf", [128, 128], F32)
    modT, modB = sb("modT", [128, NL, 2, 48], F32)
    pfm_s, pfmB = sb("pfm_s", [128, NL, Q_TOT], F32)
    fng_s, fngB = sb("fng_s", [128, KC], F32)
    ropec_s = ropecB = ropes_s = ropesB = None
    prep_t, prepB = sb("prep_t", [128, P_TOT], F32)

    ps_all = es.enter_context(nc.psum_tensor("ps_all", [128, 4096], F32))
    psB = [Buf("psb%d" % i, excl=True) for i in range(8)]
    ps_state = {"A": 0, "B": 0}

    def psum_at(i, n=1):
        return ps_all[:, i * 512:(i + n) * 512], psB[i:i + n]

    def psA():
        i = ps_state["A"] % 6
        ps_state["A"] += 1
        return psum_at(i)

    def psBr():
        i = 6 + ps_state["B"] % 2
        ps_state["B"] += 1
        return psum_at(i)

    ident = cb[:, C_ID:C_ID + 128]
    ones = cb[:, C_ONE:C_ONE + 128]
    Utri = cb[:, C_U:C_U + 128]
    Ltri = cb[:, C_L:C_L + 128]
    blk2 = cb[:, C_BLK:C_BLK + 128]
    pswap = cb[:, C_SWP:C_SWP + 128]
    nmask = [cb[:, C_NMF:C_NMF + 128], cb[:, C_NMB:C_NMB + 128]]
    tri = [Utri, Ltri]

    V, A, G, PE = nc.vector, nc.scalar, nc.gpsimd, nc.tensor

    def mm(out, lhsT, rhs, reads, writes, start=True, stop=True):
        P.add("pe", lambda: PE.matmul(out, lhsT=lhsT, rhs=rhs, start=start, stop=stop), reads, writes)

    def tp(out, in_, idn, reads, writes):
        P.add("pe", lambda: PE.transpose(out, in_, idn), reads, writes)

    def act(out, in_, func, reads, writes, bias=None, scale=None, accum=None):
        kw = {}
        if bias is not None:
            kw["bias"] = bias
        if scale is not None:
            kw["scale"] = scale
        if accum is not None:
            kw["accum_out"] = accum
        P.add("act", lambda: A.activation(out=out, in_=in_, func=func, **kw), reads, writes)

    def tt(eng, out, in0, in1, op, reads, writes):
        e = V if eng == "dve" else G
        P.add(eng, lambda: e.tensor_tensor(out=out, in0=in0, in1=in1, op=op), reads, writes)

    def ts(eng, out, in0, s1, op0, reads, writes, s2=None, op1=None):
        e = V if eng == "dve" else G
        if op1 is None:
            P.add(eng, lambda: e.tensor_scalar(out=out, in0=in0, scalar1=s1, scalar2=None, op0=op0), reads, writes)
        else:
            P.add(eng, lambda: e.tensor_scalar(out=out, in0=in0, scalar1=s1, scalar2=s2, op0=op0, op1=op1),
                  reads, writes)

    def stt(out, in0, scalar, in1, op0, op1, reads, writes):
        P.add("dve", lambda: V.scalar_tensor_tensor(out=out, in0=in0, scalar=scalar, in1=in1, op0=op0, op1=op1),
              reads, writes)

    def cp(eng, out, in_, reads, writes):
        if eng == "act":
            P.add("act", lambda: A.copy(out=out, in_=in_), reads, writes)
        else:
            e = V if eng == "dve" else G
            P.add(eng, lambda: e.tensor_copy(out=out, in_=in_), reads, writes)

    def memset(eng, ap, val, writes):
        e = V if eng == "dve" else G
        P.add(eng, lambda: e.memset(ap, val), (), writes)

    def rsqrt(v_ap, v_buf, shape2, mean_scale):
        p, n = shape2
        ts("dve", v_ap, v_ap, mean_scale, ALU.mult, [v_buf], [v_buf], s2=EPS, op1=ALU.add)
        act(v_ap, v_ap, AF.Ln, [v_buf], [v_buf])
        act(v_ap, v_ap, AF.Exp, [v_buf], [v_buf], scale=-0.5)

    def flat(ap3):
        return ap3.rearrange("p a b -> p (a b)")

    slab_p = None

    def alloc_slab(n=3):
        nonlocal slab_p
        slab_p = mkpool("slab", [128, KC, 1024], BF16, n)

    def load_slab(src2d, ncols, dst_off=0, slab=None):
        if slab is None:
            slab = slab_p.get()
        t, bfr = slab
        P.dma("pool", t[:, :, dst_off:dst_off + ncols], src2d.rearrange("(k p) n -> p k n", p=128), [], [bfr], bfr)
        return slab

    def startup():
        cst_st, cst_stB = sb("cst_st", [128, C_TOT], F32)
        P.dma("sp", cst_st[:], cst[:, :], [], [cst_stB], cst_stB)
        cp("dve", cb[:], cst_st[:], [cst_stB], [cbB])
        P.dma("sp", cf[:], cstf[:, :], [], [cfB], cfB)
        P.dma("sp", pfm_s[:], pfm[:, :, :], [], [pfmB], pfmB)
        P.dma("sp", fng_s[:], fng[:, :], [], [fngB], fngB)
        memset("dve", onesf[:], 1.0, [onesfB])
        memset("pool", flat(hT[:]), 0.0, hB + [hHalo])
        alloc_slab()
        cv, cvB = sb("cv", [128, KC, 2], F32)
        cvb, cvbB = sb("cvb", [128, KC, 2], BF16)
        P.dma("sp", cv[:], cvec[:, :, :], [], [cvB], cvB)
        act(flat(cvb[:]), flat(cv[:]), AF.Silu, [cvB], [cvbB])
        for l in range(NL):
            pm, pmB = psA()
            for s6 in range(6):
                wt, wB = load_slab(w_ada[l, :, s6 * 1024:(s6 + 1) * 1024], 1024)
                for f in range(8):
                    fc = s6 * 8 + f
                    for k in range(KC):
                        mm(pm[:, fc * 2:fc * 2 + 2], wt[:, k, f * 128:(f + 1) * 128], cvb[:, k, :],
                           [wB, cvbB], pmB, start=(k == 0), stop=(k == KC - 1))
            for m in range(2):
                tt("dve", modT[:, l, m, :], pm.rearrange("p (f m) -> p m f", m=2)[:, m, 0:48],
                   pfm_s[:, l, Q_BADA:Q_BADA + 48], ALU.add, pmB + [pfmB], [modB])
            for o in (8, 32):
                ts("dve", modT[:, l, :, o:o + 8], modT[:, l, :, o:o + 8], 1.0, ALU.add, [modB], [modB])
        alloc_norm()
        for b in range(NBLK):
            xt, xtB = xt_p.get()
            P.dma("sp", xt[:], x0[:, :, b * 256:(b + 1) * 256], [], [xtB], xtB)
            P.dma("sp", xres[b], xt[:], [xtB], [DB("x", b)], xtB)
            norm_block(xt, xtB, b, 0, 0)

    xt_p = sq_p = xn_p = rs_p = None

    def alloc_norm():
        nonlocal xt_p, sq_p, xn_p, rs_p
        xt_p = mkpool("xt", [128, KC, 256], F32, 2)
        sq_p = mkpool("sq", [128, KC, 256], BF16, 2)
        xn_p = mkpool("xn", [128, KC, 256], F32, 2)
        rs_p = mkpool("rs", [128, 256], F32, 2)


    def norm_block(xt, xtB, b, l, which, final=False, out_tile=None):
        m = SEQS[blk_seq(b)][3]
        sq, sqB = sq_p.get()
        act(flat(sq[:]), flat(xt[:]), AF.Square, [xtB], [sqB])
        pn, pnB = psBr()
        for k in range(KC):
            mm(pn[:, 0:256], ones, sq[:, k, :], [cbB, sqB], pnB, start=(k == 0), stop=(k == KC - 1))
        rs, rsB = rs_p.get()
        cp("dve", rs[:], pn[:, 0:256], pnB, [rsB])
        rsqrt(rs[:], rsB, (128, 256), 1.0 / D)
        xn, xnB = xn_p.get()
        tt("dve", xn[:], xt[:], rs[:].unsqueeze(1).broadcast_to([128, KC, 256]), ALU.mult, [xtB, rsB], [xnB])
        if final:
            ot, otB = out_tile
            for k in range(KC):
                ts("pool", ot[:, k, :], xn[:, k, :], fng_s[:, k:k + 1], ALU.mult, [xnB, fngB], [otB])
            return
        c0 = hcol(b) + 1
        so, sh = (8, 0) if which == 0 else (32, 24)
        for k in range(KC):
            ts("pool", hT[:, k, c0:c0 + 256], xn[:, k, :], modT[:, l, m, so + k:so + k + 1], ALU.mult,
               [xnB, modB], [hB[b]], s2=modT[:, l, m, sh + k:sh + k + 1], op1=ALU.add)

    def hwin_bufs(b):
        s = blk_seq(b)
        f, n = SEQS[s][0], SEQS[s][1]
        r = [hB[b], hHalo]
        if b > f:
            r.append(hB[b - 1])
        if b < f + n - 1:
            r.append(hB[b + 1])
        return r

    st_p = acc_p = cvo_p = sm_p = cv8_p = qr_p = sq4_p = dg_p = u_p = None

    def alloc_d1():
        nonlocal st_p, acc_p, cvo_p, sm_p, cv8_p, ropec_s, ropecB, ropes_s, ropesB, qr_p, sq4_p, dg_p, u_p
        dg_p = mkpool("dg", [128, 4, 128], BF16, 16)
        u_p = mkpool("ub", [128, 260], BF16, 4)
        dg_live.clear()
        qr_p = mkpool("qr", [128, 4, 256], F32, 6)
        sq4_p = mkpool("sq4", [128, 4, 256], BF16, 4)
        ropec_s, ropecB = sb("ropec_s", [128, SL], F32)
        ropes_s, ropesB = sb("ropes_s", [128, SL], F32)
        P.dma("sp", ropec_s[:], ropec[:, :], [], [ropecB], ropecB)
        P.dma("sp", ropes_s[:], ropes[:, :], [], [ropesB], ropesB)
        st_p = mkpool("st", [128, 2048], BF16, 6)
        acc_p = None
        cvo_p = mkpool("cvo", [128, 256], BF16, 4)
        sm_p = mkpool("sm", [128, 256], F32, 4)
        cv8_p = None


    def proj_fm(wt, wB, wcol, b, n, halo):
        pp, ppB = psA()
        c0 = hcol(b) + (0 if halo else 1)
        rb = hwin_bufs(b) if halo else [hB[b]]
        for k in range(KC):
            mm(pp[:, 0:n], wt[:, k, wcol:wcol + 128], hT[:, k, c0:c0 + n], [wB] + rb, ppB,
               start=(k == 0), stop=(k == KC - 1))
        return pp, ppB

    def proj_tm(wt, wB, wcol, ncol, c):
        b, cc = c // 2, c % 2
        pp, ppB = psA()
        c0 = hcol(b) + 1 + 128 * cc
        for k in range(KC):
            mm(pp[:, 0:ncol], hT[:, k, c0:c0 + 128], wt[:, k, wcol:wcol + ncol], [wB, hB[b]], ppB,
               start=(k == 0), stop=(k == KC - 1))
        return pp, ppB

    dg_live = {}

    def conv_block(wt, wB, b, nfc, l, wofs, bofs, fcbase, stv, stB):
        def conv_b(ub, ubB, fci, dst):
            key = (l, wofs, fci)
            if key not in dg_live:
                dg, dgB = dg_p.get()
                for t in range(4):
                    ts("dve", dg[:, t, :], ident, pfm_s[:, l, wofs + fci * 4 + t:wofs + fci * 4 + t + 1], ALU.mult,
                       [cbB, pfmB], [dgB])
                dg_live[key] = (dg, dgB)
            dg, dgB = dg_live[key]
            pc, pcB = psBr()
            for t in range(4):
                mm(pc[:, 0:256], dg[:, t, :], ub[:, t:t + 256], [dgB, ubB], pcB, start=(t == 0), stop=(t == 3))
            act(dst, pc[:, 0:256], AF.Silu, pcB, [stB], bias=pfm_s[:, l, bofs + fci:bofs + fci + 1])

        pend = None
        for fc in range(nfc):
            pp, ppB = proj_fm(wt, wB, fc * 128, b, 259, True)
            ub, ubB = u_p.get()
            cp("act", ub[:, 0:259], pp[:, 0:259], ppB, [ubB])
            if pend is not None:
                conv_b(*pend)
            pend = (ub, ubB, fcbase + fc, stv[:, fc, :])
        conv_b(*pend)

    def transposes_to(dst, dstB, srcs, reads, scale=None, bank=None):
        n = len(srcs)
        pt, ptB = (psBr() if bank is None else psum_at(bank))
        ptb = pt.bitcast(BF16)
        for i, s_ in enumerate(srcs):
            tp(ptb[:, i * 128:(i + 1) * 128], s_, ident, reads + [cbB], ptB)
        if scale is None:
            cp("act", dst, ptb[:, 0:128 * n], ptB, [dstB])
        else:
            ts("dve", dst, ptb[:, 0:128 * n], scale, ALU.mult, ptB, [dstB])

    D1STOP = int(os.environ.get("K_D1STOP", "99"))

    def d1_layer(l):
        W = w_in[l]

        def ld_simple(off, n):
            return lambda: load_slab(W[:, off:off + n], n)

        def ld_dtg():
            slab = slab_p.get()
            load_slab(W[:, O_SDT:O_SDT + 32], 32, 0, slab)
            return load_slab(W[:, O_MG:O_MG + 32], 32, 32, slab)

        def ld_ak():
            slab = slab_p.get()
            load_slab(W[:, O_AK:O_AK + 256], 256, 0, slab)
            r = None
            for f in range(2):
                load_slab(W[:, O_AK + f * 128 + 64:O_AK + f * 128 + 128], 64, 256 + f * 128, slab)
                r = load_slab(W[:, O_AK + f * 128:O_AK + f * 128 + 64], 64, 256 + f * 128 + 64, slab)
            return r

        loaders = [ld_simple(O_SX, 1024), ld_simple(O_SB, 512), ld_simple(O_SZ, 1024), ld_dtg,
                   ld_simple(O_AQ, 1024), ld_ak, ld_simple(O_AV, 256), ld_simple(O_MQ, 1024),
                   ld_simple(O_MK, 1024), ld_simple(O_MV, 1024), ld_simple(O_MO, 1024),
                   ld_simple(O_G, 1024), ld_simple(O_G + 1024, 1024), ld_simple(O_G + 2048, 1024)]
        loaded = {}

        def get_w(i):
            for k in range(i + 2):
                if k < len(loaders) and k not in loaded:
                    loaded[k] = loaders[k]()
            return loaded[i]

        wt, wB = get_w(0)
        for b in range(NBLK):
            sv, svB = st_p.get()
            svv = sv[:, 0:2048].rearrange("p (f t) -> p f t", f=8)
            conv_block(wt, wB, b, 8, l, Q_SCW, Q_SCB, 0, svv, svB)
            for cc in range(2):
                st, stB = st_p.get()
                transposes_to(st[:, 0:1024], stB, [svv[:, f, cc * 128:(cc + 1) * 128] for f in range(8)], [svB])
                P.dma("sp", sx_tok[2 * b + cc], st[:, 0:1024], [stB], [DB("sx", 2 * b + cc)], stB)
        if D1STOP <= 1:
            return
        wt, wB = get_w(1)
        for b in range(NBLK):
            st, stB = st_p.get()
            stv = st[:, 0:1024].rearrange("p (f t) -> p f t", f=4)
            conv_block(wt, wB, b, 4, l, Q_SCW, Q_SCB, 8, stv, stB)
            P.dma("sp", sbcT[b], stv, [stB], [DB("sbcT", b)], stB)
            for cc in range(2):
                s2, s2B = st_p.get()
                transposes_to(s2[:, 0:256], s2B, [stv[:, f, cc * 128:(cc + 1) * 128] for f in range(2)], [stB])
                P.dma("sp", sb_tok[2 * b + cc], s2[:, 0:256], [s2B], [DB("sbt", 2 * b + cc)], s2B)
        if D1STOP <= 2:
            return
        wt, wB = get_w(2)
        for c in range(NCH):
            st, stB = st_p.get()
            for hh in range(2):
                pp, ppB = proj_tm(wt, wB, hh * 512, 512, c)
                act(st[:, hh * 512:(hh + 1) * 512], pp[:, 0:512], AF.Silu, ppB, [stB])
            P.dma("sp", sz_tok[c], st[:, 0:1024], [stB], [DB("sz", c)], stB)
        if D1STOP <= 3:
            return
        wt, wB = get_w(3)
        for c in range(NCH):
            sm, smB = sm_p.get()
            pp, ppB = proj_tm(wt, wB, 0, 64, c)
            cp("dve", sm[:, 0:64], pp[:, 0:64], ppB, [smB])
            P.dma("sp", dtg_tok[c], sm[:, 0:64], [smB], [DB("dtg", c)], smB)
        if D1STOP <= 4:
            return
        def qk_s1(wt, wB, f0, b, gofs, stv, stB, nk_out):
            n = 4
            qr, qrB = qr_p.get()
            sq, sqB = sq4_p.get()
            rs, rsB = qr_p.get()
            for i in range(n):
                pp, ppB = proj_fm(wt, wB, (f0 + i) * 128, b, 256, False)
                act(sq[:, i, :], pp[:, 0:256], AF.Square, ppB, [sqB])
                cp("dve", qr[:, i, :], pp[:, 0:256], ppB, [qrB])
            for i in range(n):
                pn, pnB = psBr()
                mm(pn[:, 0:256], blk2, sq[:, i, :], [cbB, sqB], pnB)
                ts("dve", rs[:, i, :], pn[:, 0:256], 1.0 / 64, ALU.mult, pnB, [rsB], s2=EPS, op1=ALU.add)
            return qr, qrB, rs, rsB

        def qk_s2(st1, wt, wB, f0, b, gofs, stv, stB, nk_out):
            n = 4
            qr, qrB, rs, rsB = st1
            act(flat(rs[:]), flat(rs[:]), AF.Ln, [rsB], [rsB])
            act(flat(rs[:]), flat(rs[:]), AF.Exp, [rsB], [rsB], scale=-0.5)
            stt(qr[:], qr[:], pfm_s[:, l, gofs:gofs + 1], rs[:], ALU.mult, ALU.mult, [qrB, pfmB, rsB], [qrB])
            s_ = blk_seq(b)
            if nk_out and s_ != 0:
                P.dma("sp", nk_o[s_ - 1, l], qr[:, 0:2, :], [qrB], [DB("nk", s_, l)], qrB)
            if s_ != 0:
                cp("act", stv[:, f0:f0 + n, :], qr[:], [qrB], [stB])
                return
            qb, qbB = sq4_p.get()
            cp("act", flat(qb[:]), flat(qr[:]), [qrB], [qbB])
            a2, a2B = qr_p.get()
            t0 = 256 * b
            for i in range(n):
                pw, pwB = psBr()
                mm(pw[:, 0:256], pswap, qb[:, i, :], [cbB, qbB], pwB)
                tt("dve", a2[:, i, :], pw[:, 0:256], ropes_s[:, t0:t0 + 256], ALU.mult, pwB + [ropesB], [a2B])
            tt("dve", qr[:], qr[:], ropec_s[:, t0:t0 + 256].unsqueeze(1).broadcast_to([128, n, 256]), ALU.mult,
               [qrB, ropecB], [qrB])
            tt("pool", stv[:, f0:f0 + n, :], qr[:], a2[:], ALU.add, [qrB, a2B], [stB])

        def qk_job(wt, wB, nfc, gofs, dst_d, key, nk_out):
            groups = [(b, f0) for b in range(NBLK) for f0 in range(0, nfc, 4)]
            stt_ = {}

            def args(b, f0):
                if b not in stt_:
                    st, stB = st_p.get()
                    stt_[b] = (st[:, 0:nfc * 256].rearrange("p (f t) -> p f t", f=nfc), stB)
                stv, stB = stt_[b]
                return (wt, wB, f0, b, gofs, stv, stB, nk_out)

            def finish(b, f0, a, st1):
                qk_s2(st1, *a)
                if f0 + 4 >= nfc:
                    P.dma("sp", dst_d[b], a[5], [a[6]], [DB(key, b)], a[6])

            pend = None
            for (b, f0) in groups:
                a = args(b, f0)
                st1 = qk_s1(*a)
                if pend is not None:
                    finish(*pend)
                pend = (b, f0, a, st1)
            finish(*pend)

        wt, wB = get_w(4)
        qk_job(wt, wB, 8, Q_QG, qT_d, "qT", False)
        if D1STOP <= 5:
            return
        wt, wB = get_w(5)
        qk_job(wt, wB, 4, Q_KG, kT_d, "kT", True)
        if D1STOP <= 6:
            return
        wt, wB = get_w(6)
        for c in range(NCH):
            st, stB = st_p.get()
            pp, ppB = proj_tm(wt, wB, 0, 256, c)
            cp("act", st[:, 0:256], pp[:, 0:256], ppB, [stB])
            P.dma("sp", v_tok[c], st[:, 0:256], [stB], [DB("v", c)], stB)
            s_ = blk_seq(c // 2)
            if s_ != 0:
                sm, smB = sm_p.get()
                cp("dve", sm[:], pp[:, 0:256], ppB, [smB])
                t0 = (c % 2) * 128
                P.dma("sp", nv_o[s_ - 1, l, t0:t0 + 128, :], sm[:], [smB], [DB("nv", s_, l, c)], smB)
        if D1STOP <= 7:
            return
        for which, off, dstT in ((0, O_MQ, mqT), (1, O_MK, mkT)):
            wt, wB = get_w(7 + which)
            for b in range(NBLK):
                st, stB = st_p.get()
                stv = st[:, 0:2048].rearrange("p (f t) -> p f t", f=8)
                conv_block(wt, wB, b, 8, l, Q_MCW, Q_MCB, which * 8, stv, stB)
                P.dma("sp", dstT[b], stv, [stB], [DB("mqT" if which == 0 else "mkT", b)], stB)
                if which == 1:
                    for cc in range(2):
                        s2, s2B = st_p.get()
                        transposes_to(s2[:, 0:1024], s2B, [stv[:, f, cc * 128:(cc + 1) * 128] for f in range(8)],
                                      [stB], scale=MLK_SCALE)
                        P.dma("sp", mk_tok[2 * b + cc], s2[:, 0:1024], [s2B], [DB("mkt", 2 * b + cc)], s2B)
        if D1STOP <= 8:
            return
        for wi_, (off, dstT, key, fn) in enumerate(((O_MV, mv_tok, "mv", None), (O_MO, mo_tok, "mo", AF.Sigmoid))):
            wt, wB = get_w(9 + wi_)
            for c in range(NCH):
                st, stB = st_p.get()
                for hh in range(2):
                    pp, ppB = proj_tm(wt, wB, hh * 512, 512, c)
                    if fn is None:
                        cp("act", st[:, hh * 512:(hh + 1) * 512], pp[:, 0:512], ppB, [stB])
                    else:
                        act(st[:, hh * 512:(hh + 1) * 512], pp[:, 0:512], fn, ppB, [stB])
                P.dma("sp", dstT[c], st[:, 0:1024], [stB], [DB(key, c)], stB)
        if D1STOP <= 9:
            return
        for n in range(3):
            wt, wB = get_w(11 + n)
            for b in range(NBLK):
                st, stB = st_p.get()
                stv = st[:, 0:2048].rearrange("p (f t) -> p f t", f=8)
                for fc in range(8):
                    pp, ppB = proj_fm(wt, wB, fc * 128, b, 256, False)
                    act(stv[:, fc, :], pp[:, 0:256], AF.Sigmoid, ppB, [stB])
                P.dma("sp", gT[n, b], stv, [stB], [DB("gT", n, b)], stB)

    nsb = sb

    dtg_s = dtgB = g_dt = g_dtB = g_da = g_daB = g_dab = g_dabB = g_a = g_aB = g_acum = g_acumB = g_tot = g_totB = g_ea = g_eaB = g_w = g_wB = g_edec = g_edecB = g_tmp = g_tmpB = None

    def alloc_scan():
        nonlocal dtg_s, dtgB, g_dt, g_dtB, g_da, g_daB, g_dab, g_dabB, g_a, g_aB, g_acum, g_acumB, g_tot, g_totB, g_ea, g_eaB, g_w, g_wB, g_edec, g_edecB, g_tmp, g_tmpB
        dtg_s, dtgB = nsb("dtg_s", [128, NCH, 64])
        g_dt, g_dtB = nsb("g_dt", [128, NCH, 32])
        g_da, g_daB = nsb("g_da", [128, NCH, 32])
        g_dab, g_dabB = nsb("g_dab", [128, NCH, 32], BF16)
        g_a, g_aB = nsb("g_a", [128, 32])
        g_acum, g_acumB = nsb("g_acum", [128, 2, NCH, 16])
        g_tot, g_totB = nsb("g_tot", [128, 2, NCH, 16])
        g_ea, g_eaB = nsb("g_ea", [128, 2, NCH, 16])
        g_w, g_wB = nsb("g_w", [128, 2, NCH, 16])
        g_edec, g_edecB = nsb("g_edec", [128, 2, NCH, 8])
        g_tmp, g_tmpB = nsb("g_tmp", [128, 2, NCH, 16])


    def run_sweeps(chunk_loads, chunkA, chunkB, seq_begin, seq_end, PF=2):
        steps = []
        for s in range(3):
            n = SEQS[s][1]
            orders = [chunk_order(s, 0), chunk_order(s, 1)]
            for i in range(2 * n):
                for d in range(2):
                    steps.append((s, d, orders[d][i], i >= n))
        loaded, fronts = {}, {}
        for k in range(min(PF, len(steps))):
            loaded[k] = chunk_loads(*steps[k])
        fronts[0] = chunkA(*steps[0], loaded[0])
        for k, st in enumerate(steps):
            if k + PF < len(steps):
                loaded[k + PF] = chunk_loads(*steps[k + PF])
            if k + 1 < len(steps):
                fronts[k + 1] = chunkA(*steps[k + 1], loaded[k + 1])
            if k == 0 or steps[k - 1][0] != st[0]:
                seq_begin(st[0])
            chunkB(*st, loaded.pop(k), fronts.pop(k))
            if k == len(steps) - 1 or steps[k + 1][0] != st[0]:
                seq_end(st[0])

    def chunk_order(s, d):
        f, n = SEQS[s][0], SEQS[s][1]
        cs = list(range(2 * f, 2 * (f + n)))
        return cs if d == 0 else cs[::-1]

    xk_p = xk2_p = bt_p = bct_p = big_p = arg_p = yo_p = tok_p = ytT_p = s1_p = None

    def alloc_mix(ssd=True):
        nonlocal xk_p, xk2_p, bt_p, bct_p, big_p, arg_p, yo_p, tok_p, ytT_p, s1_p, cvo_p
        cvo_p = mkpool("cvo", [128, 256], BF16, 6)
        xk_p = mkpool("xk", [128, 1024], BF16, 5 if ssd else 8)
        xk2_p = mkpool("xk2", [128, 1024], BF16, 6)
        if ssd:
            bt_p = mkpool("bt", [128, 256], BF16, 5)
            bct_p = mkpool("bct", [128, 4, 128], BF16, 5)
            big_p = mkpool("big", [128, 2048], BF16, 6)
            arg_p = mkpool("arg", [128, 2048], F32, 2)
        yo_p = mkpool("yo", [128, 1024], F32, 6 if ssd else 5)
        tok_p = mkpool("tok", [128, 1024], BF16, 6)
        ytT_p = mkpool("ytT", [128, 8, 128], BF16, 2)
        s1_p = mkpool("s1", [128, 8], F32, 10)


    def out_transposed(yn, ynB, br, c, bank=7):
        b, cc = c // 2, c % 2
        yt, ytB = ytT_p.get()
        transposes_to(flat(yt[:]), ytB, [yn[:, f * 128:(f + 1) * 128] for f in range(8)], [ynB], bank=bank)
        P.dma("sp", ybr[br, b, :, :, cc * 128:(cc + 1) * 128], yt[:], [ytB], [DB("ybr", br, b)], ytB)

    Hs = Hb = None

    def alloc_ssd():
        nonlocal Hs, Hb
        Hs = [nsb("Hs%d" % d, [128, 2, 256]) for d in range(2)]
        Hb = [nsb("Hb%d" % d, [128, 2, 256], BF16) for d in range(2)]


    def ssd_layer(l, pr, prB):
        P.dma("sp", dtg_s[:], dtg_tok.rearrange("c p n -> p c n"), [DB("dtg", c) for c in range(NCH)],
              [dtgB], dtgB)
        tt("dve", g_dt[:], dtg_s[:, :, 0:32], pr[:, P_DTB:P_DTB + 32].unsqueeze(1).broadcast_to([128, NCH, 32]),
           ALU.add, [dtgB, prB], [g_dtB])
        act(flat(g_dt[:]), flat(g_dt[:]), AF.Exp, [g_dtB], [g_dtB])
        act(flat(g_dt[:]), flat(g_dt[:]), AF.Ln, [g_dtB], [g_dtB], bias=1.0)
        act(g_a[:], pr[:, P_ALOG:P_ALOG + 32], AF.Exp, [prB], [g_aB])
        ts("dve", g_a[:], g_a[:], -1.0, ALU.mult, [g_aB], [g_aB])
        tt("dve", g_da[:], g_dt[:], g_a[:].unsqueeze(1).broadcast_to([128, NCH, 32]), ALU.mult,
           [g_dtB, g_aB], [g_daB])
        cp("dve", g_dab[:], g_da[:], [g_daB], [g_dabB])
        for d in range(2):
            pa, paB = psA()
            mm(pa[:, 0:NCH * 16].rearrange("p (c h) -> p c h", h=16), tri[d], g_dab[:, :, d * 16:(d + 1) * 16],
               [cbB, g_dabB], paB)
            cp("dve", g_acum[:, d], pa[:, 0:NCH * 16].rearrange("p (c h) -> p c h", h=16), paB, [g_acumB])
            pb, pbB = psA()
            mm(pb[:, 0:NCH * 16].rearrange("p (c h) -> p c h", h=16), ones, g_dab[:, :, d * 16:(d + 1) * 16],
               [cbB, g_dabB], pbB)
            cp("dve", g_tot[:, d], pb[:, 0:NCH * 16].rearrange("p (c h) -> p c h", h=16), pbB, [g_totB])
        fl4 = lambda t: t[:].rearrange("p d c h -> p (d c h)")
        act(fl4(g_ea), fl4(g_acum), AF.Exp, [g_acumB], [g_eaB])
        tt("dve", g_tmp[:], g_tot[:], g_acum[:], ALU.subtract, [g_totB, g_acumB], [g_tmpB])
        act(fl4(g_tmp), fl4(g_tmp), AF.Exp, [g_tmpB], [g_tmpB])
        for d in range(2):
            tt("dve", g_w[:, d], g_tmp[:, d], g_dt[:, :, d * 16:(d + 1) * 16], ALU.mult, [g_tmpB, g_dtB], [g_wB])
        for d in range(2):
            tv = g_tot[:, d].rearrange("p c (g f r) -> p c g f r", g=2, f=2)
            for hf in range(2):
                ps_ = slice(hf * 64, (hf + 1) * 64)
                act(g_edec[ps_, d].rearrange("p c (g r) -> p c g r", g=2), tv[ps_, :, :, hf, :], AF.Exp,
                    [g_totB], [g_edecB])

        def chunk_loads(s, d, c, last_sweep):
            b, cc = c // 2, c % 2
            x, xB = xk_p.get()
            P.dma("sp", x[:], sx_tok[c], [DB("sx", c)], [xB], xB)
            bt, btB = bt_p.get()
            P.dma("sp", bt[:], sb_tok[c], [DB("sbt", c)], [btB], btB)
            bct, bctB = bct_p.get()
            P.dma("sp", bct[:], sbcT[b, :, :, cc * 128:(cc + 1) * 128], [DB("sbcT", b)], [bctB], bctB)
            z = zB = None
            if last_sweep:
                z, zB = tok_p.get()
                P.dma("sp", z[:], sz_tok[c], [DB("sz", c)], [zB], zB)
            return x, xB, bt, btB, bct, bctB, z, zB

        def chunk(s, d, c, last_sweep, tl):
            b, cc = c // 2, c % 2
            hs = slice(d * 16, (d + 1) * 16)
            H, HB_ = Hs[d]
            Hbf, HbB = Hb[d]
            x, xB, bt, btB, bct, bctB, z, zB = tl
            dau, dauB = big_p.get()
            dau3 = dau[:].rearrange("p (h i) -> p h i", h=16)
            tt("pool", dau3, tri[d].unsqueeze(1).broadcast_to([128, 16, 128]),
               g_dab[:, c, hs].unsqueeze(2).broadcast_to([128, 16, 128]), ALU.mult, [cbB, g_dabB], [dauB])
            sg, sgB = psum_at(0, 4)
            for q in range(4):
                mm(sg[:, q * 512:(q + 1) * 512], ones, dau[:, q * 512:(q + 1) * 512], [cbB, dauB], sgB,
                   start=True, stop=False)
                mm(sg[:, q * 512:(q + 1) * 512].rearrange("p (h i) -> p h i", h=4), ident,
                   nmask[d].unsqueeze(1).broadcast_to([128, 4, 128]), [cbB], sgB, start=False, stop=True)
            ar, arB = arg_p.get()
            tt("dve", ar[:].rearrange("p (h i) -> p h i", h=16), sg.rearrange("p (h i) -> p h i", h=16),
               g_acum[:, d, c, :].unsqueeze(2).broadcast_to([128, 16, 128]), ALU.subtract, sgB + [g_acumB], [arB])
            Lm, LmB = big_p.get()
            act(Lm[:], ar[:], AF.Exp, [arB], [LmB])
            pcs = [psum_at(4), psum_at(5)]
            for g in range(4):
                ps_ = slice((g % 2) * 64, (g % 2) * 64 + 64)
                pc, pcB = pcs[g % 2]
                mm(pc[:, (g // 2) * 128:(g // 2 + 1) * 128], bct[ps_, g // 2, :], bct[ps_, 2 + g // 2, :], [bctB], pcB)
            cbts = []
            for par in range(2):
                ct, ctB = cvo_p.get()
                cp("act", ct[:], pcs[par][0][:, 0:256], pcs[par][1], [ctB])
                cbts.append((ct, ctB))
            sc, scB = big_p.get()
            scv = sc[:].rearrange("p (gg two r i) -> p gg two r i", gg=2, two=2, r=4)
            Lmv = Lm[:].rearrange("p (gg two r i) -> p gg two r i", gg=2, two=2, r=4)
            for par, (ct, ctB) in enumerate(cbts):
                tt("pool", scv[:, :, par], Lmv[:, :, par],
                   ct[:].rearrange("p (g i) -> p g i", g=2).unsqueeze(2).broadcast_to([128, 2, 4, 128]),
                   ALU.mult, [LmB, ctB], [scB])
            return sc, scB

        def chunkB(s, d, c, last_sweep, tl, sa):
            b, cc = c // 2, c % 2
            hs = slice(d * 16, (d + 1) * 16)
            H, HB_ = Hs[d]
            Hbf, HbB = Hb[d]
            x, xB, bt, btB, bct, bctB, z, zB = tl
            sc, scB = sa
            xd, xdB = xk2_p.get()
            tt("dve", xd[:].rearrange("p (h q) -> p h q", h=16), x[:].rearrange("p (h q) -> p h q", h=16),
               g_dt[:, c, hs].unsqueeze(2).broadcast_to([128, 16, 64]), ALU.mult, [xB, g_dtB], [xdB])
            wx, wxB = xk2_p.get()
            tt("dve", wx[:].rearrange("p (h q) -> p h q", h=16), x[:].rearrange("p (h q) -> p h q", h=16),
               g_w[:, d, c, :].unsqueeze(2).broadcast_to([128, 16, 64]), ALU.mult, [xB, g_wB], [wxB])
            yi, yiB = psum_at(6, 2)
            for h in range(16):
                mm(yi[:, h * 64:(h + 1) * 64], sc[:, h * 128:(h + 1) * 128], xd[:, h * 64:(h + 1) * 64],
                   [scB, xdB], [yiB[h // 8]])
            ys, ysB = psum_at(4, 2)
            for g in range(4):
                ps_ = slice((g % 2) * 64, (g % 2) * 64 + 64)
                co = (g % 2) * 512 + (g // 2) * 256
                mm(ys[:, co:co + 256], bct[ps_, 2 + g // 2, :], Hbf[ps_, g // 2, :], [bctB, HbB], [ysB[g % 2]])
            yo, yoB = yo_p.get()
            yov = yo[:].rearrange("p (gg two r q) -> p gg two r q", gg=2, two=2, r=4)
            eav = g_ea[:, d, c, :].rearrange("p (gg two r) -> p gg two r", gg=2, two=2)
            for par in range(2):
                tt("dve", yov[:, :, par], ys[:, par * 512:(par + 1) * 512].rearrange("p (gg r q) -> p gg r q", gg=2, r=4),
                   eav[:, :, par].unsqueeze(3).broadcast_to([128, 2, 4, 64]), ALU.mult, [ysB[par], g_eaB], [yoB])
            tt("dve", yo[:], yo[:], yi, ALU.add, [yoB] + yiB, [yoB])
            dh, dhB = psum_at(4, 2)
            for gg in range(2):
                mm(dh[:, gg * 512:(gg + 1) * 512], bt[:, gg * 128:(gg + 1) * 128], wx[:, gg * 512:(gg + 1) * 512],
                   [btB, wxB], [dhB[gg]])
            tt("dve", H[:].rearrange("p g (r q) -> p g r q", r=4), H[:].rearrange("p g (r q) -> p g r q", r=4),
               g_edec[:, d, c, :].rearrange("p (g r) -> p g r", g=2).unsqueeze(3).broadcast_to([128, 2, 4, 64]),
               ALU.mult, [HB_, g_edecB], [HB_])
            dhv = dh.rearrange("p (g x) -> p g x", g=2)
            for hf in range(2):
                ps_ = slice(hf * 64, (hf + 1) * 64)
                tt("dve", H[ps_], H[ps_], dhv[ps_, :, hf * 256:(hf + 1) * 256], ALU.add, [HB_] + dhB, [HB_])
            cp("act", flat(Hbf[:]), flat(H[:]), [HB_], [HbB])
            if not last_sweep:
                P.dma("sp", yf_d[c], yo[:], [yoB], [DB("yf", c)], yoB)
                return
            yf, yfB = yo_p.get()
            P.dma("sp", yf[:], yf_d[c], [DB("yf", c)], [yfB], yfB)
            tt("dve", yo[:], yo[:], yf[:], ALU.add, [yoB, yfB], [yoB])
            xd2, xd2B = yo_p.get()
            tt("pool", xd2[:].rearrange("p (h q) -> p h q", h=16), x[:].rearrange("p (h q) -> p h q", h=16),
               pr[:, P_SD:P_SD + 16].unsqueeze(2).broadcast_to([128, 16, 64]), ALU.mult, [xB, prB], [xd2B])
            tt("dve", yo[:], yo[:], xd2[:], ALU.add, [yoB, xd2B], [yoB])
            tt("dve", yo[:], yo[:], z[:], ALU.mult, [yoB, zB], [yoB])
            s1, s1B = s1_p.get()
            act(yf[:], yo[:], AF.Square, [yoB], [yfB, s1B], accum=s1[:, 0:1])
            rsqrt(s1[:, 0:1], s1B, (128, 1), 1.0 / D)
            yn, ynB = tok_p.get()
            stt(yn[:], yo[:], s1[:, 0:1], pr[:, P_SSDN:P_SSDN + 1024], ALU.mult, ALU.mult, [yoB, s1B, prB], [ynB])
            out_transposed(yn, ynB, 0, c, bank=6)

        def seq_begin(s):
            has_ctx = SEQS[s][2]
            for d in range(2):
                H, HB_ = Hs[d]
                if has_ctx:
                    P.dma("sp", flat(H[:]), ssd0[l, d], [], [HB_], HB_)
                else:
                    memset("dve", flat(H[:]), 0.0, [HB_])
                cp("pool", Hb[d][0][:], H[:], [HB_], [Hb[d][1]])

        def seq_end(s):
            if not SEQS[s][2]:
                for d in range(2):
                    H, HB_ = Hs[d]
                    P.dma("sp", nssd_o[s - 1, l, d], flat(H[:]), [HB_], [DB("nssd", s, l, d)], HB_)

        run_sweeps(chunk_loads, chunk, chunkB, seq_begin, seq_end)

    CS = CSb = m_li = m_liB = m_lf = m_lfB = m_lfb = m_lfbB = m_b = m_bB = m_g = m_gB = m_wt = m_wtB = m_fl = m_flB = m_RB = m_RBB = m_SCB = m_SCBB = m_GM = m_GMB = m_BT = m_BTB = m_R = m_RB2 = m_MD = m_MDB = m_mp = m_mpB = m_RD = m_RDB = vp_p = mT_p = None

    def alloc_ml():
        nonlocal CS, CSb, m_li, m_liB, m_lf, m_lfB, m_lfb, m_lfbB, m_b, m_bB, m_g, m_gB, m_wt, m_wtB, m_fl, m_flB, m_RB, m_RBB, m_SCB, m_SCBB, m_GM, m_GMB, m_BT, m_BTB, m_R, m_RB2, m_MD, m_MDB, m_mp, m_mpB, m_RD, m_RDB, vp_p, mT_p, dtg_s, dtgB, g_da, g_daB
        dtg_s, dtgB = nsb("dtg_s", [128, NCH, 64])
        g_da, g_daB = nsb("g_da", [128, NCH, 32])
        CS = [nsb("CS%d" % d, [128, 8, 129]) for d in range(2)]
        CSb = [nsb("CSb%d" % d, [128, 8, 129], BF16) for d in range(2)]
        m_li, m_liB = nsb("m_li", [128, 2, NCH, 8])
        m_lf, m_lfB = nsb("m_lf", [128, 2, NCH, 8])
        m_lfb, m_lfbB = nsb("m_lfb", [128, 2, NCH, 8], BF16)
        m_b, m_bB = nsb("m_b", [128, 2, NCH, 8])
        m_g, m_gB = nsb("m_g", [128, 2, NCH, 8])
        m_wt, m_wtB = nsb("m_wt", [128, 2, NCH, 8])
        m_fl, m_flB = nsb("m_fl", [128, 2, NCH, 8])
        m_RB, m_RBB = nsb("m_RB", [128, 2, NCH, 8])
        m_SCB, m_SCBB = nsb("m_SCB", [128, 2, NCH, 8])
        m_GM, m_GMB = nsb("m_GM", [8, 2, NCH])
        m_BT, m_BTB = nsb("m_BT", [8, 2, NCH])
        m_R, m_RB2 = nsb("m_R", [8, 2, NCH])
        m_MD, m_MDB = nsb("m_MD", [8, 2, NCH])
        m_mp, m_mpB = nsb("m_mp", [8, 2, 3, NCH + 1])
        m_RD, m_RDB = nsb("m_RD", [8, 2, 2, NCH, 8])
        vp_p = mkpool("vp", [128, 8, 129], BF16, 4)
        mT_p = mkpool("mT", [128, 8, 128], BF16, 8)


    def ml_layer(l, pr, prB):
        pre = g_da
        preB = g_daB
        P.dma("sp", dtg_s[:], dtg_tok.rearrange("c p n -> p c n"), [DB("dtg", c) for c in range(NCH)],
              [dtgB], dtgB)
        tt("dve", pre[:], dtg_s[:, :, 32:64], pr[:, P_MGB:P_MGB + 32].unsqueeze(1).broadcast_to([128, NCH, 32]),
           ALU.add, [dtgB, prB], [preB])
        for d in range(2):
            cp("dve", m_li[:, d], pre[:, :, d * 16:d * 16 + 8], [preB], [m_liB])
            act(m_lf[:, d], pre[:, :, d * 16 + 8:d * 16 + 16], AF.Exp, [preB], [m_lfB], scale=-1.0)
        fl4 = lambda t: t[:].rearrange("p d c h -> p (d c h)")
        act(fl4(m_lf), fl4(m_lf), AF.Ln, [m_lfB], [m_lfB], bias=1.0)
        ts("dve", fl4(m_lf), fl4(m_lf), -1.0, ALU.mult, [m_lfB], [m_lfB])
        cp("dve", fl4(m_lfb), fl4(m_lf), [m_lfB], [m_lfbB])
        for d in range(2):
            pa, paB = psA()
            pav = pa[:, 0:NCH * 8].rearrange("p (c h) -> p c h", h=8)
            mm(pav, tri[d], m_lfb[:, d], [cbB, m_lfbB], paB)
            cp("dve", m_b[:, d], pav, paB, [m_bB])
        tt("dve", m_g[:], m_li[:], m_b[:], ALU.subtract, [m_liB, m_bB], [m_gB])
        for d in range(2):
            for c0 in range(0, NCH, 4):
                n = min(4, NCH - c0)
                pt, ptB = psA()
                for i in range(n):
                    tp(pt[0:8, i * 128:(i + 1) * 128], m_g[:, d, c0 + i, :], cf[:], [m_gB, cfB], ptB)
                P.add("dve", (lambda o=m_GM[:, d, c0:c0 + n], i_=pt[0:8, 0:n * 128].rearrange("p (c j) -> p c j", j=128):
                              V.tensor_reduce(out=o, in_=i_, axis=AX.X, op=ALU.max)), ptB, [m_GMB])
            pb, pbB = psBr()
            for c in range(NCH):
                mm(pb[0:8, c:c + 1], m_lfb[:, d, c, :], ones[:, 0:1], [m_lfbB, cbB], pbB)
            cp("dve", m_BT[:, d], pb[0:8, 0:NCH], pbB, [m_BTB])
        for d in range(2):
            for s in range(3):
                f, n, has_ctx, _ = SEQS[s]
                mp = m_mp[:, d, s]
                if has_ctx:
                    P.dma("sp", mp[:, 0:1], mlm0[l, d], [], [m_mpB], m_mpB)
                else:
                    memset("dve", mp[:, 0:1], 0.0, [m_mpB])
                for i, c in enumerate(chunk_order(s, d)):
                    tt("dve", m_R[:, d, c:c + 1], mp[:, i:i + 1], m_GM[:, d, c:c + 1], ALU.max,
                       [m_mpB, m_GMB], [m_RB2])
                    tt("dve", m_MD[:, d, c:c + 1], mp[:, i:i + 1], m_R[:, d, c:c + 1], ALU.subtract,
                       [m_mpB, m_RB2], [m_MDB])
                    tt("dve", mp[:, i + 1:i + 2], m_R[:, d, c:c + 1], m_BT[:, d, c:c + 1], ALU.add,
                       [m_RB2, m_BTB], [m_mpB])
                if not has_ctx:
                    P.dma("sp", nmlm_o[s - 1, l, d], mp[:, 2 * n:2 * n + 1], [m_mpB], [DB("nmlm", s, l, d)], m_mpB)
        for d in range(2):
            for wi, (src, srcB) in enumerate(((m_R, m_RB2), (m_MD, m_MDB))):
                tt("dve", m_RD[:, d, wi], src[:, d, :].unsqueeze(2).broadcast_to([8, NCH, 8]),
                   cf[0:8, 0:8].unsqueeze(1).broadcast_to([8, NCH, 8]), ALU.mult, [srcB, cfB], [m_RDB])
            pr_, prB_ = psBr()
            mm(pr_[:, 0:2 * NCH * 8], onesf[0:8, :], m_RD[:, d].rearrange("p w c h -> p (w c h)"),
               [onesfB, m_RDB], prB_)
            cp("dve", m_RB[:, d], pr_[:, 0:NCH * 8].rearrange("p (c h) -> p c h", h=8), prB_, [m_RBB])
            act(m_SCB[:, d], pr_[:, NCH * 8:2 * NCH * 8].rearrange("p (c h) -> p c h", h=8), AF.Exp, prB_, [m_SCBB])
        tt("dve", m_wt[:], m_g[:], m_RB[:], ALU.subtract, [m_gB, m_RBB], [m_wtB])
        act(fl4(m_wt), fl4(m_wt), AF.Exp, [m_wtB], [m_wtB])
        tt("dve", m_fl[:], m_b[:], m_RB[:], ALU.add, [m_bB, m_RBB], [m_flB])
        ts("dve", fl4(m_fl), fl4(m_fl), -1.0, ALU.mult, [m_flB], [m_flB], s2=80.0, op1=ALU.min)
        act(fl4(m_fl), fl4(m_fl), AF.Exp, [m_flB], [m_flB])

        def chunk_loads(s, d, c, last_sweep):
            b, cc = c // 2, c % 2
            q, qB = mT_p.get()
            P.dma("sp", q[:], mqT[b, :, :, cc * 128:(cc + 1) * 128], [DB("mqT", b)], [qB], qB)
            k, kB = mT_p.get()
            P.dma("sp", k[:], mkT[b, :, :, cc * 128:(cc + 1) * 128], [DB("mkT", b)], [kB], kB)
            kt, ktB = xk_p.get()
            P.dma("sp", kt[:], mk_tok[c], [DB("mkt", c)], [ktB], ktB)
            v, vB = xk_p.get()
            P.dma("sp", v[:], mv_tok[c], [DB("mv", c)], [vB], vB)
            mo = moB = None
            if last_sweep:
                mo, moB = tok_p.get()
                P.dma("sp", mo[:], mo_tok[c], [DB("mo", c)], [moB], moB)
            return q, qB, k, kB, kt, ktB, v, vB, mo, moB

        def chunk(s, d, c, last_sweep, tl):
            b, cc = c // 2, c % 2
            C, CB_ = CS[d]
            Cb, CbB = CSb[d]
            q, qB, k, kB, kt, ktB, v, vB, mo, moB = tl
            sc, scB = psum_at(0, 2)
            for h in range(8):
                mm(sc[:, h * 128:(h + 1) * 128], k[:, h, :], q[:, h, :], [kB, qB], [scB[h // 4]])
            sm_, smB_ = xk2_p.get()
            stt(sm_[:].rearrange("p (h t) -> p h t", h=8), sc.rearrange("p (h t) -> p h t", h=8), MLK_SCALE,
                tri[d].unsqueeze(1).broadcast_to([128, 8, 128]), ALU.mult, ALU.mult, scB + [cbB], [smB_])
            vp, vpB = vp_p.get()
            tt("pool", vp[:, :, 0:128], v[:].rearrange("p (h e) -> p h e", h=8),
               m_wt[:, d, c, :].unsqueeze(2).broadcast_to([128, 8, 128]), ALU.mult, [vB, m_wtB], [vpB])
            cp("pool", vp[:, :, 128:129], m_wt[:, d, c, :].unsqueeze(2), [m_wtB], [vpB])
            return sm_, smB_, vp, vpB

        def chunkB(s, d, c, last_sweep, tl, sa):
            b, cc = c // 2, c % 2
            C, CB_ = CS[d]
            Cb, CbB = CSb[d]
            q, qB, k, kB, kt, ktB, v, vB, mo, moB = tl
            sm_, smB_, vp, vpB = sa
            tt("dve", C[:], C[:], m_SCB[:, d, c, :].unsqueeze(2).broadcast_to([128, 8, 129]), ALU.mult,
               [CB_, m_SCBB], [CB_])
            cp("act", Cb[:].rearrange("p h e -> p (h e)"), C[:].rearrange("p h e -> p (h e)"), [CB_], [CbB])
            nm, nmB = psum_at(2, 2)
            dn, dnB = psum_at(4)
            for h in range(8):
                mm(nm[:, h * 128:(h + 1) * 128], sm_[:, h * 128:(h + 1) * 128], vp[:, h, 0:128], [smB_, vpB],
                   [nmB[h // 4]], start=True, stop=False)
                mm(nm[:, h * 128:(h + 1) * 128], q[:, h, :], Cb[:, h, 0:128], [qB, CbB], [nmB[h // 4]],
                   start=False, stop=True)
            for h in range(8):
                mm(dn[:, h:h + 1], sm_[:, h * 128:(h + 1) * 128], vp[:, h, 128:129], [smB_, vpB], dnB,
                   start=True, stop=False)
                mm(dn[:, h:h + 1], q[:, h, :], Cb[:, h, 128:129], [qB, CbB], dnB, start=False, stop=True)
            dc, dcB = psum_at(5, 2)
            for h in range(8):
                mm(dc[:, h * 128:(h + 1) * 128], kt[:, h * 128:(h + 1) * 128], vp[:, h, 0:128], [ktB, vpB],
                   [dcB[h // 4]])
            for h in range(8):
                mm(dn[:, 8 + h:9 + h], kt[:, h * 128:(h + 1) * 128], vp[:, h, 128:129], [ktB, vpB], dnB)
            dd, ddB = s1_p.get()
            cp("dve", dd[:], dn[:, 0:8], dnB, [ddB])
            stt(dd[:], dd[:], -1.0, dd[:], ALU.mult, ALU.max, [ddB], [ddB])
            tt("dve", dd[:], dd[:], m_fl[:, d, c, :], ALU.max, [ddB, m_flB], [ddB])
            P.add("dve", lambda o=dd[:]: V.reciprocal(out=o, in_=o), [ddB], [ddB])
            hd, hdB = yo_p.get()
            tt("dve", hd[:].rearrange("p (h e) -> p h e", h=8), nm.rearrange("p (h e) -> p h e", h=8),
               dd[:].unsqueeze(2).broadcast_to([128, 8, 128]), ALU.mult, nmB + [ddB], [hdB])
            tt("dve", C[:, :, 0:128], C[:, :, 0:128], dc.rearrange("p (h e) -> p h e", h=8), ALU.add,
               [CB_] + dcB, [CB_])
            tt("dve", C[:, :, 128:129], C[:, :, 128:129], dn[:, 8:16].unsqueeze(2), ALU.add, [CB_] + dnB, [CB_])
            if not last_sweep:
                P.dma("sp", hf_d[c], hd[:], [hdB], [DB("hf", c)], hdB)
                return
            hf, hfB = yo_p.get()
            P.dma("sp", hf[:], hf_d[c], [DB("hf", c)], [hfB], hfB)
            tt("dve", hd[:], hd[:], hf[:], ALU.add, [hdB, hfB], [hdB])
            act(hf[:], hd[:], AF.Square, [hdB], [hfB])
            s1, s1B = s1_p.get()
            P.add("dve", lambda o=s1[:], i_=hf[:].rearrange("p (h e) -> p h e", h=8):
                  V.tensor_reduce(out=o, in_=i_, axis=AX.X, op=ALU.add), [hfB], [s1B])
            rsqrt(s1[:], s1B, (128, 8), 1.0 / 128)
            tt("dve", hd[:].rearrange("p (h e) -> p h e", h=8), hd[:].rearrange("p (h e) -> p h e", h=8),
               s1[:].unsqueeze(2).broadcast_to([128, 8, 128]), ALU.mult, [hdB, s1B], [hdB])
            tt("dve", hd[:], hd[:], pr[:, P_MLN:P_MLN + 1024], ALU.mult, [hdB, prB], [hdB])
            yn, ynB = tok_p.get()
            tt("dve", yn[:], hd[:], mo[:], ALU.mult, [hdB, moB], [ynB])
            out_transposed(yn, ynB, 2, c)

        def seq_begin(s):
            for d in range(2):
                C, CB_ = CS[d]
                if SEQS[s][2]:
                    P.dma("sp", C[:].rearrange("p h e -> p (h e)"), mlc0[l, d], [], [CB_], CB_)
                else:
                    memset("dve", C[:].rearrange("p h e -> p (h e)"), 0.0, [CB_])

        def seq_end(s):
            if not SEQS[s][2]:
                for d in range(2):
                    C, CB_ = CS[d]
                    P.dma("sp", nmlc_o[s - 1, l, d], C[:].rearrange("p h e -> p (h e)"), [CB_],
                          [DB("nmlc", s, l, d)], CB_)

        run_sweeps(chunk_loads, chunk, chunkB, seq_begin, seq_end)

    NKS = NCTX + SL
    KT = KTB = VA = VAB = VB_ = VBB = ckd = ckdB = q_p = e_p = yat_p = dsb_p = ev_p = None

    def alloc_att():
        nonlocal KT, KTB, VA, VAB, VB_, VBB, ckd, ckdB, q_p, e_p, yat_p, dsb_p, ev_p
        KT, KTB = nsb("KT", [128, 8, NKS], BF16)
        memset("dve", KT[:].rearrange("p a b -> p (a b)"), 0.0, [KTB])
        VA, VAB = nsb("VA", [128, (NKS // 128), 4, 128], BF16)
        VB_, VBB = nsb("VBt", [128, (NKS // 128), 4, 128], BF16)
        ckd, ckdB = nsb("ckd", [128, 4, 4, 128], BF16)
        memset("pool", VA[:].rearrange("p a b c -> p (a b c)"), 0.0, [VAB])
        memset("pool", VB_[:].rearrange("p a b c -> p (a b c)"), 0.0, [VBB])
        for kc_ in range(NKS // 128):
            memset("pool", VA[:, kc_, :, 64:128], 1.0, [VAB])
            memset("pool", VB_[:, kc_, :, 0:64], 1.0, [VBB])
        q_p = mkpool("qatt", [128, 8, 512], BF16, 2)
        e_p = mkpool("eatt", [128, 2, 512], BF16, 2)
        yat_p = mkpool("yat", [128, 8, 512], BF16, 2)
        dsb_p = mkpool("dsb", [128, 512], F32, 4)
        ev_p = mkpool("ev", [128, 2, 512], F32, 4)


    def att_layer(l):
        for s in range(3):
            f, n, has_ctx, _ = SEQS[s]
            L = 256 * n
            nctx = NCTX if has_ctx else 0
            nk = nctx + L
            nkc = nk // 128
            if has_ctx:
                src = cache_k[l].rearrange("(kc p) (f c) -> p kc f c", p=128, f=2)
                P.dma("pool", ckd[:, :, 0:2, :], src, [], [ckdB], ckdB)
                for f2 in range(2):
                    P.dma("pool", ckd[:, :, 2 + f2, 0:64], src[:, :, f2, 64:128], [], [ckdB], ckdB)
                    P.dma("pool", ckd[:, :, 2 + f2, 64:128], src[:, :, f2, 0:64], [], [ckdB], ckdB)
                KTv = KT[:].rearrange("p (f sw hh) t -> p sw f hh t", f=2, sw=2, hh=2)
                for kc in range(4):
                    pt, ptB = psBr()
                    ptb = pt.bitcast(BF16)
                    for fv in range(4):
                        tp(ptb[:, fv * 128:(fv + 1) * 128], ckd[:, kc, fv, :], ident, [ckdB, cbB], ptB)
                    ptv = ptb[:, 0:512].rearrange("p (sw f t) -> p sw f t", sw=2, f=2)
                    ks = slice(kc * 128, (kc + 1) * 128)
                    cp("act", KTv[0:64, :, :, 0, ks], ptv[0:64], ptB, [KTB])
                    for sw in range(2):
                        cp("act", KTv[64:128, 1 - sw, :, 1, ks], ptv[64:128, sw], ptB, [KTB])
                srcv = cache_v[l].rearrange("(kc p) (g c) -> p kc g c", p=128, g=4)
                for kc in range(4):
                    P.dma("pool", VA[:, kc, :, 0:64], srcv[:, kc], [], [VAB], VAB)
                    P.dma("pool", VB_[:, kc, :, 64:128], srcv[:, kc], [], [VBB], VBB)
            for g in range(4):
                for hh in range(2):
                    fc = (g // 2) if (g % 2) == hh else 2 + g // 2
                    ps_ = slice(hh * 64, hh * 64 + 64)
                    P.dma("sp", KT[ps_, g * 2 + hh, nctx:nctx + L].rearrange("p (b t) -> p b t", b=n),
                          kT_d[f:f + n, ps_, fc, :].rearrange("b p t -> p b t"),
                          [DB("kT", f + bi) for bi in range(n)], [KTB], KTB)
            c0 = 2 * f
            kc0 = nctx // 128
            for i in range(2 * n):
                srcv = v_tok[c0 + i].rearrange("p (g e) -> p g e", g=4)
                P.dma("sp", VA[:, kc0 + i, :, 0:64], srcv, [DB("v", c0 + i)], [VAB], VAB)
                P.dma("sp", VB_[:, kc0 + i, :, 64:128], srcv, [DB("v", c0 + i)], [VBB], VBB)
            NQ = min(512, L)
            nqb = NQ // 256
            for qb in range(L // NQ):
                qt, qtB = q_p.get()
                for i in range(nqb):
                    b = f + qb * nqb + i
                    P.dma("sp", qt[:, :, i * 256:(i + 1) * 256], qT_d[b], [DB("qT", b)], [qtB], qtB)
                ya, yaB = yat_p.get()
                tasks = [(j, hh, k0) for j in range(8) for hh in range(2) for k0 in range(0, nkc, 2)]

                def hinfo(j, hh):
                    h = 2 * j + hh
                    g = h // 4
                    return g, slice(0, 128), g * 2 + hh

                def emit_scores(ti):
                    j, hh, k0 = tasks[ti]
                    g, ps_, fv = hinfo(j, hh)
                    scp, scpB = psum_at(2 + 2 * (ti % 2), 2)
                    for kk in range(2):
                        kc = k0 + kk
                        mm(scp[:, kk * 512:kk * 512 + NQ], KT[ps_, fv, kc * 128:(kc + 1) * 128],
                           qt[ps_, j, 0:NQ], [KTB, qtB], [scpB[kk]])
                    return scp, scpB

                ohs = {}

                def emit_epv(ti, scp, scpB):
                    j, hh, k0 = tasks[ti]
                    g, ps_, fv = hinfo(j, hh)
                    Vt, VtB = (VA, VAB) if hh == 0 else (VB_, VBB)
                    if k0 == 0:
                        ohs[(j, hh)] = psum_at((0 if j % 2 == 0 else 6) + hh)
                    oh, ohB = ohs[(j, hh)]
                    e, eB = e_p.get()
                    act(e[:, :, 0:NQ], scp.rearrange("p (k q) -> p k q", k=2)[:, :, 0:NQ], AF.Exp,
                        scpB, [eB], scale=0.125)
                    for kk in range(2):
                        kc = k0 + kk
                        mm(oh[:, 0:NQ], Vt[:, kc, g, :], e[:, kk, 0:NQ], [VtB, eB], ohB,
                           start=(kc == 0), stop=(kc == nkc - 1))
                    if hh == 1 and k0 + 2 >= nkc:
                        (oa, oaB), (ob, obB) = ohs[(j, 0)], ohs[(j, 1)]
                        ev, evB = ev_p.get()
                        d2, d2B = dsb_p.get()
                        cp("dve", ev[:, 0, 0:NQ], oa[:, 0:NQ], oaB, [evB])
                        cp("dve", ev[:, 1, 0:NQ], ob[:, 0:NQ], obB, [evB])
                        P.dma("sp", d2[64:128, 0:NQ], ev[0:64, 1, 0:NQ], [evB], [d2B], d2B)
                        P.dma("sp", d2[0:64, 0:NQ], ev[64:128, 0, 0:NQ], [evB], [d2B], d2B)
                        def norm(j=j, ev=ev, evB=evB, d2=d2, d2B=d2B):
                            P.add("dve", lambda o=d2[:, 0:NQ]: V.reciprocal(out=o, in_=o), [d2B], [d2B])
                            tt("dve", ya[0:64, j, 0:NQ], ev[0:64, 0, 0:NQ], d2[0:64, 0:NQ], ALU.mult,
                               [evB, d2B], [yaB])
                            tt("dve", ya[64:128, j, 0:NQ], ev[64:128, 1, 0:NQ], d2[64:128, 0:NQ], ALU.mult,
                               [evB, d2B], [yaB])
                        while len(deferred) >= 2:
                            deferred.pop(0)()
                        deferred.append(norm)

                deferred = []
                pend = emit_scores(0)
                for ti in range(len(tasks)):
                    nxt = emit_scores(ti + 1) if ti + 1 < len(tasks) else None
                    emit_epv(ti, *pend)
                    pend = nxt
                while deferred:
                    deferred.pop(0)()
                for i in range(nqb):
                    b = f + qb * nqb + i
                    P.dma("sp", ybr[1, b], ya[:, :, i * 256:(i + 1) * 256], [yaB], [DB("ybr", 1, b)], yaB)


    wbr_s = wo_s = yb_p = gt_p = mg_p = t3_p = None

    def alloc_mrg():
        nonlocal wbr_s, wo_s, yb_p, gt_p, mg_p, t3_p
        wbr_s = [nsb("wbr%d" % n, [128, KC, 1024], BF16) for n in range(3)]
        yb_p = mkpool("ybl", [128, 8, 256], BF16, 6)
        gt_p = mkpool("gtl", [128, 8, 256], BF16, 6)
        mg_p = mkpool("mgd", [128, 8, 256], BF16, 2)
        t3_p = mkpool("t3", [128, 256], F32, 9)

    def merge_a(l):
        for n in range(3):
            load_slab(w_br[l, n], 1024, 0, wbr_s[n])

        def loads(b):
            ys_, gs_ = [], []
            for n in range(3):
                y, yB = yb_p.get()
                P.dma("sp", y[:], ybr[n, b], [DB("ybr", n, b)], [yB], yB)
                g, gB = gt_p.get()
                P.dma("sp", g[:], gT[n, b], [DB("gT", n, b)], [gB], gB)
                ys_.append((y, yB))
                gs_.append((g, gB))
            return ys_, gs_

        nxt = loads(0)
        for b in range(NBLK):
            ys_, gs_ = nxt
            if b + 1 < NBLK:
                nxt = loads(b + 1)
            mg, mgB = mg_p.get()
            for oc in range(8):
                ts_ = []
                for n in range(3):
                    pp, ppB = psA()
                    for k in range(KC):
                        mm(pp[:, 0:256], wbr_s[n][0][:, k, oc * 128:(oc + 1) * 128], ys_[n][0][:, k, :],
                           [wbr_s[n][1], ys_[n][1]], ppB, start=(k == 0), stop=(k == KC - 1))
                    t, tB = t3_p.get()
                    tt("dve", t[:], pp[:, 0:256], gs_[n][0][:, oc, :], ALU.mult, ppB + [gs_[n][1]], [tB])
                    ts_.append((t, tB))
                tt("pool", ts_[0][0][:], ts_[0][0][:], ts_[1][0][:], ALU.add, [ts_[0][1], ts_[1][1]], [ts_[0][1]])
                tt("pool", mg[:, oc, :], ts_[0][0][:], ts_[2][0][:], ALU.add, [ts_[0][1], ts_[2][1]], [mgB])
            P.dma("sp", mgd_d[b], mg[:], [mgB], [DB("mgd", b)], mgB)

    def merge_b(l):
        wo_t, wo_B = nsb("wo_s", [128, KC, 1024], BF16)
        mgl_p = mkpool("mgl", [128, 8, 256], BF16, 2)
        load_slab(w_out[l], 1024, 0, (wo_t, wo_B))
        for b in range(NBLK):
            m = SEQS[blk_seq(b)][3]
            mg, mgB = mgl_p.get()
            P.dma("sp", mg[:], mgd_d[b], [DB("mgd", b)], [mgB], mgB)
            xt, xtB = xt_p.get()
            P.dma("sp", xt[:], xres[b], [DB("x", b)], [xtB], xtB)
            for oc in range(8):
                pp, ppB = psA()
                for k in range(KC):
                    mm(pp[:, 0:256], wo_t[:, k, oc * 128:(oc + 1) * 128], mg[:, k, :], [wo_B, mgB], ppB,
                       start=(k == 0), stop=(k == KC - 1))
                stt(xt[:, oc, :], pp[:, 0:256], modT[:, l, m, 16 + oc:17 + oc], xt[:, oc, :], ALU.mult, ALU.add,
                    ppB + [modB, xtB], [xtB])
            P.dma("sp", xres[b], xt[:], [xtB], [DB("x", b)], xtB)
            norm_block(xt, xtB, b, l, 1)

    FG = [6, 6, 5, 5]
    wf1_p = wf2_p = ac_p = sa_p = fo_p = None

    def alloc_ffn():
        nonlocal wf1_p, wf2_p, ac_p, sa_p, fo_p
        wf1_p = mkpool("wf1", [128, KC, 2, 768], BF16, 2)
        wf2_p = mkpool("wf2", [128, 6, 1024], BF16, 2)
        ac_p = mkpool("ffa", [128, 256], BF16, 8)
        sa_p = mkpool("ffs", [128, 256], F32, 3)
        fo_p = mkpool("fo", [128, KC, 256], F32, 1)


    def ffn_layer(l, last):
        hc0 = 0
        for gi, ng in enumerate(FG):
            w1, w1B = wf1_p.get()
            w2, w2B = wf2_p.get()
            for ab in range(2):
                P.dma("pool", w1[:, :, ab, 0:ng * 128],
                      w_f1[l, :, ab * FFN_H + hc0 * 128:ab * FFN_H + (hc0 + ng) * 128].rearrange(
                          "(k p) n -> p k n", p=128), [], [w1B], w1B)
            P.dma("pool", w2[:, 0:ng, :], w_f2[l, hc0 * 128:(hc0 + ng) * 128, :].rearrange("(c p) n -> p c n", p=128),
                  [], [w2B], w2B)
            for b in range(NBLK):
                m = SEQS[blk_seq(b)][3]
                c0 = hcol(b) + 1
                acts = []
                for hc in range(ng):
                    pa, paB = psA()
                    for k in range(KC):
                        mm(pa[:, 0:256], w1[:, k, 0, hc * 128:(hc + 1) * 128], hT[:, k, c0:c0 + 256], [w1B, hB[b]],
                           paB, start=(k == 0), stop=(k == KC - 1))
                    pb, pbB = psA()
                    for k in range(KC):
                        mm(pb[:, 0:256], w1[:, k, 1, hc * 128:(hc + 1) * 128], hT[:, k, c0:c0 + 256], [w1B, hB[b]],
                           pbB, start=(k == 0), stop=(k == KC - 1))
                    sa, saB = sa_p.get()
                    act(sa[:], pa[:, 0:256], AF.Silu, paB, [saB])
                    a, aB = ac_p.get()
                    tt("dve", a[:], pb[:, 0:256], sa[:], ALU.mult, pbB + [saB], [aB])
                    acts.append((a, aB))
                xt, xtB = xt_p.get()
                P.dma("sp", xt[:], xres[b], [DB("x", b)], [xtB], xtB)
                for oc in range(8):
                    pp, ppB = psA()
                    for hc in range(ng):
                        mm(pp[:, 0:256], w2[:, hc, oc * 128:(oc + 1) * 128], acts[hc][0][:], [w2B, acts[hc][1]], ppB,
                           start=(hc == 0), stop=(hc == ng - 1))
                    stt(xt[:, oc, :], pp[:, 0:256], modT[:, l, m, 40 + oc:41 + oc], xt[:, oc, :], ALU.mult, ALU.add,
                        ppB + [modB, xtB], [xtB])
                if gi < len(FG) - 1 or not last:
                    P.dma("sp", xres[b], xt[:], [xtB], [DB("x", b)], xtB)
                if gi == len(FG) - 1:
                    if last:
                        fo = fo_p.get()
                        norm_block(xt, xtB, b, l, 0, final=True, out_tile=(fo[0], fo[1]))
                        P.dma("sp", yT[:, :, b * 256:(b + 1) * 256], fo[0][:], [fo[1]], [DB("yT", b)], fo[1])
                    else:
                        norm_block(xt, xtB, b, l + 1, 0)
            hc0 += ng

    def phase(fn, *a):
        with contextlib.ExitStack() as pes:
            cur["es"] = pes
            fn(*a)
            P.flush()
        cur["es"] = es

    def ph_d1(l):
        alloc_slab()
        alloc_d1()
        d1_layer(l)

    def ph_ssd(l):
        alloc_scan()
        alloc_mix()
        alloc_ssd()
        ssd_layer(l, prep_t, prepB)

    def ph_att(l):
        alloc_att()
        att_layer(l)

    def ph_ml(l):
        alloc_mix(False)
        alloc_ml()
        ml_layer(l, prep_t, prepB)

    def ph_mrg_a(l):
        alloc_mrg()
        merge_a(l)

    def ph_mrg_b(l):
        alloc_norm()
        merge_b(l)

    def ph_ffn(l):
        alloc_norm()
        alloc_ffn()
        ffn_layer(l, l == NL - 1)

    nph = [0]

    def go(fn, *a):
        nph[0] += 1
        if stop_after is not None and nph[0] > stop_after:
            return
        phase(fn, *a)

    go(startup)
    for l in range(NL):
        P.dma("sp", prep_t[:], prep[l], [], [prepB], prepB)
        go(ph_d1, l)
        go(ph_ssd, l)
        go(ph_att, l)
        go(ph_ml, l)
        go(ph_mrg_a, l)
        go(ph_mrg_b, l)
        go(ph_ffn, l)
    P.flush()
    print("kernel build: ops=%d sems=%d" % (P.n_emitted, P.nsem))
    es.close()
    return nc


def _consts(SL):
    c = np.zeros((128, C_TOT), np.float32)
    k = np.arange(128)[:, None]
    i = np.arange(128)[None, :]
    c[:, C_ID:C_ID + 128] = (k == i)
    c[:, C_ONE:C_ONE + 128] = 1.0
    c[:, C_U:C_U + 128] = (k <= i)
    c[:, C_L:C_L + 128] = (k >= i)
    c[:, C_BLK:C_BLK + 128] = ((k // 64) == (i // 64))
    part = np.arange(128)
    d = part % 64
    partner = np.where((d % 32) < 16, part + 16, part - 16)
    c[:, C_SWP:C_SWP + 128] = (k == partner[None, :])
    c[:, C_NMF:C_NMF + 128] = np.where(i < k, NEG, 0.0)
    c[:, C_NMB:C_NMB + 128] = np.where(i > k, NEG, 0.0)
    cf = np.eye(128, dtype=np.float32)
    t = np.arange(SL)
    rows, cols = t // 64, t % 64
    f = d % 16
    freqs = (10000.0 ** (-(f.astype(np.float32)) / 16.0)).astype(np.float32)
    pos = np.where((d < 32)[:, None], rows[None, :], cols[None, :]).astype(np.float32)
    ang = pos * freqs[:, None]
    cosT = np.cos(ang).astype(np.float32)
    sgn = np.where((d % 32) < 16, -1.0, 1.0).astype(np.float32)
    sinT = (np.sin(ang) * sgn[:, None]).astype(np.float32)
    return c, cf, cosT, sinT


def _fm(v):
    v = np.asarray(v)
    n = v.shape[-1] // 128
    r = v.reshape(v.shape[:-1] + (n, 128))
    return np.ascontiguousarray(np.moveaxis(r, -1, 0))


def _prepare(inp, SB, NL):
    NBLK = SB + 2
    SL = 256 * SB
    f32 = np.float32
    g = lambda k: np.asarray(inp[k], dtype=f32)
    c, cf, cosT, sinT = _consts(SL)
    shared = {
        "w_ada": np.ascontiguousarray(g("w_ada")[:NL]), "w_in": np.ascontiguousarray(g("w_in")[:NL]),
        "w_branch": np.ascontiguousarray(g("w_branch")[:NL]), "w_out": np.ascontiguousarray(g("w_out")[:NL]),
        "w_ffn_in": np.ascontiguousarray(g("w_ffn_in")[:NL]), "w_ffn_out": np.ascontiguousarray(g("w_ffn_out")[:NL]),
        "cst": c, "cstf": cf, "ropec": cosT, "ropes": sinT,
    }
    prep = np.zeros((NL, 128, P_TOT), f32)
    pfm = np.zeros((128, NL, Q_TOT), f32)
    for l in range(NL):
        prep[l, :, P_DTB:P_DTB + 32] = g("ssd_dt_bias")[l].reshape(32)[None]
        prep[l, :, P_ALOG:P_ALOG + 32] = g("ssd_a_log")[l].reshape(32)[None]
        prep[l, :, P_SD:P_SD + 16] = g("ssd_d")[l][None]
        prep[l, :, P_SSDN:P_SSDN + 1024] = g("ssd_norm")[l][None]
        prep[l, :, P_MLN:P_MLN + 1024] = g("ml_norm")[l][None]
        prep[l, :, P_MGB:P_MGB + 32] = g("ml_gate_bias")[l].reshape(32)[None]
        pfm[:, l, Q_BADA:Q_BADA + 48] = _fm(g("b_ada")[l])
        pfm[:, l, Q_SCW:Q_SCW + 48] = np.moveaxis(_fm(g("ssd_conv_w")[l]), 1, 2).reshape(128, 48)
        pfm[:, l, Q_SCB:Q_SCB + 12] = _fm(g("ssd_conv_b")[l])
        pfm[:, l, Q_MCW:Q_MCW + 64] = np.moveaxis(_fm(g("ml_conv_w")[l]), 1, 2).reshape(128, 64)
        pfm[:, l, Q_MCB:Q_MCB + 16] = _fm(g("ml_conv_b")[l])
        pfm[:, l, Q_QG] = np.tile(g("att_q_norm")[l], 2)
        pfm[:, l, Q_KG] = np.tile(g("att_k_norm")[l], 2)
    shared["prep"] = prep
    shared["pfm"] = pfm
    shared["fng"] = _fm(g("final_norm"))
    xp, xs = g("x_prompt"), g("x_sample")
    ncore = xs.shape[0]
    maps = []
    for i in range(ncore):
        toks = np.concatenate([xs[i][:SL], xp[2 * i], xp[2 * i + 1]], axis=0)
        x0 = np.ascontiguousarray(toks.reshape(-1, 8, 128).transpose(2, 1, 0))
        cvec = np.stack([_fm(g("c_ctx")), _fm(g("c")[i])], axis=-1)
        st = g("state_ssd")[i][:NL]
        st = st.reshape(NL, 2, 2, 2, 4, 64, 64)
        ssd0 = np.ascontiguousarray(st.transpose(0, 1, 3, 6, 2, 4, 5)).reshape(NL, 2, 128, 512)
        mc = g("state_ml_c")[i][:NL]
        mn = g("state_ml_n")[i][:NL]
        mlc0 = np.concatenate([mc.transpose(0, 1, 3, 2, 4), mn.transpose(0, 1, 3, 2)[..., None]], axis=-1)
        m = dict(shared)
        m.update({
            "x0": x0, "cvec": np.ascontiguousarray(cvec),
            "cache_k": np.ascontiguousarray(g("cache_k")[i][:NL].reshape(NL, 512, 256)),
            "cache_v": np.ascontiguousarray(g("cache_v")[i][:NL].reshape(NL, 512, 256)),
            "ssd0": ssd0, "mlc0": np.ascontiguousarray(mlc0).reshape(NL, 2, 128, 8 * 129),
            "mlm0": np.ascontiguousarray(g("state_ml_m")[i][:NL].reshape(NL, 2, 8, 1)),
        })
        maps.append(m)
    return maps


def _assemble(results, SB, NL):
    SL = 256 * SB
    n = len(results)
    y_s = np.zeros((n, SL, D), np.float32)
    y_p = np.zeros((2 * n, 256, D), np.float32)
    nk = np.zeros((2 * n, NL, 256, 4, 64), np.float32)
    nv = np.zeros((2 * n, NL, 256, 4, 64), np.float32)
    nssd = np.zeros((2 * n, NL, 2, 16, 64, 64), np.float32)
    nc_ = np.zeros((2 * n, NL, 2, 8, 128, 128), np.float32)
    nn_ = np.zeros((2 * n, NL, 2, 8, 128), np.float32)
    nm = np.zeros((2 * n, NL, 2, 8), np.float32)
    for i, r in enumerate(results):
        y = np.asarray(r["yT"]).transpose(2, 1, 0).reshape(-1, D)
        y_s[i] = y[:SL]
        for j in range(2):
            y_p[2 * i + j] = y[SL + 256 * j:SL + 256 * (j + 1)]
            k = np.asarray(r["nk_o"])[j]
            nk[2 * i + j] = k.transpose(0, 3, 2, 1).reshape(NL, 256, 4, 64)
            nv[2 * i + j] = np.asarray(r["nv_o"])[j].reshape(NL, 256, 4, 64)
            s_ = np.asarray(r["nssd_o"])[j].reshape(NL, 2, 2, 64, 2, 4, 64)
            nssd[2 * i + j] = s_.transpose(0, 1, 4, 2, 5, 6, 3).reshape(NL, 2, 16, 64, 64)
            c_ = np.asarray(r["nmlc_o"])[j].reshape(NL, 2, 128, 8, 129)
            nc_[2 * i + j] = c_[..., :128].transpose(0, 1, 3, 2, 4)
            nn_[2 * i + j] = c_[..., 128].transpose(0, 1, 3, 2)
            nm[2 * i + j] = np.asarray(r["nmlm_o"])[j].reshape(NL, 2, 8)
    return (y_p, y_s, nk, nv, nssd, nc_, nn_, nm)


_CACHE = {}


def run(inputs, SB=8, NL=4, debug=False, stop_after=None):
    key = (SB, NL, debug, stop_after)
    if key not in _CACHE:
        _CACHE[key] = build(SB, NL, debug, stop_after)
    nc = _CACHE[key]
    maps = _prepare(inputs, SB, NL)
    res = run_bass_kernel_spmd(nc, maps, core_ids=list(range(len(maps))))
    return _assemble(res.results, SB, NL), res


def kernel(**inputs):
    out, _ = run(inputs, 8, 4, False)
    return out
```

```python
import contextlib
import os
import numpy as np
import concourse.bass as bass
import concourse.mybir as mybir
from concourse.bass_utils import run_bass_kernel_spmd

F32 = mybir.dt.float32
BF16 = mybir.dt.bfloat16
AF = mybir.ActivationFunctionType
ALU = mybir.AluOpType
AX = mybir.AxisListType

D = 1024
KC = 8
EPS = 1e-6
IN_DIM = 11328
FFN_H = 2816
O_SX, O_SZ, O_SB, O_SC, O_SDT = 0, 1024, 2048, 2304, 2560
O_AQ, O_AK, O_AV = 2592, 3616, 3872
O_MQ, O_MK, O_MV, O_MO, O_MG, O_G = 4128, 5152, 6176, 7200, 8224, 8256
P_DTB, P_ALOG, P_SD, P_SSDN, P_MLN, P_MGB, P_TOT = 0, 32, 64, 96, 1120, 2144, 2176
Q_BADA, Q_SCW, Q_SCB, Q_MCW, Q_MCB, Q_QG, Q_KG, Q_TOT = 0, 48, 96, 108, 172, 188, 189, 192
C_ID, C_ONE, C_U, C_L, C_BLK, C_SWP, C_NMF, C_NMB, C_TOT = 0, 128, 256, 384, 512, 640, 768, 896, 1024
NEG = -30000.0
MLK_SCALE = 128.0 ** -0.5


class Buf:
    __slots__ = ("name", "w", "r", "dsem", "dcnt", "last_dma", "excl", "dead")
    registry = []

    def __init__(self, name, excl=False):
        self.name = name
        self.excl = excl
        self.dead = False
        self.w = None
        self.r = []
        self.dsem = None
        self.dcnt = 0
        self.last_dma = None
        Buf.registry.append(self)


class Op:
    __slots__ = ("eng", "fn", "deps", "needs", "event", "isdma", "slot")


class Prog:
    def __init__(self, nc, es):
        self.nc = nc
        self.es = es
        self.ops = []
        self.eng = {"pe": nc.tensor, "act": nc.scalar, "dve": nc.vector, "pool": nc.gpsimd, "sp": nc.sync}
        self.esem = {k: es.enter_context(nc.semaphore("sem_" + k)) for k in self.eng}
        self.nsem = len(self.eng)
        self.dma_bufs = []
        self.free_sems = {"sp": [], "pool": []}
        self.phase_bufs = []
        self.phase_pools = []
        self.seq = {k: 0 for k in self.eng}
        self.waited = {k: {} for k in self.eng}
        self.n_emitted = 0

    def swap_buf(self, old, nb):
        self.phase_bufs.append(nb)
        for i, (b, q) in enumerate(self.dma_bufs):
            if b is old:
                self.dma_bufs[i] = (nb, q)

    def _deps(self, op, reads, writes):
        for b in reads:
            assert not b.dead, "live-range violation (read of recycled tile) " + b.name
        for b in writes:
            assert not b.dead, "live-range violation (write of recycled tile) " + b.name
        deps = {}
        wset = set(id(b) for b in writes)
        for b in reads:
            if b.w is not None:
                deps[id(b.w)] = (b.w, True)
            if b.excl:
                for r in b.r:
                    if r.eng != op.eng and id(r) not in deps:
                        deps[id(r)] = (r, False)
        for b in writes:
            if b.w is not None and id(b.w) not in deps:
                deps[id(b.w)] = (b.w, False)
            for r in b.r:
                if id(r) not in deps:
                    deps[id(r)] = (r, False)
        out = []
        for d, raw in deps.values():
            if d is op:
                continue
            if (not d.isdma) and (not op.isdma) and d.eng == op.eng and op.eng == "pe":
                continue
            d.needs = True
            out.append(d)
        for b in writes:
            b.w = op
            b.r = []
        for b in reads:
            if id(b) not in wset:
                b.r.append(op)
        return out

    def add(self, eng, fn, reads=(), writes=()):
        op = Op()
        op.eng, op.fn, op.needs, op.event, op.isdma, op.slot = eng, fn, False, None, False, None
        op.deps = self._deps(op, list(reads), list(writes))
        self.ops.append(op)
        return op

    def dma(self, q, out, in_, reads, writes, slot):
        nc = self.nc
        op = Op()
        op.eng, op.needs, op.isdma, op.slot = q, True, True, slot
        if slot.dsem is None:
            slot.dsem, slot.dcnt, slot.last_dma = {}, {}, {}
        if q not in slot.dsem:
            if self.free_sems[q]:
                slot.dsem[q], slot.dcnt[q] = self.free_sems[q].pop()
            else:
                slot.dsem[q] = self.es.enter_context(nc.semaphore("dsem_%s_%d" % (q, self.nsem)))
                slot.dcnt[q] = 0
                self.nsem += 1
                assert self.nsem <= 96, "too many semaphores"
            self.dma_bufs.append((slot, q))
        slot.dcnt[q] += 16
        op.event = (slot.dsem[q], slot.dcnt[q])
        e = self.eng[q]
        op.fn = lambda: e.dma_start(out=out, in_=in_)
        op.deps = self._deps(op, list(reads), list(writes))
        ld = slot.last_dma.get(q)
        if ld is not None and ld not in op.deps:
            op.deps.append(ld)
        slot.last_dma[q] = op
        self.ops.append(op)
        return op

    def flush(self):
        pend = set(id(o) for o in self.ops)
        for b in Buf.registry:
            if b.w is not None and id(b.w) in pend:
                b.w.needs = True
            for r in b.r:
                if id(r) in pend:
                    r.needs = True
        last = {}
        for o in self.ops:
            if not o.isdma:
                last[o.eng] = o
        for o in last.values():
            o.needs = True
        seq, waited = self.seq, self.waited
        for op in self.ops:
            e = self.eng[op.eng]
            wd = waited[op.eng]
            for d in op.deps:
                sem, val = d.event
                if wd.get(id(sem), 0) < val:
                    e.wait_ge(sem, val)
                    wd[id(sem)] = val
            ins = op.fn()
            if op.isdma:
                ins.then_inc(op.event[0], 16)
            elif op.needs:
                seq[op.eng] += 1
                ins.then_inc(self.esem[op.eng], 1)
                op.event = (self.esem[op.eng], seq[op.eng])
            op.fn = None
        self.n_emitted += len(self.ops)
        self.ops = []
        evs = [(self.esem[k], seq[k], k) for k in self.eng if seq[k] > 0]
        evs += [(b.dsem[q], b.dcnt[q], None) for b, q in self.dma_bufs]
        for k, e in self.eng.items():
            wd = waited[k]
            for sem, val, owner in evs:
                if owner == k and k != "pool":
                    continue
                if wd.get(id(sem), 0) < val:
                    e.wait_ge(sem, val)
                    wd[id(sem)] = val
        ph = set(id(b) for b in self.phase_bufs)
        for b in self.phase_bufs:
            if b.dsem is not None:
                for q in b.dsem:
                    self.free_sems[q].append((b.dsem[q], b.dcnt[q]))
                b.dsem = None
        self.dma_bufs = [(b, q) for b, q in self.dma_bufs if id(b) not in ph]
        Buf.registry = [b for b in Buf.registry if id(b) not in ph]
        self.phase_bufs = []
        for p_ in self.phase_pools:
            p_.dead = True
        self.phase_pools = []


class TPool:
    def __init__(self, P, nc, es, name, shape, dtype, n, phase=True):
        self.t = []
        for i in range(n):
            t = es.enter_context(nc.sbuf_tensor("%s_%d" % (name, i), list(shape), dtype))
            b = Buf("%s_%d" % (name, i))
            if phase:
                P.phase_bufs.append(b)
            self.t.append((t, b))
        self.i = 0
        self.dead = False
        self.name = name
        self.P = P
        self.phase = phase
        if phase:
            P.phase_pools.append(self)

    def get(self):
        assert not self.dead, "stale pool " + self.name
        k = self.i % len(self.t)
        t, old = self.t[k]
        if self.i >= len(self.t):
            nb = Buf(old.name)
            nb.w, nb.r, nb.dsem, nb.dcnt, nb.last_dma = old.w, old.r, old.dsem, old.dcnt, old.last_dma
            old.dead = True
            old.dsem = None
            self.P.swap_buf(old, nb)
            self.t[k] = (t, nb)
        self.i += 1
        return self.t[k]


def build(SB, NL, debug=False, stop_after=None):
    NBLK = SB + 2
    NCH = 2 * NBLK
    T = 256 * NBLK
    SL = 256 * SB
    NCTX = 512
    SEQS = [(0, SB, True, 1), (SB, 1, False, 0), (SB + 1, 1, False, 0)]
    hbase = [0, SL + 3, SL + 3 + 259]
    NCOL = SL + 3 + 259 * 2

    def blk_seq(b):
        return 0 if b < SB else (1 if b == SB else 2)

    def hcol(b):
        s = blk_seq(b)
        return hbase[s] + 256 * (b - SEQS[s][0])

    nc = bass.Bass("TRN2", target_bir_lowering=False)
    es = contextlib.ExitStack()
    P = Prog(nc, es)

    def din(name, shape, dt=F32):
        return nc.dram_tensor(name, list(shape), dt, kind="ExternalInput").ap()

    def dout(name, shape, dt=F32):
        return nc.dram_tensor(name, list(shape), dt, kind="ExternalOutput").ap()

    def dscr(name, shape, dt):
        return nc.dram_tensor(name, list(shape), dt, kind=("ExternalOutput" if debug else "Internal")).ap()

    x0 = din("x0", [128, KC, T])
    cvec = din("cvec", [128, KC, 2])
    w_ada = din("w_ada", [NL, D, 6 * D])
    w_in = din("w_in", [NL, D, IN_DIM])
    w_br = din("w_branch", [NL, 3, D, D])
    w_out = din("w_out", [NL, D, D])
    w_f1 = din("w_ffn_in", [NL, D, 2 * FFN_H])
    w_f2 = din("w_ffn_out", [NL, FFN_H, D])
    prep = din("prep", [NL, 128, P_TOT])
    pfm = din("pfm", [128, NL, Q_TOT])
    fng = din("fng", [128, KC])
    cst = din("cst", [128, C_TOT])
    cstf = din("cstf", [128, 128])
    ropec = din("ropec", [128, SL])
    ropes = din("ropes", [128, SL])
    cache_k = din("cache_k", [NL, NCTX, 256])
    cache_v = din("cache_v", [NL, NCTX, 256])
    ssd0 = din("ssd0", [NL, 2, 128, 512])
    mlc0 = din("mlc0", [NL, 2, 128, 8 * 129])
    mlm0 = din("mlm0", [NL, 2, 8, 1])

    yT = dout("yT", [128, KC, T])
    nk_o = dout("nk_o", [2, NL, 128, 2, 256])
    nv_o = dout("nv_o", [2, NL, 256, 256])
    nssd_o = dout("nssd_o", [2, NL, 2, 128, 512])
    nmlc_o = dout("nmlc_o", [2, NL, 2, 128, 8 * 129])
    nmlm_o = dout("nmlm_o", [2, NL, 2, 8, 1])

    xres = dscr("xres", [NBLK, 128, KC, 256], F32)
    sx_tok = dscr("sx_tok", [NCH, 128, 1024], BF16)
    sbcT = dscr("sbcT", [NBLK, 128, 4, 256], BF16)
    sb_tok = dscr("sb_tok", [NCH, 128, 256], BF16)
    sz_tok = dscr("sz_tok", [NCH, 128, 1024], BF16)
    dtg_tok = dscr("dtg_tok", [NCH, 128, 64], F32)
    qT_d = dscr("qT_d", [NBLK, 128, 8, 256], BF16)
    kT_d = dscr("kT_d", [NBLK, 128, 4, 256], BF16)
    v_tok = dscr("v_tok", [NCH, 128, 256], BF16)
    mqT = dscr("mqT", [NBLK, 128, 8, 256], BF16)
    mkT = dscr("mkT", [NBLK, 128, 8, 256], BF16)
    mk_tok = dscr("mk_tok", [NCH, 128, 1024], BF16)
    mv_tok = dscr("mv_tok", [NCH, 128, 1024], BF16)
    mo_tok = dscr("mo_tok", [NCH, 128, 1024], BF16)
    gT = dscr("gT", [3, NBLK, 128, 8, 256], BF16)
    ybr = dscr("ybr", [3, NBLK, 128, 8, 256], BF16)
    yf_d = dscr("yf_d", [NCH, 128, 1024], F32)
    hf_d = dscr("hf_d", [NCH, 128, 1024], F32)
    mgd_d = dscr("mgd_d", [NBLK, 128, 8, 256], BF16)

    dbufs = {}

    def DB(*key):
        if key not in dbufs:
            dbufs[key] = Buf(str(key))
        return dbufs[key]

    cur = {"es": es, "n": 0}

    def sb(name, shape, dt=F32):
        ph = cur["es"] is not es
        cur["n"] += 1
        b = Buf(name)
        if ph:
            P.phase_bufs.append(b)
        t = cur["es"].enter_context(nc.sbuf_tensor("%s_%d" % (name, cur["n"]), list(shape), dt))
        return t, b

    def mkpool(name, shape, dt, n):
        cur["n"] += 1
        return TPool(P, nc, cur["es"], "%s%d" % (name, cur["n"]), shape, dt, n, phase=(cur["es"] is not es))


    hT, _ = sb("hT", [128, KC, NCOL], BF16)
    hB = [Buf("hT%d" % b) for b in range(NBLK)]
    hHalo = Buf("hHalo")
    cb, cbB = sb("cb", [128, C_TOT], BF16)
    cf, cfB = sb("cf", [128, 128], F32)
    onesf, onesfB = sb("onesf", [128, 128], F32)
    modT, modB = sb("modT", [128, NL, 2, 48], F32)
    pfm_s, pfmB = sb("pfm_s", [128, NL, Q_TOT], F32)
    fng_s, fngB = sb("fng_s", [128, KC], F32)
    ropec_s = ropecB = ropes_s = ropesB = None
    prep_t, prepB = sb("prep_t", [128, P_TOT], F32)

    ps_all = es.enter_context(nc.psum_tensor("ps_all", [128, 4096], F32))
    psB = [Buf("psb%d" % i, excl=True) for i in range(8)]
    ps_state = {"A": 0, "B": 0}

    def psum_at(i, n=1):
        return ps_all[:, i * 512:(i + n) * 512], psB[i:i + n]

    def psA():
        i = ps_state["A"] % 6
        ps_state["A"] += 1
        return psum_at(i)

    def psBr():
        i = 6 + ps_state["B"] % 2
        ps_state["B"] += 1
        return psum_at(i)

    ident = cb[:, C_ID:C_ID + 128]
    ones = cb[:, C_ONE:C_ONE + 128]
    Utri = cb[:, C_U:C_U + 128]
    Ltri = cb[:, C_L:C_L + 128]
    blk2 = cb[:, C_BLK:C_BLK + 128]
    pswap = cb[:, C_SWP:C_SWP + 128]
    nmask = [cb[:, C_NMF:C_NMF + 128], cb[:, C_NMB:C_NMB + 128]]
    tri = [Utri, Ltri]

    V, A, G, PE = nc.vector, nc.scalar, nc.gpsimd, nc.tensor

    def mm(out, lhsT, rhs, reads, writes, start=True, stop=True):
        P.add("pe", lambda: PE.matmul(out, lhsT=lhsT, rhs=rhs, start=start, stop=stop), reads, writes)

    def tp(out, in_, idn, reads, writes):
        P.add("pe", lambda: PE.transpose(out, in_, idn), reads, writes)

    def act(out, in_, func, reads, writes, bias=None, scale=None, accum=None):
        kw = {}
        if bias is not None:
            kw["bias"] = bias
        if scale is not None:
            kw["scale"] = scale
        if accum is not None:
            kw["accum_out"] = accum
        P.add("act", lambda: A.activation(out=out, in_=in_, func=func, **kw), reads, writes)

    def tt(eng, out, in0, in1, op, reads, writes):
        e = V if eng == "dve" else G
        P.add(eng, lambda: e.tensor_tensor(out=out, in0=in0, in1=in1, op=op), reads, writes)

    def ts(eng, out, in0, s1, op0, reads, writes, s2=None, op1=None):
        e = V if eng == "dve" else G
        if op1 is None:
            P.add(eng, lambda: e.tensor_scalar(out=out, in0=in0, scalar1=s1, scalar2=None, op0=op0), reads, writes)
        else:
            P.add(eng, lambda: e.tensor_scalar(out=out, in0=in0, scalar1=s1, scalar2=s2, op0=op0, op1=op1),
                  reads, writes)

    def stt(out, in0, scalar, in1, op0, op1, reads, writes):
        P.add("dve", lambda: V.scalar_tensor_tensor(out=out, in0=in0, scalar=scalar, in1=in1, op0=op0, op1=op1),
              reads, writes)

    def cp(eng, out, in_, reads, writes):
        if eng == "act":
            P.add("act", lambda: A.copy(out=out, in_=in_), reads, writes)
        else:
            e = V if eng == "dve" else G
            P.add(eng, lambda: e.tensor_copy(out=out, in_=in_), reads, writes)

    def memset(eng, ap, val, writes):
        e = V if eng == "dve" else G
        P.add(eng, lambda: e.memset(ap, val), (), writes)

    def rsqrt(v_ap, v_buf, shape2, mean_scale):
        p, n = shape2
        ts("dve", v_ap, v_ap, mean_scale, ALU.mult, [v_buf], [v_buf], s2=EPS, op1=ALU.add)
        act(v_ap, v_ap, AF.Ln, [v_buf], [v_buf])
        act(v_ap, v_ap, AF.Exp, [v_buf], [v_buf], scale=-0.5)

    def flat(ap3):
        return ap3.rearrange("p a b -> p (a b)")

    slab_p = None

    def alloc_slab(n=3):
        nonlocal slab_p
        slab_p = mkpool("slab", [128, KC, 1024], BF16, n)

    def load_slab(src2d, ncols, dst_off=0, slab=None):
        if slab is None:
            slab = slab_p.get()
        t, bfr = slab
        P.dma("pool", t[:, :, dst_off:dst_off + ncols], src2d.rearrange("(k p) n -> p k n", p=128), [], [bfr], bfr)
        return slab

    def startup():
        cst_st, cst_stB = sb("cst_st", [128, C_TOT], F32)
        P.dma("sp", cst_st[:], cst[:, :], [], [cst_stB], cst_stB)
        cp("dve", cb[:], cst_st[:], [cst_stB], [cbB])
        P.dma("sp", cf[:], cstf[:, :], [], [cfB], cfB)
        P.dma("sp", pfm_s[:], pfm[:, :, :], [], [pfmB], pfmB)
        P.dma("sp", fng_s[:], fng[:, :], [], [fngB], fngB)
        memset("dve", onesf[:], 1.0, [onesfB])
        memset("pool", flat(hT[:]), 0.0, hB + [hHalo])
        alloc_slab()
        cv, cvB = sb("cv", [128, KC, 2], F32)
        cvb, cvbB = sb("cvb", [128, KC, 2], BF16)
        P.dma("sp", cv[:], cvec[:, :, :], [], [cvB], cvB)
        act(flat(cvb[:]), flat(cv[:]), AF.Silu, [cvB], [cvbB])
        for l in range(NL):
            pm, pmB = psA()
            for s6 in range(6):
                wt, wB = load_slab(w_ada[l, :, s6 * 1024:(s6 + 1) * 1024], 1024)
                for f in range(8):
                    fc = s6 * 8 + f
                    for k in range(KC):
                        mm(pm[:, fc * 2:fc * 2 + 2], wt[:, k, f * 128:(f + 1) * 128], cvb[:, k, :],
                           [wB, cvbB], pmB, start=(k == 0), stop=(k == KC - 1))
            for m in range(2):
                tt("dve", modT[:, l, m, :], pm.rearrange("p (f m) -> p m f", m=2)[:, m, 0:48],
                   pfm_s[:, l, Q_BADA:Q_BADA + 48], ALU.add, pmB + [pfmB], [modB])
            for o in (8, 32):
                ts("dve", modT[:, l, :, o:o + 8], modT[:, l, :, o:o + 8], 1.0, ALU.add, [modB], [modB])
        alloc_norm()
        for b in range(NBLK):
            xt, xtB = xt_p.get()
            P.dma("sp", xt[:], x0[:, :, b * 256:(b + 1) * 256], [], [xtB], xtB)
            P.dma("sp", xres[b], xt[:], [xtB], [DB("x", b)], xtB)
            norm_block(xt, xtB, b, 0, 0)

    xt_p = sq_p = xn_p = rs_p = None

    def alloc_norm():
        nonlocal xt_p, sq_p, xn_p, rs_p
        xt_p = mkpool("xt", [128, KC, 256], F32, 2)
        sq_p = mkpool("sq", [128, KC, 256], BF16, 2)
        xn_p = mkpool("xn", [128, KC, 256], F32, 2)
        rs_p = mkpool("rs", [128, 256], F32, 2)


    def norm_block(xt, xtB, b, l, which, final=False, out_tile=None):
        m = SEQS[blk_seq(b)][3]
        sq, sqB = sq_p.get()
        act(flat(sq[:]), flat(xt[:]), AF.Square, [xtB], [sqB])
        pn, pnB = psBr()
        for k in range(KC):
            mm(pn[:, 0:256], ones, sq[:, k, :], [cbB, sqB], pnB, start=(k == 0), stop=(k == KC - 1))
        rs, rsB = rs_p.get()
        cp("dve", rs[:], pn[:, 0:256], pnB, [rsB])
        rsqrt(rs[:], rsB, (128, 256), 1.0 / D)
        xn, xnB = xn_p.get()
        tt("dve", xn[:], xt[:], rs[:].unsqueeze(1).broadcast_to([128, KC, 256]), ALU.mult, [xtB, rsB], [xnB])
        if final:
            ot, otB = out_tile
            for k in range(KC):
                ts("pool", ot[:, k, :], xn[:, k, :], fng_s[:, k:k + 1], ALU.mult, [xnB, fngB], [otB])
            return
        c0 = hcol(b) + 1
        so, sh = (8, 0) if which == 0 else (32, 24)
        for k in range(KC):
            ts("pool", hT[:, k, c0:c0 + 256], xn[:, k, :], modT[:, l, m, so + k:so + k + 1], ALU.mult,
               [xnB, modB], [hB[b]], s2=modT[:, l, m, sh + k:sh + k + 1], op1=ALU.add)

    def hwin_bufs(b):
        s = blk_seq(b)
        f, n = SEQS[s][0], SEQS[s][1]
        r = [hB[b], hHalo]
        if b > f:
            r.append(hB[b - 1])
        if b < f + n - 1:
            r.append(hB[b + 1])
        return r

    st_p = acc_p = cvo_p = sm_p = cv8_p = qr_p = sq4_p = dg_p = u_p = None

    def alloc_d1():
        nonlocal st_p, acc_p, cvo_p, sm_p, cv8_p, ropec_s, ropecB, ropes_s, ropesB, qr_p, sq4_p, dg_p, u_p
        dg_p = mkpool("dg", [128, 4, 128], BF16, 16)
        u_p = mkpool("ub", [128, 260], BF16, 4)
        dg_live.clear()
        qr_p = mkpool("qr", [128, 4, 256], F32, 6)
        sq4_p = mkpool("sq4", [128, 4, 256], BF16, 4)
        ropec_s, ropecB = sb("ropec_s", [128, SL], F32)
        ropes_s, ropesB = sb("ropes_s", [128, SL], F32)
        P.dma("sp", ropec_s[:], ropec[:, :], [], [ropecB], ropecB)
        P.dma("sp", ropes_s[:], ropes[:, :], [], [ropesB], ropesB)
        st_p = mkpool("st", [128, 2048], BF16, 6)
        acc_p = None
        cvo_p = mkpool("cvo", [128, 256], BF16, 4)
        sm_p = mkpool("sm", [128, 256], F32, 4)
        cv8_p = None


    def proj_fm(wt, wB, wcol, b, n, halo):
        pp, ppB = psA()
        c0 = hcol(b) + (0 if halo else 1)
        rb = hwin_bufs(b) if halo else [hB[b]]
        for k in range(KC):
            mm(pp[:, 0:n], wt[:, k, wcol:wcol + 128], hT[:, k, c0:c0 + n], [wB] + rb, ppB,
               start=(k == 0), stop=(k == KC - 1))
        return pp, ppB

    def proj_tm(wt, wB, wcol, ncol, c):
        b, cc = c // 2, c % 2
        pp, ppB = psA()
        c0 = hcol(b) + 1 + 128 * cc
        for k in range(KC):
            mm(pp[:, 0:ncol], hT[:, k, c0:c0 + 128], wt[:, k, wcol:wcol + ncol], [wB, hB[b]], ppB,
               start=(k == 0), stop=(k == KC - 1))
        return pp, ppB

    dg_live = {}

    def conv_block(wt, wB, b, nfc, l, wofs, bofs, fcbase, stv, stB):
        def conv_b(ub, ubB, fci, dst):
            key = (l, wofs, fci)
            if key not in dg_live:
                dg, dgB = dg_p.get()
                for t in range(4):
                    ts("dve", dg[:, t, :], ident, pfm_s[:, l, wofs + fci * 4 + t:wofs + fci * 4 + t + 1], ALU.mult,
                       [cbB, pfmB], [dgB])
                dg_live[key] = (dg, dgB)
            dg, dgB = dg_live[key]
            pc, pcB = psBr()
            for t in range(4):
                mm(pc[:, 0:256], dg[:, t, :], ub[:, t:t + 256], [dgB, ubB], pcB, start=(t == 0), stop=(t == 3))
            act(dst, pc[:, 0:256], AF.Silu, pcB, [stB], bias=pfm_s[:, l, bofs + fci:bofs + fci + 1])

        pend = None
        for fc in range(nfc):
            pp, ppB = proj_fm(wt, wB, fc * 128, b, 259, True)
            ub, ubB = u_p.get()
            cp("act", ub[:, 0:259], pp[:, 0:259], ppB, [ubB])
            if pend is not None:
                conv_b(*pend)
            pend = (ub, ubB, fcbase + fc, stv[:, fc, :])
        conv_b(*pend)

    def transposes_to(dst, dstB, srcs, reads, scale=None, bank=None):
        n = len(srcs)
        pt, ptB = (psBr() if bank is None else psum_at(bank))
        ptb = pt.bitcast(BF16)
        for i, s_ in enumerate(srcs):
            tp(ptb[:, i * 128:(i + 1) * 128], s_, ident, reads + [cbB], ptB)
        if scale is None:
            cp("act", dst, ptb[:, 0:128 * n], ptB, [dstB])
        else:
            ts("dve", dst, ptb[:, 0:128 * n], scale, ALU.mult, ptB, [dstB])

    D1STOP = int(os.environ.get("K_D1STOP", "99"))

    def d1_layer(l):
        W = w_in[l]

        def ld_simple(off, n):
            return lambda: load_slab(W[:, off:off + n], n)

        def ld_dtg():
            slab = slab_p.get()
            load_slab(W[:, O_SDT:O_SDT + 32], 32, 0, slab)
            return load_slab(W[:, O_MG:O_MG + 32], 32, 32, slab)

        def ld_ak():
            slab = slab_p.get()
            load_slab(W[:, O_AK:O_AK + 256], 256, 0, slab)
            r = None
            for f in range(2):
                load_slab(W[:, O_AK + f * 128 + 64:O_AK + f * 128 + 128], 64, 256 + f * 128, slab)
                r = load_slab(W[:, O_AK + f * 128:O_AK + f * 128 + 64], 64, 256 + f * 128 + 64, slab)
            return r

        loaders = [ld_simple(O_SX, 1024), ld_simple(O_SB, 512), ld_simple(O_SZ, 1024), ld_dtg,
                   ld_simple(O_AQ, 1024), ld_ak, ld_simple(O_AV, 256), ld_simple(O_MQ, 1024),
                   ld_simple(O_MK, 1024), ld_simple(O_MV, 1024), ld_simple(O_MO, 1024),
                   ld_simple(O_G, 1024), ld_simple(O_G + 1024, 1024), ld_simple(O_G + 2048, 1024)]
        loaded = {}

        def get_w(i):
            for k in range(i + 2):
                if k < len(loaders) and k not in loaded:
                    loaded[k] = loaders[k]()
            return loaded[i]

        wt, wB = get_w(0)
        for b in range(NBLK):
            sv, svB = st_p.get()
            svv = sv[:, 0:2048].rearrange("p (f t) -> p f t", f=8)
            conv_block(wt, wB, b, 8, l, Q_SCW, Q_SCB, 0, svv, svB)
            for cc in range(2):
                st, stB = st_p.get()
                transposes_to(st[:, 0:1024], stB, [svv[:, f, cc * 128:(cc + 1) * 128] for f in range(8)], [svB])
                P.dma("sp", sx_tok[2 * b + cc], st[:, 0:1024], [stB], [DB("sx", 2 * b + cc)], stB)
        if D1STOP <= 1:
            return
        wt, wB = get_w(1)
        for b in range(NBLK):
            st, stB = st_p.get()
            stv = st[:, 0:1024].rearrange("p (f t) -> p f t", f=4)
            conv_block(wt, wB, b, 4, l, Q_SCW, Q_SCB, 8, stv, stB)
            P.dma("sp", sbcT[b], stv, [stB], [DB("sbcT", b)], stB)
            for cc in range(2):
                s2, s2B = st_p.get()
                transposes_to(s2[:, 0:256], s2B, [stv[:, f, cc * 128:(cc + 1) * 128] for f in range(2)], [stB])
                P.dma("sp", sb_tok[2 * b + cc], s2[:, 0:256], [s2B], [DB("sbt", 2 * b + cc)], s2B)
        if D1STOP <= 2:
            return
        wt, wB = get_w(2)
        for c in range(NCH):
            st, stB = st_p.get()
            for hh in range(2):
                pp, ppB = proj_tm(wt, wB, hh * 512, 512, c)
                act(st[:, hh * 512:(hh + 1) * 512], pp[:, 0:512], AF.Silu, ppB, [stB])
            P.dma("sp", sz_tok[c], st[:, 0:1024], [stB], [DB("sz", c)], stB)
        if D1STOP <= 3:
            return
        wt, wB = get_w(3)
        for c in range(NCH):
            sm, smB = sm_p.get()
            pp, ppB = proj_tm(wt, wB, 0, 64, c)
            cp("dve", sm[:, 0:64], pp[:, 0:64], ppB, [smB])
            P.dma("sp", dtg_tok[c], sm[:, 0:64], [smB], [DB("dtg", c)], smB)
        if D1STOP <= 4:
            return
        def qk_s1(wt, wB, f0, b, gofs, stv, stB, nk_out):
            n = 4
            qr, qrB = qr_p.get()
            sq, sqB = sq4_p.get()
            rs, rsB = qr_p.get()
            for i in range(n):
                pp, ppB = proj_fm(wt, wB, (f0 + i) * 128, b, 256, False)
                act(sq[:, i, :], pp[:, 0:256], AF.Square, ppB, [sqB])
                cp("dve", qr[:, i, :], pp[:, 0:256], ppB, [qrB])
            for i in range(n):
                pn, pnB = psBr()
                mm(pn[:, 0:256], blk2, sq[:, i, :], [cbB, sqB], pnB)
                ts("dve", rs[:, i, :], pn[:, 0:256], 1.0 / 64, ALU.mult, pnB, [rsB], s2=EPS, op1=ALU.add)
            return qr, qrB, rs, rsB

        def qk_s2(st1, wt, wB, f0, b, gofs, stv, stB, nk_out):
            n = 4
            qr, qrB, rs, rsB = st1
            act(flat(rs[:]), flat(rs[:]), AF.Ln, [rsB], [rsB])
            act(flat(rs[:]), flat(rs[:]), AF.Exp, [rsB], [rsB], scale=-0.5)
            stt(qr[:], qr[:], pfm_s[:, l, gofs:gofs + 1], rs[:], ALU.mult, ALU.mult, [qrB, pfmB, rsB], [qrB])
            s_ = blk_seq(b)
            if nk_out and s_ != 0:
                P.dma("sp", nk_o[s_ - 1, l], qr[:, 0:2, :], [qrB], [DB("nk", s_, l)], qrB)
            if s_ != 0:
                cp("act", stv[:, f0:f0 + n, :], qr[:], [qrB], [stB])
                return
            qb, qbB = sq4_p.get()
            cp("act", flat(qb[:]), flat(qr[:]), [qrB], [qbB])
            a2, a2B = qr_p.get()
            t0 = 256 * b
            for i in range(n):
                pw, pwB = psBr()
                mm(pw[:, 0:256], pswap, qb[:, i, :], [cbB, qbB], pwB)
                tt("dve", a2[:, i, :], pw[:, 0:256], ropes_s[:, t0:t0 + 256], ALU.mult, pwB + [ropesB], [a2B])
            tt("dve", qr[:], qr[:], ropec_s[:, t0:t0 + 256].unsqueeze(1).broadcast_to([128, n, 256]), ALU.mult,
               [qrB, ropecB], [qrB])
            tt("pool", stv[:, f0:f0 + n, :], qr[:], a2[:], ALU.add, [qrB, a2B], [stB])

        def qk_job(wt, wB, nfc, gofs, dst_d, key, nk_out):
            groups = [(b, f0) for b in range(NBLK) for f0 in range(0, nfc, 4)]
            stt_ = {}

            def args(b, f0):
                if b not in stt_:
                    st, stB = st_p.get()
                    stt_[b] = (st[:, 0:nfc * 256].rearrange("p (f t) -> p f t", f=nfc), stB)
                stv, stB = stt_[b]
                return (wt, wB, f0, b, gofs, stv, stB, nk_out)

            def finish(b, f0, a, st1):
                qk_s2(st1, *a)
                if f0 + 4 >= nfc:
                    P.dma("sp", dst_d[b], a[5], [a[6]], [DB(key, b)], a[6])

            pend = None
            for (b, f0) in groups:
                a = args(b, f0)
                st1 = qk_s1(*a)
                if pend is not None:
                    finish(*pend)
                pend = (b, f0, a, st1)
            finish(*pend)

        wt, wB = get_w(4)
        qk_job(wt, wB, 8, Q_QG, qT_d, "qT", False)
        if D1STOP <= 5:
            return
        wt, wB = get_w(5)
        qk_job(wt, wB, 4, Q_KG, kT_d, "kT", True)
        if D1STOP <= 6:
            return
        wt, wB = get_w(6)
        for c in range(NCH):
            st, stB = st_p.get()
            pp, ppB = proj_tm(wt, wB, 0, 256, c)
            cp("act", st[:, 0:256], pp[:, 0:256], ppB, [stB])
            P.dma("sp", v_tok[c], st[:, 0:256], [stB], [DB("v", c)], stB)
            s_ = blk_seq(c // 2)
            if s_ != 0:
                sm, smB = sm_p.get()
                cp("dve", sm[:], pp[:, 0:256], ppB, [smB])
                t0 = (c % 2) * 128
                P.dma("sp", nv_o[s_ - 1, l, t0:t0 + 128, :], sm[:], [smB], [DB("nv", s_, l, c)], smB)
        if D1STOP <= 7:
            return
        for which, off, dstT in ((0, O_MQ, mqT), (1, O_MK, mkT)):
            wt, wB = get_w(7 + which)
            for b in range(NBLK):
                st, stB = st_p.get()
                stv = st[:, 0:2048].rearrange("p (f t) -> p f t", f=8)
                conv_block(wt, wB, b, 8, l, Q_MCW, Q_MCB, which * 8, stv, stB)
                P.dma("sp", dstT[b], stv, [stB], [DB("mqT" if which == 0 else "mkT", b)], stB)
                if which == 1:
                    for cc in range(2):
                        s2, s2B = st_p.get()
                        transposes_to(s2[:, 0:1024], s2B, [stv[:, f, cc * 128:(cc + 1) * 128] for f in range(8)],
                                      [stB], scale=MLK_SCALE)
                        P.dma("sp", mk_tok[2 * b + cc], s2[:, 0:1024], [s2B], [DB("mkt", 2 * b + cc)], s2B)
        if D1STOP <= 8:
            return
        for wi_, (off, dstT, key, fn) in enumerate(((O_MV, mv_tok, "mv", None), (O_MO, mo_tok, "mo", AF.Sigmoid))):
            wt, wB = get_w(9 + wi_)
            for c in range(NCH):
                st, stB = st_p.get()
                for hh in range(2):
                    pp, ppB = proj_tm(wt, wB, hh * 512, 512, c)
                    if fn is None:
                        cp("act", st[:, hh * 512:(hh + 1) * 512], pp[:, 0:512], ppB, [stB])
                    else:
                        act(st[:, hh * 512:(hh + 1) * 512], pp[:, 0:512], fn, ppB, [stB])
                P.dma("sp", dstT[c], st[:, 0:1024], [stB], [DB(key, c)], stB)
        if D1STOP <= 9:
            return
        for n in range(3):
            wt, wB = get_w(11 + n)
            for b in range(NBLK):
                st, stB = st_p.get()
                stv = st[:, 0:2048].rearrange("p (f t) -> p f t", f=8)
                for fc in range(8):
                    pp, ppB = proj_fm(wt, wB, fc * 128, b, 256, False)
                    act(stv[:, fc, :], pp[:, 0:256], AF.Sigmoid, ppB, [stB])
                P.dma("sp", gT[n, b], stv, [stB], [DB("gT", n, b)], stB)

    nsb = sb

    dtg_s = dtgB = g_dt = g_dtB = g_da = g_daB = g_dab = g_dabB = g_a = g_aB = g_acum = g_acumB = g_tot = g_totB = g_ea = g_eaB = g_w = g_wB = g_edec = g_edecB = g_tmp = g_tmpB = None

    def alloc_scan():
        nonlocal dtg_s, dtgB, g_dt, g_dtB, g_da, g_daB, g_dab, g_dabB, g_a, g_aB, g_acum, g_acumB, g_tot, g_totB, g_ea, g_eaB, g_w, g_wB, g_edec, g_edecB, g_tmp, g_tmpB
        dtg_s, dtgB = nsb("dtg_s", [128, NCH, 64])
        g_dt, g_dtB = nsb("g_dt", [128, NCH, 32])
        g_da, g_daB = nsb("g_da", [128, NCH, 32])
        g_dab, g_dabB = nsb("g_dab", [128, NCH, 32], BF16)
        g_a, g_aB = nsb("g_a", [128, 32])
        g_acum, g_acumB = nsb("g_acum", [128, 2, NCH, 16])
        g_tot, g_totB = nsb("g_tot", [128, 2, NCH, 16])
        g_ea, g_eaB = nsb("g_ea", [128, 2, NCH, 16])
        g_w, g_wB = nsb("g_w", [128, 2, NCH, 16])
        g_edec, g_edecB = nsb("g_edec", [128, 2, NCH, 8])
        g_tmp, g_tmpB = nsb("g_tmp", [128, 2, NCH, 16])


    def run_sweeps(chunk_loads, chunkA, chunkB, seq_begin, seq_end, PF=2):
        steps = []
        for s in range(3):
            n = SEQS[s][1]
            orders = [chunk_order(s, 0), chunk_order(s, 1)]
            for i in range(2 * n):
                for d in range(2):
                    steps.append((s, d, orders[d][i], i >= n))
        loaded, fronts = {}, {}
        for k in range(min(PF, len(steps))):
            loaded[k] = chunk_loads(*steps[k])
        fronts[0] = chunkA(*steps[0], loaded[0])
        for k, st in enumerate(steps):
            if k + PF < len(steps):
                loaded[k + PF] = chunk_loads(*steps[k + PF])
            if k + 1 < len(steps):
                fronts[k + 1] = chunkA(*steps[k + 1], loaded[k + 1])
            if k == 0 or steps[k - 1][0] != st[0]:
                seq_begin(st[0])
            chunkB(*st, loaded.pop(k), fronts.pop(k))
            if k == len(steps) - 1 or steps[k + 1][0] != st[0]:
                seq_end(st[0])

    def chunk_order(s, d):
        f, n = SEQS[s][0], SEQS[s][1]
        cs = list(range(2 * f, 2 * (f + n)))
        return cs if d == 0 else cs[::-1]

    xk_p = xk2_p = bt_p = bct_p = big_p = arg_p = yo_p = tok_p = ytT_p = s1_p = None

    def alloc_mix(ssd=True):
        nonlocal xk_p, xk2_p, bt_p, bct_p, big_p, arg_p, yo_p, tok_p, ytT_p, s1_p, cvo_p
        cvo_p = mkpool("cvo", [128, 256], BF16, 6)
        xk_p = mkpool("xk", [128, 1024], BF16, 5 if ssd else 8)
        xk2_p = mkpool("xk2", [128, 1024], BF16, 6)
        if ssd:
            bt_p = mkpool("bt", [128, 256], BF16, 5)
            bct_p = mkpool("bct", [128, 4, 128], BF16, 5)
            big_p = mkpool("big", [128, 2048], BF16, 6)
            arg_p = mkpool("arg", [128, 2048], F32, 2)
        yo_p = mkpool("yo", [128, 1024], F32, 6 if ssd else 5)
        tok_p = mkpool("tok", [128, 1024], BF16, 6)
        ytT_p = mkpool("ytT", [128, 8, 128], BF16, 2)
        s1_p = mkpool("s1", [128, 8], F32, 10)


    def out_transposed(yn, ynB, br, c, bank=7):
        b, cc = c // 2, c % 2
        yt, ytB = ytT_p.get()
        transposes_to(flat(yt[:]), ytB, [yn[:, f * 128:(f + 1) * 128] for f in range(8)], [ynB], bank=bank)
        P.dma("sp", ybr[br, b, :, :, cc * 128:(cc + 1) * 128], yt[:], [ytB], [DB("ybr", br, b)], ytB)

    Hs = Hb = None

    def alloc_ssd():
        nonlocal Hs, Hb
        Hs = [nsb("Hs%d" % d, [128, 2, 256]) for d in range(2)]
        Hb = [nsb("Hb%d" % d, [128, 2, 256], BF16) for d in range(2)]


    def ssd_layer(l, pr, prB):
        P.dma("sp", dtg_s[:], dtg_tok.rearrange("c p n -> p c n"), [DB("dtg", c) for c in range(NCH)],
              [dtgB], dtgB)
        tt("dve", g_dt[:], dtg_s[:, :, 0:32], pr[:, P_DTB:P_DTB + 32].unsqueeze(1).broadcast_to([128, NCH, 32]),
           ALU.add, [dtgB, prB], [g_dtB])
        act(flat(g_dt[:]), flat(g_dt[:]), AF.Exp, [g_dtB], [g_dtB])
        act(flat(g_dt[:]), flat(g_dt[:]), AF.Ln, [g_dtB], [g_dtB], bias=1.0)
        act(g_a[:], pr[:, P_ALOG:P_ALOG + 32], AF.Exp, [prB], [g_aB])
        ts("dve", g_a[:], g_a[:], -1.0, ALU.mult, [g_aB], [g_aB])
        tt("dve", g_da[:], g_dt[:], g_a[:].unsqueeze(1).broadcast_to([128, NCH, 32]), ALU.mult,
           [g_dtB, g_aB], [g_daB])
        cp("dve", g_dab[:], g_da[:], [g_daB], [g_dabB])
        for d in range(2):
            pa, paB = psA()
            mm(pa[:, 0:NCH * 16].rearrange("p (c h) -> p c h", h=16), tri[d], g_dab[:, :, d * 16:(d + 1) * 16],
               [cbB, g_dabB], paB)
            cp("dve", g_acum[:, d], pa[:, 0:NCH * 16].rearrange("p (c h) -> p c h", h=16), paB, [g_acumB])
            pb, pbB = psA()
            mm(pb[:, 0:NCH * 16].rearrange("p (c h) -> p c h", h=16), ones, g_dab[:, :, d * 16:(d + 1) * 16],
               [cbB, g_dabB], pbB)
            cp("dve", g_tot[:, d], pb[:, 0:NCH * 16].rearrange("p (c h) -> p c h", h=16), pbB, [g_totB])
        fl4 = lambda t: t[:].rearrange("p d c h -> p (d c h)")
        act(fl4(g_ea), fl4(g_acum), AF.Exp, [g_acumB], [g_eaB])
        tt("dve", g_tmp[:], g_tot[:], g_acum[:], ALU.subtract, [g_totB, g_acumB], [g_tmpB])
        act(fl4(g_tmp), fl4(g_tmp), AF.Exp, [g_tmpB], [g_tmpB])
        for d in range(2):
            tt("dve", g_w[:, d], g_tmp[:, d], g_dt[:, :, d * 16:(d + 1) * 16], ALU.mult, [g_tmpB, g_dtB], [g_wB])
        for d in range(2):
            tv = g_tot[:, d].rearrange("p c (g f r) -> p c g f r", g=2, f=2)
            for hf in range(2):
                ps_ = slice(hf * 64, (hf + 1) * 64)
                act(g_edec[ps_, d].rearrange("p c (g r) -> p c g r", g=2), tv[ps_, :, :, hf, :], AF.Exp,
                    [g_totB], [g_edecB])

        def chunk_loads(s, d, c, last_sweep):
            b, cc = c // 2, c % 2
            x, xB = xk_p.get()
            P.dma("sp", x[:], sx_tok[c], [DB("sx", c)], [xB], xB)
            bt, btB = bt_p.get()
            P.dma("sp", bt[:], sb_tok[c], [DB("sbt", c)], [btB], btB)
            bct, bctB = bct_p.get()
            P.dma("sp", bct[:], sbcT[b, :, :, cc * 128:(cc + 1) * 128], [DB("sbcT", b)], [bctB], bctB)
            z = zB = None
            if last_sweep:
                z, zB = tok_p.get()
                P.dma("sp", z[:], sz_tok[c], [DB("sz", c)], [zB], zB)
            return x, xB, bt, btB, bct, bctB, z, zB

        def chunk(s, d, c, last_sweep, tl):
            b, cc = c // 2, c % 2
            hs = slice(d * 16, (d + 1) * 16)
            H, HB_ = Hs[d]
            Hbf, HbB = Hb[d]
            x, xB, bt, btB, bct, bctB, z, zB = tl
            dau, dauB = big_p.get()
            dau3 = dau[:].rearrange("p (h i) -> p h i", h=16)
            tt("pool", dau3, tri[d].unsqueeze(1).broadcast_to([128, 16, 128]),
               g_dab[:, c, hs].unsqueeze(2).broadcast_to([128, 16, 128]), ALU.mult, [cbB, g_dabB], [dauB])
            sg, sgB = psum_at(0, 4)
            for q in range(4):
                mm(sg[:, q * 512:(q + 1) * 512], ones, dau[:, q * 512:(q + 1) * 512], [cbB, dauB], sgB,
                   start=True, stop=False)
                mm(sg[:, q * 512:(q + 1) * 512].rearrange("p (h i) -> p h i", h=4), ident,
                   nmask[d].unsqueeze(1).broadcast_to([128, 4, 128]), [cbB], sgB, start=False, stop=True)
            ar, arB = arg_p.get()
            tt("dve", ar[:].rearrange("p (h i) -> p h i", h=16), sg.rearrange("p (h i) -> p h i", h=16),
               g_acum[:, d, c, :].unsqueeze(2).broadcast_to([128, 16, 128]), ALU.subtract, sgB + [g_acumB], [arB])
            Lm, LmB = big_p.get()
            act(Lm[:], ar[:], AF.Exp, [arB], [LmB])
            pcs = [psum_at(4), psum_at(5)]
            for g in range(4):
                ps_ = slice((g % 2) * 64, (g % 2) * 64 + 64)
                pc, pcB = pcs[g % 2]
                mm(pc[:, (g // 2) * 128:(g // 2 + 1) * 128], bct[ps_, g // 2, :], bct[ps_, 2 + g // 2, :], [bctB], pcB)
            cbts = []
            for par in range(2):
                ct, ctB = cvo_p.get()
                cp("act", ct[:], pcs[par][0][:, 0:256], pcs[par][1], [ctB])
                cbts.append((ct, ctB))
            sc, scB = big_p.get()
            scv = sc[:].rearrange("p (gg two r i) -> p gg two r i", gg=2, two=2, r=4)
            Lmv = Lm[:].rearrange("p (gg two r i) -> p gg two r i", gg=2, two=2, r=4)
            for par, (ct, ctB) in enumerate(cbts):
                tt("pool", scv[:, :, par], Lmv[:, :, par],
                   ct[:].rearrange("p (g i) -> p g i", g=2).unsqueeze(2).broadcast_to([128, 2, 4, 128]),
                   ALU.mult, [LmB, ctB], [scB])
            return sc, scB

        def chunkB(s, d, c, last_sweep, tl, sa):
            b, cc = c // 2, c % 2
            hs = slice(d * 16, (d + 1) * 16)
            H, HB_ = Hs[d]
            Hbf, HbB = Hb[d]
            x, xB, bt, btB, bct, bctB, z, zB = tl
            sc, scB = sa
            xd, xdB = xk2_p.get()
            tt("dve", xd[:].rearrange("p (h q) -> p h q", h=16), x[:].rearrange("p (h q) -> p h q", h=16),
               g_dt[:, c, hs].unsqueeze(2).broadcast_to([128, 16, 64]), ALU.mult, [xB, g_dtB], [xdB])
            wx, wxB = xk2_p.get()
            tt("dve", wx[:].rearrange("p (h q) -> p h q", h=16), x[:].rearrange("p (h q) -> p h q", h=16),
               g_w[:, d, c, :].unsqueeze(2).broadcast_to([128, 16, 64]), ALU.mult, [xB, g_wB], [wxB])
            yi, yiB = psum_at(6, 2)
            for h in range(16):
                mm(yi[:, h * 64:(h + 1) * 64], sc[:, h * 128:(h + 1) * 128], xd[:, h * 64:(h + 1) * 64],
                   [scB, xdB], [yiB[h // 8]])
            ys, ysB = psum_at(4, 2)
            for g in range(4):
                ps_ = slice((g % 2) * 64, (g % 2) * 64 + 64)
                co = (g % 2) * 512 + (g // 2) * 256
                mm(ys[:, co:co + 256], bct[ps_, 2 + g // 2, :], Hbf[ps_, g // 2, :], [bctB, HbB], [ysB[g % 2]])
            yo, yoB = yo_p.get()
            yov = yo[:].rearrange("p (gg two r q) -> p gg two r q", gg=2, two=2, r=4)
            eav = g_ea[:, d, c, :].rearrange("p (gg two r) -> p gg two r", gg=2, two=2)
            for par in range(2):
                tt("dve", yov[:, :, par], ys[:, par * 512:(par + 1) * 512].rearrange("p (gg r q) -> p gg r q", gg=2, r=4),
                   eav[:, :, par].unsqueeze(3).broadcast_to([128, 2, 4, 64]), ALU.mult, [ysB[par], g_eaB], [yoB])
            tt("dve", yo[:], yo[:], yi, ALU.add, [yoB] + yiB, [yoB])
            dh, dhB = psum_at(4, 2)
            for gg in range(2):
                mm(dh[:, gg * 512:(gg + 1) * 512], bt[:, gg * 128:(gg + 1) * 128], wx[:, gg * 512:(gg + 1) * 512],
                   [btB, wxB], [dhB[gg]])
            tt("dve", H[:].rearrange("p g (r q) -> p g r q", r=4), H[:].rearrange("p g (r q) -> p g r q", r=4),
               g_edec[:, d, c, :].rearrange("p (g r) -> p g r", g=2).unsqueeze(3).broadcast_to([128, 2, 4, 64]),
               ALU.mult, [HB_, g_edecB], [HB_])
            dhv = dh.rearrange("p (g x) -> p g x", g=2)
            for hf in range(2):
                ps_ = slice(hf * 64, (hf + 1) * 64)
                tt("dve", H[ps_], H[ps_], dhv[ps_, :, hf * 256:(hf + 1) * 256], ALU.add, [HB_] + dhB, [HB_])
            cp("act", flat(Hbf[:]), flat(H[:]), [HB_], [HbB])
            if not last_sweep:
                P.dma("sp", yf_d[c], yo[:], [yoB], [DB("yf", c)], yoB)
                return
            yf, yfB = yo_p.get()
            P.dma("sp", yf[:], yf_d[c], [DB("yf", c)], [yfB], yfB)
            tt("dve", yo[:], yo[:], yf[:], ALU.add, [yoB, yfB], [yoB])
            xd2, xd2B = yo_p.get()
            tt("pool", xd2[:].rearrange("p (h q) -> p h q", h=16), x[:].rearrange("p (h q) -> p h q", h=16),
               pr[:, P_SD:P_SD + 16].unsqueeze(2).broadcast_to([128, 16, 64]), ALU.mult, [xB, prB], [xd2B])
            tt("dve", yo[:], yo[:], xd2[:], ALU.add, [yoB, xd2B], [yoB])
            tt("dve", yo[:], yo[:], z[:], ALU.mult, [yoB, zB], [yoB])
            s1, s1B = s1_p.get()
            act(yf[:], yo[:], AF.Square, [yoB], [yfB, s1B], accum=s1[:, 0:1])
            rsqrt(s1[:, 0:1], s1B, (128, 1), 1.0 / D)
            yn, ynB = tok_p.get()
            stt(yn[:], yo[:], s1[:, 0:1], pr[:, P_SSDN:P_SSDN + 1024], ALU.mult, ALU.mult, [yoB, s1B, prB], [ynB])
            out_transposed(yn, ynB, 0, c, bank=6)

        def seq_begin(s):
            has_ctx = SEQS[s][2]
            for d in range(2):
                H, HB_ = Hs[d]
                if has_ctx:
                    P.dma("sp", flat(H[:]), ssd0[l, d], [], [HB_], HB_)
                else:
                    memset("dve", flat(H[:]), 0.0, [HB_])
                cp("pool", Hb[d][0][:], H[:], [HB_], [Hb[d][1]])

        def seq_end(s):
            if not SEQS[s][2]:
                for d in range(2):
                    H, HB_ = Hs[d]
                    P.dma("sp", nssd_o[s - 1, l, d], flat(H[:]), [HB_], [DB("nssd", s, l, d)], HB_)

        run_sweeps(chunk_loads, chunk, chunkB, seq_begin, seq_end)

    CS = CSb = m_li = m_liB = m_lf = m_lfB = m_lfb = m_lfbB = m_b = m_bB = m_g = m_gB = m_wt = m_wtB = m_fl = m_flB = m_RB = m_RBB = m_SCB = m_SCBB = m_GM = m_GMB = m_BT = m_BTB = m_R = m_RB2 = m_MD = m_MDB = m_mp = m_mpB = m_RD = m_RDB = vp_p = mT_p = None

    def alloc_ml():
        nonlocal CS, CSb, m_li, m_liB, m_lf, m_lfB, m_lfb, m_lfbB, m_b, m_bB, m_g, m_gB, m_wt, m_wtB, m_fl, m_flB, m_RB, m_RBB, m_SCB, m_SCBB, m_GM, m_GMB, m_BT, m_BTB, m_R, m_RB2, m_MD, m_MDB, m_mp, m_mpB, m_RD, m_RDB, vp_p, mT_p, dtg_s, dtgB, g_da, g_daB
        dtg_s, dtgB = nsb("dtg_s", [128, NCH, 64])
        g_da, g_daB = nsb("g_da", [128, NCH, 32])
        CS = [nsb("CS%d" % d, [128, 8, 129]) for d in range(2)]
        CSb = [nsb("CSb%d" % d, [128, 8, 129], BF16) for d in range(2)]
        m_li, m_liB = nsb("m_li", [128, 2, NCH, 8])
        m_lf, m_lfB = nsb("m_lf", [128, 2, NCH, 8])
        m_lfb, m_lfbB = nsb("m_lfb", [128, 2, NCH, 8], BF16)
        m_b, m_bB = nsb("m_b", [128, 2, NCH, 8])
        m_g, m_gB = nsb("m_g", [128, 2, NCH, 8])
        m_wt, m_wtB = nsb("m_wt", [128, 2, NCH, 8])
        m_fl, m_flB = nsb("m_fl", [128, 2, NCH, 8])
        m_RB, m_RBB = nsb("m_RB", [128, 2, NCH, 8])
        m_SCB, m_SCBB = nsb("m_SCB", [128, 2, NCH, 8])
        m_GM, m_GMB = nsb("m_GM", [8, 2, NCH])
        m_BT, m_BTB = nsb("m_BT", [8, 2, NCH])
        m_R, m_RB2 = nsb("m_R", [8, 2, NCH])
        m_MD, m_MDB = nsb("m_MD", [8, 2, NCH])
        m_mp, m_mpB = nsb("m_mp", [8, 2, 3, NCH + 1])
        m_RD, m_RDB = nsb("m_RD", [8, 2, 2, NCH, 8])
        vp_p = mkpool("vp", [128, 8, 129], BF16, 4)
        mT_p = mkpool("mT", [128, 8, 128], BF16, 8)


    def ml_layer(l, pr, prB):
        pre = g_da
        preB = g_daB
        P.dma("sp", dtg_s[:], dtg_tok.rearrange("c p n -> p c n"), [DB("dtg", c) for c in range(NCH)],
              [dtgB], dtgB)
        tt("dve", pre[:], dtg_s[:, :, 32:64], pr[:, P_MGB:P_MGB + 32].unsqueeze(1).broadcast_to([128, NCH, 32]),
           ALU.add, [dtgB, prB], [preB])
        for d in range(2):
            cp("dve", m_li[:, d], pre[:, :, d * 16:d * 16 + 8], [preB], [m_liB])
            act(m_lf[:, d], pre[:, :, d * 16 + 8:d * 16 + 16], AF.Exp, [preB], [m_lfB], scale=-1.0)
        fl4 = lambda t: t[:].rearrange("p d c h -> p (d c h)")
        act(fl4(m_lf), fl4(m_lf), AF.Ln, [m_lfB], [m_lfB], bias=1.0)
        ts("dve", fl4(m_lf), fl4(m_lf), -1.0, ALU.mult, [m_lfB], [m_lfB])
        cp("dve", fl4(m_lfb), fl4(m_lf), [m_lfB], [m_lfbB])
        for d in range(2):
            pa, paB = psA()
            pav = pa[:, 0:NCH * 8].rearrange("p (c h) -> p c h", h=8)
            mm(pav, tri[d], m_lfb[:, d], [cbB, m_lfbB], paB)
            cp("dve", m_b[:, d], pav, paB, [m_bB])
        tt("dve", m_g[:], m_li[:], m_b[:], ALU.subtract, [m_liB, m_bB], [m_gB])
        for d in range(2):
            for c0 in range(0, NCH, 4):
                n = min(4, NCH - c0)
                pt, ptB = psA()
                for i in range(n):
                    tp(pt[0:8, i * 128:(i + 1) * 128], m_g[:, d, c0 + i, :], cf[:], [m_gB, cfB], ptB)
                P.add("dve", (lambda o=m_GM[:, d, c0:c0 + n], i_=pt[0:8, 0:n * 128].rearrange("p (c j) -> p c j", j=128):
                              V.tensor_reduce(out=o, in_=i_, axis=AX.X, op=ALU.max)), ptB, [m_GMB])
            pb, pbB = psBr()
            for c in range(NCH):
                mm(pb[0:8, c:c + 1], m_lfb[:, d, c, :], ones[:, 0:1], [m_lfbB, cbB], pbB)
            cp("dve", m_BT[:, d], pb[0:8, 0:NCH], pbB, [m_BTB])
        for d in range(2):
            for s in range(3):
                f, n, has_ctx, _ = SEQS[s]
                mp = m_mp[:, d, s]
                if has_ctx:
                    P.dma("sp", mp[:, 0:1], mlm0[l, d], [], [m_mpB], m_mpB)
                else:
                    memset("dve", mp[:, 0:1], 0.0, [m_mpB])
                for i, c in enumerate(chunk_order(s, d)):
                    tt("dve", m_R[:, d, c:c + 1], mp[:, i:i + 1], m_GM[:, d, c:c + 1], ALU.max,
                       [m_mpB, m_GMB], [m_RB2])
                    tt("dve", m_MD[:, d, c:c + 1], mp[:, i:i + 1], m_R[:, d, c:c + 1], ALU.subtract,
                       [m_mpB, m_RB2], [m_MDB])
                    tt("dve", mp[:, i + 1:i + 2], m_R[:, d, c:c + 1], m_BT[:, d, c:c + 1], ALU.add,
                       [m_RB2, m_BTB], [m_mpB])
                if not has_ctx:
                    P.dma("sp", nmlm_o[s - 1, l, d], mp[:, 2 * n:2 * n + 1], [m_mpB], [DB("nmlm", s, l, d)], m_mpB)
        for d in range(2):
            for wi, (src, srcB) in enumerate(((m_R, m_RB2), (m_MD, m_MDB))):
                tt("dve", m_RD[:, d, wi], src[:, d, :].unsqueeze(2).broadcast_to([8, NCH, 8]),
                   cf[0:8, 0:8].unsqueeze(1).broadcast_to([8, NCH, 8]), ALU.mult, [srcB, cfB], [m_RDB])
            pr_, prB_ = psBr()
            mm(pr_[:, 0:2 * NCH * 8], onesf[0:8, :], m_RD[:, d].rearrange("p w c h -> p (w c h)"),
               [onesfB, m_RDB], prB_)
            cp("dve", m_RB[:, d], pr_[:, 0:NCH * 8].rearrange("p (c h) -> p c h", h=8), prB_, [m_RBB])
            act(m_SCB[:, d], pr_[:, NCH * 8:2 * NCH * 8].rearrange("p (c h) -> p c h", h=8), AF.Exp, prB_, [m_SCBB])
        tt("dve", m_wt[:], m_g[:], m_RB[:], ALU.subtract, [m_gB, m_RBB], [m_wtB])
        act(fl4(m_wt), fl4(m_wt), AF.Exp, [m_wtB], [m_wtB])
        tt("dve", m_fl[:], m_b[:], m_RB[:], ALU.add, [m_bB, m_RBB], [m_flB])
        ts("dve", fl4(m_fl), fl4(m_fl), -1.0, ALU.mult, [m_flB], [m_flB], s2=80.0, op1=ALU.min)
        act(fl4(m_fl), fl4(m_fl), AF.Exp, [m_flB], [m_flB])

        def chunk_loads(s, d, c, last_sweep):
            b, cc = c // 2, c % 2
            q, qB = mT_p.get()
            P.dma("sp", q[:], mqT[b, :, :, cc * 128:(cc + 1) * 128], [DB("mqT", b)], [qB], qB)
            k, kB = mT_p.get()
            P.dma("sp", k[:], mkT[b, :, :, cc * 128:(cc + 1) * 128], [DB("mkT", b)], [kB], kB)
            kt, ktB = xk_p.get()
            P.dma("sp", kt[:], mk_tok[c], [DB("mkt", c)], [ktB], ktB)
            v, vB = xk_p.get()
            P.dma("sp", v[:], mv_tok[c], [DB("mv", c)], [vB], vB)
            mo = moB = None
            if last_sweep:
                mo, moB = tok_p.get()
                P.dma("sp", mo[:], mo_tok[c], [DB("mo", c)], [moB], moB)
            return q, qB, k, kB, kt, ktB, v, vB, mo, moB

        def chunk(s, d, c, last_sweep, tl):
            b, cc = c // 2, c % 2
            C, CB_ = CS[d]
            Cb, CbB = CSb[d]
            q, qB, k, kB, kt, ktB, v, vB, mo, moB = tl
            sc, scB = psum_at(0, 2)
            for h in range(8):
                mm(sc[:, h * 128:(h + 1) * 128], k[:, h, :], q[:, h, :], [kB, qB], [scB[h // 4]])
            sm_, smB_ = xk2_p.get()
            stt(sm_[:].rearrange("p (h t) -> p h t", h=8), sc.rearrange("p (h t) -> p h t", h=8), MLK_SCALE,
                tri[d].unsqueeze(1).broadcast_to([128, 8, 128]), ALU.mult, ALU.mult, scB + [cbB], [smB_])
            vp, vpB = vp_p.get()
            tt("pool", vp[:, :, 0:128], v[:].rearrange("p (h e) -> p h e", h=8),
               m_wt[:, d, c, :].unsqueeze(2).broadcast_to([128, 8, 128]), ALU.mult, [vB, m_wtB], [vpB])
            cp("pool", vp[:, :, 128:129], m_wt[:, d, c, :].unsqueeze(2), [m_wtB], [vpB])
            return sm_, smB_, vp, vpB

        def chunkB(s, d, c, last_sweep, tl, sa):
            b, cc = c // 2, c % 2
            C, CB_ = CS[d]
            Cb, CbB = CSb[d]
            q, qB, k, kB, kt, ktB, v, vB, mo, moB = tl
            sm_, smB_, vp, vpB = sa
            tt("dve", C[:], C[:], m_SCB[:, d, c, :].unsqueeze(2).broadcast_to([128, 8, 129]), ALU.mult,
               [CB_, m_SCBB], [CB_])
            cp("act", Cb[:].rearrange("p h e -> p (h e)"), C[:].rearrange("p h e -> p (h e)"), [CB_], [CbB])
            nm, nmB = psum_at(2, 2)
            dn, dnB = psum_at(4)
            for h in range(8):
                mm(nm[:, h * 128:(h + 1) * 128], sm_[:, h * 128:(h + 1) * 128], vp[:, h, 0:128], [smB_, vpB],
                   [nmB[h // 4]], start=True, stop=False)
                mm(nm[:, h * 128:(h + 1) * 128], q[:, h, :], Cb[:, h, 0:128], [qB, CbB], [nmB[h // 4]],
                   start=False, stop=True)
            for h in range(8):
                mm(dn[:, h:h + 1], sm_[:, h * 128:(h + 1) * 128], vp[:, h, 128:129], [smB_, vpB], dnB,
                   start=True, stop=False)
                mm(dn[:, h:h + 1], q[:, h, :], Cb[:, h, 128:129], [qB, CbB], dnB, start=False, stop=True)
            dc, dcB = psum_at(5, 2)
            for h in range(8):
                mm(dc[:, h * 128:(h + 1) * 128], kt[:, h * 128:(h + 1) * 128], vp[:, h, 0:128], [ktB, vpB],
                   [dcB[h // 4]])
            for h in range(8):
                mm(dn[:, 8 + h:9 + h], kt[:, h * 128:(h + 1) * 128], vp[:, h, 128:129], [ktB, vpB], dnB)
            dd, ddB = s1_p.get()
            cp("dve", dd[:], dn[:, 0:8], dnB, [ddB])
            stt(dd[:], dd[:], -1.0, dd[:], ALU.mult, ALU.max, [ddB], [ddB])
            tt("dve", dd[:], dd[:], m_fl[:, d, c, :], ALU.max, [ddB, m_flB], [ddB])
            P.add("dve", lambda o=dd[:]: V.reciprocal(out=o, in_=o), [ddB], [ddB])
            hd, hdB = yo_p.get()
            tt("dve", hd[:].rearrange("p (h e) -> p h e", h=8), nm.rearrange("p (h e) -> p h e", h=8),
               dd[:].unsqueeze(2).broadcast_to([128, 8, 128]), ALU.mult, nmB + [ddB], [hdB])
            tt("dve", C[:, :, 0:128], C[:, :, 0:128], dc.rearrange("p (h e) -> p h e", h=8), ALU.add,
               [CB_] + dcB, [CB_])
            tt("dve", C[:, :, 128:129], C[:, :, 128:129], dn[:, 8:16].unsqueeze(2), ALU.add, [CB_] + dnB, [CB_])
            if not last_sweep:
                P.dma("sp", hf_d[c], hd[:], [hdB], [DB("hf", c)], hdB)
                return
            hf, hfB = yo_p.get()
            P.dma("sp", hf[:], hf_d[c], [DB("hf", c)], [hfB], hfB)
            tt("dve", hd[:], hd[:], hf[:], ALU.add, [hdB, hfB], [hdB])
            act(hf[:], hd[:], AF.Square, [hdB], [hfB])
            s1, s1B = s1_p.get()
            P.add("dve", lambda o=s1[:], i_=hf[:].rearrange("p (h e) -> p h e", h=8):
                  V.tensor_reduce(out=o, in_=i_, axis=AX.X, op=ALU.add), [hfB], [s1B])
            rsqrt(s1[:], s1B, (128, 8), 1.0 / 128)
            tt("dve", hd[:].rearrange("p (h e) -> p h e", h=8), hd[:].rearrange("p (h e) -> p h e", h=8),
               s1[:].unsqueeze(2).broadcast_to([128, 8, 128]), ALU.mult, [hdB, s1B], [hdB])
            tt("dve", hd[:], hd[:], pr[:, P_MLN:P_MLN + 1024], ALU.mult, [hdB, prB], [hdB])
            yn, ynB = tok_p.get()
            tt("dve", yn[:], hd[:], mo[:], ALU.mult, [hdB, moB], [ynB])
            out_transposed(yn, ynB, 2, c)

        def seq_begin(s):
            for d in range(2):
                C, CB_ = CS[d]
                if SEQS[s][2]:
                    P.dma("sp", C[:].rearrange("p h e -> p (h e)"), mlc0[l, d], [], [CB_], CB_)
                else:
                    memset("dve", C[:].rearrange("p h e -> p (h e)"), 0.0, [CB_])

        def seq_end(s):
            if not SEQS[s][2]:
                for d in range(2):
                    C, CB_ = CS[d]
                    P.dma("sp", nmlc_o[s - 1, l, d], C[:].rearrange("p h e -> p (h e)"), [CB_],
                          [DB("nmlc", s, l, d)], CB_)

        run_sweeps(chunk_loads, chunk, chunkB, seq_begin, seq_end)

    NKS = NCTX + SL
    KT = KTB = VA = VAB = VB_ = VBB = ckd = ckdB = q_p = e_p = yat_p = dsb_p = ev_p = None

    def alloc_att():
        nonlocal KT, KTB, VA, VAB, VB_, VBB, ckd, ckdB, q_p, e_p, yat_p, dsb_p, ev_p
        KT, KTB = nsb("KT", [128, 8, NKS], BF16)
        memset("dve", KT[:].rearrange("p a b -> p (a b)"), 0.0, [KTB])
        VA, VAB = nsb("VA", [128, (NKS // 128), 4, 128], BF16)
        VB_, VBB = nsb("VBt", [128, (NKS // 128), 4, 128], BF16)
        ckd, ckdB = nsb("ckd", [128, 4, 4, 128], BF16)
        memset("pool", VA[:].rearrange("p a b c -> p (a b c)"), 0.0, [VAB])
        memset("pool", VB_[:].rearrange("p a b c -> p (a b c)"), 0.0, [VBB])
        for kc_ in range(NKS // 128):
            memset("pool", VA[:, kc_, :, 64:128], 1.0, [VAB])
            memset("pool", VB_[:, kc_, :, 0:64], 1.0, [VBB])
        q_p = mkpool("qatt", [128, 8, 512], BF16, 2)
        e_p = mkpool("eatt", [128, 2, 512], BF16, 2)
        yat_p = mkpool("yat", [128, 8, 512], BF16, 2)
        dsb_p = mkpool("dsb", [128, 512], F32, 4)
        ev_p = mkpool("ev", [128, 2, 512], F32, 4)


    def att_layer(l):
        for s in range(3):
            f, n, has_ctx, _ = SEQS[s]
            L = 256 * n
            nctx = NCTX if has_ctx else 0
            nk = nctx + L
            nkc = nk // 128
            if has_ctx:
                src = cache_k[l].rearrange("(kc p) (f c) -> p kc f c", p=128, f=2)
                P.dma("pool", ckd[:, :, 0:2, :], src, [], [ckdB], ckdB)
                for f2 in range(2):
                    P.dma("pool", ckd[:, :, 2 + f2, 0:64], src[:, :, f2, 64:128], [], [ckdB], ckdB)
                    P.dma("pool", ckd[:, :, 2 + f2, 64:128], src[:, :, f2, 0:64], [], [ckdB], ckdB)
                KTv = KT[:].rearrange("p (f sw hh) t -> p sw f hh t", f=2, sw=2, hh=2)
                for kc in range(4):
                    pt, ptB = psBr()
                    ptb = pt.bitcast(BF16)
                    for fv in range(4):
                        tp(ptb[:, fv * 128:(fv + 1) * 128], ckd[:, kc, fv, :], ident, [ckdB, cbB], ptB)
                    ptv = ptb[:, 0:512].rearrange("p (sw f t) -> p sw f t", sw=2, f=2)
                    ks = slice(kc * 128, (kc + 1) * 128)
                    cp("act", KTv[0:64, :, :, 0, ks], ptv[0:64], ptB, [KTB])
                    for sw in range(2):
                        cp("act", KTv[64:128, 1 - sw, :, 1, ks], ptv[64:128, sw], ptB, [KTB])
                srcv = cache_v[l].rearrange("(kc p) (g c) -> p kc g c", p=128, g=4)
                for kc in range(4):
                    P.dma("pool", VA[:, kc, :, 0:64], srcv[:, kc], [], [VAB], VAB)
                    P.dma("pool", VB_[:, kc, :, 64:128], srcv[:, kc], [], [VBB], VBB)
            for g in range(4):
                for hh in range(2):
                    fc = (g // 2) if (g % 2) == hh else 2 + g // 2
                    ps_ = slice(hh * 64, hh * 64 + 64)
                    P.dma("sp", KT[ps_, g * 2 + hh, nctx:nctx + L].rearrange("p (b t) -> p b t", b=n),
                          kT_d[f:f + n, ps_, fc, :].rearrange("b p t -> p b t"),
                          [DB("kT", f + bi) for bi in range(n)], [KTB], KTB)
            c0 = 2 * f
            kc0 = nctx // 128
            for i in range(2 * n):
                srcv = v_tok[c0 + i].rearrange("p (g e) -> p g e", g=4)
                P.dma("sp", VA[:, kc0 + i, :, 0:64], srcv, [DB("v", c0 + i)], [VAB], VAB)
                P.dma("sp", VB_[:, kc0 + i, :, 64:128], srcv, [DB("v", c0 + i)], [VBB], VBB)
            NQ = min(512, L)
            nqb = NQ // 256
            for qb in range(L // NQ):
                qt, qtB = q_p.get()
                for i in range(nqb):
                    b = f + qb * nqb + i
                    P.dma("sp", qt[:, :, i * 256:(i + 1) * 256], qT_d[b], [DB("qT", b)], [qtB], qtB)
                ya, yaB = yat_p.get()
                tasks = [(j, hh, k0) for j in range(8) for hh in range(2) for k0 in range(0, nkc, 2)]

                def hinfo(j, hh):
                    h = 2 * j + hh
                    g = h // 4
                    return g, slice(0, 128), g * 2 + hh

                def emit_scores(ti):
                    j, hh, k0 = tasks[ti]
                    g, ps_, fv = hinfo(j, hh)
                    scp, scpB = psum_at(2 + 2 * (ti % 2), 2)
                    for kk in range(2):
                        kc = k0 + kk
                        mm(scp[:, kk * 512:kk * 512 + NQ], KT[ps_, fv, kc * 128:(kc + 1) * 128],
                           qt[ps_, j, 0:NQ], [KTB, qtB], [scpB[kk]])
                    return scp, scpB

                ohs = {}

                def emit_epv(ti, scp, scpB):
                    j, hh, k0 = tasks[ti]
                    g, ps_, fv = hinfo(j, hh)
                    Vt, VtB = (VA, VAB) if hh == 0 else (VB_, VBB)
                    if k0 == 0:
                        ohs[(j, hh)] = psum_at((0 if j % 2 == 0 else 6) + hh)
                    oh, ohB = ohs[(j, hh)]
                    e, eB = e_p.get()
                    act(e[:, :, 0:NQ], scp.rearrange("p (k q) -> p k q", k=2)[:, :, 0:NQ], AF.Exp,
                        scpB, [eB], scale=0.125)
                    for kk in range(2):
                        kc = k0 + kk
                        mm(oh[:, 0:NQ], Vt[:, kc, g, :], e[:, kk, 0:NQ], [VtB, eB], ohB,
                           start=(kc == 0), stop=(kc == nkc - 1))
                    if hh == 1 and k0 + 2 >= nkc:
                        (oa, oaB), (ob, obB) = ohs[(j, 0)], ohs[(j, 1)]
                        ev, evB = ev_p.get()
                        d2, d2B = dsb_p.get()
                        cp("dve", ev[:, 0, 0:NQ], oa[:, 0:NQ], oaB, [evB])
                        cp("dve", ev[:, 1, 0:NQ], ob[:, 0:NQ], obB, [evB])
                        P.dma("sp", d2[64:128, 0:NQ], ev[0:64, 1, 0:NQ], [evB], [d2B], d2B)
                        P.dma("sp", d2[0:64, 0:NQ], ev[64:128, 0, 0:NQ], [evB], [d2B], d2B)
                        def norm(j=j, ev=ev, evB=evB, d2=d2, d2B=d2B):
                            P.add("dve", lambda o=d2[:, 0:NQ]: V.reciprocal(out=o, in_=o), [d2B], [d2B])
                            tt("dve", ya[0:64, j, 0:NQ], ev[0:64, 0, 0:NQ], d2[0:64, 0:NQ], ALU.mult,
                               [evB, d2B], [yaB])
                            tt("dve", ya[64:128, j, 0:NQ], ev[64:128, 1, 0:NQ], d2[64:128, 0:NQ], ALU.mult,
                               [evB, d2B], [yaB])
                        while len(deferred) >= 2:
                            deferred.pop(0)()
                        deferred.append(norm)

                deferred = []
                pend = emit_scores(0)
                for ti in range(len(tasks)):
                    nxt = emit_scores(ti + 1) if ti + 1 < len(tasks) else None
                    emit_epv(ti, *pend)
                    pend = nxt
                while deferred:
                    deferred.pop(0)()
                for i in range(nqb):
                    b = f + qb * nqb + i
                    P.dma("sp", ybr[1, b], ya[:, :, i * 256:(i + 1) * 256], [yaB], [DB("ybr", 1, b)], yaB)


    wbr_s = wo_s = yb_p = gt_p = mg_p = t3_p = None

    def alloc_mrg():
        nonlocal wbr_s, wo_s, yb_p, gt_p, mg_p, t3_p
        wbr_s = [nsb("wbr%d" % n, [128, KC, 1024], BF16) for n in range(3)]
        yb_p = mkpool("ybl", [128, 8, 256], BF16, 6)
        gt_p = mkpool("gtl", [128, 8, 256], BF16, 6)
        mg_p = mkpool("mgd", [128, 8, 256], BF16, 2)
        t3_p = mkpool("t3", [128, 256], F32, 9)

    def merge_a(l):
        for n in range(3):
            load_slab(w_br[l, n], 1024, 0, wbr_s[n])

        def loads(b):
            ys_, gs_ = [], []
            for n in range(3):
                y, yB = yb_p.get()
                P.dma("sp", y[:], ybr[n, b], [DB("ybr", n, b)], [yB], yB)
                g, gB = gt_p.get()
                P.dma("sp", g[:], gT[n, b], [DB("gT", n, b)], [gB], gB)
                ys_.append((y, yB))
                gs_.append((g, gB))
            return ys_, gs_

        nxt = loads(0)
        for b in range(NBLK):
            ys_, gs_ = nxt
            if b + 1 < NBLK:
                nxt = loads(b + 1)
            mg, mgB = mg_p.get()
            for oc in range(8):
                ts_ = []
                for n in range(3):
                    pp, ppB = psA()
                    for k in range(KC):
                        mm(pp[:, 0:256], wbr_s[n][0][:, k, oc * 128:(oc + 1) * 128], ys_[n][0][:, k, :],
                           [wbr_s[n][1], ys_[n][1]], ppB, start=(k == 0), stop=(k == KC - 1))
                    t, tB = t3_p.get()
                    tt("dve", t[:], pp[:, 0:256], gs_[n][0][:, oc, :], ALU.mult, ppB + [gs_[n][1]], [tB])
                    ts_.append((t, tB))
                tt("pool", ts_[0][0][:], ts_[0][0][:], ts_[1][0][:], ALU.add, [ts_[0][1], ts_[1][1]], [ts_[0][1]])
                tt("pool", mg[:, oc, :], ts_[0][0][:], ts_[2][0][:], ALU.add, [ts_[0][1], ts_[2][1]], [mgB])
            P.dma("sp", mgd_d[b], mg[:], [mgB], [DB("mgd", b)], mgB)

    def merge_b(l):
        wo_t, wo_B = nsb("wo_s", [128, KC, 1024], BF16)
        mgl_p = mkpool("mgl", [128, 8, 256], BF16, 2)
        load_slab(w_out[l], 1024, 0, (wo_t, wo_B))
        def loads(b):
            mg, mgB = mgl_p.get()
            P.dma("sp", mg[:], mgd_d[b], [DB("mgd", b)], [mgB], mgB)
            xt, xtB = xt_p.get()
            P.dma("sp", xt[:], xres[b], [DB("x", b)], [xtB], xtB)
            return mg, mgB, xt, xtB

        nxt = loads(0)
        for b in range(NBLK):
            m = SEQS[blk_seq(b)][3]
            mg, mgB, xt, xtB = nxt
            if b + 1 < NBLK:
                nxt = loads(b + 1)
            for oc in range(8):
                pp, ppB = psA()
                for k in range(KC):
                    mm(pp[:, 0:256], wo_t[:, k, oc * 128:(oc + 1) * 128], mg[:, k, :], [wo_B, mgB], ppB,
                       start=(k == 0), stop=(k == KC - 1))
                stt(xt[:, oc, :], pp[:, 0:256], modT[:, l, m, 16 + oc:17 + oc], xt[:, oc, :], ALU.mult, ALU.add,
                    ppB + [modB, xtB], [xtB])
            P.dma("sp", xres[b], xt[:], [xtB], [DB("x", b)], xtB)
            norm_block(xt, xtB, b, l, 1)

    FG = [6, 6, 5, 5]
    wf1_p = wf2_p = ac_p = sa_p = fo_p = None

    def alloc_ffn():
        nonlocal wf1_p, wf2_p, ac_p, sa_p, fo_p
        wf1_p = mkpool("wf1", [128, KC, 2, 768], BF16, 2)
        wf2_p = mkpool("wf2", [128, 6, 1024], BF16, 2)
        ac_p = mkpool("ffa", [128, 256], BF16, 8)
        sa_p = mkpool("ffs", [128, 256], F32, 3)
        fo_p = mkpool("fo", [128, KC, 256], F32, 1)


    def ffn_layer(l, last):
        hc0 = 0
        for gi, ng in enumerate(FG):
            w1, w1B = wf1_p.get()
            w2, w2B = wf2_p.get()
            for ab in range(2):
                P.dma("pool", w1[:, :, ab, 0:ng * 128],
                      w_f1[l, :, ab * FFN_H + hc0 * 128:ab * FFN_H + (hc0 + ng) * 128].rearrange(
                          "(k p) n -> p k n", p=128), [], [w1B], w1B)
            P.dma("pool", w2[:, 0:ng, :], w_f2[l, hc0 * 128:(hc0 + ng) * 128, :].rearrange("(c p) n -> p c n", p=128),
                  [], [w2B], w2B)
            for b in range(NBLK):
                m = SEQS[blk_seq(b)][3]
                c0 = hcol(b) + 1
                acts = []
                for hc in range(ng):
                    pa, paB = psA()
                    for k in range(KC):
                        mm(pa[:, 0:256], w1[:, k, 0, hc * 128:(hc + 1) * 128], hT[:, k, c0:c0 + 256], [w1B, hB[b]],
                           paB, start=(k == 0), stop=(k == KC - 1))
                    pb, pbB = psA()
                    for k in range(KC):
                        mm(pb[:, 0:256], w1[:, k, 1, hc * 128:(hc + 1) * 128], hT[:, k, c0:c0 + 256], [w1B, hB[b]],
                           pbB, start=(k == 0), stop=(k == KC - 1))
                    sa, saB = sa_p.get()
                    act(sa[:], pa[:, 0:256], AF.Silu, paB, [saB])
                    a, aB = ac_p.get()
                    tt("dve", a[:], pb[:, 0:256], sa[:], ALU.mult, pbB + [saB], [aB])
                    acts.append((a, aB))
                xt, xtB = xt_p.get()
                P.dma("sp", xt[:], xres[b], [DB("x", b)], [xtB], xtB)
                for oc in range(8):
                    pp, ppB = psA()
                    for hc in range(ng):
                        mm(pp[:, 0:256], w2[:, hc, oc * 128:(oc + 1) * 128], acts[hc][0][:], [w2B, acts[hc][1]], ppB,
                           start=(hc == 0), stop=(hc == ng - 1))
                    stt(xt[:, oc, :], pp[:, 0:256], modT[:, l, m, 40 + oc:41 + oc], xt[:, oc, :], ALU.mult, ALU.add,
                        ppB + [modB, xtB], [xtB])
                if gi < len(FG) - 1 or not last:
                    P.dma("sp", xres[b], xt[:], [xtB], [DB("x", b)], xtB)
                if gi == len(FG) - 1:
                    if last:
                        fo = fo_p.get()
                        norm_block(xt, xtB, b, l, 0, final=True, out_tile=(fo[0], fo[1]))
                        P.dma("sp", yT[:, :, b * 256:(b + 1) * 256], fo[0][:], [fo[1]], [DB("yT", b)], fo[1])
                    else:
                        norm_block(xt, xtB, b, l + 1, 0)
            hc0 += ng

    def phase(fn, *a):
        with contextlib.ExitStack() as pes:
            cur["es"] = pes
            fn(*a)
            P.flush()
        cur["es"] = es

    def ph_d1(l):
        alloc_slab()
        alloc_d1()
        d1_layer(l)

    def ph_ssd(l):
        alloc_scan()
        alloc_mix()
        alloc_ssd()
        ssd_layer(l, prep_t, prepB)

    def ph_att(l):
        alloc_att()
        att_layer(l)

    def ph_ml(l):
        alloc_mix(False)
        alloc_ml()
        ml_layer(l, prep_t, prepB)

    def ph_mrg_a(l):
        alloc_mrg()
        merge_a(l)

    def ph_mrg_b(l):
        alloc_norm()
        merge_b(l)

    def ph_ffn(l):
        alloc_norm()
        alloc_ffn()
        ffn_layer(l, l == NL - 1)

    nph = [0]

    def go(fn, *a):
        nph[0] += 1
        if stop_after is not None and nph[0] > stop_after:
            return
        phase(fn, *a)

    go(startup)
    for l in range(NL):
        P.dma("sp", prep_t[:], prep[l], [], [prepB], prepB)
        go(ph_d1, l)
        go(ph_ssd, l)
        go(ph_att, l)
        go(ph_ml, l)
        go(ph_mrg_a, l)
        go(ph_mrg_b, l)
        go(ph_ffn, l)
    P.flush()
    print("kernel build: ops=%d sems=%d" % (P.n_emitted, P.nsem))
    es.close()
    return nc


def _consts(SL):
    c = np.zeros((128, C_TOT), np.float32)
    k = np.arange(128)[:, None]
    i = np.arange(128)[None, :]
    c[:, C_ID:C_ID + 128] = (k == i)
    c[:, C_ONE:C_ONE + 128] = 1.0
    c[:, C_U:C_U + 128] = (k <= i)
    c[:, C_L:C_L + 128] = (k >= i)
    c[:, C_BLK:C_BLK + 128] = ((k // 64) == (i // 64))
    part = np.arange(128)
    d = part % 64
    partner = np.where((d % 32) < 16, part + 16, part - 16)
    c[:, C_SWP:C_SWP + 128] = (k == partner[None, :])
    c[:, C_NMF:C_NMF + 128] = np.where(i < k, NEG, 0.0)
    c[:, C_NMB:C_NMB + 128] = np.where(i > k, NEG, 0.0)
    cf = np.eye(128, dtype=np.float32)
    t = np.arange(SL)
    rows, cols = t // 64, t % 64
    f = d % 16
    freqs = (10000.0 ** (-(f.astype(np.float32)) / 16.0)).astype(np.float32)
    pos = np.where((d < 32)[:, None], rows[None, :], cols[None, :]).astype(np.float32)
    ang = pos * freqs[:, None]
    cosT = np.cos(ang).astype(np.float32)
    sgn = np.where((d % 32) < 16, -1.0, 1.0).astype(np.float32)
    sinT = (np.sin(ang) * sgn[:, None]).astype(np.float32)
    return c, cf, cosT, sinT


def _fm(v):
    v = np.asarray(v)
    n = v.shape[-1] // 128
    r = v.reshape(v.shape[:-1] + (n, 128))
    return np.ascontiguousarray(np.moveaxis(r, -1, 0))


def _prepare(inp, SB, NL):
    NBLK = SB + 2
    SL = 256 * SB
    f32 = np.float32
    g = lambda k: np.asarray(inp[k], dtype=f32)
    c, cf, cosT, sinT = _consts(SL)
    shared = {
        "w_ada": np.ascontiguousarray(g("w_ada")[:NL]), "w_in": np.ascontiguousarray(g("w_in")[:NL]),
        "w_branch": np.ascontiguousarray(g("w_branch")[:NL]), "w_out": np.ascontiguousarray(g("w_out")[:NL]),
        "w_ffn_in": np.ascontiguousarray(g("w_ffn_in")[:NL]), "w_ffn_out": np.ascontiguousarray(g("w_ffn_out")[:NL]),
        "cst": c, "cstf": cf, "ropec": cosT, "ropes": sinT,
    }
    prep = np.zeros((NL, 128, P_TOT), f32)
    pfm = np.zeros((128, NL, Q_TOT), f32)
    for l in range(NL):
        prep[l, :, P_DTB:P_DTB + 32] = g("ssd_dt_bias")[l].reshape(32)[None]
        prep[l, :, P_ALOG:P_ALOG + 32] = g("ssd_a_log")[l].reshape(32)[None]
        prep[l, :, P_SD:P_SD + 16] = g("ssd_d")[l][None]
        prep[l, :, P_SSDN:P_SSDN + 1024] = g("ssd_norm")[l][None]
        prep[l, :, P_MLN:P_MLN + 1024] = g("ml_norm")[l][None]
        prep[l, :, P_MGB:P_MGB + 32] = g("ml_gate_bias")[l].reshape(32)[None]
        pfm[:, l, Q_BADA:Q_BADA + 48] = _fm(g("b_ada")[l])
        pfm[:, l, Q_SCW:Q_SCW + 48] = np.moveaxis(_fm(g("ssd_conv_w")[l]), 1, 2).reshape(128, 48)
        pfm[:, l, Q_SCB:Q_SCB + 12] = _fm(g("ssd_conv_b")[l])
        pfm[:, l, Q_MCW:Q_MCW + 64] = np.moveaxis(_fm(g("ml_conv_w")[l]), 1, 2).reshape(128, 64)
        pfm[:, l, Q_MCB:Q_MCB + 16] = _fm(g("ml_conv_b")[l])
        pfm[:, l, Q_QG] = np.tile(g("att_q_norm")[l], 2)
        pfm[:, l, Q_KG] = np.tile(g("att_k_norm")[l], 2)
    shared["prep"] = prep
    shared["pfm"] = pfm
    shared["fng"] = _fm(g("final_norm"))
    xp, xs = g("x_prompt"), g("x_sample")
    ncore = xs.shape[0]
    maps = []
    for i in range(ncore):
        toks = np.concatenate([xs[i][:SL], xp[2 * i], xp[2 * i + 1]], axis=0)
        x0 = np.ascontiguousarray(toks.reshape(-1, 8, 128).transpose(2, 1, 0))
        cvec = np.stack([_fm(g("c_ctx")), _fm(g("c")[i])], axis=-1)
        st = g("state_ssd")[i][:NL]
        st = st.reshape(NL, 2, 2, 2, 4, 64, 64)
        ssd0 = np.ascontiguousarray(st.transpose(0, 1, 3, 6, 2, 4, 5)).reshape(NL, 2, 128, 512)
        mc = g("state_ml_c")[i][:NL]
        mn = g("state_ml_n")[i][:NL]
        mlc0 = np.concatenate([mc.transpose(0, 1, 3, 2, 4), mn.transpose(0, 1, 3, 2)[..., None]], axis=-1)
        m = dict(shared)
        m.update({
            "x0": x0, "cvec": np.ascontiguousarray(cvec),
            "cache_k": np.ascontiguousarray(g("cache_k")[i][:NL].reshape(NL, 512, 256)),
            "cache_v": np.ascontiguousarray(g("cache_v")[i][:NL].reshape(NL, 512, 256)),
            "ssd0": ssd0, "mlc0": np.ascontiguousarray(mlc0).reshape(NL, 2, 128, 8 * 129),
            "mlm0": np.ascontiguousarray(g("state_ml_m")[i][:NL].reshape(NL, 2, 8, 1)),
        })
        maps.append(m)
    return maps


def _assemble(results, SB, NL):
    SL = 256 * SB
    n = len(results)
    y_s = np.zeros((n, SL, D), np.float32)
    y_p = np.zeros((2 * n, 256, D), np.float32)
    nk = np.zeros((2 * n, NL, 256, 4, 64), np.float32)
    nv = np.zeros((2 * n, NL, 256, 4, 64), np.float32)
    nssd = np.zeros((2 * n, NL, 2, 16, 64, 64), np.float32)
    nc_ = np.zeros((2 * n, NL, 2, 8, 128, 128), np.float32)
    nn_ = np.zeros((2 * n, NL, 2, 8, 128), np.float32)
    nm = np.zeros((2 * n, NL, 2, 8), np.float32)
    for i, r in enumerate(results):
        y = np.asarray(r["yT"]).transpose(2, 1, 0).reshape(-1, D)
        y_s[i] = y[:SL]
        for j in range(2):
            y_p[2 * i + j] = y[SL + 256 * j:SL + 256 * (j + 1)]
            k = np.asarray(r["nk_o"])[j]
            nk[2 * i + j] = k.transpose(0, 3, 2, 1).reshape(NL, 256, 4, 64)
            nv[2 * i + j] = np.asarray(r["nv_o"])[j].reshape(NL, 256, 4, 64)
            s_ = np.asarray(r["nssd_o"])[j].reshape(NL, 2, 2, 64, 2, 4, 64)
            nssd[2 * i + j] = s_.transpose(0, 1, 4, 2, 5, 6, 3).reshape(NL, 2, 16, 64, 64)
            c_ = np.asarray(r["nmlc_o"])[j].reshape(NL, 2, 128, 8, 129)
            nc_[2 * i + j] = c_[..., :128].transpose(0, 1, 3, 2, 4)
            nn_[2 * i + j] = c_[..., 128].transpose(0, 1, 3, 2)
            nm[2 * i + j] = np.asarray(r["nmlm_o"])[j].reshape(NL, 2, 8)
    return (y_p, y_s, nk, nv, nssd, nc_, nn_, nm)


_CACHE = {}


def run(inputs, SB=8, NL=4, debug=False, stop_after=None):
    key = (SB, NL, debug, stop_after)
    if key not in _CACHE:
        _CACHE[key] = build(SB, NL, debug, stop_after)
    nc = _CACHE[key]
    maps = _prepare(inputs, SB, NL)
    res = run_bass_kernel_spmd(nc, maps, core_ids=list(range(len(maps))))
    return _assemble(res.results, SB, NL), res


def kernel(**inputs):
    out, _ = run(inputs, 8, 4, False)
    return out
```
